# Optimizing a Trainium2 kernel written in Bass

```python
import jax, jax.numpy as jnp
from jax import lax
import numpy as np

D_MODEL = 1024
BATCH = 32
SEQ = 256
DEPTH = 2
DEC_BATCH = 4
DEC_SEQ = 2048
PAST_LEN = 512

GRID_W = 64
ROPE_BASE = 10000.0
EPS = 1e-6
NEG_INF = -1e30
F32 = jnp.float32
N_HEADS_A = 8
N_KV_A = 2
HD_A = 64
GQA_GROUP = N_HEADS_A // N_KV_A
WINDOW = 128
BAND_BLK = 128
N_HEADS_B = 8
QK_NOPE_B = 64
QK_ROPE_B = 32
V_HD_B = 64
Q_RANK_B = 384
KV_RANK_B = 256
MLA_SCALE = (QK_NOPE_B + QK_ROPE_B) ** -0.5
N_HEADS_C = 4
DK_C = 128
DV_C = 128
CONV_K = 3
CHUNK_C = 64
Q_BLOCK = 128
W_A = N_HEADS_A * HD_A
W_B = N_HEADS_B * V_HD_B
W_C = N_HEADS_C * DV_C
QKV_C = 2 * N_HEADS_C * DK_C + W_C
IN_SIZES = (W_A, N_KV_A * HD_A, N_KV_A * HD_A, W_A, Q_RANK_B, KV_RANK_B, QK_ROPE_B, W_B, QKV_C, 2 * N_HEADS_C, 2 * N_HEADS_C, W_C, 3 * D_MODEL)
IN_WIDTH = sum(IN_SIZES)

kernel_name = 'hybrid_diffusion_swa_mla_gdn_step'


def _rmsnorm(x, g):
    xf = x.astype(F32)
    y = xf * lax.rsqrt(jnp.mean(xf * xf, axis=-1, keepdims=True) + EPS)
    return (y * g.astype(F32)).astype(x.dtype)


def _l2norm(x):
    return x * lax.rsqrt(jnp.sum(x * x, axis=-1, keepdims=True) + EPS)


def _split_inputs(u):
    points = [int(p) for p in np.cumsum(IN_SIZES)[:-1]]
    return jnp.split(u, points, axis=-1)


def _axial_rope(n_tokens, rot_dim):
    rows = n_tokens // GRID_W
    row = jnp.repeat(jnp.arange(rows), GRID_W).astype(F32)
    col = jnp.tile(jnp.arange(GRID_W), rows).astype(F32)
    n_pairs = rot_dim // 4
    inv = ROPE_BASE ** (-jnp.arange(n_pairs, dtype=F32) / n_pairs)
    ang = jnp.concatenate([row[:, None] * inv, col[:, None] * inv], axis=-1)
    return jnp.cos(ang), jnp.sin(ang)


def _apply_rope(x, cos, sin):
    xf = x.astype(F32).reshape(*x.shape[:-1], -1, 2)
    x1, x2 = xf[..., 0], xf[..., 1]
    c, s = cos[:, None, :], sin[:, None, :]
    out = jnp.stack([x1 * c - x2 * s, x1 * s + x2 * c], axis=-1).reshape(x.shape)
    return out.astype(x.dtype)


def _attend_block(q, k, v, sink, scale):
    s = jnp.einsum('bqkgd,bskd->bkgqs', q, k).astype(F32) * scale
    m = s.max(-1, keepdims=True)
    if sink is None:
        e = jnp.exp(s - m)
        p = e / e.sum(-1, keepdims=True)
    else:
        sk = sink.astype(F32).reshape(1, k.shape[2], q.shape[3], 1, 1)
        m = jnp.maximum(m, sk)
        e = jnp.exp(s - m)
        p = e / (e.sum(-1, keepdims=True) + jnp.exp(sk - m))
    return jnp.einsum('bkgqs,bskd->bqkgd', p.astype(v.dtype), v)


def _dense_attention(q, k, v, sink, scale):
    B, Q = q.shape[:2]
    nb = Q // Q_BLOCK
    qb = jnp.moveaxis(q.reshape(B, nb, Q_BLOCK, *q.shape[2:]), 1, 0)
    out = lax.map(lambda qi: _attend_block(qi, k, v, sink, scale), qb)
    return jnp.moveaxis(out, 0, 1).reshape(B, Q, *out.shape[3:])


def _band_context_attention(q, k, v, k_ctx, v_ctx, sink, scale):
    B, T, K, G, d = q.shape
    nb = T // BAND_BLK
    qb = q.reshape(B, nb, BAND_BLK, K, G, d)

    def neighbours(a):
        ap = jnp.pad(a, ((0, 0), (BAND_BLK, BAND_BLK), (0, 0), (0, 0)))
        ap = ap.reshape(B, nb + 2, BAND_BLK, K, a.shape[-1])
        return jnp.concatenate([ap[:, :-2], ap[:, 1:-1], ap[:, 2:]], axis=2)

    kb, vb = neighbours(k), neighbours(v)
    n_loc = 3 * BAND_BLK
    qi = jnp.arange(BAND_BLK)[:, None]
    kj = jnp.arange(n_loc)[None, :]
    in_window = jnp.abs(BAND_BLK + qi - kj) <= WINDOW
    key_pos = (jnp.arange(nb)[:, None] - 1) * BAND_BLK + jnp.arange(n_loc)[None, :]
    in_range = (key_pos >= 0) & (key_pos < T)
    valid = in_window[None] & in_range[:, None, :]
    s_loc = jnp.einsum('bnqkgd,bnskd->bnkgqs', qb, kb).astype(F32) * scale
    s_loc = jnp.where(valid[None, :, None, None], s_loc, NEG_INF)
    s_ctx = jnp.einsum('bnqkgd,bckd->bnkgqc', qb, k_ctx).astype(F32) * scale
    s = jnp.concatenate([s_loc, s_ctx], axis=-1)
    sk = sink.astype(F32).reshape(1, 1, K, G, 1, 1)
    m = jnp.maximum(s.max(-1, keepdims=True), sk)
    e = jnp.exp(s - m)
    p = (e / (e.sum(-1, keepdims=True) + jnp.exp(sk - m))).astype(v.dtype)
    o = (jnp.einsum('bnkgqs,bnskd->bnqkgd', p[..., :n_loc], vb)
         + jnp.einsum('bnkgqc,bckd->bnqkgd', p[..., n_loc:], v_ctx))
    return o.reshape(B, T, K, G, d)


def _short_conv(x, w):
    pad = CONV_K // 2
    return lax.conv_general_dilated(x, w[:, None, :].astype(x.dtype), window_strides=(1,),
                                    padding=[(pad, pad)], dimension_numbers=('NWC', 'WIO', 'NWC'),
                                    feature_group_count=x.shape[-1])


def _gated_delta_chunked(q, k, v, g, beta, s0):
    B, T, H, _ = q.shape
    n = T // CHUNK_C

    def chunks(a):
        return jnp.moveaxis(a.reshape(B, n, CHUNK_C, H, *a.shape[3:]), 3, 2)

    q, k, v, g, beta = chunks(q), chunks(k), chunks(v), chunks(g), chunks(beta)
    g = jnp.cumsum(g, axis=-1)
    kb = k * beta[..., None]
    vb = v * beta[..., None]
    tri = jnp.tril(jnp.ones((CHUNK_C, CHUNK_C), bool))
    strict = jnp.tril(jnp.ones((CHUNK_C, CHUNK_C), bool), -1)
    diff = g[..., :, None] - g[..., None, :]
    decay = jnp.where(tri, jnp.exp(jnp.where(tri, diff, 0.0)), 0.0)
    low = jnp.where(strict, jnp.einsum('bnhid,bnhjd->bnhij', kb, k) * decay, 0.0)
    a_mat = low + jnp.eye(CHUNK_C, dtype=F32)
    u = lax.linalg.triangular_solve(a_mat, vb, left_side=True, lower=True)
    w = lax.linalg.triangular_solve(a_mat, kb * jnp.exp(g)[..., None], left_side=True, lower=True)
    attn = jnp.einsum('bnhid,bnhjd->bnhij', q, k) * decay

    def step(s, inp):
        qc, kc, uc, wc, ac, gc = inp
        v_new = uc - jnp.einsum('bhcd,bhde->bhce', wc, s)
        o = (jnp.einsum('bhcd,bhde->bhce', qc * jnp.exp(gc)[..., None], s)
             + jnp.einsum('bhij,bhje->bhie', ac, v_new))
        g_last = gc[..., -1]
        s = (s * jnp.exp(g_last)[..., None, None]
             + jnp.einsum('bhcd,bhce->bhde', kc * jnp.exp(g_last[..., None] - gc)[..., None], v_new))
        return s, o

    xs = tuple(jnp.moveaxis(a, 1, 0) for a in (q, k, u, w, attn, g))
    s_final, o = lax.scan(step, s0.astype(F32), xs)
    o = jnp.swapaxes(jnp.moveaxis(o, 0, 1), 2, 3).reshape(B, T, H, v.shape[-1])
    return o, s_final


def _gdn_inputs(c_qkv, c_a, c_b, p):
    B, T, _ = c_qkv.shape
    x = jax.nn.silu(_short_conv(c_qkv, p['gdn_conv'])).astype(F32)
    nq = N_HEADS_C * DK_C
    q = _l2norm(x[..., :nq].reshape(B, T, N_HEADS_C, DK_C)) * (DK_C ** -0.5)
    k = _l2norm(x[..., nq:2 * nq].reshape(B, T, N_HEADS_C, DK_C))
    v = x[..., 2 * nq:].reshape(B, T, N_HEADS_C, DV_C)
    a = c_a.astype(F32).reshape(B, T, 2, N_HEADS_C)
    g = -jnp.exp(p['gdn_a_log'].astype(F32)) * jax.nn.softplus(a + p['gdn_dt_bias'].astype(F32))
    beta = jax.nn.sigmoid(c_b.astype(F32).reshape(B, T, 2, N_HEADS_C))
    return q, k, v, g, beta


def _gdn_bidirectional(q, k, v, g, beta, s_fwd, s_bwd, norm_g):
    o_f, st_f = _gated_delta_chunked(q, k, v, g[:, :, 0], beta[:, :, 0], s_fwd)
    fl = lambda a: jnp.flip(a, axis=1)
    o_b, st_b = _gated_delta_chunked(fl(q), fl(k), fl(v), fl(g[:, :, 1]), fl(beta[:, :, 1]), s_bwd)
    o = _rmsnorm(o_f + fl(o_b), norm_g)
    return o.reshape(o.shape[0], o.shape[1], W_C), jnp.stack([st_f, st_b], axis=1)


def _mla_queries(b_cq, p):
    B, T, _ = b_cq.shape
    q = (_rmsnorm(b_cq, p['mla_q_norm']) @ p['mla_w_uq']).reshape(B, T, N_HEADS_B, QK_NOPE_B + QK_ROPE_B)
    return q[..., :QK_NOPE_B], q[..., QK_NOPE_B:]


def _mla_keys_values(ckv_n, kpe, w_ukv):
    B, T, _ = ckv_n.shape
    kv = (ckv_n @ w_ukv).reshape(B, T, N_HEADS_B, QK_NOPE_B + V_HD_B)
    k_pe = jnp.broadcast_to(kpe[:, :, None, :], (B, T, N_HEADS_B, QK_ROPE_B)).astype(kv.dtype)
    return jnp.concatenate([kv[..., :QK_NOPE_B], k_pe], axis=-1), kv[..., QK_NOPE_B:]


def _modulated_inputs(x, mod, p):
    shift, scale, gate = jnp.split(mod, 3, axis=-1)
    h = _rmsnorm(x, p['norm_g']) * (1.0 + scale) + shift
    return _split_inputs(h @ p['w_in']), gate


def _merge_branches(x, o_a, o_b, o_c, z_a, z_b, z_c, gates, gate, p):
    dt = x.dtype
    p_a = (o_a.astype(dt) * jax.nn.silu(z_a)) @ p['w_branch_a']
    p_b = (o_b.astype(dt) * jax.nn.silu(z_b)) @ p['w_branch_b']
    p_c = (o_c.astype(dt) * jax.nn.silu(z_c)) @ p['w_branch_c']
    g_a, g_b, g_c = jnp.split(jax.nn.sigmoid(gates), 3, axis=-1)
    y = (g_a * p_a + g_b * p_b + g_c * p_c) @ p['w_out']
    return x + gate * y


def _layer_context(x, mod, p):
    B, T, _ = x.shape
    (a_q, a_k, a_v, z_a, b_cq, b_ckv, b_kpe, z_b, c_qkv, c_a, c_b, z_c, gates), gate = _modulated_inputs(x, mod, p)
    k_a = a_k.reshape(B, T, N_KV_A, HD_A)
    v_a = a_v.reshape(B, T, N_KV_A, HD_A)
    q_a = a_q.reshape(B, T, N_KV_A, GQA_GROUP, HD_A)
    o_a = _dense_attention(q_a, k_a, v_a, p['attn_sink'], HD_A ** -0.5).reshape(B, T, W_A)
    q_nope, q_pe = _mla_queries(b_cq, p)
    ckv_n = _rmsnorm(b_ckv, p['mla_kv_norm'])
    k_b, v_b = _mla_keys_values(ckv_n, b_kpe, p['mla_w_ukv'])
    q_b = jnp.concatenate([q_nope, q_pe], axis=-1)[:, :, :, None, :]
    o_b = _dense_attention(q_b, k_b, v_b, None, MLA_SCALE).reshape(B, T, W_B)
    q_c, k_c, v_c, g_c, beta_c = _gdn_inputs(c_qkv, c_a, c_b, p)
    s0 = jnp.zeros((B, N_HEADS_C, DK_C, DV_C), F32)
    o_c, st = _gdn_bidirectional(q_c, k_c, v_c, g_c, beta_c, s0, s0, p['gdn_norm'])
    y = _merge_branches(x, o_a, o_b, o_c, z_a, z_b, z_c, gates, gate, p)
    return y, (k_a, v_a, ckv_n, b_kpe, st)


def _layer_latent(x, mod, p, rope_a, rope_b, k_a_ctx, v_a_ctx, ckv_ctx, kpe_ctx, st_ctx):
    B, T, _ = x.shape
    (a_q, a_k, a_v, z_a, b_cq, b_ckv, b_kpe, z_b, c_qkv, c_a, c_b, z_c, gates), gate = _modulated_inputs(x, mod, p)
    q_a = _apply_rope(a_q.reshape(B, T, N_HEADS_A, HD_A), *rope_a).reshape(B, T, N_KV_A, GQA_GROUP, HD_A)
    k_a = _apply_rope(a_k.reshape(B, T, N_KV_A, HD_A), *rope_a)
    v_a = a_v.reshape(B, T, N_KV_A, HD_A)
    o_a = _band_context_attention(q_a, k_a, v_a, k_a_ctx, v_a_ctx, p['attn_sink'], HD_A ** -0.5).reshape(B, T, W_A)
    q_nope, q_pe = _mla_queries(b_cq, p)
    q_pe = _apply_rope(q_pe, *rope_b)
    ckv_n = _rmsnorm(b_ckv, p['mla_kv_norm'])
    kpe = _apply_rope(b_kpe[:, :, None, :], *rope_b)[:, :, 0, :]
    k_l, v_l = _mla_keys_values(ckv_n, kpe, p['mla_w_ukv'])
    k_c, v_c = _mla_keys_values(ckv_ctx, kpe_ctx, p['mla_w_ukv'])
    k_b = jnp.concatenate([k_l, k_c], axis=1)
    v_b = jnp.concatenate([v_l, v_c], axis=1)
    q_b = jnp.concatenate([q_nope, q_pe], axis=-1)[:, :, :, None, :]
    o_b = _dense_attention(q_b, k_b, v_b, None, MLA_SCALE).reshape(B, T, W_B)
    q_c, k_c2, v_c2, g_c, beta_c = _gdn_inputs(c_qkv, c_a, c_b, p)
    o_c, _ = _gdn_bidirectional(q_c, k_c2, v_c2, g_c, beta_c, st_ctx[:, 0], st_ctx[:, 1], p['gdn_norm'])
    return _merge_branches(x, o_a, o_b, o_c, z_a, z_b, z_c, gates, gate, p)


def setup_inputs(seed: int = 0) -> dict:
    key = jax.random.key(seed)
    ks = iter(jax.random.split(key, 40))

    def nrm(shape, s=1.0):
        return jax.random.normal(next(ks), shape, F32) * s

    def gain(shape):
        return 1.0 + nrm(shape, 0.02)

    dt = jnp.exp(jax.random.uniform(next(ks), (DEPTH, 2, N_HEADS_C), F32,
                                    minval=float(np.log(1e-3)), maxval=float(np.log(0.1))))
    return {
        'x_prompt': nrm((BATCH, SEQ, D_MODEL)),
        'x_sample': nrm((DEC_BATCH, DEC_SEQ, D_MODEL)),
        'cache_attn_k': nrm((DEC_BATCH, DEPTH, PAST_LEN, N_KV_A, HD_A)),
        'cache_attn_v': nrm((DEC_BATCH, DEPTH, PAST_LEN, N_KV_A, HD_A)),
        'cache_mla_ckv': nrm((DEC_BATCH, DEPTH, PAST_LEN, KV_RANK_B)),
        'cache_mla_kpe': nrm((DEC_BATCH, DEPTH, PAST_LEN, QK_ROPE_B)),
        'state_gdn': nrm((DEC_BATCH, DEPTH, 2, N_HEADS_C, DK_C, DV_C), 0.3),
        'c': nrm((DEC_BATCH, D_MODEL)),
        'c_ctx': nrm((D_MODEL,)),
        'norm_g': gain((DEPTH, D_MODEL)),
        'w_ada': nrm((DEPTH, D_MODEL, 3 * D_MODEL), 0.5 * D_MODEL ** -0.5),
        'b_ada': nrm((DEPTH, 3 * D_MODEL), 0.02),
        'w_in': nrm((DEPTH, D_MODEL, IN_WIDTH), D_MODEL ** -0.5),
        'attn_sink': nrm((DEPTH, N_HEADS_A)),
        'mla_q_norm': gain((DEPTH, Q_RANK_B)),
        'mla_w_uq': nrm((DEPTH, Q_RANK_B, N_HEADS_B * (QK_NOPE_B + QK_ROPE_B)), Q_RANK_B ** -0.5),
        'mla_kv_norm': gain((DEPTH, KV_RANK_B)),
        'mla_w_ukv': nrm((DEPTH, KV_RANK_B, N_HEADS_B * (QK_NOPE_B + V_HD_B)), KV_RANK_B ** -0.5),
        'gdn_conv': nrm((DEPTH, CONV_K, QKV_C), CONV_K ** -0.5),
        'gdn_a_log': jnp.log(jax.random.uniform(next(ks), (DEPTH, 2, N_HEADS_C), F32, minval=1.0, maxval=16.0)),
        'gdn_dt_bias': dt + jnp.log(-jnp.expm1(-dt)),
        'gdn_norm': gain((DEPTH, DV_C)),
        'w_branch_a': nrm((DEPTH, W_A, D_MODEL), W_A ** -0.5),
        'w_branch_b': nrm((DEPTH, W_B, D_MODEL), W_B ** -0.5),
        'w_branch_c': nrm((DEPTH, W_C, D_MODEL), W_C ** -0.5),
        'w_out': nrm((DEPTH, D_MODEL, D_MODEL), D_MODEL ** -0.5),
        'final_norm_g': gain((D_MODEL,)),
    }


def reference(x_prompt, x_sample, cache_attn_k, cache_attn_v, cache_mla_ckv, cache_mla_kpe, state_gdn,
              c, c_ctx, norm_g, w_ada, b_ada, w_in, attn_sink, mla_q_norm, mla_w_uq, mla_kv_norm,
              mla_w_ukv, gdn_conv, gdn_a_log, gdn_dt_bias, gdn_norm, w_branch_a, w_branch_b,
              w_branch_c, w_out, final_norm_g):
    t_lat = x_sample.shape[1]
    rope_a = _axial_rope(t_lat, HD_A)
    rope_b = _axial_rope(t_lat, QK_ROPE_B)
    y_p, y_s = x_prompt, x_sample
    ks, vs, ckvs, kpes, sts = [], [], [], [], []
    for l in range(DEPTH):
        p = {'norm_g': norm_g[l], 'w_in': w_in[l], 'attn_sink': attn_sink[l],
             'mla_q_norm': mla_q_norm[l], 'mla_w_uq': mla_w_uq[l], 'mla_kv_norm': mla_kv_norm[l],
             'mla_w_ukv': mla_w_ukv[l], 'gdn_conv': gdn_conv[l], 'gdn_a_log': gdn_a_log[l],
             'gdn_dt_bias': gdn_dt_bias[l], 'gdn_norm': gdn_norm[l], 'w_branch_a': w_branch_a[l],
             'w_branch_b': w_branch_b[l], 'w_branch_c': w_branch_c[l], 'w_out': w_out[l]}
        mod_ctx = (jax.nn.silu(c_ctx) @ w_ada[l] + b_ada[l])[None, None, :]
        mod_lat = (jax.nn.silu(c) @ w_ada[l] + b_ada[l])[:, None, :]
        y_p, (k_l, v_l, ckv_l, kpe_l, st_l) = _layer_context(y_p, mod_ctx, p)
        ks.append(k_l)
        vs.append(v_l)
        ckvs.append(ckv_l)
        kpes.append(kpe_l)
        sts.append(st_l)
        y_s = _layer_latent(y_s, mod_lat, p, rope_a, rope_b, cache_attn_k[:, l], cache_attn_v[:, l],
                            cache_mla_ckv[:, l], cache_mla_kpe[:, l], state_gdn[:, l])
    y_prompt = _rmsnorm(y_p, final_norm_g)
    y_sample = _rmsnorm(y_s, final_norm_g)
    new_attn_k = jnp.stack(ks, axis=1)
    new_attn_v = jnp.stack(vs, axis=1)
    new_mla_ckv = jnp.stack(ckvs, axis=1)
    new_mla_kpe = jnp.stack(kpes, axis=1)
    new_state_gdn = jnp.stack(sts, axis=1)
    return (y_prompt, y_sample, new_attn_k, new_attn_v, new_mla_ckv, new_mla_kpe, new_state_gdn)
```

```python
import numpy as np
from contextlib import ExitStack
import concourse.bass as bass
import concourse.mybir as mybir
from concourse.bass_utils import run_bass_kernel_spmd

F32 = mybir.dt.float32
BF16 = mybir.dt.bfloat16
AF = mybir.ActivationFunctionType
ALU = mybir.AluOpType

T = 2048
NB = 16
NSB = 4
KT = 2560
NKB = 20
D = 1024
BIGM = 2048.0
NEG = -30000.0
N_DSEM = 40
LIMIT = None
LAST_KB = None
C_AQ, C_AK, C_AV, C_ZA, C_BCQ, C_BCKV, C_BKPE, C_ZB, C_CQKV, C_CA, C_CB, C_ZC, C_G = (
    0, 512, 640, 768, 1280, 1664, 1920, 1952, 2464, 4000, 4008, 4016, 4528)


class KB:
    def __init__(self, nc, es):
        self.nc = nc
        self.E = {'pe': nc.tensor, 'act': nc.scalar, 'dve': nc.vector, 'pool': nc.gpsimd, 'sp': nc.sync}
        self.sem = {e: es.enter_context(nc.semaphore("s_" + e)) for e in self.E}
        self.cnt = {e: 0 for e in self.E}
        self.seen = {e: {} for e in self.E}
        self.dsem = [es.enter_context(nc.semaphore("d%d" % i)) for i in range(N_DSEM)]
        self.dcnt = [0] * N_DSEM
        self.dnext = 0
        self.reg = {}
        self.n_ins = 0
        self.limit = LIMIT
        self.n_calls = 0

    def _wait(self, eng, tok):
        if tok is None:
            return
        key = (tok[0], tok[1])
        if self.seen[eng].get(key, 0) >= tok[2]:
            return
        if eng == 'pe' and tok[0] == 'e' and tok[1] == 'pe':
            return
        if tok[0] == 'e':
            self.E[eng].wait_ge(self.sem[tok[1]], tok[2])
        else:
            self.E[eng].wait_ge(self.dsem[tok[1]], tok[2])
        self.seen[eng][key] = tok[2]

    def _entries(self, r):
        if isinstance(r, tuple):
            name, sub = r[0], (r[1] if len(r) == 2 else r[1:])
        else:
            name, sub = r, None
        d = self.reg.setdefault(name, {})
        if sub is None:
            if None not in d:
                d[None] = [None, []]
            return [d[k] for k in d], d, None
        out = []
        if None in d:
            out.append(d[None])
        if sub not in d:
            d[sub] = [None, []]
        out.append(d[sub])
        return out, d, sub

    @staticmethod
    def _norm(reads, writes):
        r2, w2 = [], []
        for r in reads:
            nm = r[0] if isinstance(r, tuple) else r
            if nm.startswith("pb"):
                w2.append(nm)
            else:
                r2.append(r)
        for w in writes:
            nm = w[0] if isinstance(w, tuple) else w
            w2.append(nm if nm.startswith("pb") else w)
        return r2, w2

    def _deps(self, eng, reads, writes):
        for r in reads:
            for en in self._entries(r)[0]:
                self._wait(eng, en[0])
        for r in writes:
            for en in self._entries(r)[0]:
                self._wait(eng, en[0])
                for t in en[1]:
                    self._wait(eng, t)

    def _record(self, tok, reads, writes):
        for r in reads:
            _, d, sub = self._entries(r)
            lst = d[sub][1]
            if tok[0] == 'e':
                lst[:] = [t for t in lst if not (t[0] == 'e' and t[1] == tok[1])]
            lst.append(tok)
            if len(lst) > 48:
                del lst[0:len(lst) - 48]
        for r in writes:
            _, d, sub = self._entries(r)
            if sub is None:
                for k in list(d.keys()):
                    if k is not None:
                        del d[k]
            d[sub] = [tok, []]

    def op(self, eng, fn, reads=(), writes=()):
        reads, writes = self._norm(reads, writes)
        self.n_calls += 1
        if self.limit is not None and self.n_calls > self.limit:
            return None
        self._deps(eng, reads, writes)
        ins = fn(self.E[eng])
        self.cnt[eng] += 1
        ins.then_inc(self.sem[eng], 1)
        tok = ('e', eng, self.cnt[eng])
        self._record(tok, reads, writes)
        self.n_ins += 1
        return tok

    def mmgroup(self, fns, reads=(), writes=()):
        reads, writes = self._norm(reads, writes)
        self.n_calls += 1
        if self.limit is not None and self.n_calls > self.limit:
            return None
        self._deps('pe', reads, writes)
        ins = None
        for f in fns:
            ins = f(self.E['pe'])
        self.cnt['pe'] += 1
        ins.then_inc(self.sem['pe'], 1)
        tok = ('e', 'pe', self.cnt['pe'])
        self._record(tok, reads, writes)
        self.n_ins += len(fns)
        return tok

    def dma(self, q, out, in_, reads=(), writes=(), **kw):
        reads, writes = self._norm(reads, writes)
        self.n_calls += 1
        if self.limit is not None and self.n_calls > self.limit:
            return None
        self._deps(q, reads, writes)
        s = self.dnext
        self.dnext = (self.dnext + 1) % N_DSEM
        if self.dcnt[s] > 0:
            self._wait(q, ('d', s, 16 * self.dcnt[s]))
        self.dcnt[s] += 1
        self.E[q].dma_start(out=out, in_=in_, **kw).then_inc(self.dsem[s], 16)
        tok = ('d', s, 16 * self.dcnt[s])
        self._record(tok, reads, writes)
        self.n_ins += 1
        return tok

    def mark(self, label):
        self.marks = getattr(self, "marks", [])
        self.marks.append((label, dict(self.cnt)))
        global LAST_KB
        LAST_KB = self

    def barrier(self):
        for e in self.E:
            for e2 in self.E:
                if e2 != e and self.cnt[e2] > 0:
                    self._wait(e, ('e', e2, self.cnt[e2]))
            for sx in range(N_DSEM):
                if self.dcnt[sx] > 0:
                    self._wait(e, ('d', sx, 16 * self.dcnt[sx]))

    def finish(self):
        for e in self.E:
            if self.cnt[e] > 0:
                self._wait('sp', ('e', e, self.cnt[e]))
        for s in range(N_DSEM):
            if self.dcnt[s] > 0:
                self._wait('sp', ('d', s, 16 * self.dcnt[s]))


IN_SPECS = [
    ("x", [T, D]), ("cond", [D]), ("norm_g", [2, D]), ("w_ada", [2, D, 3 * D]), ("b_ada", [2, 3 * D]),
    ("w_in", [2, D, 7600]), ("w_inp", [2, D, 672]), ("attn_sink", [2, 8]), ("mla_q_norm", [2, 384]),
    ("mla_w_uq", [2, 384, 768]), ("mla_w_uqp", [2, 384, 768]), ("mla_kv_norm", [2, 256]),
    ("mla_w_ukv", [2, 256, 1024]), ("gdn_conv", [2, 3, 1536]), ("gdn_a_log", [2, 8]), ("gdn_dt_bias", [2, 8]),
    ("gdn_norm", [2, 128]), ("w_branch_a", [2, 512, D]), ("w_branch_b", [2, 512, D]), ("w_branch_c", [2, 512, D]),
    ("w_out", [2, D, D]), ("final_norm_g", [D]),
    ("ck", [2, 512, 128]), ("cv", [2, 512, 128]), ("cckv", [2, 512, 256]), ("ckpe", [2, 512, 32]),
    ("st", [2, 8, 128, 128]),
    ("ropeC", [128, T]), ("ropeS", [128, T]), ("maskA", [6, 128, 512]), ("qoh", [8, T]), ("koh", [8, KT]),
    ("flags", [128, 4]), ("ident", [128, 128]), ("gm1", [128, 8, 128]), ("gm2", [128, 8, 128]),
    ("triF", [128, 128]), ("triB", [128, 128]), ("sel1", [128, 128]), ("sel2", [128, 128]),
]
OUT_SPECS = [
    ("y", [T, D]), ("nk", [2, T, 128]), ("nv", [2, T, 128]), ("nckv", [2, T, 256]), ("nkpe", [2, T, 32]),
    ("nst", [2, 8, 2, 4, 128, 128]),
]


def build(stop=None, taps=None):
    nc = bass.Bass("TRN2", target_bir_lowering=False)
    I = {n: nc.dram_tensor(n, s, F32, kind="ExternalInput").ap() for n, s in IN_SPECS}
    O = {n: nc.dram_tensor(n, s, F32, kind="ExternalOutput").ap() for n, s in OUT_SPECS}
    xs = nc.dram_tensor("xs", [T, D], F32, kind="Internal").ap()
    TAP = {}
    if taps:
        for n, s in taps.items():
            TAP[n] = nc.dram_tensor("tap_" + n, s, F32, kind="ExternalOutput").ap()
    with ExitStack() as es:
        kb = KB(nc, es)
        SB = lambda name, shape, dt: es.enter_context(nc.sbuf_tensor("sb_" + name, shape, dt))
        PS = lambda name, shape, dt: es.enter_context(nc.psum_tensor("ps_" + name, shape, dt))
        _body(nc, kb, SB, PS, I, O, xs, TAP, stop)
        kb.mark('end')
        kb.finish()
    return nc


def _body(nc, kb, SB, PS, I, O, xs, TAP, stop):
    op, dma, mm = kb.op, kb.dma, kb.mmgroup
    ident = SB("ident", [128, 128], BF16)
    ident32 = SB("ident32", [128, 128], F32)
    ropeC = SB("ropeC", [128, T], BF16)
    ropeS = SB("ropeS", [128, T], BF16)
    flags = SB("flags", [128, 4], F32)
    ones16 = SB("ones16", [128, 128], BF16)
    op('dve', lambda e: e.memset(ones16[:], 1.0), writes=["ones16"])
    dma('pool', ident[:], I["ident"], writes=["ident"])
    dma('sp', ident32[:], I["ident"], writes=["ident32"])
    dma('pool', ropeC[:], I["ropeC"], writes=["ropeC"])
    dma('pool', ropeS[:], I["ropeS"], writes=["ropeS"])
    dma('sp', flags[:], I["flags"], writes=["flags"])

    hT = SB("hT", [128, 8, T], BF16)
    mergeT = SB("mergeT", [128, 8, T], BF16)
    ozT = SB("ozT", [128, 4, T], BF16)
    pb = [PS("pb%d" % i, [128, 512], F32) for i in range(8)]
    pbn = ["pb%d" % i for i in range(8)]

    def hTr(sb):
        return [("hT", sb * 4 + i) for i in range(4)]

    NW = 2
    WCOLS = 512
    wbuf = [SB("wbuf%d" % i, [128, 8 * WCOLS], BF16) for i in range(NW)]
    wstate = {'i': 0}

    def load_w(src2d, kch, cols, q='pool', prows=128):
        i = wstate['i']
        wstate['i'] = (i + 1) % NW
        name = "wbuf%d" % i
        tot = sum(n for _, n in cols)
        assert kch * tot <= 8 * WCOLS, (kch, tot)
        view = wbuf[i][0:prows, 0:kch * tot].rearrange("p (k n) -> p k n", k=kch)
        srcv = src2d.rearrange("(k p) n -> p k n", p=prows)
        o = 0
        for c0, n in cols:
            dma(q, view[:, :, o:o + n], srcv[:, :, c0:c0 + n], writes=[name])
            o += n
        return view, name

    def tap(name, ap_sb, reads):
        if name in TAP:
            dma('sp', TAP[name], ap_sb, reads=reads)

    condsb = SB("condsb", [128, 8], F32)
    scond = SB("scond", [128, 8], BF16)
    modfm = SB("modfm", [128, 24], F32)
    badafm = SB("badafm", [128, 24], F32)
    ngfm = SB("ngfm", [128, 8], F32)
    Afm = SB("Afm", [128, 8], F32)
    gbc = SB("gbc", [128, 8, 128], F32)
    gateb = SB("gateb", [128, D], F32)
    xblk = [SB("xblk%d" % i, [128, D], F32) for i in range(2)]
    xn = [SB("xn%d" % i, [128, D], BF16) for i in range(2)]
    junk = SB("junk", [128, D], BF16)
    stat = SB("stat", [128, NB, 4], F32)
    qTh = [SB("qTh%d" % i, [104, T], BF16) for i in range(1)] * 2
    wukv = SB("wukv", [128, 2, 1024], BF16)
    ARN = 21952
    arena = SB("arena", [128, ARN], BF16)
    maskA = arena[:, 4 * KT:4 * KT + 3072].rearrange("p (o n) -> p o n", o=6)
    o_ = 0
    kTa = arena[0:64, 0:2 * KT].rearrange("p (g n) -> p g n", g=2)
    Va = arena[:, 2 * KT:2 * KT + NKB * 256].rearrange("p (k g d) -> p k g d", k=NKB, g=2)
    kTb1 = arena[0:104, 0:KT]
    kpeT = arena[0:96, KT:2 * KT]
    Vb1 = arena[:, 2 * KT:2 * KT + NKB * 128].rearrange("p (k d) -> p k d", k=NKB)
    o_ = 2 * KT + NKB * 128
    ckvT = arena[:, o_:o_ + 2 * KT].rearrange("p (c n) -> p c n", c=2)
    cqnT = arena[:, o_ + 2 * KT:o_ + 2 * KT + 3 * T].rearrange("p (c n) -> p c n", c=3)
    kTb = [kTb1, kTb1]
    Vb = [Vb1, Vb1]
    pT = [SB("pT%d" % i, [128, 512], BF16) for i in range(3)]
    t1 = [SB("t1_%d" % i, [128, 512], F32) for i in range(2)]
    t2 = [SB("t2_%d" % i, [128, 512], F32) for i in range(2)]
    rr = [SB("rr%d" % i, [128, 512], F32) for i in range(1)] * 2
    r3 = [SB("r3_%d" % i, [64, 512], F32) for i in range(1)] * 2
    kvout = [SB("kvout%d" % i, [128, 288], F32) for i in range(2)]
    kvn16 = [SB("kvn16_%d" % i, [128, 384], BF16) for i in range(2)]
    ctx16 = SB("ctx16", [128, 4, 256], BF16)
    ctxp = SB("ctxp", [128, 4, 96], BF16)
    esink = SB("esink", [128, 8], F32)
    kvng = SB("kvng", [128, 256], F32)
    qng = SB("qng", [128, 384], F32)
    st2 = SB("st2", [128, NB, 4], F32)
    sig = [SB("sig%d" % i, [128, 512], F32) for i in range(2)]
    op('dve', lambda e: e.memset(ctxp[:], 0.0), writes=["ctxp"])
    dma('pool', qTh[0][96:104, :], I["qoh"], writes=["qTh0"])
    cnt = {'rot': 0, 'p': 0, 'o': 0}
    if stop == "c":
        return

    pT.append(SB("pT3", [128, 512], BF16))

    def attend_stream(groups):
        SBK = [2, 3, 6, 7]
        tiles = []
        for gi, g_ in enumerate(groups):
            g_['ob'] = 4 + cnt['o'] % 2
            cnt['o'] += 1
            for idx in range(len(g_['klist'])):
                tiles.append((gi, idx))
        info = {}

        def emit_S(t):
            gi, idx = tiles[t]
            g_ = groups[gi]
            kblk, mi, isctx = g_['klist'][idx]
            sbk = SBK[cnt['p'] % 4]
            pt = cnt['p'] % 4
            cnt['p'] += 1
            kap, kname = g_['kfn'](kblk)
            qtile, sbi, K = g_['qtile'], g_['sbi'], g_['K']
            fns = [lambda e: e.matmul(pb[sbk][:], kap, qtile[0:K, sbi * 512:(sbi + 1) * 512], start=True, stop=(mi is None))]
            rd = [kname, (g_['qname'], sbi)]
            if mi is not None:
                fns.append(lambda e: e.matmul(pb[sbk][:], ident[:], maskA[:, mi, :], start=False, stop=True))
                rd += ["ident", "maskA"]
            mm(fns, reads=rd, writes=[pbn[sbk]])
            b_ = g_['bias_fn'](isctx)
            op('act', lambda e: e.activation(pT[pt][:], pb[sbk][:], AF.Exp, scale=g_['scale'], bias=b_),
               reads=[pbn[sbk], "flags"], writes=["pT%d" % pt])
            info[t] = pt

        LOOK = 3
        nt = len(tiles)
        for t in range(min(LOOK, nt)):
            emit_S(t)
        for t in range(nt):
            if t + LOOK < nt:
                emit_S(t + LOOK)
            gi, idx = tiles[t]
            g_ = groups[gi]
            n = len(g_['klist'])
            kblk = g_['klist'][idx][0]
            pt = info[t]
            ob = g_['ob']
            vap, vname = g_['vfn'](kblk)
            mm([lambda e: e.matmul(pb[ob][:], vap, pT[pt][:], start=(idx == 0), stop=(idx == n - 1))],
               reads=[vname, "pT%d" % pt], writes=[pbn[ob]])
            if idx == n - 1:
                g_['fin'](ob)

    def run_pairs(genf, n):
        for b0_ in range(0, n, 2):
            gs = [genf(b0_), genf(b0_ + 1)]
            alive = [True, True]
            while any(alive):
                for q_ in range(2):
                    if alive[q_]:
                        try:
                            next(gs[q_])
                        except StopIteration:
                            alive[q_] = False

    for l in range(2):
        xsrc = I["x"] if l == 0 else xs
        Win = I["w_in"][l]
        Winp = I["w_inp"][l]
        kb.mark('L%d start' % l)
        dma('sp', condsb[:], I["cond"].rearrange("(c p) -> p c", p=128), writes=["cond"], allow_slow_non_contiguous=True)
        dma('sp', badafm[:], I["b_ada"][l].rearrange("(c p) -> p c", p=128), writes=["bada"], allow_slow_non_contiguous=True)
        dma('sp', ngfm[:], I["norm_g"][l].rearrange("(c p) -> p c", p=128), writes=["ngfm"], allow_slow_non_contiguous=True)
        op('act', lambda e: e.activation(scond[:], condsb[:], AF.Silu), reads=["cond"], writes=["scond"])
        for nt in range(6):
            wv, wn = load_w(I["w_ada"][l], 8, [(nt * 512, 512)])
            for jj in range(4):
                j = nt * 4 + jj
                mm([(lambda e, c=c, jj=jj, j=j, wv=wv: e.matmul(pb[0][:, j:j + 1], wv[:, c, jj * 128:(jj + 1) * 128], scond[:, c:c + 1],
                                                               start=(c == 0), stop=(c == 7))) for c in range(8)],
                   reads=[wn, "scond"], writes=[(pbn[0], j)])
        op('dve', lambda e: e.tensor_tensor(modfm[:], pb[0][:, 0:24], badafm[:], ALU.add), reads=[pbn[0], "bada"], writes=["modfm"])
        op('dve', lambda e: e.scalar_tensor_tensor(Afm[:], modfm[:, 8:16], 1.0, ngfm[:], ALU.add, ALU.mult), reads=["modfm", "ngfm"], writes=["Afm"])
        op('dve', lambda e: e.tensor_copy(gbc[:], modfm[:, 16:24].unsqueeze(2).to_broadcast([128, 8, 128])), reads=["modfm"], writes=["gbc"])
        for c in range(8):
            bkc = 1 + c // 4
            mm([lambda e, c=c, bkc=bkc: e.matmul(pb[bkc][:, (c % 4) * 128:(c % 4 + 1) * 128], gbc[:, c, :], ident32[:], start=True, stop=True)],
               reads=["gbc", "ident32"], writes=[(pbn[bkc], c % 4)])
        op('act', lambda e: e.copy(gateb[:, 0:512], pb[1][:]), reads=[pbn[1]], writes=[("gateb", 0)])
        op('act', lambda e: e.copy(gateb[:, 512:1024], pb[2][:]), reads=[pbn[2]], writes=[("gateb", 1)])
        tap("modfm%d" % l, modfm[:], ["modfm"])
        if stop == "p0":
            return

        kb.mark('L%d p1' % l)
        def gen_p1(b):
            xb, xbn = xblk[b % 2], "xblk%d" % (b % 2)
            xnb, xnn = xn[b % 2], "xn%d" % (b % 2)
            dma('sp', xb[:], xsrc[b * 128:(b + 1) * 128, :], reads=(["xs"] if l == 1 else []), writes=[xbn])
            yield None
            op('act', lambda e: e.activation(junk[:], xb[:], AF.Square, accum_out=stat[:, b, 0:1]), reads=[xbn], writes=["junk", ("stat", b)])
            yield None
            op('dve', lambda e: e.tensor_scalar(stat[:, b, 1:2], stat[:, b, 0:1], 1.0 / D, 1e-6, ALU.mult, ALU.add), reads=[("stat", b)], writes=[("stat", b)])
            yield None
            op('act', lambda e: e.activation(stat[:, b, 2:3], stat[:, b, 1:2], AF.Ln), reads=[("stat", b)], writes=[("stat", b)])
            yield None
            op('act', lambda e: e.activation(stat[:, b, 3:4], stat[:, b, 2:3], AF.Exp, scale=-0.5), reads=[("stat", b)], writes=[("stat", b)])
            yield None
            op('dve', lambda e: e.tensor_scalar(xnb[:], xb[:], stat[:, b, 3:4], None, ALU.mult), reads=[xbn, ("stat", b)], writes=[xnn])
            yield None
            for half in range(2):
                bk = 4 + (2 * b + half) % 4
                pview = pb[bk][:].bitcast(BF16)
                mm([(lambda e, c=c, half=half, pview=pview: e.transpose(pview[:, c * 128:(c + 1) * 128], xnb[:, (half * 4 + c) * 128:(half * 4 + c + 1) * 128], ident[:]))
                    for c in range(4)], reads=[xnn, "ident"], writes=[pbn[bk]])
                yield None
                for c in range(4):
                    cc = half * 4 + c
                    if True:
                        op('act', lambda e, c=c, cc=cc, pview=pview: e.activation(hT[:, cc, b * 128:(b + 1) * 128], pview[:, c * 128:(c + 1) * 128], AF.Identity,
                                                                                 scale=Afm[:, cc:cc + 1], bias=modfm[:, cc:cc + 1]),
                           reads=[pbn[bk], "Afm", "modfm"], writes=[("hT", b, cc)])
                        yield None
                    else:
                        op('dve', lambda e, c=c, cc=cc, pview=pview: e.scalar_tensor_tensor(hT[:, cc, b * 128:(b + 1) * 128], pview[:, c * 128:(c + 1) * 128],
                                                                                    Afm[:, cc:cc + 1], modfm[:, cc:cc + 1].to_broadcast([128, 128]), ALU.mult, ALU.add),
                           reads=[pbn[bk], "Afm", "modfm"], writes=[("hT", b, cc)])
                        yield None
        run_pairs(gen_p1, NB)
        op('pool', lambda e: e.memset(junk[0:1, 0:1], 0.0), reads=[], writes=["hT"])
        HR = ["hT"]
        if "hT%d" % l in TAP:
            for c8 in range(8):
                op('dve', lambda e: e.tensor_copy(xblk[0][:].rearrange("p (a b) -> p a b", a=1)[:, 0, :], hT[:, c8, 0:1024]), reads=["hT"], writes=["xblk0"])
                dma('sp', TAP["hT%d" % l][:, c8, 0:1024], xblk[0][:], reads=["xblk0"])
                op('dve', lambda e: e.tensor_copy(xblk[0][:], hT[:, c8, 1024:2048]), reads=["hT"], writes=["xblk0"])
                dma('sp', TAP["hT%d" % l][:, c8, 1024:2048], xblk[0][:], reads=["xblk0"])
        if stop == "p1":
            return

        def lin(bank, wv, wn, col0, M, sbi, kch=8, rhs_fn=None, extra_reads=()):
            if rhs_fn is None:
                rhs_fn = lambda c: hT[:, c, sbi * 512:(sbi + 1) * 512]
            mm([(lambda e, c=c: e.matmul(pb[bank][0:M, :], wv[:, c, col0:col0 + M], rhs_fn(c), start=(c == 0), stop=(c == kch - 1)))
                for c in range(kch)], reads=[wn] + HR + list(extra_reads), writes=[pbn[bank]])

        kb.mark('L%d A-pre' % l)
        dma('sp', esink[:], I["attn_sink"][l].partition_broadcast(128), writes=["esink"])
        op('act', lambda e: e.activation(esink[:], esink[:], AF.Exp), reads=["esink"], writes=["esink"])
        dma('sp', kvng[:], I["mla_kv_norm"][l].partition_broadcast(128), writes=["kvng"])
        kb.barrier()
        op('dve', lambda e: e.memset(Va[:, :, :, 64:128], 1.0), writes=["Va"])
        dma('pool', maskA, I["maskA"].rearrange("o p n -> p o n"), writes=["maskA"])
        wq = arena[:, 13312:13312 + 4096].rearrange("p (k n) -> p k n", k=8)
        wqp = arena[:, 17408:17408 + 4096].rearrange("p (k n) -> p k n", k=8)
        wqn, wqpn = "mwq", "mwqp"
        wkv, wkvn = load_w(Win, 8, [(C_AK, 256)])
        def gen_akv(b):
            bk = 6 + b % 2
            ko = kvout[b % 2]
            kon = "kvout%d" % (b % 2)
            mm([(lambda e, c=c: e.matmul(pb[bk][:, 0:256], hT[:, c, b * 128:(b + 1) * 128], wkv[:, c, 0:256], start=(c == 0), stop=(c == 7))) for c in range(8)],
               reads=[wkvn] + HR, writes=[(pbn[bk], 0)])
            yield None
            op('act', lambda e: e.copy(ko[:, 0:256], pb[bk][:, 0:256]), reads=[(pbn[bk], 0)], writes=[kon])
            yield None
            op('dve', lambda e: e.tensor_copy(Va[:, b, :, 0:64], pb[bk][:, 128:256].rearrange("p (g d) -> p g d", g=2)), reads=[(pbn[bk], 0), kon], writes=[("Va", b)])
            yield None
            dma('sp', O["nk"][l, b * 128:(b + 1) * 128, :], ko[:, 0:128], reads=[kon])
            yield None
            dma('sp', O["nv"][l, b * 128:(b + 1) * 128, :], ko[:, 128:256], reads=[kon])
            yield None
        run_pairs(gen_akv, NB)
        dma('pool', ctx16[:, :, 0:128], I["ck"][l].rearrange("(j p) n -> p j n", p=128), writes=["ctx16"])
        for g in range(2):
            pview = pb[6 + g][:].bitcast(BF16)
            mm([(lambda e, j=j, pview=pview: e.transpose(pview[0:64, j * 128:(j + 1) * 128], ctx16[:, j, g * 64:(g + 1) * 64], ident[:])) for j in range(4)],
               reads=["ctx16", "ident"], writes=[pbn[6 + g]])
            op('dve', lambda e, pview=pview: e.tensor_copy(kTa[:, g, T:KT], pview[0:64, 0:512]), reads=[pbn[6 + g]], writes=[("kTa", g, 4)])
        for g in range(2):
            dma('pool', Va[:, NB:NKB, g, 0:64], I["cv"][l].rearrange("(j p) (g d) -> p j g d", p=128, g=2)[:, :, g, :], writes=[("Va", "ctx", g)])
        wk, wkn = load_w(Win, 8, [(C_AK, 128)])
        wkp, wkpn = load_w(Winp, 8, [(512, 128)])
        dma('pool', wq, Win.rearrange("(k p) n -> p k n", p=128)[:, :, C_AQ:C_AQ + 512], writes=[wqn])
        dma('pool', wqp, Winp.rearrange("(k p) n -> p k n", p=128)[:, :, 0:512], writes=[wqpn])
        for g in range(2):
            def gen_ka(sbi, g=g):
                r = sbi % 2
                ba, bb_ = 2 * r, 2 * r + 1
                lin(ba, wk, wkn, g * 64, 64, sbi)
                yield None
                lin(bb_, wkp, wkpn, g * 64, 64, sbi)
                yield None
                op('dve', lambda e: e.tensor_tensor(t1[r][0:64, :], pb[ba][0:64, :], ropeC[0:64, sbi * 512:(sbi + 1) * 512], ALU.mult), reads=[pbn[ba], "ropeC"], writes=["t1_%d" % r])
                yield None
                op('dve', lambda e: e.tensor_tensor(t2[r][0:64, :], pb[bb_][0:64, :], ropeS[0:64, sbi * 512:(sbi + 1) * 512], ALU.mult), reads=[pbn[bb_], "ropeS"], writes=["t2_%d" % r])
                yield None
                op('pool', lambda e: e.tensor_tensor(kTa[:, g, sbi * 512:(sbi + 1) * 512], t1[r][0:64, :], t2[r][0:64, :], ALU.add), reads=["t1_%d" % r, "t2_%d" % r], writes=[("kTa", g, sbi)])
                yield None
            run_pairs(gen_ka, NSB)
        kb.mark('L%d A-attn' % l)
        for h in range(8):
            g = h // 4
            qt, qn = qTh[h % 2], "qTh0"
            def gen_qa(sbi, h=h, qt=qt, qn=qn):
                r = sbi % 2
                ba, bb_ = 2 * r, 2 * r + 1
                lin(ba, wq, wqn, h * 64, 64, sbi)
                yield None
                lin(bb_, wqp, wqpn, h * 64, 64, sbi)
                yield None
                op('dve', lambda e: e.tensor_tensor(t1[r][0:64, :], pb[ba][0:64, :], ropeC[0:64, sbi * 512:(sbi + 1) * 512], ALU.mult), reads=[pbn[ba], "ropeC"], writes=["t1_%d" % r])
                yield None
                op('dve', lambda e: e.tensor_tensor(t2[r][0:64, :], pb[bb_][0:64, :], ropeS[0:64, sbi * 512:(sbi + 1) * 512], ALU.mult), reads=[pbn[bb_], "ropeS"], writes=["t2_%d" % r])
                yield None
                op('pool', lambda e: e.tensor_tensor(qt[0:64, sbi * 512:(sbi + 1) * 512], t1[r][0:64, :], t2[r][0:64, :], ALU.add), reads=["t1_%d" % r, "t2_%d" % r], writes=[(qn, sbi)])
                yield None
            run_pairs(gen_qa, NSB)
            groups = []
            for sbi in range(NSB):
                klist = []
                for o in range(6):
                    j = 4 * sbi - 1 + o
                    if 0 <= j < NB:
                        klist.append((j, o, False))
                for j in range(NB, NKB):
                    klist.append((j, None, True))

                def kfn(kblk, g=g):
                    return kTa[:, g, kblk * 128:(kblk + 1) * 128], ("kTa", g, kblk // 4)

                def vfn(kblk, g=g):
                    return Va[:, kblk, g, :], (("Va", kblk) if kblk < NB else ("Va", "ctx", g))

                def fin(ob, h=h, sbi=sbi):
                    r = cnt['rot'] % 2
                    cnt['rot'] += 1
                    op('dve', lambda e: e.tensor_scalar(rr[r][64:128, :], pb[ob][64:128, :], esink[64:128, h:h + 1], None, ALU.add), reads=[pbn[ob], "esink"], writes=["rr0"])
                    op('dve', lambda e: e.reciprocal(rr[r][64:128, :], rr[r][64:128, :]), reads=["rr0"], writes=["rr0"])
                    op('pool', lambda e: e.tensor_copy(r3[r][0:64, :], rr[r][64:128, :]), reads=["rr0"], writes=["r3_0"])
                    po = (h % 2) * 64
                    op('dve', lambda e: e.tensor_tensor(ozT[po:po + 64, h // 2, sbi * 512:(sbi + 1) * 512], pb[ob][0:64, :], r3[r][0:64, :], ALU.mult),
                       reads=[pbn[ob], "r3_0"], writes=[("ozT", h // 2, sbi)])

                groups.append(dict(qtile=qt, qname=qn, sbi=sbi, kfn=kfn, klist=klist, vfn=vfn, K=64, scale=0.125,
                                   bias_fn=(lambda isctx: (flags[:, 1:2] if isctx else 0.0)), fin=fin))
            attend_stream(groups)
        if "ozA%d" % l in TAP:
            for c8 in range(4):
                for hf in range(2):
                    op('dve', lambda e: e.tensor_copy(xblk[0][:], ozT[:, c8, hf * 1024:(hf + 1) * 1024]), reads=["ozT"], writes=["xblk0"])
                    dma('sp', TAP["ozA%d" % l][:, c8, hf * 1024:(hf + 1) * 1024], xblk[0][:], reads=["xblk0"])

        def zmul_and_merge(zcol, wbr_src, gcol, first, after_loads=None):
            kb.barrier()

            def load_into(k_, name, src2d, kch, c0, n):
                v = arena[:, k_ * 4096:k_ * 4096 + kch * n].rearrange("p (k n) -> p k n", k=kch)
                dma('pool', v, src2d.rearrange("(k p) n -> p k n", p=128)[:, :, c0:c0 + n], writes=[name])
                return v, name
            wz, wzn = load_into(0, "mw0", Win, 8, zcol, 512)
            wbs, wgs = [], []
            for ch in range(2):
                wbs.append(load_into(1 + 2 * ch, "mw%d" % (1 + 2 * ch), wbr_src, 4, ch * 512, 512))
                wgs.append(load_into(2 + 2 * ch, "mw%d" % (2 + 2 * ch), Win, 8, gcol + ch * 512, 512))
            if after_loads is not None:
                after_loads()
            for c in range(4):
                for sbi in range(NSB):
                    r = cnt['rot'] % 2
                    cnt['rot'] += 1
                    lin(r, wz, wzn, c * 128, 128, sbi)
                    op('act', lambda e: e.activation(t1[r][:], pb[r][:], AF.Silu), reads=[pbn[r]], writes=["t1_%d" % r])
                    op('pool', lambda e: e.tensor_tensor(ozT[:, c, sbi * 512:(sbi + 1) * 512], ozT[:, c, sbi * 512:(sbi + 1) * 512], t1[r][:], ALU.mult),
                       reads=["t1_%d" % r, ("ozT", c, sbi)], writes=[("ozT", c, sbi)])
            for ch in range(2):
                wb, wbn = wbs[ch]
                wg, wgn = wgs[ch]
                for cc in range(4):
                    c = ch * 4 + cc
                    for sbi in range(NSB):
                        r = cnt['rot'] % 2
                        cnt['rot'] += 1
                        lin(r, wg, wgn, cc * 128, 128, sbi)
                        op('act', lambda e: e.activation(sig[r][:], pb[r][:], AF.Sigmoid), reads=[pbn[r]], writes=["sig%d" % r])
                        lin(2 + r, wb, wbn, cc * 128, 128, sbi, kch=4, rhs_fn=lambda k: ozT[:, k, sbi * 512:(sbi + 1) * 512],
                            extra_reads=[("ozT", k, sbi) for k in range(4)])
                        dst = mergeT[:, c, sbi * 512:(sbi + 1) * 512]
                        if first:
                            op('dve', lambda e: e.tensor_tensor(dst, pb[2 + r][:], sig[r][:], ALU.mult), reads=[pbn[2 + r], "sig%d" % r], writes=[("mergeT", c, sbi)])
                        else:
                            op('dve', lambda e: e.tensor_tensor(t2[r][:], pb[2 + r][:], sig[r][:], ALU.mult), reads=[pbn[2 + r], "sig%d" % r], writes=["t2_%d" % r])
                            op('pool', lambda e: e.tensor_tensor(dst, dst, t2[r][:], ALU.add), reads=["t2_%d" % r, ("mergeT", c, sbi)], writes=[("mergeT", c, sbi)])

        kb.mark('L%d A-merge' % l)
        zmul_and_merge(C_ZA, I["w_branch_a"][l], C_G, True)

        def tap_big(nm, src, nch, rd):
            if nm in TAP:
                for c8 in range(nch):
                    for hf in range(2):
                        op('dve', lambda e: e.tensor_copy(xblk[0][:], src[:, c8, hf * 1024:(hf + 1) * 1024]), reads=rd, writes=["xblk0"])
                        dma('sp', TAP[nm][:, c8, hf * 1024:(hf + 1) * 1024], xblk[0][:], reads=["xblk0"])
        tap_big("mergeA%d" % l, mergeT, 8, ["mergeT"])
        if stop == "A":
            return

        kb.mark('L%d B-pre' % l)
        kb.barrier()
        op('dve', lambda e: e.memset(Vb1[:, :, 64:128], 1.0), writes=["Vb0"])
        dma('pool', kTb1[96:104, :], I["koh"], writes=["kTb0"])
        dma('pool', qTh[0][96:104, :], I["qoh"], writes=["qTh0"])
        wkv, wkvn = load_w(Win, 8, [(C_BCKV, 288)])
        def gen_bckv(b):
            bk = 6 + b % 2
            mm([(lambda e, c=c: e.matmul(pb[bk][:, 256:512 + 32 - 512] if False else pb[bk][:, 256:512], hT[:, c, b * 128:(b + 1) * 128], wkv[:, c, 0:256], start=(c == 0), stop=(c == 7))) for c in range(8)],
               reads=[wkvn] + HR, writes=[(pbn[bk], 1)])
            yield None
            bk2 = 0 + b % 2
            mm([(lambda e, c=c: e.matmul(pb[bk2][:, 0:32], hT[:, c, b * 128:(b + 1) * 128], wkv[:, c, 256:288], start=(c == 0), stop=(c == 7))) for c in range(8)],
               reads=[wkvn] + HR, writes=[(pbn[bk2], 0)])
            yield None
            ko2 = kvout[(b + 1) % 2]
            ko2n = "kvout%d" % ((b + 1) % 2)
            op('act', lambda e: e.activation(junk[:, 0:256], pb[bk][:, 256:512], AF.Square, accum_out=st2[:, b, 0:1]), reads=[(pbn[bk], 1)], writes=["junk", ("st2", b)])
            yield None
            op('dve', lambda e: e.tensor_scalar(st2[:, b, 1:2], st2[:, b, 0:1], 1.0 / 256, 1e-6, ALU.mult, ALU.add), reads=[("st2", b)], writes=[("st2", b)])
            yield None
            op('act', lambda e: e.activation(st2[:, b, 2:3], st2[:, b, 1:2], AF.Ln), reads=[("st2", b)], writes=[("st2", b)])
            yield None
            op('act', lambda e: e.activation(st2[:, b, 3:4], st2[:, b, 2:3], AF.Exp, scale=-0.5), reads=[("st2", b)], writes=[("st2", b)])
            yield None
            op('dve', lambda e: e.scalar_tensor_tensor(ko2[:, 0:256], pb[bk][:, 256:512], st2[:, b, 3:4], kvng[:], ALU.mult, ALU.mult),
               reads=[(pbn[bk], 1), ("st2", b), "kvng"], writes=[ko2n])
            yield None
            op('act', lambda e: e.copy(ko2[:, 256:288], pb[bk2][:, 0:32]), reads=[(pbn[bk2], 0)], writes=[ko2n])
            yield None
            dma('sp', O["nckv"][l, b * 128:(b + 1) * 128, :], ko2[:, 0:256], reads=[ko2n])
            yield None
            dma('sp', O["nkpe"][l, b * 128:(b + 1) * 128, :], ko2[:, 256:288], reads=[ko2n])
            yield None
            k16 = kvn16[b % 2]
            k16n = "kvn16_%d" % (b % 2)
            op('dve', lambda e: e.tensor_copy(k16[:, 0:256], ko2[:, 0:256]), reads=[ko2n], writes=[k16n])
            yield None
            bk3 = 2 + b % 2
            pview = pb[bk3][:].bitcast(BF16)
            mm([(lambda e, c=c, pview=pview: e.transpose(pview[:, c * 128:(c + 1) * 128], k16[:, c * 128:(c + 1) * 128], ident[:])) for c in range(2)],
               reads=[k16n, "ident"], writes=[pbn[bk3]])
            yield None
            op('dve', lambda e, pview=pview: e.tensor_copy(ckvT[:, :, b * 128:(b + 1) * 128], pview[:, 0:256].rearrange("p (c n) -> p c n", c=2)),
               reads=[pbn[bk3]], writes=[("ckvT", b)])
            yield None
        run_pairs(gen_bckv, NB)
        ctx16b = ctx16
        dma('pool', ctx16b[:], I["cckv"][l].rearrange("(j p) n -> p j n", p=128), writes=["ctx16"])
        for j in range(4):
            pview = pb[6 + j % 2][:].bitcast(BF16)
            mm([(lambda e, c=c, pview=pview: e.transpose(pview[:, c * 128:(c + 1) * 128], ctx16b[:, j, c * 128:(c + 1) * 128], ident[:])) for c in range(2)],
               reads=["ctx16", "ident"], writes=[pbn[6 + j % 2]])
            op('dve', lambda e, pview=pview: e.tensor_copy(ckvT[:, :, T + j * 128:T + (j + 1) * 128], pview[:, 0:256].rearrange("p (c n) -> p c n", c=2)),
               reads=[pbn[6 + j % 2]], writes=[("ckvT", NB + j)])
        dma('pool', ctxp[:, :, 64:96], I["ckpe"][l].rearrange("(j p) n -> p j n", p=128), writes=["ctxp"])
        pview = pb[6][:].bitcast(BF16)
        mm([(lambda e, j=j, pview=pview: e.transpose(pview[0:96, j * 128:(j + 1) * 128], ctxp[:, j, :], ident[:])) for j in range(4)],
           reads=["ctxp", "ident"], writes=[pbn[6]])
        op('dve', lambda e, pview=pview: e.tensor_copy(kpeT[64:96, T:KT], pview[64:96, 0:512]), reads=[pbn[6]], writes=[("kpeT", 4)])

        dma('sp', qng[:], I["mla_q_norm"][l].partition_broadcast(128), writes=["qng"])
        wcq, wcqn = load_w(Win, 8, [(C_BCQ, 384)])
        def gen_bcq(b):
            bk = 6 + b % 2
            mm([(lambda e, c=c: e.matmul(pb[bk][:, 0:384], hT[:, c, b * 128:(b + 1) * 128], wcq[:, c, 0:384], start=(c == 0), stop=(c == 7))) for c in range(8)],
               reads=[wcqn] + HR, writes=[pbn[bk]])
            yield None
            op('act', lambda e: e.activation(junk[:, 0:384], pb[bk][:, 0:384], AF.Square, accum_out=st2[:, b, 0:1]), reads=[pbn[bk]], writes=["junk", ("st2", b)])
            yield None
            op('dve', lambda e: e.tensor_scalar(st2[:, b, 1:2], st2[:, b, 0:1], 1.0 / 384, 1e-6, ALU.mult, ALU.add), reads=[("st2", b)], writes=[("st2", b)])
            yield None
            op('act', lambda e: e.activation(st2[:, b, 2:3], st2[:, b, 1:2], AF.Ln), reads=[("st2", b)], writes=[("st2", b)])
            yield None
            op('act', lambda e: e.activation(st2[:, b, 3:4], st2[:, b, 2:3], AF.Exp, scale=-0.5), reads=[("st2", b)], writes=[("st2", b)])
            yield None
            k16 = kvn16[b % 2]
            k16n = "kvn16_%d" % (b % 2)
            op('dve', lambda e: e.scalar_tensor_tensor(k16[:, 0:384], pb[bk][:, 0:384], st2[:, b, 3:4], qng[:], ALU.mult, ALU.mult),
               reads=[pbn[bk], ("st2", b), "qng"], writes=[k16n])
            yield None
            bk3 = 2 + b % 2
            pview = pb[bk3][:].bitcast(BF16)
            mm([(lambda e, c=c, pview=pview: e.transpose(pview[:, c * 128:(c + 1) * 128], k16[:, c * 128:(c + 1) * 128], ident[:])) for c in range(3)],
               reads=[k16n, "ident"], writes=[pbn[bk3]])
            yield None
            op('dve', lambda e, pview=pview: e.tensor_copy(cqnT[:, :, b * 128:(b + 1) * 128], pview[:, 0:384].rearrange("p (c n) -> p c n", c=3)),
               reads=[pbn[bk3]], writes=[("cqnT", b)])
            yield None
        run_pairs(gen_bcq, NB)
        wpe, wpen = load_w(Win, 8, [(C_BKPE - 64, 96)])
        wpep, wpepn = load_w(Winp, 8, [(640 - 64, 96)])
        for sbi in range(NSB):
            r = cnt['rot'] % 2
            cnt['rot'] += 1
            lin(0, wpe, wpen, 0, 96, sbi)
            lin(1, wpep, wpepn, 0, 96, sbi)
            op('dve', lambda e: e.tensor_tensor(t1[r][64:96, :], pb[0][64:96, :], ropeC[64:96, sbi * 512:(sbi + 1) * 512], ALU.mult), reads=[pbn[0], "ropeC"], writes=["t1_%d" % r])
            op('dve', lambda e: e.tensor_tensor(t2[r][64:96, :], pb[1][64:96, :], ropeS[64:96, sbi * 512:(sbi + 1) * 512], ALU.mult), reads=[pbn[1], "ropeS"], writes=["t2_%d" % r])
            op('pool', lambda e: e.tensor_tensor(kpeT[64:96, sbi * 512:(sbi + 1) * 512], t1[r][64:96, :], t2[r][64:96, :], ALU.add), reads=["t1_%d" % r, "t2_%d" % r], writes=[("kpeT", sbi)])
        wuq, wuqn = load_w(I["mla_w_uq"][l], 3, [(0, 768)])
        wuqp, wuqpn = load_w(I["mla_w_uqp"][l], 3, [(0, 768)])
        dma('pool', wukv[:], I["mla_w_ukv"][l].rearrange("(k p) n -> p k n", p=128), writes=["wukv"])
        kb.mark('L%d B-attn' % l)
        CQR = [("cqnT", b) for b in range(NB)]
        CKR = [("ckvT", b) for b in range(NKB)]
        for h in range(8):
            qt, qn = qTh[h % 2], "qTh0"
            kt, ktn = kTb[0], "kTb0"
            vt, vtn = Vb[0], "Vb0"
            for s5 in range(5):
                bk = 0 + s5 % 2
                mm([(lambda e, c=c: e.matmul(pb[bk][0:64, :], wukv[:, c, h * 128:h * 128 + 64], ckvT[:, c, s5 * 512:(s5 + 1) * 512], start=(c == 0), stop=(c == 1))) for c in range(2)],
                   reads=["wukv"] + CKR, writes=[pbn[bk]])
                op('act', lambda e: e.copy(kt[0:64, s5 * 512:(s5 + 1) * 512], pb[bk][0:64, :]), reads=[pbn[bk]], writes=[(ktn, s5)])
                op('pool', lambda e: e.tensor_copy(kt[64:96, s5 * 512:(s5 + 1) * 512], kpeT[64:96, s5 * 512:(s5 + 1) * 512]), reads=[("kpeT", s5)], writes=[(ktn, s5, 'pe')])
            for kblk in range(NKB):
                bk = 6 + kblk % 2
                mm([(lambda e, c=c: e.matmul(pb[bk][:, 0:64], ckvT[:, c, kblk * 128:(kblk + 1) * 128], wukv[:, c, h * 128 + 64:h * 128 + 128], start=(c == 0), stop=(c == 1))) for c in range(2)],
                   reads=["wukv"] + CKR, writes=[pbn[bk]])
                op('dve', lambda e: e.tensor_copy(vt[:, kblk, 0:64], pb[bk][:, 0:64]), reads=[pbn[bk]], writes=[(vtn, kblk)])
            def gen_qb(sbi, h=h, qt=qt, qn=qn):
                r = sbi % 2
                ba, bb_ = 2 * r, 2 * r + 1
                rf = lambda c: cqnT[:, c, sbi * 512:(sbi + 1) * 512]
                lin(ba, wuq, wuqn, h * 96, 96, sbi, kch=3, rhs_fn=rf, extra_reads=CQR)
                yield None
                lin(bb_, wuqp, wuqpn, h * 96, 96, sbi, kch=3, rhs_fn=rf, extra_reads=CQR)
                yield None
                op('act', lambda e: e.copy(qt[0:64, sbi * 512:(sbi + 1) * 512], pb[ba][0:64, :]), reads=[pbn[ba]], writes=[(qn, sbi)])
                yield None
                op('dve', lambda e: e.tensor_tensor(t1[r][64:96, :], pb[ba][64:96, :], ropeC[64:96, sbi * 512:(sbi + 1) * 512], ALU.mult), reads=[pbn[ba], "ropeC"], writes=["t1_%d" % r])
                yield None
                op('dve', lambda e: e.tensor_tensor(t2[r][64:96, :], pb[bb_][64:96, :], ropeS[64:96, sbi * 512:(sbi + 1) * 512], ALU.mult), reads=[pbn[bb_], "ropeS"], writes=["t2_%d" % r])
                yield None
                op('pool', lambda e: e.tensor_tensor(qt[64:96, sbi * 512:(sbi + 1) * 512], t1[r][64:96, :], t2[r][64:96, :], ALU.add), reads=["t1_%d" % r, "t2_%d" % r], writes=[(qn, sbi, 'pe')])
                yield None
            run_pairs(gen_qb, NSB)
            groups = []
            MS = 96.0 ** -0.5
            for sbi in range(NSB):
                klist = [(j, None, False) for j in range(NKB)]

                def kfn(kblk, kt=kt, ktn=ktn):
                    return kt[0:104, kblk * 128:(kblk + 1) * 128], ktn

                def vfn(kblk, vt=vt, vtn=vtn):
                    return vt[:, kblk, :], vtn

                def fin(ob, h=h, sbi=sbi):
                    r = cnt['rot'] % 2
                    cnt['rot'] += 1
                    op('dve', lambda e: e.reciprocal(rr[r][64:128, :], pb[ob][64:128, :]), reads=[pbn[ob]], writes=["rr0"])
                    op('pool', lambda e: e.tensor_copy(r3[r][0:64, :], rr[r][64:128, :]), reads=["rr0"], writes=["r3_0"])
                    po = (h % 2) * 64
                    op('dve', lambda e: e.tensor_tensor(ozT[po:po + 64, h // 2, sbi * 512:(sbi + 1) * 512], pb[ob][0:64, :], r3[r][0:64, :], ALU.mult),
                       reads=[pbn[ob], "r3_0"], writes=[("ozT", h // 2, sbi)])

                groups.append(dict(qtile=qt, qname=qn, sbi=sbi, kfn=kfn, klist=klist, vfn=vfn, K=104, scale=MS,
                                   bias_fn=(lambda isctx: -MS * BIGM), fin=fin))
            attend_stream(groups)
        kb.mark('L%d B-merge' % l)
        zmul_and_merge(C_ZB, I["w_branch_b"][l], C_G + 1024, False)
        tap_big("ozB%d" % l, ozT, 4, ["ozT"])
        tap_big("mergeB%d" % l, mergeT, 8, ["mergeT"])
        if stop == "B":
            return

        kb.mark('L%d C' % l)
        kb.barrier()
        _gdn(kb, I, O, l, dict(arena=arena, pb=pb, pbn=pbn, hT=hT, ozT=ozT, HR=HR, load_w=load_w, lin=lin, ident=ident, ident32=ident32,
                               flags=flags, ones16=ones16, wukv=wukv, run_pairs=run_pairs, xblk=xblk, t1=t1, t2=t2, sig=sig, junk=junk, Win=Win, TAP=TAP, stat=stat, st2=st2, kvn16=kvn16, rr=rr))
        tap_big("ozC%d" % l, ozT, 4, ["ozT"])
        kb.mark('L%d C-merge' % l)
        wo = []

        def _prefetch_wo():
            for ch in range(2):
                wo.append(load_w(I["w_out"][l], 8, [(ch * 512, 512)]))
        zmul_and_merge(C_ZC, I["w_branch_c"][l], C_G + 2048, False, after_loads=_prefetch_wo)
        kb.barrier()
        tap_big("mergeC%d" % l, mergeT, 8, ["mergeT"])
        if stop == "C":
            return

        kb.mark('L%d out' % l)
        MR = [("mergeT", c, s) for c in range(8) for s in range(NSB)]
        if l == 1:
            dma('sp', gbc[:].rearrange("p a b -> p (a b)"), I["final_norm_g"].partition_broadcast(128), writes=["gbc"])
        def gen_out(b):
            xb, xbn = xblk[b % 2], "xblk%d" % (b % 2)
            dma('sp', xb[:], xsrc[b * 128:(b + 1) * 128, :], reads=(["xs"] if l == 1 else []), writes=[xbn])
            yield None
            for ch in range(2):
                bk = 4 + 2 * (b % 2) + ch
                wv, wn = wo[ch]
                mm([(lambda e, c=c, wv=wv: e.matmul(pb[bk][:], mergeT[:, c, b * 128:(b + 1) * 128], wv[:, c, :], start=(c == 0), stop=(c == 7))) for c in range(8)],
                   reads=[wn] + MR, writes=[pbn[bk]])
                yield None
                tt_, ttn_ = ((t1[b % 2], "t1_%d" % (b % 2)) if ch == 0 else (t2[b % 2], "t2_%d" % (b % 2)))
                op('dve', lambda e: e.tensor_tensor(tt_[:], pb[bk][:], gateb[:, ch * 512:(ch + 1) * 512], ALU.mult), reads=[pbn[bk], ("gateb", ch)], writes=[ttn_])
                yield None
                op('pool', lambda e: e.tensor_tensor(xb[:, ch * 512:(ch + 1) * 512], xb[:, ch * 512:(ch + 1) * 512], tt_[:], ALU.add), reads=[ttn_, xbn], writes=[xbn])
                yield None
            if l == 0:
                dma('sp', xs[b * 128:(b + 1) * 128, :], xb[:], reads=[xbn], writes=["xs"])
                yield None
            else:
                op('act', lambda e: e.activation(junk[:], xb[:], AF.Square, accum_out=stat[:, b, 0:1]), reads=[xbn], writes=["junk", ("stat", b)])
                yield None
                op('dve', lambda e: e.tensor_scalar(stat[:, b, 1:2], stat[:, b, 0:1], 1.0 / D, 1e-6, ALU.mult, ALU.add), reads=[("stat", b)], writes=[("stat", b)])
                yield None
                op('act', lambda e: e.activation(stat[:, b, 2:3], stat[:, b, 1:2], AF.Ln), reads=[("stat", b)], writes=[("stat", b)])
                yield None
                op('act', lambda e: e.activation(stat[:, b, 3:4], stat[:, b, 2:3], AF.Exp, scale=-0.5), reads=[("stat", b)], writes=[("stat", b)])
                yield None
                op('dve', lambda e: e.scalar_tensor_tensor(xb[:], xb[:], stat[:, b, 3:4], gbc[:].rearrange("p a b -> p (a b)"), ALU.mult, ALU.mult), reads=[xbn, ("stat", b), "gbc"], writes=[xbn])
                yield None
                dma('sp', O["y"][b * 128:(b + 1) * 128, :], xb[:], reads=[xbn])
                yield None
        run_pairs(gen_out, NB)

def _gdn(kb, I, O, l, env):
    op, dma, mm = kb.op, kb.dma, kb.mmgroup
    arena, pb, pbn, hT, ozT, HR = env["arena"], env["pb"], env["pbn"], env["hT"], env["ozT"], env["HR"]
    load_w, lin, ident, ident32, flags = env["load_w"], env["lin"], env["ident"], env["ident32"], env["flags"]
    xblk, t1, t2, sig, junk, Win, TAP = env["xblk"], env["t1"], env["t2"], env["sig"], env["junk"], env["Win"], env["TAP"]
    st2, kvn16 = env["st2"], env["kvn16"]
    pos = [0]

    def carve(n_units, dt, shape_str=None, **kw):
        a = arena[:, pos[0]:pos[0] + n_units]
        pos[0] += n_units
        if dt == F32:
            a = a.bitcast(F32)
        if shape_str:
            a = a.rearrange(shape_str, **kw)
        return a
    qkvh = carve(3 * T, BF16, "p (c n) -> p c n", c=3)
    oacc = carve(NB * 128, BF16, "p (b d) -> p b d", b=NB)
    ab = carve(2 * NB * 16, F32, "p (b d) -> p b d", b=NB)
    names = ["g", "bt", "nbt", "gam", "ngam", "egam", "begam", "edel", "dec1", "dec2", "gt1", "gt2"]
    stt_ = {n: carve(2 * NB * 8, F32, "p (b d) -> p b d", b=NB) for n in names}
    S = carve(2 * 2 * 128, F32, "p (s d) -> p s d", s=2)
    Sbf = carve(2 * 128, BF16, "p (s d) -> p s d", s=2)
    bt_names = ["Kbg", "Kd0", "Kd1", "Vb", "Qg", "M0", "MTa", "MTb", "Ma", "Mb", "PTb", "Wt", "Wn", "U", "CT", "attnT", "Mc"]
    B = {n: carve(256, BF16, "p (s d) -> p s d", s=2) for n in bt_names}
    X = carve(512, BF16, "p (s h d) -> p s h d", s=2, h=2)
    wk_ = env["wukv"][:].rearrange("p a b -> p (a b)")
    B1 = {}
    for q_, n in enumerate(bt_names):
        if q_ < 6:
            B1[n] = wk_[:, 512 + q_ * 256:512 + (q_ + 1) * 256].rearrange("p (s d) -> p s d", s=2)
        else:
            B1[n] = carve(256, BF16, "p (s d) -> p s d", s=2)
    X1 = wk_[:, 0:512].rearrange("p (s h d) -> p s h d", s=2, h=2)
    cw = carve(2 * 36, F32, "p (c j) -> p c j", c=12)
    nw = carve(2 * 36, F32, "p (c j) -> p c j", c=12)
    pw = carve(2 * 36, F32, "p (c j) -> p c j", c=12)
    gnb = carve(2 * 128, F32)
    dtb = carve(2 * 8, F32)
    negA = carve(2 * 8, F32)
    onorm = carve(128, BF16)
    ident2 = carve(256, BF16, "p (s d) -> p s d", s=2)
    assert pos[0] <= 21952, pos[0]
    tri = t1[0][:].rearrange("p (k n) -> p k n", k=4)
    gmc = t1[1][:].rearrange("p (k s n) -> p k s n", k=2, s=2)
    Grhs = t2[0][:, 0:256].rearrange("p (s n) -> p s n", s=2)
    ones32 = t2[0][:, 256:384]
    Em = t2[1][:].rearrange("p (k s n) -> p k s n", k=2, s=2)
    accs = [xblk[0], xblk[1]]
    SETS = [dict(B=B, X=X, Grhs=Grhs, Em=Em, grn="Grhs0", emn="Em0", sx="", banks=(0, 1, 2, 3)),
            dict(B=B1, X=X1, Grhs=sig[0][:, 0:256].rearrange("p (s n) -> p s n", s=2),
                 Em=sig[1][:].rearrange("p (k s n) -> p k s n", k=2, s=2), grn="sig0", emn="sig1", sx="_1", banks=(4, 5, 6, 7))]

    for k_, nm in enumerate(["triF", "triB", "sel1", "sel2"]):
        dma('sp', tri[:, k_, :], I[nm], writes=["t1_0"])
    for k_, nm in enumerate(["gm1", "gm2"]):
        for s_ in range(2):
            dma('sp', gmc[:, k_, s_, :], I[nm][:, 4 * s_, :], writes=["t1_1"])
    op('dve', lambda e: e.memset(ones32, 1.0), writes=["ones32"])
    op('dve', lambda e: e.memset(B1["Kd0"][:], 0.0), writes=["Kd0_1"])
    op('dve', lambda e: e.memset(B1["Kd1"][:], 0.0), writes=["Kd1_1"])
    op('dve', lambda e: e.memset(B["Kd0"][:], 0.0), writes=["Kd0"])
    for s_ in range(2):
        op('dve', lambda e: e.tensor_copy(ident2[:, s_, :], ident[:]), reads=["ident"], writes=["ident2"])
    op('dve', lambda e: e.memset(B["Kd1"][:], 0.0), writes=["Kd1"])
    for j_ in range(3):
        dma('sp', cw[:, :, j_], I["gdn_conv"][l][j_].rearrange("(c p) -> p c", p=128), writes=["cw"], allow_slow_non_contiguous=True)
    dma('sp', gnb, I["gdn_norm"][l].partition_broadcast(128), writes=["gnb"])
    dma('sp', dtb, I["gdn_dt_bias"][l].partition_broadcast(128), writes=["dtb"])
    dma('sp', negA, I["gdn_a_log"][l].partition_broadcast(128), writes=["negA"])
    op('act', lambda e: e.activation(negA, negA, AF.Exp), reads=["negA"], writes=["negA"])
    op('dve', lambda e: e.tensor_scalar(negA, negA, -1.0, None, ALU.mult), reads=["negA"], writes=["negA"])
    op('dve', lambda e: e.tensor_scalar(nw, cw, flags[:, 2:3], None, ALU.mult), reads=["cw", "flags"], writes=["nw"])
    op('dve', lambda e: e.tensor_tensor(pw, cw, nw, ALU.add), reads=["cw", "nw"], writes=["pw"])

    wab, wabn = load_w(Win, 8, [(C_CA, 16)])
    for b in range(NB):
        mm([(lambda e, c=c: e.matmul(pb[0][:, b * 16:(b + 1) * 16], hT[:, c, b * 128:(b + 1) * 128], wab[:, c, :], start=(c == 0), stop=(c == 7))) for c in range(8)],
           reads=[wabn] + HR, writes=[pbn[0]])
    op('dve', lambda e: e.tensor_copy(ab, pb[0][:, 0:256].rearrange("p (b d) -> p b d", b=NB)), reads=[pbn[0]], writes=["ab"])
    g, bt, nbt, gam, ngam, egam, begam, edel, dec1, dec2, gt1, gt2 = [stt_[n] for n in names]
    bc8 = lambda a: a.unsqueeze(1).to_broadcast([128, NB, 8])
    op('dve', lambda e: e.tensor_tensor(g, ab[:, :, 0:8], bc8(dtb), ALU.add), reads=["ab", "dtb"], writes=["g"])
    op('act', lambda e: e.activation(g, g, AF.Exp), reads=["g"], writes=["g"])
    op('act', lambda e: e.activation(g, g, AF.Ln, bias=1.0), reads=["g"], writes=["g"])
    op('dve', lambda e: e.tensor_tensor(g, g, bc8(negA), ALU.mult), reads=["g", "negA"], writes=["g"])
    op('act', lambda e: e.activation(bt, ab[:, :, 8:16], AF.Sigmoid), reads=["ab"], writes=["bt"])
    op('dve', lambda e: e.tensor_scalar(nbt, bt, -1.0, None, ALU.mult), reads=["bt"], writes=["nbt"])
    for b in range(NB):
        mm([lambda e: e.matmul(pb[1][:, b * 8:b * 8 + 4], tri[:, 0, :], g[:, b, 0:4], start=True, stop=True),
            lambda e: e.matmul(pb[1][:, b * 8 + 4:b * 8 + 8], tri[:, 1, :], g[:, b, 4:8], start=True, stop=True)], reads=["t1_0", "g"], writes=[pbn[1]])
        mm([lambda e: e.matmul(pb[2][:, b * 8:b * 8 + 8], tri[:, 2, :], g[:, b, :], start=True, stop=True)], reads=["t1_0", "g"], writes=[pbn[2]])
        mm([lambda e: e.matmul(pb[3][:, b * 8:b * 8 + 8], tri[:, 3, :], g[:, b, :], start=True, stop=True)], reads=["t1_0", "g"], writes=[pbn[3]])
    v3 = lambda bank: pb[bank][:, 0:128].rearrange("p (b d) -> p b d", b=NB)
    op('dve', lambda e: e.tensor_copy(gam, v3(1)), reads=[pbn[1]], writes=["gam"])
    op('dve', lambda e: e.tensor_copy(gt1, v3(2)), reads=[pbn[2]], writes=["gt1"])
    op('dve', lambda e: e.tensor_copy(gt2, v3(3)), reads=[pbn[3]], writes=["gt2"])
    op('dve', lambda e: e.tensor_scalar(ngam, gam, -1.0, None, ALU.mult), reads=["gam"], writes=["ngam"])
    op('act', lambda e: e.activation(egam, gam, AF.Exp), reads=["gam"], writes=["egam"])
    op('dve', lambda e: e.tensor_tensor(begam, egam, bt, ALU.mult), reads=["egam", "bt"], writes=["begam"])
    op('dve', lambda e: e.tensor_tensor(edel[0:64], gt1[0:64], gam[0:64], ALU.subtract), reads=["gt1", "gam"], writes=["edel"])
    op('dve', lambda e: e.tensor_tensor(edel[64:128], gt2[64:128], gam[64:128], ALU.subtract), reads=["gt2", "gam"], writes=["edel"])
    op('act', lambda e: e.activation(edel, edel, AF.Exp), reads=["edel"], writes=["edel"])
    op('act', lambda e: e.activation(dec1, gt1, AF.Exp), reads=["gt1"], writes=["dec1"])
    op('act', lambda e: e.activation(dec2, gt2, AF.Exp), reads=["gt2"], writes=["dec2"])
    if "gstats%d" % l in TAP:
        for k_, n in enumerate(["g", "bt", "gam", "edel", "dec1", "dec2"]):
            dma('sp', TAP["gstats%d" % l][k_], stt_[n], reads=[n])

    for h in range(4):
        kb.mark('L%d C h%d conv' % (l, h))
        if h == 0:
            nxt_wc = load_w(Win, 8, [(C_CQKV + h * 128, 128), (C_CQKV + 512 + h * 128, 128), (C_CQKV + 1024 + h * 128, 128)])
        wc, wcn = nxt_wc
        for ci in range(3):
            ch = ci * 4 + h
            for sbi in range(NSB):
                lin(sbi, wc, wcn, ci * 128, 128, sbi)
            for sbi in range(NSB):
                acc = accs[sbi // 2][:, (sbi % 2) * 512:(sbi % 2 + 1) * 512]
                an = "xblk%d" % (sbi // 2)
                P_ = pb[sbi]
                rd = [pbn[sbi], "cw", "nw", "pw"]
                op('dve', lambda e: e.tensor_scalar(acc, P_[:], cw[:, ch, 1:2], None, ALU.mult), reads=rd, writes=[an])
                op('dve', lambda e: e.scalar_tensor_tensor(acc[:, 1:512], P_[:, 0:511], cw[:, ch, 0:1], acc[:, 1:512], ALU.mult, ALU.add), reads=rd + [an], writes=[an])
                op('dve', lambda e: e.scalar_tensor_tensor(acc[:, 0:511], P_[:, 1:512], cw[:, ch, 2:3], acc[:, 0:511], ALU.mult, ALU.add), reads=rd + [an], writes=[an])
                op('dve', lambda e: e.scalar_tensor_tensor(acc[:, 256:257], P_[:, 255:256], nw[:, ch, 0:1], acc[:, 256:257], ALU.mult, ALU.add), reads=rd + [an], writes=[an])
                op('dve', lambda e: e.scalar_tensor_tensor(acc[:, 255:256], P_[:, 256:257], nw[:, ch, 2:3], acc[:, 255:256], ALU.mult, ALU.add), reads=rd + [an], writes=[an])
                if sbi > 0:
                    op('dve', lambda e: e.scalar_tensor_tensor(acc[:, 0:1], pb[sbi - 1][:, 511:512], pw[:, ch, 0:1], acc[:, 0:1], ALU.mult, ALU.add),
                       reads=rd + [an, pbn[sbi - 1]], writes=[an])
                if sbi < NSB - 1:
                    op('dve', lambda e: e.scalar_tensor_tensor(acc[:, 511:512], pb[sbi + 1][:, 0:1], pw[:, ch, 2:3], acc[:, 511:512], ALU.mult, ALU.add),
                       reads=rd + [an, pbn[sbi + 1]], writes=[an])
            def gen_l2(sbi, ci=ci):
                acc = accs[sbi // 2][:, (sbi % 2) * 512:(sbi % 2 + 1) * 512]
                an = ("xblk%d" % (sbi // 2), sbi % 2)
                dst = qkvh[:, ci, sbi * 512:(sbi + 1) * 512]
                p_ = sbi % 2
                sqb, sqn = ((sig[0][:].bitcast(BF16)[:, 0:512], "sig0") if p_ == 0 else (env["rr"][0][:].bitcast(BF16)[:, 0:512], "rr0"))
                rvb, rvn = ((sig[1][:], "sig1") if p_ == 0 else (t2[1][:], "Em0"))
                bk_ = 4 + p_
                if ci == 2:
                    op('act', lambda e: e.activation(dst, acc, AF.Silu), reads=[an], writes=[("qkvh", ci, sbi)])
                    yield None
                else:
                    op('act', lambda e: e.activation(acc, acc, AF.Silu), reads=[an], writes=[an])
                    yield None
                    op('pool', lambda e: e.tensor_tensor(sqb, acc, acc, ALU.mult), reads=[an], writes=[sqn])
                    yield None
                    mm([lambda e: e.matmul(pb[bk_][:], env["ones16"][:], sqb, start=True, stop=True)], reads=[sqn, "ones16"], writes=[pbn[bk_]])
                    yield None
                    op('act', lambda e: e.activation(rvb, pb[bk_][:], AF.Ln, bias=1e-6), reads=[pbn[bk_]], writes=[rvn])
                    yield None
                    op('act', lambda e: e.activation(rvb, rvb, AF.Exp, scale=-0.5, bias=(-0.5 * float(np.log(128.0)) if ci == 0 else 0.0)), reads=[rvn], writes=[rvn])
                    yield None
                    op('dve', lambda e: e.tensor_tensor(dst, acc, rvb, ALU.mult), reads=[an, rvn], writes=[("qkvh", ci, sbi)])
                    yield None
            env["run_pairs"](gen_l2, NSB)
        if h == 0 and "qkvh%d" % l in TAP:
            for ci in range(3):
                for hf in range(2):
                    op('dve', lambda e: e.tensor_copy(xblk[0][:], qkvh[:, ci, hf * 1024:(hf + 1) * 1024]), reads=["qkvh"], writes=["xblk0"])
                    dma('sp', TAP["qkvh%d" % l][:, ci, hf * 1024:(hf + 1) * 1024], xblk[0][:], reads=["xblk0"])
        qT_, kT_, vT_ = qkvh[:, 0, :], qkvh[:, 1, :], qkvh[:, 2, :]
        QK = ["qkvh"]
        kb.mark('L%d C h%d scan' % (l, h))
        if h < 3:
            nxt_wc = load_w(Win, 8, [(C_CQKV + (h + 1) * 128, 128), (C_CQKV + 512 + (h + 1) * 128, 128), (C_CQKV + 1024 + (h + 1) * 128, 128)])
        op('pool', lambda e: e.memset(oacc, 0.0), writes=["oacc"])
        def gen_iter(i, SET):
            B, X, Grhs, Em = SET['B'], SET['X'], SET['Grhs'], SET['Em']
            grn, emn, sx = SET['grn'], SET['emn'], SET['sx']
            b0, b1, b2, b3 = SET['banks']
            N = lambda n_: n_ + sx
            slots = [(i, h, 0), (NB - 1 - i, 4 + h, 1)]
            for s_, (blk, dh, d_) in enumerate(slots):
                if i == 0:
                    dma('sp', S[:, s_, :], I["st"][l, dh], writes=[("S", s_)])
                    op('act', lambda e: e.copy(Sbf[:, s_, :], S[:, s_, :]), reads=[("S", s_)], writes=[("Sbf", s_)])
                elif i % 2 == 0:
                    op('dve', lambda e: e.tensor_scalar(S[:, s_, :], S[:, s_, :], flags[:, 0:1], None, ALU.mult), reads=[("S", s_), "flags"], writes=[("S", s_)])
                    op('act', lambda e: e.copy(Sbf[:, s_, :], S[:, s_, :]), reads=[("S", s_)], writes=[("Sbf", s_)])
            yield None
            pv = pb[b0][:].bitcast(BF16)
            fns = []
            for s_, (blk, dh, d_) in enumerate(slots):
                for k_, src in enumerate([kT_, vT_, qT_]):
                    fns.append(lambda e, s_=s_, k_=k_, src=src, blk=blk: e.transpose(pv[:, (k_ * 2 + s_) * 128:(k_ * 2 + s_ + 1) * 128], src[:, blk * 128:(blk + 1) * 128], ident[:]))
            mm(fns, reads=QK + ["ident"], writes=[pbn[b0]])
            yield None
            for s_, (blk, dh, d_) in enumerate(slots):
                for nm, k_, sc in [("Kbg", 0, begam), ("Vb", 1, bt), ("Qg", 2, egam)]:
                    op('act', lambda e: e.activation(B[nm][:, s_, :], pv[:, (k_ * 2 + s_) * 128:(k_ * 2 + s_ + 1) * 128], AF.Copy, scale=sc[:, blk, dh:dh + 1]),
                       reads=[pbn[b0], "begam", "edel", "bt", "egam"], writes=[(N(nm), s_)])
                    yield None
                for hf_ in range(2):
                    R_ = slice(hf_ * 64, (hf_ + 1) * 64)
                    op('act', lambda e: e.activation(B["Kd%d" % hf_][R_, s_, :], pv[R_, s_ * 128:(s_ + 1) * 128], AF.Copy, scale=edel[R_, blk, dh:dh + 1]),
                       reads=[pbn[b0], "edel"], writes=[(N("Kd%d" % hf_), s_)])
                    yield None
            fns = []
            for s_, (blk, dh, d_) in enumerate(slots):
                ks = kT_[:, blk * 128:(blk + 1) * 128]
                qs = qT_[:, blk * 128:(blk + 1) * 128]
                fns.append(lambda e, s_=s_, ks=ks: e.matmul(pb[b1][:, s_ * 128:(s_ + 1) * 128], ks, ks, start=True, stop=True))
                fns.append(lambda e, s_=s_, ks=ks, qs=qs: e.matmul(pb[b1][:, (2 + s_) * 128:(3 + s_) * 128], ks, qs, start=True, stop=True))
            mm(fns, reads=QK, writes=[pbn[b1]])
            yield None
            for s_, (blk, dh, d_) in enumerate(slots):
                op('pool', lambda e: e.tensor_scalar(Grhs[:, s_, :], ident32[:], ngam[:, blk, dh:dh + 1], None, ALU.mult), reads=["ident32", "ngam"], writes=[(grn, s_)])
                yield None
            mm([lambda e: e.matmul(pb[b2][:, 0:256], ones32, Grhs.rearrange("p s n -> p (s n)"), start=True, stop=True)], reads=[grn, "ones32"], writes=[pbn[b2]])
            yield None
            p2 = pb[b2][:, 0:256].rearrange("p (s n) -> p s n", s=2)
            op('dve', lambda e: e.tensor_tensor(Em[:, 0], p2, gmc[:, 0], ALU.add), reads=[pbn[b2], "t1_1"], writes=[(emn, 0)])
            yield None
            op('dve', lambda e: e.scalar_tensor_tensor(Em[:, 1], p2, -1.0, gmc[:, 1], ALU.mult, ALU.add), reads=[pbn[b2], "t1_1"], writes=[(emn, 1)])
            yield None
            for s_, (blk, dh, d_) in enumerate(slots):
                op('act', lambda e: e.activation(Em[:, 0, s_, :], Em[:, 0, s_, :], AF.Exp, bias=gam[:, blk, dh:dh + 1]), reads=[(emn, 0), "gam"], writes=[(emn, 0)])
                yield None
                op('act', lambda e: e.activation(Em[:, 1, s_, :], Em[:, 1, s_, :], AF.Exp, bias=ngam[:, blk, dh:dh + 1]), reads=[(emn, 1), "ngam"], writes=[(emn, 1)])
                yield None
                op('dve', lambda e: e.scalar_tensor_tensor(B["M0"][:, s_, :], pb[b1][:, s_ * 128:(s_ + 1) * 128], nbt[:, blk, dh:dh + 1], Em[:, 0, s_, :], ALU.mult, ALU.mult),
                   reads=[pbn[b1], "nbt", (emn, 0)], writes=[(N("M0"), s_)])
                yield None
            op('dve', lambda e: e.tensor_tensor(B["attnT"][:], pb[b1][:, 256:512].rearrange("p (s n) -> p s n", s=2), Em[:, 1], ALU.mult), reads=[pbn[b1], (emn, 1)], writes=[N("attnT")])
            yield None
            M0 = B["M0"]
            mm([lambda e, s_=s_: e.matmul(pb[b1][:, s_ * 128:(s_ + 1) * 128], M0[:, s_, :], ident[:], start=True, stop=True) for s_ in range(2)], reads=[N("M0"), "ident"], writes=[pbn[b1]])
            yield None
            op('act', lambda e: e.copy(B["MTa"][:], pb[b1][:, 0:256].rearrange("p (s n) -> p s n", s=2)), reads=[pbn[b1]], writes=[N("MTa")])
            yield None
            fns = [lambda e: e.matmul(pb[b3][:, 0:256], ident[:], ident2[:].rearrange("p s d -> p (s d)"), start=True, stop=False)]
            for s_ in range(2):
                fns.append(lambda e, s_=s_: e.matmul(pb[b3][:, s_ * 128:(s_ + 1) * 128], M0[:, s_, :], ident[:], start=False, stop=True))
            mm(fns, reads=[N("M0"), "ident", "ident2"], writes=[pbn[b3]])
            yield None
            Mprev, MTprev, Mn, MTn = "M0", "MTa", "Ma", "MTb"
            pend = None

            def pt_update(mname):
                op('dve', lambda e: e.tensor_copy(B["PTb"][:], pb[b3][:, 0:256].rearrange("p (s n) -> p s n", s=2)), reads=[pbn[b3]], writes=[N("PTb")])
                mm([lambda e, s_=s_: e.matmul(pb[b3][:, s_ * 128:(s_ + 1) * 128], B[mname][:, s_, :], B["PTb"][:, s_, :], start=False, stop=True) for s_ in range(2)],
                   reads=[N(mname), N("PTb")], writes=[pbn[b3]])
            MN3 = ["Ma", "Mb", "Mc"]
            for k_ in range(1, 6):
                Mn = MN3[k_ % 3]
                mm([lambda e, s_=s_: e.matmul(pb[b2][:, s_ * 128:(s_ + 1) * 128], B[MTprev][:, s_, :], B[Mprev][:, s_, :], start=True, stop=True) for s_ in range(2)],
                   reads=[N(MTprev), N(Mprev)], writes=[pbn[b2]])
                yield None
                if k_ < 5:
                    mm([lambda e, s_=s_: e.matmul(pb[b1][:, s_ * 128:(s_ + 1) * 128], B[Mprev][:, s_, :], B[MTprev][:, s_, :], start=True, stop=True) for s_ in range(2)],
                       reads=[N(MTprev), N(Mprev)], writes=[pbn[b1]])
                    yield None
                if pend is not None:
                    pt_update(pend)
                    yield None
                op('act', lambda e: e.copy(B[Mn][:], pb[b2][:, 0:256].rearrange("p (s n) -> p s n", s=2)), reads=[pbn[b2]], writes=[N(Mn)])
                yield None
                if k_ < 5:
                    op('dve', lambda e: e.tensor_copy(B[MTn][:], pb[b1][:, 0:256].rearrange("p (s n) -> p s n", s=2)), reads=[pbn[b1]], writes=[N(MTn)])
                    yield None
                pend = Mn
                Mprev, MTprev, MTn = Mn, MTn, ("MTa" if MTn == "MTb" else "MTb")
            pt_update(pend)
            yield None
            op('act', lambda e: e.copy(B["PTb"][:], pb[b3][:, 0:256].rearrange("p (s n) -> p s n", s=2)), reads=[pbn[b3]], writes=[N("PTb")])
            yield None
            mm([lambda e, s_=s_: e.matmul(pb[b2][:, s_ * 128:(s_ + 1) * 128], M0[:, s_, :], B["PTb"][:, s_, :], start=True, stop=True) for s_ in range(2)],
               reads=[N("M0"), N("PTb")], writes=[pbn[b2]])
            yield None
            mm([lambda e, s_=s_: e.matmul(pb[b1][:, s_ * 128:(s_ + 1) * 128], B["PTb"][:, s_, :], ident[:], start=True, stop=True) for s_ in range(2)],
               reads=[N("PTb"), "ident"], writes=[pbn[b1]])
            yield None
            op('dve', lambda e: e.scalar_tensor_tensor(Em[:, 0], B["PTb"][:], -1.0, pb[b2][:, 0:256].rearrange("p (s n) -> p s n", s=2), ALU.mult, ALU.add),
               reads=[pbn[b2], N("PTb")], writes=[(emn, 0)])
            yield None
            op('act', lambda e: e.copy(B["Mb"][:], pb[b1][:, 0:256].rearrange("p (s n) -> p s n", s=2)), reads=[pbn[b1]], writes=[N("Mb")])
            yield None
            op('pool', lambda e: e.tensor_tensor(B["Ma"][:], Em[:, 0], ident2[:], ALU.add), reads=[(emn, 0), "ident2"], writes=[N("Ma")])
            yield None
            fns = [lambda e: e.matmul(pb[b3][:, 0:256], ident[:], B["PTb"][:].rearrange("p s d -> p (s d)"), start=True, stop=False)]
            for s_ in range(2):
                fns.append(lambda e, s_=s_: e.matmul(pb[b3][:, s_ * 128:(s_ + 1) * 128], B["Mb"][:, s_, :], B["Ma"][:, s_, :], start=False, stop=True))
            mm(fns, reads=[N("Mb"), N("Ma"), N("PTb"), "ident"], writes=[pbn[b3]])
            yield None
            op('act', lambda e: e.copy(B["Wt"][:], pb[b3][:, 0:256].rearrange("p (s n) -> p s n", s=2)), reads=[pbn[b3]], writes=[N("Wt")])
            yield None
            op('pool', lambda e: e.tensor_copy(B["PTb"][:], B["Wt"][:]), reads=[N("Wt")], writes=[N("PTb")])
            yield None
            AT = B["PTb"]
            fns = []
            for s_ in range(2):
                fns.append(lambda e, s_=s_: e.matmul(pb[b0][:, s_ * 128:(s_ + 1) * 128], AT[:, s_, :], B["Kbg"][:, s_, :], start=True, stop=True))
                fns.append(lambda e, s_=s_: e.matmul(pb[b0][:, (2 + s_) * 128:(3 + s_) * 128], AT[:, s_, :], B["Vb"][:, s_, :], start=True, stop=True))
            mm(fns, reads=[N("PTb"), N("Kbg"), N("Vb")], writes=[pbn[b0]])
            yield None
            p6a = pb[b0][:, 0:256].rearrange("p (s n) -> p s n", s=2)
            p6b = pb[b0][:, 256:512].rearrange("p (s n) -> p s n", s=2)
            op('act', lambda e: e.copy(B["Wt"][:], p6a), reads=[pbn[b0]], writes=[N("Wt")])
            yield None
            op('act', lambda e: e.activation(B["Wn"][:], p6a, AF.Copy, scale=-1.0), reads=[pbn[b0]], writes=[N("Wn")])
            yield None
            op('dve', lambda e: e.tensor_copy(B["U"][:], p6b), reads=[pbn[b0]], writes=[N("U")])
            yield None
            fns = []
            for s_ in range(2):
                for hf in range(2):
                    fns.append(lambda e, s_=s_, hf=hf: e.matmul(pb[b0][:, (s_ * 2 + hf) * 128:(s_ * 2 + hf + 1) * 128], B["Wt"][:, s_, :], B["Kd%d" % hf][:, s_, :], start=True, stop=True))
            mm(fns, reads=[N("Wt"), N("Kd0"), N("Kd1")], writes=[pbn[b0]])
            yield None
            op('act', lambda e: e.activation(X, pb[b0][:].rearrange("p (s h n) -> p s h n", s=2, h=2), AF.Copy, scale=-1.0), reads=[pbn[b0]], writes=[N("X")])
            yield None
            fns = []
            for s_ in range(2):
                fns.append(lambda e, s_=s_: e.matmul(pb[b1][:, s_ * 128:(s_ + 1) * 128], B["Qg"][:, s_, :], ident[:], start=True, stop=False))
                fns.append(lambda e, s_=s_: e.matmul(pb[b1][:, s_ * 128:(s_ + 1) * 128], B["Wn"][:, s_, :], B["attnT"][:, s_, :], start=False, stop=True))
            mm(fns, reads=[N("Qg"), N("Wn"), N("attnT"), "ident"], writes=[pbn[b1]])
            yield None
            op('act', lambda e: e.copy(B["CT"][:], pb[b1][:, 0:256].rearrange("p (s n) -> p s n", s=2)), reads=[pbn[b1]], writes=[N("CT")])
            yield 'SCAN'
            for step in range(2):
                for s_, (blk, dh, d_) in enumerate(slots):
                    hf = step if d_ == 0 else 1 - step
                    R = slice(hf * 64, (hf + 1) * 64)
                    mm([lambda e: e.matmul(pb[b1][:, s_ * 128:(s_ + 1) * 128], B["attnT"][:, s_, :], B["U"][:, s_, :], start=True, stop=False),
                        lambda e: e.matmul(pb[b1][:, s_ * 128:(s_ + 1) * 128], B["CT"][:, s_, :], Sbf[:, s_, :], start=False, stop=True)],
                       reads=[N("attnT"), N("U"), N("CT"), ("Sbf", s_)], writes=[pbn[b1]])
                    yield None
                    op('dve', lambda e: e.tensor_tensor(oacc[R, blk, :], oacc[R, blk, :], pb[b1][R, s_ * 128:(s_ + 1) * 128], ALU.add), reads=[pbn[b1], ("oacc", blk)], writes=[("oacc", blk)])
                    yield None
                    mm([lambda e: e.matmul(pb[b2][:, s_ * 128:(s_ + 1) * 128], B["Kd%d" % hf][:, s_, :], B["U"][:, s_, :], start=True, stop=False),
                        lambda e: e.matmul(pb[b2][:, s_ * 128:(s_ + 1) * 128], X[:, s_, hf, :], Sbf[:, s_, :], start=False, stop=True)],
                       reads=[N("Kd0"), N("Kd1"), N("U"), N("X"), ("Sbf", s_)], writes=[pbn[b2]])
                    yield None
                    dec = dec1 if hf == 0 else dec2
                    op('dve', lambda e: e.scalar_tensor_tensor(S[:, s_, :], S[:, s_, :], dec[:, blk, dh:dh + 1], pb[b2][:, s_ * 128:(s_ + 1) * 128], ALU.mult, ALU.add),
                       reads=[pbn[b2], ("S", s_), "dec1", "dec2"], writes=[("S", s_)])
                    yield None
                    op('pool', lambda e: e.tensor_copy(Sbf[:, s_, :], S[:, s_, :]), reads=[("S", s_)], writes=[("Sbf", s_)])
                    yield None
            for s_, (blk, dh, d_) in enumerate(slots):
                if (d_ == 0 and blk % 2 == 1) or (d_ == 1 and blk % 2 == 0):
                    dma('sp', O["nst"][l, blk // 2, d_, h], S[:, s_, :], reads=[("S", s_)])
            yield None

        for j in range(NB // 2):
            gA = gen_iter(2 * j, SETS[0])
            gB = gen_iter(2 * j + 1, SETS[1])
            dA = dB = False
            while not (dA and dB):
                if not dA:
                    dA = (next(gA) == 'SCAN')
                if not dB:
                    dB = (next(gB) == 'SCAN')
            for _ in gA:
                pass
            for _ in gB:
                pass

        kb.mark('L%d C h%d post' % (l, h))
        for b in range(NB):
            op('act', lambda e: e.activation(junk[:, 0:128], oacc[:, b, :], AF.Square, accum_out=st2[:, b, 0:1]), reads=[("oacc", b)], writes=["junk", ("st2", b)])
        op('dve', lambda e: e.tensor_scalar(st2[:, :, 1:2], st2[:, :, 0:1], 1.0 / 128, 1e-6, ALU.mult, ALU.add), reads=["st2"], writes=["st2"])
        op('act', lambda e: e.activation(st2[:, :, 2:3], st2[:, :, 1:2], AF.Ln), reads=["st2"], writes=["st2"])
        op('act', lambda e: e.activation(st2[:, :, 3:4], st2[:, :, 2:3], AF.Exp, scale=-0.5), reads=["st2"], writes=["st2"])
        for b in range(NB):
            k16 = kvn16[b % 2]
            k16n = "kvn16_%d" % (b % 2)
            op('dve', lambda e: e.scalar_tensor_tensor(k16[:, 0:128], oacc[:, b, :], st2[:, b, 3:4], gnb, ALU.mult, ALU.mult), reads=[("oacc", b), "st2", "gnb"], writes=[k16n])
            bk = 4 + b % 2
            pview = pb[bk][:].bitcast(BF16)
            mm([lambda e: e.transpose(pview[:, 0:128], k16[:, 0:128], ident[:])], reads=[k16n, "ident"], writes=[pbn[bk]])
            op('act', lambda e: e.copy(ozT[:, h, b * 128:(b + 1) * 128], pview[:, 0:128]), reads=[pbn[bk]], writes=[("ozT", h, b // 4)])


def _rope_tables(sample):
    C = np.ones((128, T), np.float32)
    S = np.zeros((128, T), np.float32)
    if not sample:
        C[96:] = 0
        return C, S
    tok = np.arange(T)
    row = (tok // 64).astype(np.float32)
    col = (tok % 64).astype(np.float32)
    def tab(rot):
        npairs = rot // 4
        inv = (10000.0 ** (-np.arange(npairs, dtype=np.float32) / npairs)).astype(np.float32)
        ang = np.concatenate([row[:, None] * inv, col[:, None] * inv], axis=-1).astype(np.float32)
        c = np.cos(ang).astype(np.float32)
        s = np.sin(ang).astype(np.float32)
        Cd = np.repeat(c, 2, axis=1).T
        Sd = np.repeat(s, 2, axis=1).T
        sign = np.where(np.arange(rot) % 2 == 0, -1.0, 1.0).astype(np.float32)[:, None]
        return Cd, Sd * sign
    Ca, Sa = tab(64)
    Cb, Sb = tab(32)
    C[0:64], S[0:64] = Ca, Sa
    C[64:96], S[64:96] = Cb, Sb
    return C, S


def _mask_a(sample):
    m = np.full((6, 128, 512), NEG, np.float32)
    kj = np.arange(128)[:, None]
    qi = np.arange(128)[None, :]
    for o in range(6):
        for qb in range(4):
            blk = m[o, :, qb * 128:(qb + 1) * 128]
            if sample:
                rel = o - 1 - qb
                if rel == 0:
                    blk[:] = 0
                elif rel == -1:
                    blk[kj >= qi] = 0
                elif rel == 1:
                    blk[kj <= qi] = 0
            else:
                if (o - 1) // 2 == qb // 2 and o >= 1:
                    blk[:] = 0
    return m


def _perm_pairs(n):
    idx = np.arange(n)
    return idx ^ 1


def kernel(**inp):
    f = lambda a: np.ascontiguousarray(np.asarray(a, dtype=np.float32))
    w_in = f(inp["w_in"])
    pcols = np.concatenate([C_AQ + _perm_pairs(512), C_AK + _perm_pairs(128), C_BKPE + _perm_pairs(32)])
    w_inp = np.ascontiguousarray(w_in[:, :, pcols])
    uq = f(inp["mla_w_uq"])
    uqcols = np.arange(768).reshape(8, 96)
    uqcols[:, 64:] = uqcols[:, 64:] ^ 1
    uqp = np.ascontiguousarray(uq[:, :, uqcols.reshape(-1)])
    shared = {k: f(inp[k]) for k in ["norm_g", "w_ada", "b_ada", "attn_sink", "mla_q_norm", "mla_kv_norm", "mla_w_ukv", "gdn_conv",
                                     "gdn_norm", "w_branch_a", "w_branch_b", "w_branch_c", "w_out", "final_norm_g"]}
    shared["w_in"] = w_in
    shared["w_inp"] = w_inp
    shared["mla_w_uq"] = uq
    shared["mla_w_uqp"] = uqp
    shared["gdn_a_log"] = f(inp["gdn_a_log"]).reshape(2, 8)
    shared["gdn_dt_bias"] = f(inp["gdn_dt_bias"]).reshape(2, 8)
    shared["ident"] = np.eye(128, dtype=np.float32)
    a = np.arange(128)
    same = (a[:, None] // 64) == (a[None, :] // 64)
    gm1 = np.full((128, 8, 128), NEG, np.float32)
    gm2 = np.full((128, 8, 128), NEG, np.float32)
    for dh in range(8):
        if dh < 4:
            gm1[:, dh][(a[:, None] > a[None, :]) & same] = 0
            gm2[:, dh][(a[None, :] >= a[:, None]) & same] = 0
        else:
            gm1[:, dh][(a[:, None] < a[None, :]) & same] = 0
            gm2[:, dh][(a[None, :] <= a[:, None]) & same] = 0
    shared["gm1"], shared["gm2"] = gm1, gm2
    shared["triF"] = ((a[:, None] <= a[None, :]) & same).astype(np.float32)
    shared["triB"] = ((a[:, None] >= a[None, :]) & same).astype(np.float32)
    shared["sel1"] = np.repeat((a < 64).astype(np.float32)[:, None], 128, 1)
    shared["sel2"] = np.repeat((a >= 64).astype(np.float32)[:, None], 128, 1)
    xp = f(inp["x_prompt"]); xsm = f(inp["x_sample"])
    in_maps = []
    for c in range(8):
        m = dict(shared)
        sample = c < 4
        if sample:
            m["x"] = xsm[c]
            m["cond"] = f(inp["c"])[c]
            m["ck"] = f(inp["cache_attn_k"])[c].reshape(2, 512, 128)
            m["cv"] = f(inp["cache_attn_v"])[c].reshape(2, 512, 128)
            m["cckv"] = f(inp["cache_mla_ckv"])[c]
            m["ckpe"] = f(inp["cache_mla_kpe"])[c]
            m["st"] = f(inp["state_gdn"])[c].reshape(2, 8, 128, 128)
            qoh = np.zeros((8, T), np.float32); qoh[0] = 1
            koh = np.zeros((8, KT), np.float32); koh[0] = BIGM
            flags = np.zeros((128, 4), np.float32); flags[:, 0] = 1.0
        else:
            k = c - 4
            m["x"] = xp[8 * k:8 * k + 8].reshape(T, D)
            m["cond"] = f(inp["c_ctx"])
            m["ck"] = np.zeros((2, 512, 128), np.float32)
            m["cv"] = np.zeros((2, 512, 128), np.float32)
            m["cckv"] = np.zeros((2, 512, 256), np.float32)
            m["ckpe"] = np.zeros((2, 512, 32), np.float32)
            m["st"] = np.zeros((2, 8, 128, 128), np.float32)
            qoh = np.zeros((8, T), np.float32); koh = np.zeros((8, KT), np.float32)
            for s in range(8):
                qoh[s, s * 256:(s + 1) * 256] = 1
                koh[s, s * 256:(s + 1) * 256] = BIGM
            flags = np.zeros((128, 4), np.float32); flags[:, 1] = NEG; flags[:, 2] = -1.0
        m["qoh"], m["koh"], m["flags"] = qoh, koh, flags
        m["ropeC"], m["ropeS"] = _rope_tables(sample)
        m["maskA"] = _mask_a(sample)
        in_maps.append({n: np.ascontiguousarray(m[n], dtype=np.float32).reshape(s) for n, s in IN_SPECS})
    nc = build()
    res = run_bass_kernel_spmd(nc, in_maps, core_ids=list(range(8)))
    R = res.results
    y_sample = np.stack([R[c]["y"] for c in range(4)], 0)
    y_prompt = np.concatenate([R[c]["y"].reshape(8, 256, D) for c in range(4, 8)], 0)
    def pc(name, tail):
        return np.concatenate([np.moveaxis(R[c][name].reshape(2, 8, 256, *tail), 0, 1) for c in range(4, 8)], 0)
    nk = pc("nk", (2, 64)); nv = pc("nv", (2, 64)); nckv = pc("nckv", (256,)); nkpe = pc("nkpe", (32,))
    nst = np.concatenate([np.moveaxis(R[c]["nst"], 0, 1) for c in range(4, 8)], 0)
    return (y_prompt.astype(np.float32), y_sample.astype(np.float32), nk.astype(np.float32), nv.astype(np.float32),
            nckv.astype(np.float32), nkpe.astype(np.float32), nst.astype(np.float32))
```

```python
import numpy as np
from contextlib import ExitStack
import concourse.bass as bass
import concourse.mybir as mybir
from concourse.bass_utils import run_bass_kernel_spmd

F32 = mybir.dt.float32
BF16 = mybir.dt.bfloat16
AF = mybir.ActivationFunctionType
ALU = mybir.AluOpType

T = 2048
NB = 16
NSB = 4
KT = 2560
NKB = 20
D = 1024
BIGM = 2048.0
NEG = -30000.0
N_DSEM = 40
LIMIT = None
LAST_KB = None
C_AQ, C_AK, C_AV, C_ZA, C_BCQ, C_BCKV, C_BKPE, C_ZB, C_CQKV, C_CA, C_CB, C_ZC, C_G = (
    0, 512, 640, 768, 1280, 1664, 1920, 1952, 2464, 4000, 4008, 4016, 4528)


class KB:
    def __init__(self, nc, es):
        self.nc = nc
        self.E = {'pe': nc.tensor, 'act': nc.scalar, 'dve': nc.vector, 'pool': nc.gpsimd, 'sp': nc.sync}
        self.sem = {e: es.enter_context(nc.semaphore("s_" + e)) for e in self.E}
        self.cnt = {e: 0 for e in self.E}
        self.seen = {e: {} for e in self.E}
        self.dsem = [es.enter_context(nc.semaphore("d%d" % i)) for i in range(N_DSEM)]
        self.dcnt = [0] * N_DSEM
        self.dnext = 0
        self.reg = {}
        self.n_ins = 0
        self.limit = LIMIT
        self.n_calls = 0

    def _wait(self, eng, tok):
        if tok is None:
            return
        key = (tok[0], tok[1])
        if self.seen[eng].get(key, 0) >= tok[2]:
            return
        if eng == 'pe' and tok[0] == 'e' and tok[1] == 'pe':
            return
        if tok[0] == 'e':
            self.E[eng].wait_ge(self.sem[tok[1]], tok[2])
        else:
            self.E[eng].wait_ge(self.dsem[tok[1]], tok[2])
        self.seen[eng][key] = tok[2]

    def _entries(self, r):
        if isinstance(r, tuple):
            name, sub = r[0], (r[1] if len(r) == 2 else r[1:])
        else:
            name, sub = r, None
        d = self.reg.setdefault(name, {})
        if sub is None:
            if None not in d:
                d[None] = [None, []]
            return [d[k] for k in d], d, None
        out = []
        if None in d:
            out.append(d[None])
        if sub not in d:
            d[sub] = [None, []]
        out.append(d[sub])
        return out, d, sub

    @staticmethod
    def _norm(reads, writes):
        r2, w2 = [], []
        for r in reads:
            nm = r[0] if isinstance(r, tuple) else r
            if nm.startswith("pb"):
                w2.append(nm)
            else:
                r2.append(r)
        for w in writes:
            nm = w[0] if isinstance(w, tuple) else w
            w2.append(nm if nm.startswith("pb") else w)
        return r2, w2

    def _deps(self, eng, reads, writes):
        for r in reads:
            for en in self._entries(r)[0]:
                self._wait(eng, en[0])
        for r in writes:
            for en in self._entries(r)[0]:
                self._wait(eng, en[0])
                for t in en[1]:
                    self._wait(eng, t)

    def _record(self, tok, reads, writes):
        for r in reads:
            _, d, sub = self._entries(r)
            lst = d[sub][1]
            if tok[0] == 'e':
                lst[:] = [t for t in lst if not (t[0] == 'e' and t[1] == tok[1])]
            lst.append(tok)
            if len(lst) > 48:
                del lst[0:len(lst) - 48]
        for r in writes:
            _, d, sub = self._entries(r)
            if sub is None:
                for k in list(d.keys()):
                    if k is not None:
                        del d[k]
            d[sub] = [tok, []]

    def op(self, eng, fn, reads=(), writes=()):
        reads, writes = self._norm(reads, writes)
        self.n_calls += 1
        if self.limit is not None and self.n_calls > self.limit:
            return None
        self._deps(eng, reads, writes)
        ins = fn(self.E[eng])
        self.cnt[eng] += 1
        ins.then_inc(self.sem[eng], 1)
        tok = ('e', eng, self.cnt[eng])
        self._record(tok, reads, writes)
        self.n_ins += 1
        return tok

    def mmgroup(self, fns, reads=(), writes=()):
        reads, writes = self._norm(reads, writes)
        self.n_calls += 1
        if self.limit is not None and self.n_calls > self.limit:
            return None
        self._deps('pe', reads, writes)
        ins = None
        for f in fns:
            ins = f(self.E['pe'])
        self.cnt['pe'] += 1
        ins.then_inc(self.sem['pe'], 1)
        tok = ('e', 'pe', self.cnt['pe'])
        self._record(tok, reads, writes)
        self.n_ins += len(fns)
        return tok

    def dma(self, q, out, in_, reads=(), writes=(), **kw):
        reads, writes = self._norm(reads, writes)
        self.n_calls += 1
        if self.limit is not None and self.n_calls > self.limit:
            return None
        self._deps(q, reads, writes)
        s = self.dnext
        self.dnext = (self.dnext + 1) % N_DSEM
        if self.dcnt[s] > 0:
            self._wait(q, ('d', s, 16 * self.dcnt[s]))
        self.dcnt[s] += 1
        self.E[q].dma_start(out=out, in_=in_, **kw).then_inc(self.dsem[s], 16)
        tok = ('d', s, 16 * self.dcnt[s])
        self._record(tok, reads, writes)
        self.n_ins += 1
        return tok

    def mark(self, label):
        self.marks = getattr(self, "marks", [])
        self.marks.append((label, dict(self.cnt)))
        global LAST_KB
        LAST_KB = self

    def barrier(self):
        for e in self.E:
            for e2 in self.E:
                if e2 != e and self.cnt[e2] > 0:
                    self._wait(e, ('e', e2, self.cnt[e2]))
            for sx in range(N_DSEM):
                if self.dcnt[sx] > 0:
                    self._wait(e, ('d', sx, 16 * self.dcnt[sx]))

    def finish(self):
        for e in self.E:
            if self.cnt[e] > 0:
                self._wait('sp', ('e', e, self.cnt[e]))
        for s in range(N_DSEM):
            if self.dcnt[s] > 0:
                self._wait('sp', ('d', s, 16 * self.dcnt[s]))


IN_SPECS = [
    ("x", [T, D]), ("cond", [D]), ("norm_g", [2, D]), ("w_ada", [2, D, 3 * D]), ("b_ada", [2, 3 * D]),
    ("w_in", [2, D, 7600]), ("w_inp", [2, D, 672]), ("attn_sink", [2, 8]), ("mla_q_norm", [2, 384]),
    ("mla_w_uq", [2, 384, 768]), ("mla_w_uqp", [2, 384, 768]), ("mla_kv_norm", [2, 256]),
    ("mla_w_ukv", [2, 256, 1024]), ("gdn_conv", [2, 3, 1536]), ("gdn_a_log", [2, 8]), ("gdn_dt_bias", [2, 8]),
    ("gdn_norm", [2, 128]), ("w_branch_a", [2, 512, D]), ("w_branch_b", [2, 512, D]), ("w_branch_c", [2, 512, D]),
    ("w_out", [2, D, D]), ("final_norm_g", [D]),
    ("ck", [2, 512, 128]), ("cv", [2, 512, 128]), ("cckv", [2, 512, 256]), ("ckpe", [2, 512, 32]),
    ("st", [2, 8, 128, 128]),
    ("ropeC", [128, T]), ("ropeS", [128, T]), ("maskA", [6, 128, 512]), ("qoh", [8, T]), ("koh", [8, KT]),
    ("flags", [128, 4]), ("ident", [128, 128]), ("gm1", [128, 8, 128]), ("gm2", [128, 8, 128]),
    ("triF", [128, 128]), ("triB", [128, 128]), ("sel1", [128, 128]), ("sel2", [128, 128]),
]
OUT_SPECS = [
    ("y", [T, D]), ("nk", [2, T, 128]), ("nv", [2, T, 128]), ("nckv", [2, T, 256]), ("nkpe", [2, T, 32]),
    ("nst", [2, 8, 2, 4, 128, 128]),
]


def build(stop=None, taps=None):
    nc = bass.Bass("TRN2", target_bir_lowering=False)
    I = {n: nc.dram_tensor(n, s, F32, kind="ExternalInput").ap() for n, s in IN_SPECS}
    O = {n: nc.dram_tensor(n, s, F32, kind="ExternalOutput").ap() for n, s in OUT_SPECS}
    xs = nc.dram_tensor("xs", [T, D], F32, kind="Internal").ap()
    TAP = {}
    if taps:
        for n, s in taps.items():
            TAP[n] = nc.dram_tensor("tap_" + n, s, F32, kind="ExternalOutput").ap()
    with ExitStack() as es:
        kb = KB(nc, es)
        SB = lambda name, shape, dt: es.enter_context(nc.sbuf_tensor("sb_" + name, shape, dt))
        PS = lambda name, shape, dt: es.enter_context(nc.psum_tensor("ps_" + name, shape, dt))
        _body(nc, kb, SB, PS, I, O, xs, TAP, stop)
        kb.mark('end')
        kb.finish()
    return nc


def _body(nc, kb, SB, PS, I, O, xs, TAP, stop):
    op, dma, mm = kb.op, kb.dma, kb.mmgroup
    ident = SB("ident", [128, 128], BF16)
    ident32 = SB("ident32", [128, 128], F32)
    ropeC = SB("ropeC", [128, T], BF16)
    ropeS = SB("ropeS", [128, T], BF16)
    flags = SB("flags", [128, 4], F32)
    ones16 = SB("ones16", [128, 128], BF16)
    op('dve', lambda e: e.memset(ones16[:], 1.0), writes=["ones16"])
    dma('pool', ident[:], I["ident"], writes=["ident"])
    dma('sp', ident32[:], I["ident"], writes=["ident32"])
    dma('pool', ropeC[:], I["ropeC"], writes=["ropeC"])
    dma('pool', ropeS[:], I["ropeS"], writes=["ropeS"])
    dma('sp', flags[:], I["flags"], writes=["flags"])

    hT = SB("hT", [128, 8, T], BF16)
    mergeT = SB("mergeT", [128, 8, T], BF16)
    ozT = SB("ozT", [128, 4, T], BF16)
    pb = [PS("pb%d" % i, [128, 512], F32) for i in range(8)]
    pbn = ["pb%d" % i for i in range(8)]

    def hTr(sb):
        return [("hT", sb * 4 + i) for i in range(4)]

    NW = 2
    WCOLS = 512
    wbuf = [SB("wbuf%d" % i, [128, 8 * WCOLS], BF16) for i in range(NW)]
    wstate = {'i': 0}

    def load_w(src2d, kch, cols, q='pool', prows=128):
        i = wstate['i']
        wstate['i'] = (i + 1) % NW
        name = "wbuf%d" % i
        tot = sum(n for _, n in cols)
        assert kch * tot <= 8 * WCOLS, (kch, tot)
        view = wbuf[i][0:prows, 0:kch * tot].rearrange("p (k n) -> p k n", k=kch)
        srcv = src2d.rearrange("(k p) n -> p k n", p=prows)
        o = 0
        for c0, n in cols:
            dma(q, view[:, :, o:o + n], srcv[:, :, c0:c0 + n], writes=[name])
            o += n
        return view, name

    def tap(name, ap_sb, reads):
        if name in TAP:
            dma('sp', TAP[name], ap_sb, reads=reads)

    condsb = SB("condsb", [128, 8], F32)
    scond = SB("scond", [128, 8], BF16)
    modfm = SB("modfm", [128, 24], F32)
    badafm = SB("badafm", [128, 24], F32)
    ngfm = SB("ngfm", [128, 8], F32)
    Afm = SB("Afm", [128, 8], F32)
    gbc = SB("gbc", [128, 8, 128], F32)
    gateb = SB("gateb", [128, D], F32)
    xblk = [SB("xblk%d" % i, [128, D], F32) for i in range(2)]
    xn = [SB("xn%d" % i, [128, D], BF16) for i in range(2)]
    junk = SB("junk", [128, D], BF16)
    stat = SB("stat", [128, NB, 4], F32)
    qTh = [SB("qTh%d" % i, [104, T], BF16) for i in range(1)] * 2
    wukv = SB("wukv", [128, 2, 1024], BF16)
    ARN = 21952
    arena = SB("arena", [128, ARN], BF16)
    maskA = arena[:, 4 * KT:4 * KT + 3072].rearrange("p (o n) -> p o n", o=6)
    o_ = 0
    kTa = arena[0:64, 0:2 * KT].rearrange("p (g n) -> p g n", g=2)
    Va = arena[:, 2 * KT:2 * KT + NKB * 256].rearrange("p (k g d) -> p k g d", k=NKB, g=2)
    kTb1 = arena[0:104, 0:KT]
    kpeT = arena[0:96, KT:2 * KT]
    Vb1 = arena[:, 2 * KT:2 * KT + NKB * 128].rearrange("p (k d) -> p k d", k=NKB)
    o_ = 2 * KT + NKB * 128
    ckvT = arena[:, o_:o_ + 2 * KT].rearrange("p (c n) -> p c n", c=2)
    cqnT = arena[:, o_ + 2 * KT:o_ + 2 * KT + 3 * T].rearrange("p (c n) -> p c n", c=3)
    kTb = [kTb1, kTb1]
    Vb = [Vb1, Vb1]
    pT = [SB("pT%d" % i, [128, 512], BF16) for i in range(3)]
    t1 = [SB("t1_%d" % i, [128, 512], F32) for i in range(2)]
    t2 = [SB("t2_%d" % i, [128, 512], F32) for i in range(2)]
    rr = [SB("rr%d" % i, [128, 512], F32) for i in range(1)] * 2
    r3 = [SB("r3_%d" % i, [64, 512], F32) for i in range(1)] * 2
    kvout = [SB("kvout%d" % i, [128, 288], F32) for i in range(2)]
    kvn16 = [SB("kvn16_%d" % i, [128, 384], BF16) for i in range(2)]
    ctx16 = SB("ctx16", [128, 4, 256], BF16)
    ctxp = SB("ctxp", [128, 4, 96], BF16)
    esink = SB("esink", [128, 8], F32)
    kvng = SB("kvng", [128, 256], F32)
    qng = SB("qng", [128, 384], F32)
    st2 = SB("st2", [128, NB, 4], F32)
    sig = [SB("sig%d" % i, [128, 512], F32) for i in range(2)]
    op('dve', lambda e: e.memset(ctxp[:], 0.0), writes=["ctxp"])
    dma('pool', qTh[0][96:104, :], I["qoh"], writes=["qTh0"])
    cnt = {'rot': 0, 'p': 0, 'o': 0}
    if stop == "c":
        return

    pT.append(SB("pT3", [128, 512], BF16))

    def attend_stream(groups):
        SBK = [2, 3, 6, 7]
        tiles = []
        for gi, g_ in enumerate(groups):
            g_['ob'] = 4 + cnt['o'] % 2
            cnt['o'] += 1
            for idx in range(len(g_['klist'])):
                tiles.append((gi, idx))
        info = {}

        def emit_S(t):
            gi, idx = tiles[t]
            g_ = groups[gi]
            kblk, mi, isctx = g_['klist'][idx]
            sbk = SBK[cnt['p'] % 4]
            pt = cnt['p'] % 4
            cnt['p'] += 1
            kap, kname = g_['kfn'](kblk)
            qtile, sbi, K = g_['qtile'], g_['sbi'], g_['K']
            fns = [lambda e: e.matmul(pb[sbk][:], kap, qtile[0:K, sbi * 512:(sbi + 1) * 512], start=True, stop=(mi is None))]
            rd = [kname, (g_['qname'], sbi)]
            if mi is not None:
                fns.append(lambda e: e.matmul(pb[sbk][:], ident[:], maskA[:, mi, :], start=False, stop=True))
                rd += ["ident", "maskA"]
            mm(fns, reads=rd, writes=[pbn[sbk]])
            b_ = g_['bias_fn'](isctx)
            op('act', lambda e: e.activation(pT[pt][:], pb[sbk][:], AF.Exp, scale=g_['scale'], bias=b_),
               reads=[pbn[sbk], "flags"], writes=["pT%d" % pt])
            info[t] = pt

        LOOK = 3
        nt = len(tiles)
        for t in range(min(LOOK, nt)):
            emit_S(t)
        for t in range(nt):
            if t + LOOK < nt:
                emit_S(t + LOOK)
            gi, idx = tiles[t]
            g_ = groups[gi]
            n = len(g_['klist'])
            kblk = g_['klist'][idx][0]
            pt = info[t]
            ob = g_['ob']
            vap, vname = g_['vfn'](kblk)
            mm([lambda e: e.matmul(pb[ob][:], vap, pT[pt][:], start=(idx == 0), stop=(idx == n - 1))],
               reads=[vname, "pT%d" % pt], writes=[pbn[ob]])
            if idx == n - 1:
                g_['fin'](ob)

    def run_pairs(genf, n):
        for b0_ in range(0, n, 2):
            gs = [genf(b0_), genf(b0_ + 1)]
            alive = [True, True]
            while any(alive):
                for q_ in range(2):
                    if alive[q_]:
                        try:
                            next(gs[q_])
                        except StopIteration:
                            alive[q_] = False

    for l in range(2):
        xsrc = I["x"] if l == 0 else xs
        Win = I["w_in"][l]
        Winp = I["w_inp"][l]
        kb.mark('L%d start' % l)
        dma('sp', condsb[:], I["cond"].rearrange("(c p) -> p c", p=128), writes=["cond"], allow_slow_non_contiguous=True)
        dma('sp', badafm[:], I["b_ada"][l].rearrange("(c p) -> p c", p=128), writes=["bada"], allow_slow_non_contiguous=True)
        dma('sp', ngfm[:], I["norm_g"][l].rearrange("(c p) -> p c", p=128), writes=["ngfm"], allow_slow_non_contiguous=True)
        op('act', lambda e: e.activation(scond[:], condsb[:], AF.Silu), reads=["cond"], writes=["scond"])
        for nt in range(6):
            wv, wn = load_w(I["w_ada"][l], 8, [(nt * 512, 512)])
            for jj in range(4):
                j = nt * 4 + jj
                mm([(lambda e, c=c, jj=jj, j=j, wv=wv: e.matmul(pb[0][:, j:j + 1], wv[:, c, jj * 128:(jj + 1) * 128], scond[:, c:c + 1],
                                                               start=(c == 0), stop=(c == 7))) for c in range(8)],
                   reads=[wn, "scond"], writes=[(pbn[0], j)])
        op('dve', lambda e: e.tensor_tensor(modfm[:], pb[0][:, 0:24], badafm[:], ALU.add), reads=[pbn[0], "bada"], writes=["modfm"])
        op('dve', lambda e: e.scalar_tensor_tensor(Afm[:], modfm[:, 8:16], 1.0, ngfm[:], ALU.add, ALU.mult), reads=["modfm", "ngfm"], writes=["Afm"])
        op('dve', lambda e: e.tensor_copy(gbc[:], modfm[:, 16:24].unsqueeze(2).to_broadcast([128, 8, 128])), reads=["modfm"], writes=["gbc"])
        for c in range(8):
            bkc = 1 + c // 4
            mm([lambda e, c=c, bkc=bkc: e.matmul(pb[bkc][:, (c % 4) * 128:(c % 4 + 1) * 128], gbc[:, c, :], ident32[:], start=True, stop=True)],
               reads=["gbc", "ident32"], writes=[(pbn[bkc], c % 4)])
        op('act', lambda e: e.copy(gateb[:, 0:512], pb[1][:]), reads=[pbn[1]], writes=[("gateb", 0)])
        op('act', lambda e: e.copy(gateb[:, 512:1024], pb[2][:]), reads=[pbn[2]], writes=[("gateb", 1)])
        tap("modfm%d" % l, modfm[:], ["modfm"])
        if stop == "p0":
            return

        kb.mark('L%d p1' % l)
        def gen_p1(b):
            xb, xbn = xblk[b % 2], "xblk%d" % (b % 2)
            xnb, xnn = xn[b % 2], "xn%d" % (b % 2)
            dma('sp', xb[:], xsrc[b * 128:(b + 1) * 128, :], reads=(["xs"] if l == 1 else []), writes=[xbn])
            yield None
            op('act', lambda e: e.activation(junk[:], xb[:], AF.Square, accum_out=stat[:, b, 0:1]), reads=[xbn], writes=["junk", ("stat", b)])
            yield None
            op('dve', lambda e: e.tensor_scalar(stat[:, b, 1:2], stat[:, b, 0:1], 1.0 / D, 1e-6, ALU.mult, ALU.add), reads=[("stat", b)], writes=[("stat", b)])
            yield None
            op('act', lambda e: e.activation(stat[:, b, 2:3], stat[:, b, 1:2], AF.Ln), reads=[("stat", b)], writes=[("stat", b)])
            yield None
            op('act', lambda e: e.activation(stat[:, b, 3:4], stat[:, b, 2:3], AF.Exp, scale=-0.5), reads=[("stat", b)], writes=[("stat", b)])
            yield None
            op('dve', lambda e: e.tensor_scalar(xnb[:], xb[:], stat[:, b, 3:4], None, ALU.mult), reads=[xbn, ("stat", b)], writes=[xnn])
            yield None
            for half in range(2):
                bk = 4 + (2 * b + half) % 4
                pview = pb[bk][:].bitcast(BF16)
                mm([(lambda e, c=c, half=half, pview=pview: e.transpose(pview[:, c * 128:(c + 1) * 128], xnb[:, (half * 4 + c) * 128:(half * 4 + c + 1) * 128], ident[:]))
                    for c in range(4)], reads=[xnn, "ident"], writes=[pbn[bk]])
                yield None
                for c in range(4):
                    cc = half * 4 + c
                    if True:
                        op('act', lambda e, c=c, cc=cc, pview=pview: e.activation(hT[:, cc, b * 128:(b + 1) * 128], pview[:, c * 128:(c + 1) * 128], AF.Identity,
                                                                                 scale=Afm[:, cc:cc + 1], bias=modfm[:, cc:cc + 1]),
                           reads=[pbn[bk], "Afm", "modfm"], writes=[("hT", b, cc)])
                        yield None
                    else:
                        op('dve', lambda e, c=c, cc=cc, pview=pview: e.scalar_tensor_tensor(hT[:, cc, b * 128:(b + 1) * 128], pview[:, c * 128:(c + 1) * 128],
                                                                                    Afm[:, cc:cc + 1], modfm[:, cc:cc + 1].to_broadcast([128, 128]), ALU.mult, ALU.add),
                           reads=[pbn[bk], "Afm", "modfm"], writes=[("hT", b, cc)])
                        yield None
        run_pairs(gen_p1, NB)
        op('pool', lambda e: e.memset(junk[0:1, 0:1], 0.0), reads=[], writes=["hT"])
        HR = ["hT"]
        if "hT%d" % l in TAP:
            for c8 in range(8):
                op('dve', lambda e: e.tensor_copy(xblk[0][:].rearrange("p (a b) -> p a b", a=1)[:, 0, :], hT[:, c8, 0:1024]), reads=["hT"], writes=["xblk0"])
                dma('sp', TAP["hT%d" % l][:, c8, 0:1024], xblk[0][:], reads=["xblk0"])
                op('dve', lambda e: e.tensor_copy(xblk[0][:], hT[:, c8, 1024:2048]), reads=["hT"], writes=["xblk0"])
                dma('sp', TAP["hT%d" % l][:, c8, 1024:2048], xblk[0][:], reads=["xblk0"])
        if stop == "p1":
            return

        def lin(bank, wv, wn, col0, M, sbi, kch=8, rhs_fn=None, extra_reads=()):
            if rhs_fn is None:
                rhs_fn = lambda c: hT[:, c, sbi * 512:(sbi + 1) * 512]
            mm([(lambda e, c=c: e.matmul(pb[bank][0:M, :], wv[:, c, col0:col0 + M], rhs_fn(c), start=(c == 0), stop=(c == kch - 1)))
                for c in range(kch)], reads=[wn] + HR + list(extra_reads), writes=[pbn[bank]])

        kb.mark('L%d A-pre' % l)
        dma('sp', esink[:], I["attn_sink"][l].partition_broadcast(128), writes=["esink"])
        op('act', lambda e: e.activation(esink[:], esink[:], AF.Exp), reads=["esink"], writes=["esink"])
        dma('sp', kvng[:], I["mla_kv_norm"][l].partition_broadcast(128), writes=["kvng"])
        kb.barrier()
        op('dve', lambda e: e.memset(Va[:, :, :, 64:128], 1.0), writes=["Va"])
        dma('pool', maskA, I["maskA"].rearrange("o p n -> p o n"), writes=["maskA"])
        wq = arena[:, 13312:13312 + 4096].rearrange("p (k n) -> p k n", k=8)
        wqp = arena[:, 17408:17408 + 4096].rearrange("p (k n) -> p k n", k=8)
        wqn, wqpn = "mwq", "mwqp"
        wkv, wkvn = load_w(Win, 8, [(C_AK, 256)])
        def gen_akv(b):
            bk = 6 + b % 2
            ko = kvout[b % 2]
            kon = "kvout%d" % (b % 2)
            mm([(lambda e, c=c: e.matmul(pb[bk][:, 0:256], hT[:, c, b * 128:(b + 1) * 128], wkv[:, c, 0:256], start=(c == 0), stop=(c == 7))) for c in range(8)],
               reads=[wkvn] + HR, writes=[(pbn[bk], 0)])
            yield None
            op('act', lambda e: e.copy(ko[:, 0:256], pb[bk][:, 0:256]), reads=[(pbn[bk], 0)], writes=[kon])
            yield None
            op('dve', lambda e: e.tensor_copy(Va[:, b, :, 0:64], pb[bk][:, 128:256].rearrange("p (g d) -> p g d", g=2)), reads=[(pbn[bk], 0), kon], writes=[("Va", b)])
            yield None
            dma('sp', O["nk"][l, b * 128:(b + 1) * 128, :], ko[:, 0:128], reads=[kon])
            yield None
            dma('sp', O["nv"][l, b * 128:(b + 1) * 128, :], ko[:, 128:256], reads=[kon])
            yield None
        run_pairs(gen_akv, NB)
        dma('pool', ctx16[:, :, 0:128], I["ck"][l].rearrange("(j p) n -> p j n", p=128), writes=["ctx16"])
        for g in range(2):
            pview = pb[6 + g][:].bitcast(BF16)
            mm([(lambda e, j=j, pview=pview: e.transpose(pview[0:64, j * 128:(j + 1) * 128], ctx16[:, j, g * 64:(g + 1) * 64], ident[:])) for j in range(4)],
               reads=["ctx16", "ident"], writes=[pbn[6 + g]])
            op('dve', lambda e, pview=pview: e.tensor_copy(kTa[:, g, T:KT], pview[0:64, 0:512]), reads=[pbn[6 + g]], writes=[("kTa", g, 4)])
        for g in range(2):
            dma('pool', Va[:, NB:NKB, g, 0:64], I["cv"][l].rearrange("(j p) (g d) -> p j g d", p=128, g=2)[:, :, g, :], writes=[("Va", "ctx", g)])
        wk, wkn = load_w(Win, 8, [(C_AK, 128)])
        wkp, wkpn = load_w(Winp, 8, [(512, 128)])
        dma('pool', wq, Win.rearrange("(k p) n -> p k n", p=128)[:, :, C_AQ:C_AQ + 512], writes=[wqn])
        dma('pool', wqp, Winp.rearrange("(k p) n -> p k n", p=128)[:, :, 0:512], writes=[wqpn])
        for g in range(2):
            def gen_ka(sbi, g=g):
                r = sbi % 2
                ba, bb_ = 2 * r, 2 * r + 1
                lin(ba, wk, wkn, g * 64, 64, sbi)
                yield None
                lin(bb_, wkp, wkpn, g * 64, 64, sbi)
                yield None
                op('dve', lambda e: e.tensor_tensor(t1[r][0:64, :], pb[ba][0:64, :], ropeC[0:64, sbi * 512:(sbi + 1) * 512], ALU.mult), reads=[pbn[ba], "ropeC"], writes=["t1_%d" % r])
                yield None
                op('dve', lambda e: e.tensor_tensor(t2[r][0:64, :], pb[bb_][0:64, :], ropeS[0:64, sbi * 512:(sbi + 1) * 512], ALU.mult), reads=[pbn[bb_], "ropeS"], writes=["t2_%d" % r])
                yield None
                op('pool', lambda e: e.tensor_tensor(kTa[:, g, sbi * 512:(sbi + 1) * 512], t1[r][0:64, :], t2[r][0:64, :], ALU.add), reads=["t1_%d" % r, "t2_%d" % r], writes=[("kTa", g, sbi)])
                yield None
            run_pairs(gen_ka, NSB)
        kb.mark('L%d A-attn' % l)
        for h in range(8):
            g = h // 4
            qt, qn = qTh[h % 2], "qTh0"
            def gen_qa(sbi, h=h, qt=qt, qn=qn):
                r = sbi % 2
                ba, bb_ = 2 * r, 2 * r + 1
                lin(ba, wq, wqn, h * 64, 64, sbi)
                yield None
                lin(bb_, wqp, wqpn, h * 64, 64, sbi)
                yield None
                op('dve', lambda e: e.tensor_tensor(t1[r][0:64, :], pb[ba][0:64, :], ropeC[0:64, sbi * 512:(sbi + 1) * 512], ALU.mult), reads=[pbn[ba], "ropeC"], writes=["t1_%d" % r])
                yield None
                op('dve', lambda e: e.tensor_tensor(t2[r][0:64, :], pb[bb_][0:64, :], ropeS[0:64, sbi * 512:(sbi + 1) * 512], ALU.mult), reads=[pbn[bb_], "ropeS"], writes=["t2_%d" % r])
                yield None
                op('pool', lambda e: e.tensor_tensor(qt[0:64, sbi * 512:(sbi + 1) * 512], t1[r][0:64, :], t2[r][0:64, :], ALU.add), reads=["t1_%d" % r, "t2_%d" % r], writes=[(qn, sbi)])
                yield None
            run_pairs(gen_qa, NSB)
            groups = []
            for sbi in range(NSB):
                klist = []
                for o in range(6):
                    j = 4 * sbi - 1 + o
                    if 0 <= j < NB:
                        klist.append((j, o, False))
                for j in range(NB, NKB):
                    klist.append((j, None, True))

                def kfn(kblk, g=g):
                    return kTa[:, g, kblk * 128:(kblk + 1) * 128], ("kTa", g, kblk // 4)

                def vfn(kblk, g=g):
                    return Va[:, kblk, g, :], (("Va", kblk) if kblk < NB else ("Va", "ctx", g))

                def fin(ob, h=h, sbi=sbi):
                    r = cnt['rot'] % 2
                    cnt['rot'] += 1
                    op('dve', lambda e: e.tensor_scalar(rr[r][64:128, :], pb[ob][64:128, :], esink[64:128, h:h + 1], None, ALU.add), reads=[pbn[ob], "esink"], writes=["rr0"])
                    op('dve', lambda e: e.reciprocal(rr[r][64:128, :], rr[r][64:128, :]), reads=["rr0"], writes=["rr0"])
                    op('pool', lambda e: e.tensor_copy(r3[r][0:64, :], rr[r][64:128, :]), reads=["rr0"], writes=["r3_0"])
                    po = (h % 2) * 64
                    op('dve', lambda e: e.tensor_tensor(ozT[po:po + 64, h // 2, sbi * 512:(sbi + 1) * 512], pb[ob][0:64, :], r3[r][0:64, :], ALU.mult),
                       reads=[pbn[ob], "r3_0"], writes=[("ozT", h // 2, sbi)])

                groups.append(dict(qtile=qt, qname=qn, sbi=sbi, kfn=kfn, klist=klist, vfn=vfn, K=64, scale=0.125,
                                   bias_fn=(lambda isctx: (flags[:, 1:2] if isctx else 0.0)), fin=fin))
            attend_stream(groups)
        if "ozA%d" % l in TAP:
            for c8 in range(4):
                for hf in range(2):
                    op('dve', lambda e: e.tensor_copy(xblk[0][:], ozT[:, c8, hf * 1024:(hf + 1) * 1024]), reads=["ozT"], writes=["xblk0"])
                    dma('sp', TAP["ozA%d" % l][:, c8, hf * 1024:(hf + 1) * 1024], xblk[0][:], reads=["xblk0"])

        def zmul_and_merge(zcol, wbr_src, gcol, first, after_loads=None):
            kb.barrier()

            def load_into(k_, name, src2d, kch, c0, n):
                v = arena[:, k_ * 4096:k_ * 4096 + kch * n].rearrange("p (k n) -> p k n", k=kch)
                dma('pool', v, src2d.rearrange("(k p) n -> p k n", p=128)[:, :, c0:c0 + n], writes=[name])
                return v, name
            wz, wzn = load_into(0, "mw0", Win, 8, zcol, 512)
            wbs, wgs = [], []
            for ch in range(2):
                wbs.append(load_into(1 + 2 * ch, "mw%d" % (1 + 2 * ch), wbr_src, 4, ch * 512, 512))
                wgs.append(load_into(2 + 2 * ch, "mw%d" % (2 + 2 * ch), Win, 8, gcol + ch * 512, 512))
            if after_loads is not None:
                after_loads()
            for c in range(4):
                for sbi in range(NSB):
                    r = cnt['rot'] % 2
                    cnt['rot'] += 1
                    lin(r, wz, wzn, c * 128, 128, sbi)
                    op('act', lambda e: e.activation(t1[r][:], pb[r][:], AF.Silu), reads=[pbn[r]], writes=["t1_%d" % r])
                    op('pool', lambda e: e.tensor_tensor(ozT[:, c, sbi * 512:(sbi + 1) * 512], ozT[:, c, sbi * 512:(sbi + 1) * 512], t1[r][:], ALU.mult),
                       reads=["t1_%d" % r, ("ozT", c, sbi)], writes=[("ozT", c, sbi)])
            for ch in range(2):
                wb, wbn = wbs[ch]
                wg, wgn = wgs[ch]
                for cc in range(4):
                    c = ch * 4 + cc
                    for sbi in range(NSB):
                        r = cnt['rot'] % 2
                        cnt['rot'] += 1
                        lin(r, wg, wgn, cc * 128, 128, sbi)
                        op('act', lambda e: e.activation(sig[r][:], pb[r][:], AF.Sigmoid), reads=[pbn[r]], writes=["sig%d" % r])
                        lin(2 + r, wb, wbn, cc * 128, 128, sbi, kch=4, rhs_fn=lambda k: ozT[:, k, sbi * 512:(sbi + 1) * 512],
                            extra_reads=[("ozT", k, sbi) for k in range(4)])
                        dst = mergeT[:, c, sbi * 512:(sbi + 1) * 512]
                        if first:
                            op('dve', lambda e: e.tensor_tensor(dst, pb[2 + r][:], sig[r][:], ALU.mult), reads=[pbn[2 + r], "sig%d" % r], writes=[("mergeT", c, sbi)])
                        else:
                            op('dve', lambda e: e.tensor_tensor(t2[r][:], pb[2 + r][:], sig[r][:], ALU.mult), reads=[pbn[2 + r], "sig%d" % r], writes=["t2_%d" % r])
                            op('pool', lambda e: e.tensor_tensor(dst, dst, t2[r][:], ALU.add), reads=["t2_%d" % r, ("mergeT", c, sbi)], writes=[("mergeT", c, sbi)])

        kb.mark('L%d A-merge' % l)
        zmul_and_merge(C_ZA, I["w_branch_a"][l], C_G, True)

        def tap_big(nm, src, nch, rd):
            if nm in TAP:
                for c8 in range(nch):
                    for hf in range(2):
                        op('dve', lambda e: e.tensor_copy(xblk[0][:], src[:, c8, hf * 1024:(hf + 1) * 1024]), reads=rd, writes=["xblk0"])
                        dma('sp', TAP[nm][:, c8, hf * 1024:(hf + 1) * 1024], xblk[0][:], reads=["xblk0"])
        tap_big("mergeA%d" % l, mergeT, 8, ["mergeT"])
        if stop == "A":
            return

        kb.mark('L%d B-pre' % l)
        kb.barrier()
        op('dve', lambda e: e.memset(Vb1[:, :, 64:128], 1.0), writes=["Vb0"])
        dma('pool', kTb1[96:104, :], I["koh"], writes=["kTb0"])
        dma('pool', qTh[0][96:104, :], I["qoh"], writes=["qTh0"])
        wkv, wkvn = load_w(Win, 8, [(C_BCKV, 288)])
        def gen_bckv(b):
            bk = 6 + b % 2
            mm([(lambda e, c=c: e.matmul(pb[bk][:, 256:512 + 32 - 512] if False else pb[bk][:, 256:512], hT[:, c, b * 128:(b + 1) * 128], wkv[:, c, 0:256], start=(c == 0), stop=(c == 7))) for c in range(8)],
               reads=[wkvn] + HR, writes=[(pbn[bk], 1)])
            yield None
            bk2 = 0 + b % 2
            mm([(lambda e, c=c: e.matmul(pb[bk2][:, 0:32], hT[:, c, b * 128:(b + 1) * 128], wkv[:, c, 256:288], start=(c == 0), stop=(c == 7))) for c in range(8)],
               reads=[wkvn] + HR, writes=[(pbn[bk2], 0)])
            yield None
            ko2 = kvout[(b + 1) % 2]
            ko2n = "kvout%d" % ((b + 1) % 2)
            op('act', lambda e: e.activation(junk[:, 0:256], pb[bk][:, 256:512], AF.Square, accum_out=st2[:, b, 0:1]), reads=[(pbn[bk], 1)], writes=["junk", ("st2", b)])
            yield None
            op('dve', lambda e: e.tensor_scalar(st2[:, b, 1:2], st2[:, b, 0:1], 1.0 / 256, 1e-6, ALU.mult, ALU.add), reads=[("st2", b)], writes=[("st2", b)])
            yield None
            op('act', lambda e: e.activation(st2[:, b, 2:3], st2[:, b, 1:2], AF.Ln), reads=[("st2", b)], writes=[("st2", b)])
            yield None
            op('act', lambda e: e.activation(st2[:, b, 3:4], st2[:, b, 2:3], AF.Exp, scale=-0.5), reads=[("st2", b)], writes=[("st2", b)])
            yield None
            op('dve', lambda e: e.scalar_tensor_tensor(ko2[:, 0:256], pb[bk][:, 256:512], st2[:, b, 3:4], kvng[:], ALU.mult, ALU.mult),
               reads=[(pbn[bk], 1), ("st2", b), "kvng"], writes=[ko2n])
            yield None
            op('act', lambda e: e.copy(ko2[:, 256:288], pb[bk2][:, 0:32]), reads=[(pbn[bk2], 0)], writes=[ko2n])
            yield None
            dma('sp', O["nckv"][l, b * 128:(b + 1) * 128, :], ko2[:, 0:256], reads=[ko2n])
            yield None
            dma('sp', O["nkpe"][l, b * 128:(b + 1) * 128, :], ko2[:, 256:288], reads=[ko2n])
            yield None
            k16 = kvn16[b % 2]
            k16n = "kvn16_%d" % (b % 2)
            op('dve', lambda e: e.tensor_copy(k16[:, 0:256], ko2[:, 0:256]), reads=[ko2n], writes=[k16n])
            yield None
            bk3 = 2 + b % 2
            pview = pb[bk3][:].bitcast(BF16)
            mm([(lambda e, c=c, pview=pview: e.transpose(pview[:, c * 128:(c + 1) * 128], k16[:, c * 128:(c + 1) * 128], ident[:])) for c in range(2)],
               reads=[k16n, "ident"], writes=[pbn[bk3]])
            yield None
            op('dve', lambda e, pview=pview: e.tensor_copy(ckvT[:, :, b * 128:(b + 1) * 128], pview[:, 0:256].rearrange("p (c n) -> p c n", c=2)),
               reads=[pbn[bk3]], writes=[("ckvT", b)])
            yield None
        run_pairs(gen_bckv, NB)
        ctx16b = ctx16
        dma('pool', ctx16b[:], I["cckv"][l].rearrange("(j p) n -> p j n", p=128), writes=["ctx16"])
        for j in range(4):
            pview = pb[6 + j % 2][:].bitcast(BF16)
            mm([(lambda e, c=c, pview=pview: e.transpose(pview[:, c * 128:(c + 1) * 128], ctx16b[:, j, c * 128:(c + 1) * 128], ident[:])) for c in range(2)],
               reads=["ctx16", "ident"], writes=[pbn[6 + j % 2]])
            op('dve', lambda e, pview=pview: e.tensor_copy(ckvT[:, :, T + j * 128:T + (j + 1) * 128], pview[:, 0:256].rearrange("p (c n) -> p c n", c=2)),
               reads=[pbn[6 + j % 2]], writes=[("ckvT", NB + j)])
        dma('pool', ctxp[:, :, 64:96], I["ckpe"][l].rearrange("(j p) n -> p j n", p=128), writes=["ctxp"])
        pview = pb[6][:].bitcast(BF16)
        mm([(lambda e, j=j, pview=pview: e.transpose(pview[0:96, j * 128:(j + 1) * 128], ctxp[:, j, :], ident[:])) for j in range(4)],
           reads=["ctxp", "ident"], writes=[pbn[6]])
        op('dve', lambda e, pview=pview: e.tensor_copy(kpeT[64:96, T:KT], pview[64:96, 0:512]), reads=[pbn[6]], writes=[("kpeT", 4)])

        dma('sp', qng[:], I["mla_q_norm"][l].partition_broadcast(128), writes=["qng"])
        wcq, wcqn = load_w(Win, 8, [(C_BCQ, 384)])
        def gen_bcq(b):
            bk = 6 + b % 2
            mm([(lambda e, c=c: e.matmul(pb[bk][:, 0:384], hT[:, c, b * 128:(b + 1) * 128], wcq[:, c, 0:384], start=(c == 0), stop=(c == 7))) for c in range(8)],
               reads=[wcqn] + HR, writes=[pbn[bk]])
            yield None
            op('act', lambda e: e.activation(junk[:, 0:384], pb[bk][:, 0:384], AF.Square, accum_out=st2[:, b, 0:1]), reads=[pbn[bk]], writes=["junk", ("st2", b)])
            yield None
            op('dve', lambda e: e.tensor_scalar(st2[:, b, 1:2], st2[:, b, 0:1], 1.0 / 384, 1e-6, ALU.mult, ALU.add), reads=[("st2", b)], writes=[("st2", b)])
            yield None
            op('act', lambda e: e.activation(st2[:, b, 2:3], st2[:, b, 1:2], AF.Ln), reads=[("st2", b)], writes=[("st2", b)])
            yield None
            op('act', lambda e: e.activation(st2[:, b, 3:4], st2[:, b, 2:3], AF.Exp, scale=-0.5), reads=[("st2", b)], writes=[("st2", b)])
            yield None
            k16 = kvn16[b % 2]
            k16n = "kvn16_%d" % (b % 2)
            op('dve', lambda e: e.scalar_tensor_tensor(k16[:, 0:384], pb[bk][:, 0:384], st2[:, b, 3:4], qng[:], ALU.mult, ALU.mult),
               reads=[pbn[bk], ("st2", b), "qng"], writes=[k16n])
            yield None
            bk3 = 2 + b % 2
            pview = pb[bk3][:].bitcast(BF16)
            mm([(lambda e, c=c, pview=pview: e.transpose(pview[:, c * 128:(c + 1) * 128], k16[:, c * 128:(c + 1) * 128], ident[:])) for c in range(3)],
               reads=[k16n, "ident"], writes=[pbn[bk3]])
            yield None
            op('dve', lambda e, pview=pview: e.tensor_copy(cqnT[:, :, b * 128:(b + 1) * 128], pview[:, 0:384].rearrange("p (c n) -> p c n", c=3)),
               reads=[pbn[bk3]], writes=[("cqnT", b)])
            yield None
        run_pairs(gen_bcq, NB)
        wpe, wpen = load_w(Win, 8, [(C_BKPE - 64, 96)])
        wpep, wpepn = load_w(Winp, 8, [(640 - 64, 96)])
        for sbi in range(NSB):
            r = cnt['rot'] % 2
            cnt['rot'] += 1
            lin(0, wpe, wpen, 0, 96, sbi)
            lin(1, wpep, wpepn, 0, 96, sbi)
            op('dve', lambda e: e.tensor_tensor(t1[r][64:96, :], pb[0][64:96, :], ropeC[64:96, sbi * 512:(sbi + 1) * 512], ALU.mult), reads=[pbn[0], "ropeC"], writes=["t1_%d" % r])
            op('dve', lambda e: e.tensor_tensor(t2[r][64:96, :], pb[1][64:96, :], ropeS[64:96, sbi * 512:(sbi + 1) * 512], ALU.mult), reads=[pbn[1], "ropeS"], writes=["t2_%d" % r])
            op('pool', lambda e: e.tensor_tensor(kpeT[64:96, sbi * 512:(sbi + 1) * 512], t1[r][64:96, :], t2[r][64:96, :], ALU.add), reads=["t1_%d" % r, "t2_%d" % r], writes=[("kpeT", sbi)])
        wuq, wuqn = load_w(I["mla_w_uq"][l], 3, [(0, 768)])
        wuqp, wuqpn = load_w(I["mla_w_uqp"][l], 3, [(0, 768)])
        dma('pool', wukv[:], I["mla_w_ukv"][l].rearrange("(k p) n -> p k n", p=128), writes=["wukv"])
        kb.mark('L%d B-attn' % l)
        CQR = [("cqnT", b) for b in range(NB)]
        CKR = [("ckvT", b) for b in range(NKB)]
        for h in range(8):
            qt, qn = qTh[h % 2], "qTh0"
            kt, ktn = kTb[0], "kTb0"
            vt, vtn = Vb[0], "Vb0"
            for s5 in range(5):
                bk = 0 + s5 % 2
                mm([(lambda e, c=c: e.matmul(pb[bk][0:64, :], wukv[:, c, h * 128:h * 128 + 64], ckvT[:, c, s5 * 512:(s5 + 1) * 512], start=(c == 0), stop=(c == 1))) for c in range(2)],
                   reads=["wukv"] + CKR, writes=[pbn[bk]])
                op('act', lambda e: e.copy(kt[0:64, s5 * 512:(s5 + 1) * 512], pb[bk][0:64, :]), reads=[pbn[bk]], writes=[(ktn, s5)])
                op('pool', lambda e: e.tensor_copy(kt[64:96, s5 * 512:(s5 + 1) * 512], kpeT[64:96, s5 * 512:(s5 + 1) * 512]), reads=[("kpeT", s5)], writes=[(ktn, s5, 'pe')])
            for gi_, k0 in enumerate(range(0, NKB, 8)):
                nb_ = min(8, NKB - k0)
                bk = 6 + gi_ % 2
                fns = []
                for j_ in range(nb_):
                    kblk = k0 + j_
                    for c in range(2):
                        fns.append(lambda e, c=c, j_=j_, kblk=kblk: e.matmul(pb[bk][:, j_ * 64:(j_ + 1) * 64], ckvT[:, c, kblk * 128:(kblk + 1) * 128],
                                                                            wukv[:, c, h * 128 + 64:h * 128 + 128], start=(c == 0), stop=(c == 1)))
                mm(fns, reads=["wukv"] + CKR, writes=[pbn[bk]])
                op('dve', lambda e: e.tensor_copy(vt[:, k0:k0 + nb_, 0:64], pb[bk][:, 0:nb_ * 64].rearrange("p (j d) -> p j d", j=nb_)),
                   reads=[pbn[bk]], writes=[vtn])
            def gen_qb(sbi, h=h, qt=qt, qn=qn):
                r = sbi % 2
                ba, bb_ = 2 * r, 2 * r + 1
                rf = lambda c: cqnT[:, c, sbi * 512:(sbi + 1) * 512]
                lin(ba, wuq, wuqn, h * 96, 96, sbi, kch=3, rhs_fn=rf, extra_reads=CQR)
                yield None
                lin(bb_, wuqp, wuqpn, h * 96, 96, sbi, kch=3, rhs_fn=rf, extra_reads=CQR)
                yield None
                op('act', lambda e: e.copy(qt[0:64, sbi * 512:(sbi + 1) * 512], pb[ba][0:64, :]), reads=[pbn[ba]], writes=[(qn, sbi)])
                yield None
                op('dve', lambda e: e.tensor_tensor(t1[r][64:96, :], pb[ba][64:96, :], ropeC[64:96, sbi * 512:(sbi + 1) * 512], ALU.mult), reads=[pbn[ba], "ropeC"], writes=["t1_%d" % r])
                yield None
                op('dve', lambda e: e.tensor_tensor(t2[r][64:96, :], pb[bb_][64:96, :], ropeS[64:96, sbi * 512:(sbi + 1) * 512], ALU.mult), reads=[pbn[bb_], "ropeS"], writes=["t2_%d" % r])
                yield None
                op('pool', lambda e: e.tensor_tensor(qt[64:96, sbi * 512:(sbi + 1) * 512], t1[r][64:96, :], t2[r][64:96, :], ALU.add), reads=["t1_%d" % r, "t2_%d" % r], writes=[(qn, sbi, 'pe')])
                yield None
            run_pairs(gen_qb, NSB)
            groups = []
            MS = 96.0 ** -0.5
            for sbi in range(NSB):
                klist = [(j, None, False) for j in range(NKB)]

                def kfn(kblk, kt=kt, ktn=ktn):
                    return kt[0:104, kblk * 128:(kblk + 1) * 128], ktn

                def vfn(kblk, vt=vt, vtn=vtn):
                    return vt[:, kblk, :], vtn

                def fin(ob, h=h, sbi=sbi):
                    r = cnt['rot'] % 2
                    cnt['rot'] += 1
                    op('dve', lambda e: e.reciprocal(rr[r][64:128, :], pb[ob][64:128, :]), reads=[pbn[ob]], writes=["rr0"])
                    op('pool', lambda e: e.tensor_copy(r3[r][0:64, :], rr[r][64:128, :]), reads=["rr0"], writes=["r3_0"])
                    po = (h % 2) * 64
                    op('dve', lambda e: e.tensor_tensor(ozT[po:po + 64, h // 2, sbi * 512:(sbi + 1) * 512], pb[ob][0:64, :], r3[r][0:64, :], ALU.mult),
                       reads=[pbn[ob], "r3_0"], writes=[("ozT", h // 2, sbi)])

                groups.append(dict(qtile=qt, qname=qn, sbi=sbi, kfn=kfn, klist=klist, vfn=vfn, K=104, scale=MS,
                                   bias_fn=(lambda isctx: -MS * BIGM), fin=fin))
            attend_stream(groups)
        kb.mark('L%d B-merge' % l)
        zmul_and_merge(C_ZB, I["w_branch_b"][l], C_G + 1024, False)
        tap_big("ozB%d" % l, ozT, 4, ["ozT"])
        tap_big("mergeB%d" % l, mergeT, 8, ["mergeT"])
        if stop == "B":
            return

        kb.mark('L%d C' % l)
        kb.barrier()
        _gdn(kb, I, O, l, dict(arena=arena, pb=pb, pbn=pbn, hT=hT, ozT=ozT, HR=HR, load_w=load_w, lin=lin, ident=ident, ident32=ident32,
                               flags=flags, ones16=ones16, wukv=wukv, run_pairs=run_pairs, xblk=xblk, t1=t1, t2=t2, sig=sig, junk=junk, Win=Win, TAP=TAP, stat=stat, st2=st2, kvn16=kvn16, rr=rr))
        tap_big("ozC%d" % l, ozT, 4, ["ozT"])
        kb.mark('L%d C-merge' % l)
        wo = []

        def _prefetch_wo():
            for ch in range(2):
                wo.append(load_w(I["w_out"][l], 8, [(ch * 512, 512)]))
        zmul_and_merge(C_ZC, I["w_branch_c"][l], C_G + 2048, False, after_loads=_prefetch_wo)
        kb.barrier()
        tap_big("mergeC%d" % l, mergeT, 8, ["mergeT"])
        if stop == "C":
            return

        kb.mark('L%d out' % l)
        MR = [("mergeT", c, s) for c in range(8) for s in range(NSB)]
        if l == 1:
            dma('sp', gbc[:].rearrange("p a b -> p (a b)"), I["final_norm_g"].partition_broadcast(128), writes=["gbc"])
        def gen_out(b):
            xb, xbn = xblk[b % 2], "xblk%d" % (b % 2)
            dma('sp', xb[:], xsrc[b * 128:(b + 1) * 128, :], reads=(["xs"] if l == 1 else []), writes=[xbn])
            yield None
            for ch in range(2):
                bk = 4 + 2 * (b % 2) + ch
                wv, wn = wo[ch]
                mm([(lambda e, c=c, wv=wv: e.matmul(pb[bk][:], mergeT[:, c, b * 128:(b + 1) * 128], wv[:, c, :], start=(c == 0), stop=(c == 7))) for c in range(8)],
                   reads=[wn] + MR, writes=[pbn[bk]])
                yield None
                tt_, ttn_ = ((t1[b % 2], "t1_%d" % (b % 2)) if ch == 0 else (t2[b % 2], "t2_%d" % (b % 2)))
                op('dve', lambda e: e.tensor_tensor(tt_[:], pb[bk][:], gateb[:, ch * 512:(ch + 1) * 512], ALU.mult), reads=[pbn[bk], ("gateb", ch)], writes=[ttn_])
                yield None
                op('pool', lambda e: e.tensor_tensor(xb[:, ch * 512:(ch + 1) * 512], xb[:, ch * 512:(ch + 1) * 512], tt_[:], ALU.add), reads=[ttn_, xbn], writes=[xbn])
                yield None
            if l == 0:
                dma('sp', xs[b * 128:(b + 1) * 128, :], xb[:], reads=[xbn], writes=["xs"])
                yield None
            else:
                op('act', lambda e: e.activation(junk[:], xb[:], AF.Square, accum_out=stat[:, b, 0:1]), reads=[xbn], writes=["junk", ("stat", b)])
                yield None
                op('dve', lambda e: e.tensor_scalar(stat[:, b, 1:2], stat[:, b, 0:1], 1.0 / D, 1e-6, ALU.mult, ALU.add), reads=[("stat", b)], writes=[("stat", b)])
                yield None
                op('act', lambda e: e.activation(stat[:, b, 2:3], stat[:, b, 1:2], AF.Ln), reads=[("stat", b)], writes=[("stat", b)])
                yield None
                op('act', lambda e: e.activation(stat[:, b, 3:4], stat[:, b, 2:3], AF.Exp, scale=-0.5), reads=[("stat", b)], writes=[("stat", b)])
                yield None
                op('dve', lambda e: e.scalar_tensor_tensor(xb[:], xb[:], stat[:, b, 3:4], gbc[:].rearrange("p a b -> p (a b)"), ALU.mult, ALU.mult), reads=[xbn, ("stat", b), "gbc"], writes=[xbn])
                yield None
                dma('sp', O["y"][b * 128:(b + 1) * 128, :], xb[:], reads=[xbn])
                yield None
        run_pairs(gen_out, NB)

def _gdn(kb, I, O, l, env):
    op, dma, mm = kb.op, kb.dma, kb.mmgroup
    arena, pb, pbn, hT, ozT, HR = env["arena"], env["pb"], env["pbn"], env["hT"], env["ozT"], env["HR"]
    load_w, lin, ident, ident32, flags = env["load_w"], env["lin"], env["ident"], env["ident32"], env["flags"]
    xblk, t1, t2, sig, junk, Win, TAP = env["xblk"], env["t1"], env["t2"], env["sig"], env["junk"], env["Win"], env["TAP"]
    st2, kvn16 = env["st2"], env["kvn16"]
    pos = [0]

    def carve(n_units, dt, shape_str=None, **kw):
        a = arena[:, pos[0]:pos[0] + n_units]
        pos[0] += n_units
        if dt == F32:
            a = a.bitcast(F32)
        if shape_str:
            a = a.rearrange(shape_str, **kw)
        return a
    qkvh = carve(3 * T, BF16, "p (c n) -> p c n", c=3)
    oacc = carve(NB * 128, BF16, "p (b d) -> p b d", b=NB)
    ab = carve(2 * NB * 16, F32, "p (b d) -> p b d", b=NB)
    names = ["g", "bt", "nbt", "gam", "ngam", "egam", "begam", "edel", "dec1", "dec2", "gt1", "gt2"]
    stt_ = {n: carve(2 * NB * 8, F32, "p (b d) -> p b d", b=NB) for n in names}
    S = carve(2 * 2 * 128, F32, "p (s d) -> p s d", s=2)
    Sbf = carve(2 * 128, BF16, "p (s d) -> p s d", s=2)
    bt_names = ["Kbg", "Kd0", "Kd1", "Vb", "Qg", "M0", "MTa", "MTb", "Ma", "Mb", "PTb", "Wt", "Wn", "U", "CT", "attnT", "Mc"]
    B = {n: carve(256, BF16, "p (s d) -> p s d", s=2) for n in bt_names}
    X = carve(512, BF16, "p (s h d) -> p s h d", s=2, h=2)
    wk_ = env["wukv"][:].rearrange("p a b -> p (a b)")
    B1 = {}
    for q_, n in enumerate(bt_names):
        if q_ < 6:
            B1[n] = wk_[:, 512 + q_ * 256:512 + (q_ + 1) * 256].rearrange("p (s d) -> p s d", s=2)
        else:
            B1[n] = carve(256, BF16, "p (s d) -> p s d", s=2)
    X1 = wk_[:, 0:512].rearrange("p (s h d) -> p s h d", s=2, h=2)
    cw = carve(2 * 36, F32, "p (c j) -> p c j", c=12)
    nw = carve(2 * 36, F32, "p (c j) -> p c j", c=12)
    pw = carve(2 * 36, F32, "p (c j) -> p c j", c=12)
    gnb = carve(2 * 128, F32)
    dtb = carve(2 * 8, F32)
    negA = carve(2 * 8, F32)
    onorm = carve(128, BF16)
    ident2 = carve(256, BF16, "p (s d) -> p s d", s=2)
    assert pos[0] <= 21952, pos[0]
    tri = t1[0][:].rearrange("p (k n) -> p k n", k=4)
    gmc = t1[1][:].rearrange("p (k s n) -> p k s n", k=2, s=2)
    Grhs = t2[0][:, 0:256].rearrange("p (s n) -> p s n", s=2)
    ones32 = t2[0][:, 256:384]
    Em = t2[1][:].rearrange("p (k s n) -> p k s n", k=2, s=2)
    accs = [xblk[0], xblk[1]]
    SETS = [dict(B=B, X=X, Grhs=Grhs, Em=Em, grn="Grhs0", emn="Em0", sx="", banks=(0, 1, 2, 3)),
            dict(B=B1, X=X1, Grhs=sig[0][:, 0:256].rearrange("p (s n) -> p s n", s=2),
                 Em=sig[1][:].rearrange("p (k s n) -> p k s n", k=2, s=2), grn="sig0", emn="sig1", sx="_1", banks=(4, 5, 6, 7))]

    for k_, nm in enumerate(["triF", "triB", "sel1", "sel2"]):
        dma('sp', tri[:, k_, :], I[nm], writes=["t1_0"])
    for k_, nm in enumerate(["gm1", "gm2"]):
        for s_ in range(2):
            dma('sp', gmc[:, k_, s_, :], I[nm][:, 4 * s_, :], writes=["t1_1"])
    op('dve', lambda e: e.memset(ones32, 1.0), writes=["ones32"])
    op('dve', lambda e: e.memset(B1["Kd0"][:], 0.0), writes=["Kd0_1"])
    op('dve', lambda e: e.memset(B1["Kd1"][:], 0.0), writes=["Kd1_1"])
    op('dve', lambda e: e.memset(B["Kd0"][:], 0.0), writes=["Kd0"])
    for s_ in range(2):
        op('dve', lambda e: e.tensor_copy(ident2[:, s_, :], ident[:]), reads=["ident"], writes=["ident2"])
    op('dve', lambda e: e.memset(B["Kd1"][:], 0.0), writes=["Kd1"])
    for j_ in range(3):
        dma('sp', cw[:, :, j_], I["gdn_conv"][l][j_].rearrange("(c p) -> p c", p=128), writes=["cw"], allow_slow_non_contiguous=True)
    dma('sp', gnb, I["gdn_norm"][l].partition_broadcast(128), writes=["gnb"])
    dma('sp', dtb, I["gdn_dt_bias"][l].partition_broadcast(128), writes=["dtb"])
    dma('sp', negA, I["gdn_a_log"][l].partition_broadcast(128), writes=["negA"])
    op('act', lambda e: e.activation(negA, negA, AF.Exp), reads=["negA"], writes=["negA"])
    op('dve', lambda e: e.tensor_scalar(negA, negA, -1.0, None, ALU.mult), reads=["negA"], writes=["negA"])
    op('dve', lambda e: e.tensor_scalar(nw, cw, flags[:, 2:3], None, ALU.mult), reads=["cw", "flags"], writes=["nw"])
    op('dve', lambda e: e.tensor_tensor(pw, cw, nw, ALU.add), reads=["cw", "nw"], writes=["pw"])

    wab, wabn = load_w(Win, 8, [(C_CA, 16)])
    for b in range(NB):
        mm([(lambda e, c=c: e.matmul(pb[0][:, b * 16:(b + 1) * 16], hT[:, c, b * 128:(b + 1) * 128], wab[:, c, :], start=(c == 0), stop=(c == 7))) for c in range(8)],
           reads=[wabn] + HR, writes=[pbn[0]])
    op('dve', lambda e: e.tensor_copy(ab, pb[0][:, 0:256].rearrange("p (b d) -> p b d", b=NB)), reads=[pbn[0]], writes=["ab"])
    g, bt, nbt, gam, ngam, egam, begam, edel, dec1, dec2, gt1, gt2 = [stt_[n] for n in names]
    bc8 = lambda a: a.unsqueeze(1).to_broadcast([128, NB, 8])
    op('dve', lambda e: e.tensor_tensor(g, ab[:, :, 0:8], bc8(dtb), ALU.add), reads=["ab", "dtb"], writes=["g"])
    op('act', lambda e: e.activation(g, g, AF.Exp), reads=["g"], writes=["g"])
    op('act', lambda e: e.activation(g, g, AF.Ln, bias=1.0), reads=["g"], writes=["g"])
    op('dve', lambda e: e.tensor_tensor(g, g, bc8(negA), ALU.mult), reads=["g", "negA"], writes=["g"])
    op('act', lambda e: e.activation(bt, ab[:, :, 8:16], AF.Sigmoid), reads=["ab"], writes=["bt"])
    op('dve', lambda e: e.tensor_scalar(nbt, bt, -1.0, None, ALU.mult), reads=["bt"], writes=["nbt"])
    for b in range(NB):
        mm([lambda e: e.matmul(pb[1][:, b * 8:b * 8 + 4], tri[:, 0, :], g[:, b, 0:4], start=True, stop=True),
            lambda e: e.matmul(pb[1][:, b * 8 + 4:b * 8 + 8], tri[:, 1, :], g[:, b, 4:8], start=True, stop=True)], reads=["t1_0", "g"], writes=[pbn[1]])
        mm([lambda e: e.matmul(pb[2][:, b * 8:b * 8 + 8], tri[:, 2, :], g[:, b, :], start=True, stop=True)], reads=["t1_0", "g"], writes=[pbn[2]])
        mm([lambda e: e.matmul(pb[3][:, b * 8:b * 8 + 8], tri[:, 3, :], g[:, b, :], start=True, stop=True)], reads=["t1_0", "g"], writes=[pbn[3]])
    v3 = lambda bank: pb[bank][:, 0:128].rearrange("p (b d) -> p b d", b=NB)
    op('dve', lambda e: e.tensor_copy(gam, v3(1)), reads=[pbn[1]], writes=["gam"])
    op('dve', lambda e: e.tensor_copy(gt1, v3(2)), reads=[pbn[2]], writes=["gt1"])
    op('dve', lambda e: e.tensor_copy(gt2, v3(3)), reads=[pbn[3]], writes=["gt2"])
    op('dve', lambda e: e.tensor_scalar(ngam, gam, -1.0, None, ALU.mult), reads=["gam"], writes=["ngam"])
    op('act', lambda e: e.activation(egam, gam, AF.Exp), reads=["gam"], writes=["egam"])
    op('dve', lambda e: e.tensor_tensor(begam, egam, bt, ALU.mult), reads=["egam", "bt"], writes=["begam"])
    op('dve', lambda e: e.tensor_tensor(edel[0:64], gt1[0:64], gam[0:64], ALU.subtract), reads=["gt1", "gam"], writes=["edel"])
    op('dve', lambda e: e.tensor_tensor(edel[64:128], gt2[64:128], gam[64:128], ALU.subtract), reads=["gt2", "gam"], writes=["edel"])
    op('act', lambda e: e.activation(edel, edel, AF.Exp), reads=["edel"], writes=["edel"])
    op('act', lambda e: e.activation(dec1, gt1, AF.Exp), reads=["gt1"], writes=["dec1"])
    op('act', lambda e: e.activation(dec2, gt2, AF.Exp), reads=["gt2"], writes=["dec2"])
    if "gstats%d" % l in TAP:
        for k_, n in enumerate(["g", "bt", "gam", "edel", "dec1", "dec2"]):
            dma('sp', TAP["gstats%d" % l][k_], stt_[n], reads=[n])

    for h in range(4):
        kb.mark('L%d C h%d conv' % (l, h))
        if h == 0:
            nxt_wc = load_w(Win, 8, [(C_CQKV + h * 128, 128), (C_CQKV + 512 + h * 128, 128), (C_CQKV + 1024 + h * 128, 128)])
        wc, wcn = nxt_wc
        for ci in range(3):
            ch = ci * 4 + h
            for sbi in range(NSB):
                lin(sbi, wc, wcn, ci * 128, 128, sbi)
            for sbi in range(NSB):
                acc = accs[sbi // 2][:, (sbi % 2) * 512:(sbi % 2 + 1) * 512]
                an = "xblk%d" % (sbi // 2)
                P_ = pb[sbi]
                rd = [pbn[sbi], "cw", "nw", "pw"]
                op('dve', lambda e: e.tensor_scalar(acc, P_[:], cw[:, ch, 1:2], None, ALU.mult), reads=rd, writes=[an])
                op('dve', lambda e: e.scalar_tensor_tensor(acc[:, 1:512], P_[:, 0:511], cw[:, ch, 0:1], acc[:, 1:512], ALU.mult, ALU.add), reads=rd + [an], writes=[an])
                op('dve', lambda e: e.scalar_tensor_tensor(acc[:, 0:511], P_[:, 1:512], cw[:, ch, 2:3], acc[:, 0:511], ALU.mult, ALU.add), reads=rd + [an], writes=[an])
                op('dve', lambda e: e.scalar_tensor_tensor(acc[:, 256:257], P_[:, 255:256], nw[:, ch, 0:1], acc[:, 256:257], ALU.mult, ALU.add), reads=rd + [an], writes=[an])
                op('dve', lambda e: e.scalar_tensor_tensor(acc[:, 255:256], P_[:, 256:257], nw[:, ch, 2:3], acc[:, 255:256], ALU.mult, ALU.add), reads=rd + [an], writes=[an])
                if sbi > 0:
                    op('dve', lambda e: e.scalar_tensor_tensor(acc[:, 0:1], pb[sbi - 1][:, 511:512], pw[:, ch, 0:1], acc[:, 0:1], ALU.mult, ALU.add),
                       reads=rd + [an, pbn[sbi - 1]], writes=[an])
                if sbi < NSB - 1:
                    op('dve', lambda e: e.scalar_tensor_tensor(acc[:, 511:512], pb[sbi + 1][:, 0:1], pw[:, ch, 2:3], acc[:, 511:512], ALU.mult, ALU.add),
                       reads=rd + [an, pbn[sbi + 1]], writes=[an])
            def gen_l2(sbi, ci=ci):
                acc = accs[sbi // 2][:, (sbi % 2) * 512:(sbi % 2 + 1) * 512]
                an = ("xblk%d" % (sbi // 2), sbi % 2)
                dst = qkvh[:, ci, sbi * 512:(sbi + 1) * 512]
                p_ = sbi % 2
                sqb, sqn = ((sig[0][:].bitcast(BF16)[:, 0:512], "sig0") if p_ == 0 else (env["rr"][0][:].bitcast(BF16)[:, 0:512], "rr0"))
                rvb, rvn = ((sig[1][:], "sig1") if p_ == 0 else (t2[1][:], "Em0"))
                bk_ = 4 + p_
                if ci == 2:
                    op('act', lambda e: e.activation(dst, acc, AF.Silu), reads=[an], writes=[("qkvh", ci, sbi)])
                    yield None
                else:
                    op('act', lambda e: e.activation(acc, acc, AF.Silu), reads=[an], writes=[an])
                    yield None
                    op('pool', lambda e: e.tensor_tensor(sqb, acc, acc, ALU.mult), reads=[an], writes=[sqn])
                    yield None
                    mm([lambda e: e.matmul(pb[bk_][:], env["ones16"][:], sqb, start=True, stop=True)], reads=[sqn, "ones16"], writes=[pbn[bk_]])
                    yield None
                    op('act', lambda e: e.activation(rvb, pb[bk_][:], AF.Ln, bias=1e-6), reads=[pbn[bk_]], writes=[rvn])
                    yield None
                    op('act', lambda e: e.activation(rvb, rvb, AF.Exp, scale=-0.5, bias=(-0.5 * float(np.log(128.0)) if ci == 0 else 0.0)), reads=[rvn], writes=[rvn])
                    yield None
                    op('dve', lambda e: e.tensor_tensor(dst, acc, rvb, ALU.mult), reads=[an, rvn], writes=[("qkvh", ci, sbi)])
                    yield None
            env["run_pairs"](gen_l2, NSB)
        if h == 0 and "qkvh%d" % l in TAP:
            for ci in range(3):
                for hf in range(2):
                    op('dve', lambda e: e.tensor_copy(xblk[0][:], qkvh[:, ci, hf * 1024:(hf + 1) * 1024]), reads=["qkvh"], writes=["xblk0"])
                    dma('sp', TAP["qkvh%d" % l][:, ci, hf * 1024:(hf + 1) * 1024], xblk[0][:], reads=["xblk0"])
        qT_, kT_, vT_ = qkvh[:, 0, :], qkvh[:, 1, :], qkvh[:, 2, :]
        QK = ["qkvh"]
        kb.mark('L%d C h%d scan' % (l, h))
        if h < 3:
            nxt_wc = load_w(Win, 8, [(C_CQKV + (h + 1) * 128, 128), (C_CQKV + 512 + (h + 1) * 128, 128), (C_CQKV + 1024 + (h + 1) * 128, 128)])
        op('pool', lambda e: e.memset(oacc, 0.0), writes=["oacc"])
        def gen_iter(i, SET):
            B, X, Grhs, Em = SET['B'], SET['X'], SET['Grhs'], SET['Em']
            grn, emn, sx = SET['grn'], SET['emn'], SET['sx']
            b0, b1, b2, b3 = SET['banks']
            N = lambda n_: n_ + sx
            slots = [(i, h, 0), (NB - 1 - i, 4 + h, 1)]
            for s_, (blk, dh, d_) in enumerate(slots):
                if i == 0:
                    dma('sp', S[:, s_, :], I["st"][l, dh], writes=[("S", s_)])
                    op('act', lambda e: e.copy(Sbf[:, s_, :], S[:, s_, :]), reads=[("S", s_)], writes=[("Sbf", s_)])
                elif i % 2 == 0:
                    op('dve', lambda e: e.tensor_scalar(S[:, s_, :], S[:, s_, :], flags[:, 0:1], None, ALU.mult), reads=[("S", s_), "flags"], writes=[("S", s_)])
                    op('act', lambda e: e.copy(Sbf[:, s_, :], S[:, s_, :]), reads=[("S", s_)], writes=[("Sbf", s_)])
            yield None
            pv = pb[b0][:].bitcast(BF16)
            fns = []
            for s_, (blk, dh, d_) in enumerate(slots):
                for k_, src in enumerate([kT_, vT_, qT_]):
                    fns.append(lambda e, s_=s_, k_=k_, src=src, blk=blk: e.transpose(pv[:, (k_ * 2 + s_) * 128:(k_ * 2 + s_ + 1) * 128], src[:, blk * 128:(blk + 1) * 128], ident[:]))
            mm(fns, reads=QK + ["ident"], writes=[pbn[b0]])
            yield None
            for s_, (blk, dh, d_) in enumerate(slots):
                for nm, k_, sc in [("Kbg", 0, begam), ("Vb", 1, bt), ("Qg", 2, egam)]:
                    op('act', lambda e: e.activation(B[nm][:, s_, :], pv[:, (k_ * 2 + s_) * 128:(k_ * 2 + s_ + 1) * 128], AF.Copy, scale=sc[:, blk, dh:dh + 1]),
                       reads=[pbn[b0], "begam", "edel", "bt", "egam"], writes=[(N(nm), s_)])
                    yield None
                for hf_ in range(2):
                    R_ = slice(hf_ * 64, (hf_ + 1) * 64)
                    op('act', lambda e: e.activation(B["Kd%d" % hf_][R_, s_, :], pv[R_, s_ * 128:(s_ + 1) * 128], AF.Copy, scale=edel[R_, blk, dh:dh + 1]),
                       reads=[pbn[b0], "edel"], writes=[(N("Kd%d" % hf_), s_)])
                    yield None
            fns = []
            for s_, (blk, dh, d_) in enumerate(slots):
                ks = kT_[:, blk * 128:(blk + 1) * 128]
                qs = qT_[:, blk * 128:(blk + 1) * 128]
                fns.append(lambda e, s_=s_, ks=ks: e.matmul(pb[b1][:, s_ * 128:(s_ + 1) * 128], ks, ks, start=True, stop=True))
                fns.append(lambda e, s_=s_, ks=ks, qs=qs: e.matmul(pb[b1][:, (2 + s_) * 128:(3 + s_) * 128], ks, qs, start=True, stop=True))
            mm(fns, reads=QK, writes=[pbn[b1]])
            yield None
            for s_, (blk, dh, d_) in enumerate(slots):
                op('dve', lambda e: e.tensor_scalar(Grhs[:, s_, :], ident32[:], ngam[:, blk, dh:dh + 1], None, ALU.mult), reads=["ident32", "ngam"], writes=[(grn, s_)])
                yield None
            mm([lambda e: e.matmul(pb[b2][:, 0:256], ones32, Grhs.rearrange("p s n -> p (s n)"), start=True, stop=True)], reads=[grn, "ones32"], writes=[pbn[b2]])
            yield None
            p2 = pb[b2][:, 0:256].rearrange("p (s n) -> p s n", s=2)
            op('dve', lambda e: e.tensor_tensor(Em[:, 0], p2, gmc[:, 0], ALU.add), reads=[pbn[b2], "t1_1"], writes=[(emn, 0)])
            yield None
            op('dve', lambda e: e.scalar_tensor_tensor(Em[:, 1], p2, -1.0, gmc[:, 1], ALU.mult, ALU.add), reads=[pbn[b2], "t1_1"], writes=[(emn, 1)])
            yield None
            for s_, (blk, dh, d_) in enumerate(slots):
                op('act', lambda e: e.activation(Em[:, 0, s_, :], Em[:, 0, s_, :], AF.Exp, bias=gam[:, blk, dh:dh + 1]), reads=[(emn, 0), "gam"], writes=[(emn, 0)])
                yield None
                op('act', lambda e: e.activation(Em[:, 1, s_, :], Em[:, 1, s_, :], AF.Exp, bias=ngam[:, blk, dh:dh + 1]), reads=[(emn, 1), "ngam"], writes=[(emn, 1)])
                yield None
                op('dve', lambda e: e.scalar_tensor_tensor(B["M0"][:, s_, :], pb[b1][:, s_ * 128:(s_ + 1) * 128], nbt[:, blk, dh:dh + 1], Em[:, 0, s_, :], ALU.mult, ALU.mult),
                   reads=[pbn[b1], "nbt", (emn, 0)], writes=[(N("M0"), s_)])
                yield None
            op('dve', lambda e: e.tensor_tensor(B["attnT"][:], pb[b1][:, 256:512].rearrange("p (s n) -> p s n", s=2), Em[:, 1], ALU.mult), reads=[pbn[b1], (emn, 1)], writes=[N("attnT")])
            yield None
            M0 = B["M0"]
            mm([lambda e, s_=s_: e.matmul(pb[b1][:, s_ * 128:(s_ + 1) * 128], M0[:, s_, :], ident[:], start=True, stop=True) for s_ in range(2)], reads=[N("M0"), "ident"], writes=[pbn[b1]])
            yield None
            op('act', lambda e: e.copy(B["MTa"][:], pb[b1][:, 0:256].rearrange("p (s n) -> p s n", s=2)), reads=[pbn[b1]], writes=[N("MTa")])
            yield None
            fns = [lambda e: e.matmul(pb[b3][:, 0:256], ident[:], ident2[:].rearrange("p s d -> p (s d)"), start=True, stop=False)]
            for s_ in range(2):
                fns.append(lambda e, s_=s_: e.matmul(pb[b3][:, s_ * 128:(s_ + 1) * 128], M0[:, s_, :], ident[:], start=False, stop=True))
            mm(fns, reads=[N("M0"), "ident", "ident2"], writes=[pbn[b3]])
            yield None
            Mprev, MTprev, Mn, MTn = "M0", "MTa", "Ma", "MTb"
            pend = None

            def pt_update(mname):
                op('act', lambda e: e.copy(B["PTb"][:], pb[b3][:, 0:256].rearrange("p (s n) -> p s n", s=2)), reads=[pbn[b3]], writes=[N("PTb")])
                mm([lambda e, s_=s_: e.matmul(pb[b3][:, s_ * 128:(s_ + 1) * 128], B[mname][:, s_, :], B["PTb"][:, s_, :], start=False, stop=True) for s_ in range(2)],
                   reads=[N(mname), N("PTb")], writes=[pbn[b3]])
            MN3 = ["Ma", "Mb", "Mc"]
            for k_ in range(1, 6):
                Mn = MN3[k_ % 3]
                mm([lambda e, s_=s_: e.matmul(pb[b2][:, s_ * 128:(s_ + 1) * 128], B[MTprev][:, s_, :], B[Mprev][:, s_, :], start=True, stop=True) for s_ in range(2)],
                   reads=[N(MTprev), N(Mprev)], writes=[pbn[b2]])
                yield None
                if k_ < 5:
                    mm([lambda e, s_=s_: e.matmul(pb[b1][:, s_ * 128:(s_ + 1) * 128], B[Mprev][:, s_, :], B[MTprev][:, s_, :], start=True, stop=True) for s_ in range(2)],
                       reads=[N(MTprev), N(Mprev)], writes=[pbn[b1]])
                    yield None
                if pend is not None:
                    pt_update(pend)
                    yield None
                op('act', lambda e: e.copy(B[Mn][:], pb[b2][:, 0:256].rearrange("p (s n) -> p s n", s=2)), reads=[pbn[b2]], writes=[N(Mn)])
                yield None
                if k_ < 5:
                    op('dve', lambda e: e.tensor_copy(B[MTn][:], pb[b1][:, 0:256].rearrange("p (s n) -> p s n", s=2)), reads=[pbn[b1]], writes=[N(MTn)])
                    yield None
                pend = Mn
                Mprev, MTprev, MTn = Mn, MTn, ("MTa" if MTn == "MTb" else "MTb")
            pt_update(pend)
            yield None
            op('act', lambda e: e.copy(B["PTb"][:], pb[b3][:, 0:256].rearrange("p (s n) -> p s n", s=2)), reads=[pbn[b3]], writes=[N("PTb")])
            yield None
            mm([lambda e, s_=s_: e.matmul(pb[b2][:, s_ * 128:(s_ + 1) * 128], M0[:, s_, :], B["PTb"][:, s_, :], start=True, stop=True) for s_ in range(2)],
               reads=[N("M0"), N("PTb")], writes=[pbn[b2]])
            yield None
            mm([lambda e, s_=s_: e.matmul(pb[b1][:, s_ * 128:(s_ + 1) * 128], B["PTb"][:, s_, :], ident[:], start=True, stop=True) for s_ in range(2)],
               reads=[N("PTb"), "ident"], writes=[pbn[b1]])
            yield None
            op('dve', lambda e: e.scalar_tensor_tensor(Em[:, 0], B["PTb"][:], -1.0, pb[b2][:, 0:256].rearrange("p (s n) -> p s n", s=2), ALU.mult, ALU.add),
               reads=[pbn[b2], N("PTb")], writes=[(emn, 0)])
            yield None
            op('act', lambda e: e.copy(B["Mb"][:], pb[b1][:, 0:256].rearrange("p (s n) -> p s n", s=2)), reads=[pbn[b1]], writes=[N("Mb")])
            yield None
            op('dve', lambda e: e.tensor_tensor(B["Ma"][:], Em[:, 0], ident2[:], ALU.add), reads=[(emn, 0), "ident2"], writes=[N("Ma")])
            yield None
            fns = [lambda e: e.matmul(pb[b3][:, 0:256], ident[:], B["PTb"][:].rearrange("p s d -> p (s d)"), start=True, stop=False)]
            for s_ in range(2):
                fns.append(lambda e, s_=s_: e.matmul(pb[b3][:, s_ * 128:(s_ + 1) * 128], B["Mb"][:, s_, :], B["Ma"][:, s_, :], start=False, stop=True))
            mm(fns, reads=[N("Mb"), N("Ma"), N("PTb"), "ident"], writes=[pbn[b3]])
            yield None
            op('act', lambda e: e.copy(B["Wt"][:], pb[b3][:, 0:256].rearrange("p (s n) -> p s n", s=2)), reads=[pbn[b3]], writes=[N("Wt")])
            yield None
            op('pool', lambda e: e.tensor_copy(B["PTb"][:], B["Wt"][:]), reads=[N("Wt")], writes=[N("PTb")])
            yield None
            AT = B["PTb"]
            fns = []
            for s_ in range(2):
                fns.append(lambda e, s_=s_: e.matmul(pb[b0][:, s_ * 128:(s_ + 1) * 128], AT[:, s_, :], B["Kbg"][:, s_, :], start=True, stop=True))
                fns.append(lambda e, s_=s_: e.matmul(pb[b0][:, (2 + s_) * 128:(3 + s_) * 128], AT[:, s_, :], B["Vb"][:, s_, :], start=True, stop=True))
            mm(fns, reads=[N("PTb"), N("Kbg"), N("Vb")], writes=[pbn[b0]])
            yield None
            p6a = pb[b0][:, 0:256].rearrange("p (s n) -> p s n", s=2)
            p6b = pb[b0][:, 256:512].rearrange("p (s n) -> p s n", s=2)
            op('act', lambda e: e.copy(B["Wt"][:], p6a), reads=[pbn[b0]], writes=[N("Wt")])
            yield None
            op('act', lambda e: e.activation(B["Wn"][:], p6a, AF.Copy, scale=-1.0), reads=[pbn[b0]], writes=[N("Wn")])
            yield None
            op('act', lambda e: e.copy(B["U"][:], p6b), reads=[pbn[b0]], writes=[N("U")])
            yield None
            fns = []
            for s_ in range(2):
                for hf in range(2):
                    fns.append(lambda e, s_=s_, hf=hf: e.matmul(pb[b0][:, (s_ * 2 + hf) * 128:(s_ * 2 + hf + 1) * 128], B["Wt"][:, s_, :], B["Kd%d" % hf][:, s_, :], start=True, stop=True))
            mm(fns, reads=[N("Wt"), N("Kd0"), N("Kd1")], writes=[pbn[b0]])
            yield None
            op('act', lambda e: e.activation(X, pb[b0][:].rearrange("p (s h n) -> p s h n", s=2, h=2), AF.Copy, scale=-1.0), reads=[pbn[b0]], writes=[N("X")])
            yield None
            fns = []
            for s_ in range(2):
                fns.append(lambda e, s_=s_: e.matmul(pb[b1][:, s_ * 128:(s_ + 1) * 128], B["Qg"][:, s_, :], ident[:], start=True, stop=False))
                fns.append(lambda e, s_=s_: e.matmul(pb[b1][:, s_ * 128:(s_ + 1) * 128], B["Wn"][:, s_, :], B["attnT"][:, s_, :], start=False, stop=True))
            mm(fns, reads=[N("Qg"), N("Wn"), N("attnT"), "ident"], writes=[pbn[b1]])
            yield None
            op('act', lambda e: e.copy(B["CT"][:], pb[b1][:, 0:256].rearrange("p (s n) -> p s n", s=2)), reads=[pbn[b1]], writes=[N("CT")])
            yield 'SCAN'
            for step in range(2):
                for s_, (blk, dh, d_) in enumerate(slots):
                    hf = step if d_ == 0 else 1 - step
                    R = slice(hf * 64, (hf + 1) * 64)
                    mm([lambda e: e.matmul(pb[b1][:, s_ * 128:(s_ + 1) * 128], B["attnT"][:, s_, :], B["U"][:, s_, :], start=True, stop=False),
                        lambda e: e.matmul(pb[b1][:, s_ * 128:(s_ + 1) * 128], B["CT"][:, s_, :], Sbf[:, s_, :], start=False, stop=True)],
                       reads=[N("attnT"), N("U"), N("CT"), ("Sbf", s_)], writes=[pbn[b1]])
                    yield None
                    op('dve', lambda e: e.tensor_tensor(oacc[R, blk, :], oacc[R, blk, :], pb[b1][R, s_ * 128:(s_ + 1) * 128], ALU.add), reads=[pbn[b1], ("oacc", blk)], writes=[("oacc", blk)])
                    yield None
                    mm([lambda e: e.matmul(pb[b2][:, s_ * 128:(s_ + 1) * 128], B["Kd%d" % hf][:, s_, :], B["U"][:, s_, :], start=True, stop=False),
                        lambda e: e.matmul(pb[b2][:, s_ * 128:(s_ + 1) * 128], X[:, s_, hf, :], Sbf[:, s_, :], start=False, stop=True)],
                       reads=[N("Kd0"), N("Kd1"), N("U"), N("X"), ("Sbf", s_)], writes=[pbn[b2]])
                    yield None
                    dec = dec1 if hf == 0 else dec2
                    op('dve', lambda e: e.scalar_tensor_tensor(S[:, s_, :], S[:, s_, :], dec[:, blk, dh:dh + 1], pb[b2][:, s_ * 128:(s_ + 1) * 128], ALU.mult, ALU.add),
                       reads=[pbn[b2], ("S", s_), "dec1", "dec2"], writes=[("S", s_)])
                    yield None
                    op('pool', lambda e: e.tensor_copy(Sbf[:, s_, :], S[:, s_, :]), reads=[("S", s_)], writes=[("Sbf", s_)])
                    yield None
            for s_, (blk, dh, d_) in enumerate(slots):
                if (d_ == 0 and blk % 2 == 1) or (d_ == 1 and blk % 2 == 0):
                    dma('sp', O["nst"][l, blk // 2, d_, h], S[:, s_, :], reads=[("S", s_)])
            yield None

        for j in range(NB // 2):
            gA = gen_iter(2 * j, SETS[0])
            gB = gen_iter(2 * j + 1, SETS[1])
            dA = dB = False
            while not (dA and dB):
                if not dA:
                    dA = (next(gA) == 'SCAN')
                if not dB:
                    dB = (next(gB) == 'SCAN')
            for _ in gA:
                pass
            for _ in gB:
                pass

        kb.mark('L%d C h%d post' % (l, h))
        for b in range(NB):
            op('act', lambda e: e.activation(junk[:, 0:128], oacc[:, b, :], AF.Square, accum_out=st2[:, b, 0:1]), reads=[("oacc", b)], writes=["junk", ("st2", b)])
        op('dve', lambda e: e.tensor_scalar(st2[:, :, 1:2], st2[:, :, 0:1], 1.0 / 128, 1e-6, ALU.mult, ALU.add), reads=["st2"], writes=["st2"])
        op('act', lambda e: e.activation(st2[:, :, 2:3], st2[:, :, 1:2], AF.Ln), reads=["st2"], writes=["st2"])
        op('act', lambda e: e.activation(st2[:, :, 3:4], st2[:, :, 2:3], AF.Exp, scale=-0.5), reads=["st2"], writes=["st2"])
        for b in range(NB):
            k16 = kvn16[b % 2]
            k16n = "kvn16_%d" % (b % 2)
            op('dve', lambda e: e.scalar_tensor_tensor(k16[:, 0:128], oacc[:, b, :], st2[:, b, 3:4], gnb, ALU.mult, ALU.mult), reads=[("oacc", b), "st2", "gnb"], writes=[k16n])
            bk = 4 + b % 2
            pview = pb[bk][:].bitcast(BF16)
            mm([lambda e: e.transpose(pview[:, 0:128], k16[:, 0:128], ident[:])], reads=[k16n, "ident"], writes=[pbn[bk]])
            op('act', lambda e: e.copy(ozT[:, h, b * 128:(b + 1) * 128], pview[:, 0:128]), reads=[pbn[bk]], writes=[("ozT", h, b // 4)])


def _rope_tables(sample):
    C = np.ones((128, T), np.float32)
    S = np.zeros((128, T), np.float32)
    if not sample:
        C[96:] = 0
        return C, S
    tok = np.arange(T)
    row = (tok // 64).astype(np.float32)
    col = (tok % 64).astype(np.float32)
    def tab(rot):
        npairs = rot // 4
        inv = (10000.0 ** (-np.arange(npairs, dtype=np.float32) / npairs)).astype(np.float32)
        ang = np.concatenate([row[:, None] * inv, col[:, None] * inv], axis=-1).astype(np.float32)
        c = np.cos(ang).astype(np.float32)
        s = np.sin(ang).astype(np.float32)
        Cd = np.repeat(c, 2, axis=1).T
        Sd = np.repeat(s, 2, axis=1).T
        sign = np.where(np.arange(rot) % 2 == 0, -1.0, 1.0).astype(np.float32)[:, None]
        return Cd, Sd * sign
    Ca, Sa = tab(64)
    Cb, Sb = tab(32)
    C[0:64], S[0:64] = Ca, Sa
    C[64:96], S[64:96] = Cb, Sb
    return C, S


def _mask_a(sample):
    m = np.full((6, 128, 512), NEG, np.float32)
    kj = np.arange(128)[:, None]
    qi = np.arange(128)[None, :]
    for o in range(6):
        for qb in range(4):
            blk = m[o, :, qb * 128:(qb + 1) * 128]
            if sample:
                rel = o - 1 - qb
                if rel == 0:
                    blk[:] = 0
                elif rel == -1:
                    blk[kj >= qi] = 0
                elif rel == 1:
                    blk[kj <= qi] = 0
            else:
                if (o - 1) // 2 == qb // 2 and o >= 1:
                    blk[:] = 0
    return m


def _perm_pairs(n):
    idx = np.arange(n)
    return idx ^ 1


def kernel(**inp):
    f = lambda a: np.ascontiguousarray(np.asarray(a, dtype=np.float32))
    w_in = f(inp["w_in"])
    pcols = np.concatenate([C_AQ + _perm_pairs(512), C_AK + _perm_pairs(128), C_BKPE + _perm_pairs(32)])
    w_inp = np.ascontiguousarray(w_in[:, :, pcols])
    uq = f(inp["mla_w_uq"])
    uqcols = np.arange(768).reshape(8, 96)
    uqcols[:, 64:] = uqcols[:, 64:] ^ 1
    uqp = np.ascontiguousarray(uq[:, :, uqcols.reshape(-1)])
    shared = {k: f(inp[k]) for k in ["norm_g", "w_ada", "b_ada", "attn_sink", "mla_q_norm", "mla_kv_norm", "mla_w_ukv", "gdn_conv",
                                     "gdn_norm", "w_branch_a", "w_branch_b", "w_branch_c", "w_out", "final_norm_g"]}
    shared["w_in"] = w_in
    shared["w_inp"] = w_inp
    shared["mla_w_uq"] = uq
    shared["mla_w_uqp"] = uqp
    shared["gdn_a_log"] = f(inp["gdn_a_log"]).reshape(2, 8)
    shared["gdn_dt_bias"] = f(inp["gdn_dt_bias"]).reshape(2, 8)
    shared["ident"] = np.eye(128, dtype=np.float32)
    a = np.arange(128)
    same = (a[:, None] // 64) == (a[None, :] // 64)
    gm1 = np.full((128, 8, 128), NEG, np.float32)
    gm2 = np.full((128, 8, 128), NEG, np.float32)
    for dh in range(8):
        if dh < 4:
            gm1[:, dh][(a[:, None] > a[None, :]) & same] = 0
            gm2[:, dh][(a[None, :] >= a[:, None]) & same] = 0
        else:
            gm1[:, dh][(a[:, None] < a[None, :]) & same] = 0
            gm2[:, dh][(a[None, :] <= a[:, None]) & same] = 0
    shared["gm1"], shared["gm2"] = gm1, gm2
    shared["triF"] = ((a[:, None] <= a[None, :]) & same).astype(np.float32)
    shared["triB"] = ((a[:, None] >= a[None, :]) & same).astype(np.float32)
    shared["sel1"] = np.repeat((a < 64).astype(np.float32)[:, None], 128, 1)
    shared["sel2"] = np.repeat((a >= 64).astype(np.float32)[:, None], 128, 1)
    xp = f(inp["x_prompt"]); xsm = f(inp["x_sample"])
    in_maps = []
    for c in range(8):
        m = dict(shared)
        sample = c < 4
        if sample:
            m["x"] = xsm[c]
            m["cond"] = f(inp["c"])[c]
            m["ck"] = f(inp["cache_attn_k"])[c].reshape(2, 512, 128)
            m["cv"] = f(inp["cache_attn_v"])[c].reshape(2, 512, 128)
            m["cckv"] = f(inp["cache_mla_ckv"])[c]
            m["ckpe"] = f(inp["cache_mla_kpe"])[c]
            m["st"] = f(inp["state_gdn"])[c].reshape(2, 8, 128, 128)
            qoh = np.zeros((8, T), np.float32); qoh[0] = 1
            koh = np.zeros((8, KT), np.float32); koh[0] = BIGM
            flags = np.zeros((128, 4), np.float32); flags[:, 0] = 1.0
        else:
            k = c - 4
            m["x"] = xp[8 * k:8 * k + 8].reshape(T, D)
            m["cond"] = f(inp["c_ctx"])
            m["ck"] = np.zeros((2, 512, 128), np.float32)
            m["cv"] = np.zeros((2, 512, 128), np.float32)
            m["cckv"] = np.zeros((2, 512, 256), np.float32)
            m["ckpe"] = np.zeros((2, 512, 32), np.float32)
            m["st"] = np.zeros((2, 8, 128, 128), np.float32)
            qoh = np.zeros((8, T), np.float32); koh = np.zeros((8, KT), np.float32)
            for s in range(8):
                qoh[s, s * 256:(s + 1) * 256] = 1
                koh[s, s * 256:(s + 1) * 256] = BIGM
            flags = np.zeros((128, 4), np.float32); flags[:, 1] = NEG; flags[:, 2] = -1.0
        m["qoh"], m["koh"], m["flags"] = qoh, koh, flags
        m["ropeC"], m["ropeS"] = _rope_tables(sample)
        m["maskA"] = _mask_a(sample)
        in_maps.append({n: np.ascontiguousarray(m[n], dtype=np.float32).reshape(s) for n, s in IN_SPECS})
    nc = build()
    res = run_bass_kernel_spmd(nc, in_maps, core_ids=list(range(8)))
    R = res.results
    y_sample = np.stack([R[c]["y"] for c in range(4)], 0)
    y_prompt = np.concatenate([R[c]["y"].reshape(8, 256, D) for c in range(4, 8)], 0)
    def pc(name, tail):
        return np.concatenate([np.moveaxis(R[c][name].reshape(2, 8, 256, *tail), 0, 1) for c in range(4, 8)], 0)
    nk = pc("nk", (2, 64)); nv = pc("nv", (2, 64)); nckv = pc("nckv", (256,)); nkpe = pc("nkpe", (32,))
    nst = np.concatenate([np.moveaxis(R[c]["nst"], 0, 1) for c in range(4, 8)], 0)
    return (y_prompt.astype(np.float32), y_sample.astype(np.float32), nk.astype(np.float32), nv.astype(np.float32),
            nckv.astype(np.float32), nkpe.astype(np.float32), nst.astype(np.float32))
```

```python
import numpy as np
from contextlib import ExitStack
import concourse.bass as bass
import concourse.mybir as mybir
from concourse.bass_utils import run_bass_kernel_spmd

F32 = mybir.dt.float32
BF16 = mybir.dt.bfloat16
AF = mybir.ActivationFunctionType
ALU = mybir.AluOpType

T = 2048
NB = 16
NSB = 4
KT = 2560
NKB = 20
D = 1024
BIGM = 2048.0
NEG = -30000.0
N_DSEM = 40
LIMIT = None
LAST_KB = None
C_AQ, C_AK, C_AV, C_ZA, C_BCQ, C_BCKV, C_BKPE, C_ZB, C_CQKV, C_CA, C_CB, C_ZC, C_G = (
    0, 512, 640, 768, 1280, 1664, 1920, 1952, 2464, 4000, 4008, 4016, 4528)


class KB:
    def __init__(self, nc, es):
        self.nc = nc
        self.E = {'pe': nc.tensor, 'act': nc.scalar, 'dve': nc.vector, 'pool': nc.gpsimd, 'sp': nc.sync}
        self.sem = {e: es.enter_context(nc.semaphore("s_" + e)) for e in self.E}
        self.cnt = {e: 0 for e in self.E}
        self.seen = {e: {} for e in self.E}
        self.dsem = [es.enter_context(nc.semaphore("d%d" % i)) for i in range(N_DSEM)]
        self.dcnt = [0] * N_DSEM
        self.dnext = 0
        self.reg = {}
        self.n_ins = 0
        self.limit = LIMIT
        self.n_calls = 0

    def _wait(self, eng, tok):
        if tok is None:
            return
        key = (tok[0], tok[1])
        if self.seen[eng].get(key, 0) >= tok[2]:
            return
        if eng == 'pe' and tok[0] == 'e' and tok[1] == 'pe':
            return
        if tok[0] == 'e':
            self.E[eng].wait_ge(self.sem[tok[1]], tok[2])
        else:
            self.E[eng].wait_ge(self.dsem[tok[1]], tok[2])
        self.seen[eng][key] = tok[2]

    def _entries(self, r):
        if isinstance(r, tuple):
            name, sub = r[0], (r[1] if len(r) == 2 else r[1:])
        else:
            name, sub = r, None
        d = self.reg.setdefault(name, {})
        if sub is None:
            if None not in d:
                d[None] = [None, []]
            return [d[k] for k in d], d, None
        out = []
        if None in d:
            out.append(d[None])
        if sub not in d:
            d[sub] = [None, []]
        out.append(d[sub])
        return out, d, sub

    @staticmethod
    def _norm(reads, writes):
        r2, w2 = [], []
        for r in reads:
            nm = r[0] if isinstance(r, tuple) else r
            if nm.startswith("pb"):
                w2.append(nm)
            else:
                r2.append(r)
        for w in writes:
            nm = w[0] if isinstance(w, tuple) else w
            w2.append(nm if nm.startswith("pb") else w)
        return r2, w2

    def _deps(self, eng, reads, writes):
        for r in reads:
            for en in self._entries(r)[0]:
                self._wait(eng, en[0])
        for r in writes:
            for en in self._entries(r)[0]:
                self._wait(eng, en[0])
                for t in en[1]:
                    self._wait(eng, t)

    def _record(self, tok, reads, writes):
        for r in reads:
            _, d, sub = self._entries(r)
            lst = d[sub][1]
            if tok[0] == 'e':
                lst[:] = [t for t in lst if not (t[0] == 'e' and t[1] == tok[1])]
            lst.append(tok)
            if len(lst) > 48:
                del lst[0:len(lst) - 48]
        for r in writes:
            _, d, sub = self._entries(r)
            if sub is None:
                for k in list(d.keys()):
                    if k is not None:
                        del d[k]
            d[sub] = [tok, []]

    def op(self, eng, fn, reads=(), writes=()):
        reads, writes = self._norm(reads, writes)
        self.n_calls += 1
        if self.limit is not None and self.n_calls > self.limit:
            return None
        self._deps(eng, reads, writes)
        ins = fn(self.E[eng])
        self.cnt[eng] += 1
        ins.then_inc(self.sem[eng], 1)
        tok = ('e', eng, self.cnt[eng])
        self._record(tok, reads, writes)
        self.n_ins += 1
        return tok

    def mmgroup(self, fns, reads=(), writes=()):
        reads, writes = self._norm(reads, writes)
        self.n_calls += 1
        if self.limit is not None and self.n_calls > self.limit:
            return None
        self._deps('pe', reads, writes)
        ins = None
        for f in fns:
            ins = f(self.E['pe'])
        self.cnt['pe'] += 1
        ins.then_inc(self.sem['pe'], 1)
        tok = ('e', 'pe', self.cnt['pe'])
        self._record(tok, reads, writes)
        self.n_ins += len(fns)
        return tok

    def dma(self, q, out, in_, reads=(), writes=(), **kw):
        reads, writes = self._norm(reads, writes)
        self.n_calls += 1
        if self.limit is not None and self.n_calls > self.limit:
            return None
        self._deps(q, reads, writes)
        s = self.dnext
        self.dnext = (self.dnext + 1) % N_DSEM
        if self.dcnt[s] > 0:
            self._wait(q, ('d', s, 16 * self.dcnt[s]))
        self.dcnt[s] += 1
        self.E[q].dma_start(out=out, in_=in_, **kw).then_inc(self.dsem[s], 16)
        tok = ('d', s, 16 * self.dcnt[s])
        self._record(tok, reads, writes)
        self.n_ins += 1
        return tok

    def mark(self, label):
        self.marks = getattr(self, "marks", [])
        self.marks.append((label, dict(self.cnt)))
        global LAST_KB
        LAST_KB = self

    def barrier(self):
        for e in self.E:
            for e2 in self.E:
                if e2 != e and self.cnt[e2] > 0:
                    self._wait(e, ('e', e2, self.cnt[e2]))
            for sx in range(N_DSEM):
                if self.dcnt[sx] > 0:
                    self._wait(e, ('d', sx, 16 * self.dcnt[sx]))

    def finish(self):
        for e in self.E:
            if self.cnt[e] > 0:
                self._wait('sp', ('e', e, self.cnt[e]))
        for s in range(N_DSEM):
            if self.dcnt[s] > 0:
                self._wait('sp', ('d', s, 16 * self.dcnt[s]))


IN_SPECS = [
    ("x", [T, D]), ("cond", [D]), ("norm_g", [2, D]), ("w_ada", [2, D, 3 * D]), ("b_ada", [2, 3 * D]),
    ("w_in", [2, D, 7600]), ("w_inp", [2, D, 672]), ("attn_sink", [2, 8]), ("mla_q_norm", [2, 384]),
    ("mla_w_uq", [2, 384, 768]), ("mla_w_uqp", [2, 384, 768]), ("mla_kv_norm", [2, 256]),
    ("mla_w_ukv", [2, 256, 1024]), ("gdn_conv", [2, 3, 1536]), ("gdn_a_log", [2, 8]), ("gdn_dt_bias", [2, 8]),
    ("gdn_norm", [2, 128]), ("w_branch_a", [2, 512, D]), ("w_branch_b", [2, 512, D]), ("w_branch_c", [2, 512, D]),
    ("w_out", [2, D, D]), ("final_norm_g", [D]),
    ("ck", [2, 512, 128]), ("cv", [2, 512, 128]), ("cckv", [2, 512, 256]), ("ckpe", [2, 512, 32]),
    ("st", [2, 8, 128, 128]),
    ("ropeC", [128, T]), ("ropeS", [128, T]), ("maskA", [6, 128, 512]), ("qoh", [8, T]), ("koh", [8, KT]),
    ("flags", [128, 4]), ("ident", [128, 128]), ("gm1", [128, 8, 128]), ("gm2", [128, 8, 128]),
    ("triF", [128, 128]), ("triB", [128, 128]), ("sel1", [128, 128]), ("sel2", [128, 128]),
]
OUT_SPECS = [
    ("y", [T, D]), ("nk", [2, T, 128]), ("nv", [2, T, 128]), ("nckv", [2, T, 256]), ("nkpe", [2, T, 32]),
    ("nst", [2, 8, 2, 4, 128, 128]),
]


def build(stop=None, taps=None):
    nc = bass.Bass("TRN2", target_bir_lowering=False)
    I = {n: nc.dram_tensor(n, s, F32, kind="ExternalInput").ap() for n, s in IN_SPECS}
    O = {n: nc.dram_tensor(n, s, F32, kind="ExternalOutput").ap() for n, s in OUT_SPECS}
    xs = nc.dram_tensor("xs", [T, D], F32, kind="Internal").ap()
    TAP = {}
    if taps:
        for n, s in taps.items():
            TAP[n] = nc.dram_tensor("tap_" + n, s, F32, kind="ExternalOutput").ap()
    with ExitStack() as es:
        kb = KB(nc, es)
        SB = lambda name, shape, dt: es.enter_context(nc.sbuf_tensor("sb_" + name, shape, dt))
        PS = lambda name, shape, dt: es.enter_context(nc.psum_tensor("ps_" + name, shape, dt))
        _body(nc, kb, SB, PS, I, O, xs, TAP, stop)
        kb.mark('end')
        kb.finish()
    return nc


def _body(nc, kb, SB, PS, I, O, xs, TAP, stop):
    op, dma, mm = kb.op, kb.dma, kb.mmgroup
    ident = SB("ident", [128, 128], BF16)
    ident32 = SB("ident32", [128, 128], F32)
    ropeC = SB("ropeC", [128, T], BF16)
    ropeS = SB("ropeS", [128, T], BF16)
    flags = SB("flags", [128, 4], F32)
    ones16 = SB("ones16", [128, 128], BF16)
    op('dve', lambda e: e.memset(ones16[:], 1.0), writes=["ones16"])
    dma('pool', ident[:], I["ident"], writes=["ident"])
    dma('sp', ident32[:], I["ident"], writes=["ident32"])
    dma('pool', ropeC[:], I["ropeC"], writes=["ropeC"])
    dma('pool', ropeS[:], I["ropeS"], writes=["ropeS"])
    dma('sp', flags[:], I["flags"], writes=["flags"])

    hT = SB("hT", [128, 8, T], BF16)
    mergeT = SB("mergeT", [128, 8, T], BF16)
    ozT = SB("ozT", [128, 4, T], BF16)
    pb = [PS("pb%d" % i, [128, 512], F32) for i in range(8)]
    pbn = ["pb%d" % i for i in range(8)]

    def hTr(sb):
        return [("hT", sb * 4 + i) for i in range(4)]

    NW = 2
    WCOLS = 512
    wbuf = [SB("wbuf%d" % i, [128, 8 * WCOLS], BF16) for i in range(NW)]
    wstate = {'i': 0}

    def load_w(src2d, kch, cols, q='pool', prows=128):
        i = wstate['i']
        wstate['i'] = (i + 1) % NW
        name = "wbuf%d" % i
        tot = sum(n for _, n in cols)
        assert kch * tot <= 8 * WCOLS, (kch, tot)
        view = wbuf[i][0:prows, 0:kch * tot].rearrange("p (k n) -> p k n", k=kch)
        srcv = src2d.rearrange("(k p) n -> p k n", p=prows)
        o = 0
        for c0, n in cols:
            dma(q, view[:, :, o:o + n], srcv[:, :, c0:c0 + n], writes=[name])
            o += n
        return view, name

    def tap(name, ap_sb, reads):
        if name in TAP:
            dma('sp', TAP[name], ap_sb, reads=reads)

    condsb = SB("condsb", [128, 8], F32)
    scond = SB("scond", [128, 8], BF16)
    modfm = SB("modfm", [128, 24], F32)
    badafm = SB("badafm", [128, 24], F32)
    ngfm = SB("ngfm", [128, 8], F32)
    Afm = SB("Afm", [128, 8], F32)
    gbc = SB("gbc", [128, 8, 128], F32)
    gateb = SB("gateb", [128, D], F32)
    xblk = [SB("xblk%d" % i, [128, D], F32) for i in range(2)]
    xn = [SB("xn%d" % i, [128, D], BF16) for i in range(2)]
    junk = SB("junk", [128, D], BF16)
    stat = SB("stat", [128, NB, 4], F32)
    qTh = [SB("qTh%d" % i, [104, T], BF16) for i in range(1)] * 2
    wukv = SB("wukv", [128, 2, 1024], BF16)
    ARN = 21952
    arena = SB("arena", [128, ARN], BF16)
    maskA = arena[:, 4 * KT:4 * KT + 3072].rearrange("p (o n) -> p o n", o=6)
    o_ = 0
    kTa = arena[0:64, 0:2 * KT].rearrange("p (g n) -> p g n", g=2)
    Va = arena[:, 2 * KT:2 * KT + NKB * 256].rearrange("p (k g d) -> p k g d", k=NKB, g=2)
    kTb1 = arena[0:104, 0:KT]
    kpeT = arena[0:96, KT:2 * KT]
    Vb1 = arena[:, 2 * KT:2 * KT + NKB * 128].rearrange("p (k d) -> p k d", k=NKB)
    o_ = 2 * KT + NKB * 128
    ckvT = arena[:, o_:o_ + 2 * KT].rearrange("p (c n) -> p c n", c=2)
    cqnT = arena[:, o_ + 2 * KT:o_ + 2 * KT + 3 * T].rearrange("p (c n) -> p c n", c=3)
    kTb = [kTb1, kTb1]
    Vb = [Vb1, Vb1]
    pT = [SB("pT%d" % i, [128, 512], BF16) for i in range(3)]
    t1 = [SB("t1_%d" % i, [128, 512], F32) for i in range(2)]
    t2 = [SB("t2_%d" % i, [128, 512], F32) for i in range(2)]
    rr = [SB("rr%d" % i, [128, 512], F32) for i in range(1)] * 2
    r3 = [SB("r3_%d" % i, [64, 512], F32) for i in range(1)] * 2
    kvout = [SB("kvout%d" % i, [128, 288], F32) for i in range(2)]
    kvn16 = [SB("kvn16_%d" % i, [128, 384], BF16) for i in range(2)]
    ctx16 = SB("ctx16", [128, 4, 256], BF16)
    ctxp = SB("ctxp", [128, 4, 96], BF16)
    esink = SB("esink", [128, 8], F32)
    kvng = SB("kvng", [128, 256], F32)
    qng = SB("qng", [128, 384], F32)
    st2 = SB("st2", [128, NB, 4], F32)
    sig = [SB("sig%d" % i, [128, 512], F32) for i in range(2)]
    op('dve', lambda e: e.memset(ctxp[:], 0.0), writes=["ctxp"])
    dma('pool', qTh[0][96:104, :], I["qoh"], writes=["qTh0"])
    cnt = {'rot': 0, 'p': 0, 'o': 0}
    if stop == "c":
        return

    pT.append(SB("pT3", [128, 512], BF16))

    def attend_stream(groups):
        SBK = [2, 3, 6, 7]
        tiles = []
        for gi, g_ in enumerate(groups):
            g_['ob'] = 4 + cnt['o'] % 2
            cnt['o'] += 1
            for idx in range(len(g_['klist'])):
                tiles.append((gi, idx))
        info = {}

        def emit_S(t):
            gi, idx = tiles[t]
            g_ = groups[gi]
            kblk, mi, isctx = g_['klist'][idx]
            sbk = SBK[cnt['p'] % 4]
            pt = cnt['p'] % 4
            cnt['p'] += 1
            kap, kname = g_['kfn'](kblk)
            qtile, sbi, K = g_['qtile'], g_['sbi'], g_['K']
            fns = [lambda e: e.matmul(pb[sbk][:], kap, qtile[0:K, sbi * 512:(sbi + 1) * 512], start=True, stop=(mi is None))]
            rd = [kname, (g_['qname'], sbi)]
            if mi is not None:
                fns.append(lambda e: e.matmul(pb[sbk][:], ident[:], maskA[:, mi, :], start=False, stop=True))
                rd += ["ident", "maskA"]
            mm(fns, reads=rd, writes=[pbn[sbk]])
            b_ = g_['bias_fn'](isctx)
            op('act', lambda e: e.activation(pT[pt][:], pb[sbk][:], AF.Exp, scale=g_['scale'], bias=b_),
               reads=[pbn[sbk], "flags"], writes=["pT%d" % pt])
            info[t] = pt

        LOOK = 3
        nt = len(tiles)
        for t in range(min(LOOK, nt)):
            emit_S(t)
        for t in range(nt):
            if t + LOOK < nt:
                emit_S(t + LOOK)
            gi, idx = tiles[t]
            g_ = groups[gi]
            n = len(g_['klist'])
            kblk = g_['klist'][idx][0]
            pt = info[t]
            ob = g_['ob']
            vap, vname = g_['vfn'](kblk)
            mm([lambda e: e.matmul(pb[ob][:], vap, pT[pt][:], start=(idx == 0), stop=(idx == n - 1))],
               reads=[vname, "pT%d" % pt], writes=[pbn[ob]])
            if idx == n - 1:
                g_['fin'](ob)

    def run_pairs(genf, n):
        for b0_ in range(0, n, 2):
            gs = [genf(b0_), genf(b0_ + 1)]
            alive = [True, True]
            while any(alive):
                for q_ in range(2):
                    if alive[q_]:
                        try:
                            next(gs[q_])
                        except StopIteration:
                            alive[q_] = False

    for l in range(2):
        xsrc = I["x"] if l == 0 else xs
        Win = I["w_in"][l]
        Winp = I["w_inp"][l]
        kb.mark('L%d start' % l)
        dma('sp', condsb[:], I["cond"].rearrange("(c p) -> p c", p=128), writes=["cond"], allow_slow_non_contiguous=True)
        dma('sp', badafm[:], I["b_ada"][l].rearrange("(c p) -> p c", p=128), writes=["bada"], allow_slow_non_contiguous=True)
        dma('sp', ngfm[:], I["norm_g"][l].rearrange("(c p) -> p c", p=128), writes=["ngfm"], allow_slow_non_contiguous=True)
        op('act', lambda e: e.activation(scond[:], condsb[:], AF.Silu), reads=["cond"], writes=["scond"])
        for nt in range(6):
            wv, wn = load_w(I["w_ada"][l], 8, [(nt * 512, 512)])
            for jj in range(4):
                j = nt * 4 + jj
                mm([(lambda e, c=c, jj=jj, j=j, wv=wv: e.matmul(pb[0][:, j:j + 1], wv[:, c, jj * 128:(jj + 1) * 128], scond[:, c:c + 1],
                                                               start=(c == 0), stop=(c == 7))) for c in range(8)],
                   reads=[wn, "scond"], writes=[(pbn[0], j)])
        op('dve', lambda e: e.tensor_tensor(modfm[:], pb[0][:, 0:24], badafm[:], ALU.add), reads=[pbn[0], "bada"], writes=["modfm"])
        op('dve', lambda e: e.scalar_tensor_tensor(Afm[:], modfm[:, 8:16], 1.0, ngfm[:], ALU.add, ALU.mult), reads=["modfm", "ngfm"], writes=["Afm"])
        op('dve', lambda e: e.tensor_copy(gbc[:], modfm[:, 16:24].unsqueeze(2).to_broadcast([128, 8, 128])), reads=["modfm"], writes=["gbc"])
        for c in range(8):
            bkc = 1 + c // 4
            mm([lambda e, c=c, bkc=bkc: e.matmul(pb[bkc][:, (c % 4) * 128:(c % 4 + 1) * 128], gbc[:, c, :], ident32[:], start=True, stop=True)],
               reads=["gbc", "ident32"], writes=[(pbn[bkc], c % 4)])
        op('act', lambda e: e.copy(gateb[:, 0:512], pb[1][:]), reads=[pbn[1]], writes=[("gateb", 0)])
        op('act', lambda e: e.copy(gateb[:, 512:1024], pb[2][:]), reads=[pbn[2]], writes=[("gateb", 1)])
        tap("modfm%d" % l, modfm[:], ["modfm"])
        if stop == "p0":
            return

        kb.mark('L%d p1' % l)
        def gen_p1(b):
            xb, xbn = xblk[b % 2], "xblk%d" % (b % 2)
            xnb, xnn = xn[b % 2], "xn%d" % (b % 2)
            dma('sp', xb[:], xsrc[b * 128:(b + 1) * 128, :], reads=(["xs"] if l == 1 else []), writes=[xbn])
            yield None
            op('act', lambda e: e.activation(junk[:], xb[:], AF.Square, accum_out=stat[:, b, 0:1]), reads=[xbn], writes=["junk", ("stat", b)])
            yield None
            op('dve', lambda e: e.tensor_scalar(stat[:, b, 1:2], stat[:, b, 0:1], 1.0 / D, 1e-6, ALU.mult, ALU.add), reads=[("stat", b)], writes=[("stat", b)])
            yield None
            op('act', lambda e: e.activation(stat[:, b, 2:3], stat[:, b, 1:2], AF.Ln), reads=[("stat", b)], writes=[("stat", b)])
            yield None
            op('act', lambda e: e.activation(stat[:, b, 3:4], stat[:, b, 2:3], AF.Exp, scale=-0.5), reads=[("stat", b)], writes=[("stat", b)])
            yield None
            op('dve', lambda e: e.tensor_scalar(xnb[:], xb[:], stat[:, b, 3:4], None, ALU.mult), reads=[xbn, ("stat", b)], writes=[xnn])
            yield None
            for half in range(2):
                bk = 4 + (2 * b + half) % 4
                pview = pb[bk][:].bitcast(BF16)
                mm([(lambda e, c=c, half=half, pview=pview: e.transpose(pview[:, c * 128:(c + 1) * 128], xnb[:, (half * 4 + c) * 128:(half * 4 + c + 1) * 128], ident[:]))
                    for c in range(4)], reads=[xnn, "ident"], writes=[pbn[bk]])
                yield None
                for c in range(4):
                    cc = half * 4 + c
                    if True:
                        op('act', lambda e, c=c, cc=cc, pview=pview: e.activation(hT[:, cc, b * 128:(b + 1) * 128], pview[:, c * 128:(c + 1) * 128], AF.Identity,
                                                                                 scale=Afm[:, cc:cc + 1], bias=modfm[:, cc:cc + 1]),
                           reads=[pbn[bk], "Afm", "modfm"], writes=[("hT", b, cc)])
                        yield None
                    else:
                        op('dve', lambda e, c=c, cc=cc, pview=pview: e.scalar_tensor_tensor(hT[:, cc, b * 128:(b + 1) * 128], pview[:, c * 128:(c + 1) * 128],
                                                                                    Afm[:, cc:cc + 1], modfm[:, cc:cc + 1].to_broadcast([128, 128]), ALU.mult, ALU.add),
                           reads=[pbn[bk], "Afm", "modfm"], writes=[("hT", b, cc)])
                        yield None
        run_pairs(gen_p1, NB)
        op('pool', lambda e: e.memset(junk[0:1, 0:1], 0.0), reads=[], writes=["hT"])
        HR = ["hT"]
        if "hT%d" % l in TAP:
            for c8 in range(8):
                op('dve', lambda e: e.tensor_copy(xblk[0][:].rearrange("p (a b) -> p a b", a=1)[:, 0, :], hT[:, c8, 0:1024]), reads=["hT"], writes=["xblk0"])
                dma('sp', TAP["hT%d" % l][:, c8, 0:1024], xblk[0][:], reads=["xblk0"])
                op('dve', lambda e: e.tensor_copy(xblk[0][:], hT[:, c8, 1024:2048]), reads=["hT"], writes=["xblk0"])
                dma('sp', TAP["hT%d" % l][:, c8, 1024:2048], xblk[0][:], reads=["xblk0"])
        if stop == "p1":
            return

        def lin(bank, wv, wn, col0, M, sbi, kch=8, rhs_fn=None, extra_reads=()):
            if rhs_fn is None:
                rhs_fn = lambda c: hT[:, c, sbi * 512:(sbi + 1) * 512]
            mm([(lambda e, c=c: e.matmul(pb[bank][0:M, :], wv[:, c, col0:col0 + M], rhs_fn(c), start=(c == 0), stop=(c == kch - 1)))
                for c in range(kch)], reads=[wn] + HR + list(extra_reads), writes=[pbn[bank]])

        kb.mark('L%d A-pre' % l)
        dma('sp', esink[:], I["attn_sink"][l].partition_broadcast(128), writes=["esink"])
        op('act', lambda e: e.activation(esink[:], esink[:], AF.Exp), reads=["esink"], writes=["esink"])
        dma('sp', kvng[:], I["mla_kv_norm"][l].partition_broadcast(128), writes=["kvng"])
        kb.barrier()
        op('dve', lambda e: e.memset(Va[:, :, :, 64:128], 1.0), writes=["Va"])
        dma('pool', maskA, I["maskA"].rearrange("o p n -> p o n"), writes=["maskA"])
        wq = arena[:, 13312:13312 + 4096].rearrange("p (k n) -> p k n", k=8)
        wqp = arena[:, 17408:17408 + 4096].rearrange("p (k n) -> p k n", k=8)
        wqn, wqpn = "mwq", "mwqp"
        wkv, wkvn = load_w(Win, 8, [(C_AK, 256)])
        def gen_akv(b):
            bk = 6 + b % 2
            ko = kvout[b % 2]
            kon = "kvout%d" % (b % 2)
            mm([(lambda e, c=c: e.matmul(pb[bk][:, 0:256], hT[:, c, b * 128:(b + 1) * 128], wkv[:, c, 0:256], start=(c == 0), stop=(c == 7))) for c in range(8)],
               reads=[wkvn] + HR, writes=[(pbn[bk], 0)])
            yield None
            op('act', lambda e: e.copy(ko[:, 0:256], pb[bk][:, 0:256]), reads=[(pbn[bk], 0)], writes=[kon])
            yield None
            op('dve', lambda e: e.tensor_copy(Va[:, b, :, 0:64], pb[bk][:, 128:256].rearrange("p (g d) -> p g d", g=2)), reads=[(pbn[bk], 0), kon], writes=[("Va", b)])
            yield None
            dma('sp', O["nk"][l, b * 128:(b + 1) * 128, :], ko[:, 0:128], reads=[kon])
            yield None
            dma('sp', O["nv"][l, b * 128:(b + 1) * 128, :], ko[:, 128:256], reads=[kon])
            yield None
        run_pairs(gen_akv, NB)
        dma('pool', ctx16[:, :, 0:128], I["ck"][l].rearrange("(j p) n -> p j n", p=128), writes=["ctx16"])
        for g in range(2):
            pview = pb[6 + g][:].bitcast(BF16)
            mm([(lambda e, j=j, pview=pview: e.transpose(pview[0:64, j * 128:(j + 1) * 128], ctx16[:, j, g * 64:(g + 1) * 64], ident[:])) for j in range(4)],
               reads=["ctx16", "ident"], writes=[pbn[6 + g]])
            op('dve', lambda e, pview=pview: e.tensor_copy(kTa[:, g, T:KT], pview[0:64, 0:512]), reads=[pbn[6 + g]], writes=[("kTa", g, 4)])
        for g in range(2):
            dma('pool', Va[:, NB:NKB, g, 0:64], I["cv"][l].rearrange("(j p) (g d) -> p j g d", p=128, g=2)[:, :, g, :], writes=[("Va", "ctx", g)])
        wk, wkn = load_w(Win, 8, [(C_AK, 128)])
        wkp, wkpn = load_w(Winp, 8, [(512, 128)])
        dma('pool', wq, Win.rearrange("(k p) n -> p k n", p=128)[:, :, C_AQ:C_AQ + 512], writes=[wqn])
        dma('pool', wqp, Winp.rearrange("(k p) n -> p k n", p=128)[:, :, 0:512], writes=[wqpn])
        for g in range(2):
            def gen_ka(sbi, g=g):
                r = sbi % 2
                ba, bb_ = 2 * r, 2 * r + 1
                lin(ba, wk, wkn, g * 64, 64, sbi)
                yield None
                lin(bb_, wkp, wkpn, g * 64, 64, sbi)
                yield None
                op('dve', lambda e: e.tensor_tensor(t1[r][0:64, :], pb[ba][0:64, :], ropeC[0:64, sbi * 512:(sbi + 1) * 512], ALU.mult), reads=[pbn[ba], "ropeC"], writes=["t1_%d" % r])
                yield None
                op('dve', lambda e: e.tensor_tensor(t2[r][0:64, :], pb[bb_][0:64, :], ropeS[0:64, sbi * 512:(sbi + 1) * 512], ALU.mult), reads=[pbn[bb_], "ropeS"], writes=["t2_%d" % r])
                yield None
                op('pool', lambda e: e.tensor_tensor(kTa[:, g, sbi * 512:(sbi + 1) * 512], t1[r][0:64, :], t2[r][0:64, :], ALU.add), reads=["t1_%d" % r, "t2_%d" % r], writes=[("kTa", g, sbi)])
                yield None
            run_pairs(gen_ka, NSB)
        kb.mark('L%d A-attn' % l)
        for h in range(8):
            g = h // 4
            qt, qn = qTh[h % 2], "qTh0"
            def gen_qa(sbi, h=h, qt=qt, qn=qn):
                r = sbi % 2
                ba, bb_ = 2 * r, 2 * r + 1
                lin(ba, wq, wqn, h * 64, 64, sbi)
                yield None
                lin(bb_, wqp, wqpn, h * 64, 64, sbi)
                yield None
                op('dve', lambda e: e.tensor_tensor(t1[r][0:64, :], pb[ba][0:64, :], ropeC[0:64, sbi * 512:(sbi + 1) * 512], ALU.mult), reads=[pbn[ba], "ropeC"], writes=["t1_%d" % r])
                yield None
                op('dve', lambda e: e.tensor_tensor(t2[r][0:64, :], pb[bb_][0:64, :], ropeS[0:64, sbi * 512:(sbi + 1) * 512], ALU.mult), reads=[pbn[bb_], "ropeS"], writes=["t2_%d" % r])
                yield None
                op('pool', lambda e: e.tensor_tensor(qt[0:64, sbi * 512:(sbi + 1) * 512], t1[r][0:64, :], t2[r][0:64, :], ALU.add), reads=["t1_%d" % r, "t2_%d" % r], writes=[(qn, sbi)])
                yield None
            run_pairs(gen_qa, NSB)
            groups = []
            for sbi in range(NSB):
                klist = []
                for o in range(6):
                    j = 4 * sbi - 1 + o
                    if 0 <= j < NB:
                        klist.append((j, o, False))
                for j in range(NB, NKB):
                    klist.append((j, None, True))

                def kfn(kblk, g=g):
                    return kTa[:, g, kblk * 128:(kblk + 1) * 128], ("kTa", g, kblk // 4)

                def vfn(kblk, g=g):
                    return Va[:, kblk, g, :], (("Va", kblk) if kblk < NB else ("Va", "ctx", g))

                def fin(ob, h=h, sbi=sbi):
                    r = cnt['rot'] % 2
                    cnt['rot'] += 1
                    op('dve', lambda e: e.tensor_scalar(rr[r][64:128, :], pb[ob][64:128, :], esink[64:128, h:h + 1], None, ALU.add), reads=[pbn[ob], "esink"], writes=["rr0"])
                    op('dve', lambda e: e.reciprocal(rr[r][64:128, :], rr[r][64:128, :]), reads=["rr0"], writes=["rr0"])
                    op('pool', lambda e: e.tensor_copy(r3[r][0:64, :], rr[r][64:128, :]), reads=["rr0"], writes=["r3_0"])
                    po = (h % 2) * 64
                    op('dve', lambda e: e.tensor_tensor(ozT[po:po + 64, h // 2, sbi * 512:(sbi + 1) * 512], pb[ob][0:64, :], r3[r][0:64, :], ALU.mult),
                       reads=[pbn[ob], "r3_0"], writes=[("ozT", h // 2, sbi)])

                groups.append(dict(qtile=qt, qname=qn, sbi=sbi, kfn=kfn, klist=klist, vfn=vfn, K=64, scale=0.125,
                                   bias_fn=(lambda isctx: (flags[:, 1:2] if isctx else 0.0)), fin=fin))
            attend_stream(groups)
        if "ozA%d" % l in TAP:
            for c8 in range(4):
                for hf in range(2):
                    op('dve', lambda e: e.tensor_copy(xblk[0][:], ozT[:, c8, hf * 1024:(hf + 1) * 1024]), reads=["ozT"], writes=["xblk0"])
                    dma('sp', TAP["ozA%d" % l][:, c8, hf * 1024:(hf + 1) * 1024], xblk[0][:], reads=["xblk0"])

        def zmul_and_merge(zcol, wbr_src, gcol, first, after_loads=None):
            kb.barrier()

            def load_into(k_, name, src2d, kch, c0, n):
                v = arena[:, k_ * 4096:k_ * 4096 + kch * n].rearrange("p (k n) -> p k n", k=kch)
                dma('pool', v, src2d.rearrange("(k p) n -> p k n", p=128)[:, :, c0:c0 + n], writes=[name])
                return v, name
            wz, wzn = load_into(0, "mw0", Win, 8, zcol, 512)
            wbs, wgs = [], []
            for ch in range(2):
                wbs.append(load_into(1 + 2 * ch, "mw%d" % (1 + 2 * ch), wbr_src, 4, ch * 512, 512))
                wgs.append(load_into(2 + 2 * ch, "mw%d" % (2 + 2 * ch), Win, 8, gcol + ch * 512, 512))
            if after_loads is not None:
                after_loads()
            for c in range(4):
                for sbi in range(NSB):
                    r = cnt['rot'] % 2
                    cnt['rot'] += 1
                    lin(r, wz, wzn, c * 128, 128, sbi)
                    op('act', lambda e: e.activation(t1[r][:], pb[r][:], AF.Silu), reads=[pbn[r]], writes=["t1_%d" % r])
                    op('pool', lambda e: e.tensor_tensor(ozT[:, c, sbi * 512:(sbi + 1) * 512], ozT[:, c, sbi * 512:(sbi + 1) * 512], t1[r][:], ALU.mult),
                       reads=["t1_%d" % r, ("ozT", c, sbi)], writes=[("ozT", c, sbi)])
            for ch in range(2):
                wb, wbn = wbs[ch]
                wg, wgn = wgs[ch]
                for cc in range(4):
                    c = ch * 4 + cc
                    for sbi in range(NSB):
                        r = cnt['rot'] % 2
                        cnt['rot'] += 1
                        lin(r, wg, wgn, cc * 128, 128, sbi)
                        op('act', lambda e: e.activation(sig[r][:], pb[r][:], AF.Sigmoid), reads=[pbn[r]], writes=["sig%d" % r])
                        lin(2 + r, wb, wbn, cc * 128, 128, sbi, kch=4, rhs_fn=lambda k: ozT[:, k, sbi * 512:(sbi + 1) * 512],
                            extra_reads=[("ozT", k, sbi) for k in range(4)])
                        dst = mergeT[:, c, sbi * 512:(sbi + 1) * 512]
                        if first:
                            op('dve', lambda e: e.tensor_tensor(dst, pb[2 + r][:], sig[r][:], ALU.mult), reads=[pbn[2 + r], "sig%d" % r], writes=[("mergeT", c, sbi)])
                        else:
                            op('dve', lambda e: e.tensor_tensor(t2[r][:], pb[2 + r][:], sig[r][:], ALU.mult), reads=[pbn[2 + r], "sig%d" % r], writes=["t2_%d" % r])
                            op('pool', lambda e: e.tensor_tensor(dst, dst, t2[r][:], ALU.add), reads=["t2_%d" % r, ("mergeT", c, sbi)], writes=[("mergeT", c, sbi)])

        kb.mark('L%d A-merge' % l)
        zmul_and_merge(C_ZA, I["w_branch_a"][l], C_G, True)

        def tap_big(nm, src, nch, rd):
            if nm in TAP:
                for c8 in range(nch):
                    for hf in range(2):
                        op('dve', lambda e: e.tensor_copy(xblk[0][:], src[:, c8, hf * 1024:(hf + 1) * 1024]), reads=rd, writes=["xblk0"])
                        dma('sp', TAP[nm][:, c8, hf * 1024:(hf + 1) * 1024], xblk[0][:], reads=["xblk0"])
        tap_big("mergeA%d" % l, mergeT, 8, ["mergeT"])
        if stop == "A":
            return

        kb.mark('L%d B-pre' % l)
        kb.barrier()
        op('dve', lambda e: e.memset(Vb1[:, :, 64:128], 1.0), writes=["Vb0"])
        dma('pool', kTb1[96:104, :], I["koh"], writes=["kTb0"])
        dma('pool', qTh[0][96:104, :], I["qoh"], writes=["qTh0"])
        wkv, wkvn = load_w(Win, 8, [(C_BCKV, 288)])
        def gen_bckv(b):
            bk = 6 + b % 2
            mm([(lambda e, c=c: e.matmul(pb[bk][:, 256:512 + 32 - 512] if False else pb[bk][:, 256:512], hT[:, c, b * 128:(b + 1) * 128], wkv[:, c, 0:256], start=(c == 0), stop=(c == 7))) for c in range(8)],
               reads=[wkvn] + HR, writes=[(pbn[bk], 1)])
            yield None
            bk2 = 0 + b % 2
            mm([(lambda e, c=c: e.matmul(pb[bk2][:, 0:32], hT[:, c, b * 128:(b + 1) * 128], wkv[:, c, 256:288], start=(c == 0), stop=(c == 7))) for c in range(8)],
               reads=[wkvn] + HR, writes=[(pbn[bk2], 0)])
            yield None
            ko2 = kvout[(b + 1) % 2]
            ko2n = "kvout%d" % ((b + 1) % 2)
            op('act', lambda e: e.activation(junk[:, 0:256], pb[bk][:, 256:512], AF.Square, accum_out=st2[:, b, 0:1]), reads=[(pbn[bk], 1)], writes=["junk", ("st2", b)])
            yield None
            op('dve', lambda e: e.tensor_scalar(st2[:, b, 1:2], st2[:, b, 0:1], 1.0 / 256, 1e-6, ALU.mult, ALU.add), reads=[("st2", b)], writes=[("st2", b)])
            yield None
            op('act', lambda e: e.activation(st2[:, b, 2:3], st2[:, b, 1:2], AF.Ln), reads=[("st2", b)], writes=[("st2", b)])
            yield None
            op('act', lambda e: e.activation(st2[:, b, 3:4], st2[:, b, 2:3], AF.Exp, scale=-0.5), reads=[("st2", b)], writes=[("st2", b)])
            yield None
            op('dve', lambda e: e.scalar_tensor_tensor(ko2[:, 0:256], pb[bk][:, 256:512], st2[:, b, 3:4], kvng[:], ALU.mult, ALU.mult),
               reads=[(pbn[bk], 1), ("st2", b), "kvng"], writes=[ko2n])
            yield None
            op('act', lambda e: e.copy(ko2[:, 256:288], pb[bk2][:, 0:32]), reads=[(pbn[bk2], 0)], writes=[ko2n])
            yield None
            dma('sp', O["nckv"][l, b * 128:(b + 1) * 128, :], ko2[:, 0:256], reads=[ko2n])
            yield None
            dma('sp', O["nkpe"][l, b * 128:(b + 1) * 128, :], ko2[:, 256:288], reads=[ko2n])
            yield None
            k16 = kvn16[b % 2]
            k16n = "kvn16_%d" % (b % 2)
            op('dve', lambda e: e.tensor_copy(k16[:, 0:256], ko2[:, 0:256]), reads=[ko2n], writes=[k16n])
            yield None
            bk3 = 2 + b % 2
            pview = pb[bk3][:].bitcast(BF16)
            mm([(lambda e, c=c, pview=pview: e.transpose(pview[:, c * 128:(c + 1) * 128], k16[:, c * 128:(c + 1) * 128], ident[:])) for c in range(2)],
               reads=[k16n, "ident"], writes=[pbn[bk3]])
            yield None
            op('dve', lambda e, pview=pview: e.tensor_copy(ckvT[:, :, b * 128:(b + 1) * 128], pview[:, 0:256].rearrange("p (c n) -> p c n", c=2)),
               reads=[pbn[bk3]], writes=[("ckvT", b)])
            yield None
        run_pairs(gen_bckv, NB)
        ctx16b = ctx16
        dma('pool', ctx16b[:], I["cckv"][l].rearrange("(j p) n -> p j n", p=128), writes=["ctx16"])
        for j in range(4):
            pview = pb[6 + j % 2][:].bitcast(BF16)
            mm([(lambda e, c=c, pview=pview: e.transpose(pview[:, c * 128:(c + 1) * 128], ctx16b[:, j, c * 128:(c + 1) * 128], ident[:])) for c in range(2)],
               reads=["ctx16", "ident"], writes=[pbn[6 + j % 2]])
            op('dve', lambda e, pview=pview: e.tensor_copy(ckvT[:, :, T + j * 128:T + (j + 1) * 128], pview[:, 0:256].rearrange("p (c n) -> p c n", c=2)),
               reads=[pbn[6 + j % 2]], writes=[("ckvT", NB + j)])
        dma('pool', ctxp[:, :, 64:96], I["ckpe"][l].rearrange("(j p) n -> p j n", p=128), writes=["ctxp"])
        pview = pb[6][:].bitcast(BF16)
        mm([(lambda e, j=j, pview=pview: e.transpose(pview[0:96, j * 128:(j + 1) * 128], ctxp[:, j, :], ident[:])) for j in range(4)],
           reads=["ctxp", "ident"], writes=[pbn[6]])
        op('dve', lambda e, pview=pview: e.tensor_copy(kpeT[64:96, T:KT], pview[64:96, 0:512]), reads=[pbn[6]], writes=[("kpeT", 4)])

        dma('sp', qng[:], I["mla_q_norm"][l].partition_broadcast(128), writes=["qng"])
        wcq, wcqn = load_w(Win, 8, [(C_BCQ, 384)])
        def gen_bcq(b):
            bk = 6 + b % 2
            mm([(lambda e, c=c: e.matmul(pb[bk][:, 0:384], hT[:, c, b * 128:(b + 1) * 128], wcq[:, c, 0:384], start=(c == 0), stop=(c == 7))) for c in range(8)],
               reads=[wcqn] + HR, writes=[pbn[bk]])
            yield None
            op('act', lambda e: e.activation(junk[:, 0:384], pb[bk][:, 0:384], AF.Square, accum_out=st2[:, b, 0:1]), reads=[pbn[bk]], writes=["junk", ("st2", b)])
            yield None
            op('dve', lambda e: e.tensor_scalar(st2[:, b, 1:2], st2[:, b, 0:1], 1.0 / 384, 1e-6, ALU.mult, ALU.add), reads=[("st2", b)], writes=[("st2", b)])
            yield None
            op('act', lambda e: e.activation(st2[:, b, 2:3], st2[:, b, 1:2], AF.Ln), reads=[("st2", b)], writes=[("st2", b)])
            yield None
            op('act', lambda e: e.activation(st2[:, b, 3:4], st2[:, b, 2:3], AF.Exp, scale=-0.5), reads=[("st2", b)], writes=[("st2", b)])
            yield None
            k16 = kvn16[b % 2]
            k16n = "kvn16_%d" % (b % 2)
            op('dve', lambda e: e.scalar_tensor_tensor(k16[:, 0:384], pb[bk][:, 0:384], st2[:, b, 3:4], qng[:], ALU.mult, ALU.mult),
               reads=[pbn[bk], ("st2", b), "qng"], writes=[k16n])
            yield None
            bk3 = 2 + b % 2
            pview = pb[bk3][:].bitcast(BF16)
            mm([(lambda e, c=c, pview=pview: e.transpose(pview[:, c * 128:(c + 1) * 128], k16[:, c * 128:(c + 1) * 128], ident[:])) for c in range(3)],
               reads=[k16n, "ident"], writes=[pbn[bk3]])
            yield None
            op('dve', lambda e, pview=pview: e.tensor_copy(cqnT[:, :, b * 128:(b + 1) * 128], pview[:, 0:384].rearrange("p (c n) -> p c n", c=3)),
               reads=[pbn[bk3]], writes=[("cqnT", b)])
            yield None
        run_pairs(gen_bcq, NB)
        wpe, wpen = load_w(Win, 8, [(C_BKPE - 64, 96)])
        wpep, wpepn = load_w(Winp, 8, [(640 - 64, 96)])
        for sbi in range(NSB):
            r = cnt['rot'] % 2
            cnt['rot'] += 1
            lin(0, wpe, wpen, 0, 96, sbi)
            lin(1, wpep, wpepn, 0, 96, sbi)
            op('dve', lambda e: e.tensor_tensor(t1[r][64:96, :], pb[0][64:96, :], ropeC[64:96, sbi * 512:(sbi + 1) * 512], ALU.mult), reads=[pbn[0], "ropeC"], writes=["t1_%d" % r])
            op('dve', lambda e: e.tensor_tensor(t2[r][64:96, :], pb[1][64:96, :], ropeS[64:96, sbi * 512:(sbi + 1) * 512], ALU.mult), reads=[pbn[1], "ropeS"], writes=["t2_%d" % r])
            op('pool', lambda e: e.tensor_tensor(kpeT[64:96, sbi * 512:(sbi + 1) * 512], t1[r][64:96, :], t2[r][64:96, :], ALU.add), reads=["t1_%d" % r, "t2_%d" % r], writes=[("kpeT", sbi)])
        wuq, wuqn = load_w(I["mla_w_uq"][l], 3, [(0, 768)])
        wuqp, wuqpn = load_w(I["mla_w_uqp"][l], 3, [(0, 768)])
        dma('pool', wukv[:], I["mla_w_ukv"][l].rearrange("(k p) n -> p k n", p=128), writes=["wukv"])
        kb.mark('L%d B-attn' % l)
        CQR = [("cqnT", b) for b in range(NB)]
        CKR = [("ckvT", b) for b in range(NKB)]
        for h in range(8):
            qt, qn = qTh[h % 2], "qTh0"
            kt, ktn = kTb[0], "kTb0"
            vt, vtn = Vb[0], "Vb0"
            for s5 in range(5):
                bk = 0 + s5 % 2
                mm([(lambda e, c=c: e.matmul(pb[bk][0:64, :], wukv[:, c, h * 128:h * 128 + 64], ckvT[:, c, s5 * 512:(s5 + 1) * 512], start=(c == 0), stop=(c == 1))) for c in range(2)],
                   reads=["wukv"] + CKR, writes=[pbn[bk]])
                op('act', lambda e: e.copy(kt[0:64, s5 * 512:(s5 + 1) * 512], pb[bk][0:64, :]), reads=[pbn[bk]], writes=[(ktn, s5)])
                op('pool', lambda e: e.tensor_copy(kt[64:96, s5 * 512:(s5 + 1) * 512], kpeT[64:96, s5 * 512:(s5 + 1) * 512]), reads=[("kpeT", s5)], writes=[(ktn, s5, 'pe')])
            for gi_, k0 in enumerate(range(0, NKB, 8)):
                nb_ = min(8, NKB - k0)
                bk = 6 + gi_ % 2
                fns = []
                for j_ in range(nb_):
                    kblk = k0 + j_
                    for c in range(2):
                        fns.append(lambda e, c=c, j_=j_, kblk=kblk: e.matmul(pb[bk][:, j_ * 64:(j_ + 1) * 64], ckvT[:, c, kblk * 128:(kblk + 1) * 128],
                                                                            wukv[:, c, h * 128 + 64:h * 128 + 128], start=(c == 0), stop=(c == 1)))
                mm(fns, reads=["wukv"] + CKR, writes=[pbn[bk]])
                op('dve', lambda e: e.tensor_copy(vt[:, k0:k0 + nb_, 0:64], pb[bk][:, 0:nb_ * 64].rearrange("p (j d) -> p j d", j=nb_)),
                   reads=[pbn[bk]], writes=[vtn])
            def gen_qb(sbi, h=h, qt=qt, qn=qn):
                r = sbi % 2
                ba, bb_ = 2 * r, 2 * r + 1
                rf = lambda c: cqnT[:, c, sbi * 512:(sbi + 1) * 512]
                lin(ba, wuq, wuqn, h * 96, 96, sbi, kch=3, rhs_fn=rf, extra_reads=CQR)
                yield None
                lin(bb_, wuqp, wuqpn, h * 96, 96, sbi, kch=3, rhs_fn=rf, extra_reads=CQR)
                yield None
                op('act', lambda e: e.copy(qt[0:64, sbi * 512:(sbi + 1) * 512], pb[ba][0:64, :]), reads=[pbn[ba]], writes=[(qn, sbi)])
                yield None
                op('dve', lambda e: e.tensor_tensor(t1[r][64:96, :], pb[ba][64:96, :], ropeC[64:96, sbi * 512:(sbi + 1) * 512], ALU.mult), reads=[pbn[ba], "ropeC"], writes=["t1_%d" % r])
                yield None
                op('dve', lambda e: e.tensor_tensor(t2[r][64:96, :], pb[bb_][64:96, :], ropeS[64:96, sbi * 512:(sbi + 1) * 512], ALU.mult), reads=[pbn[bb_], "ropeS"], writes=["t2_%d" % r])
                yield None
                op('pool', lambda e: e.tensor_tensor(qt[64:96, sbi * 512:(sbi + 1) * 512], t1[r][64:96, :], t2[r][64:96, :], ALU.add), reads=["t1_%d" % r, "t2_%d" % r], writes=[(qn, sbi, 'pe')])
                yield None
            run_pairs(gen_qb, NSB)
            groups = []
            MS = 96.0 ** -0.5
            for sbi in range(NSB):
                klist = [(j, None, False) for j in range(NKB)]

                def kfn(kblk, kt=kt, ktn=ktn):
                    return kt[0:104, kblk * 128:(kblk + 1) * 128], ktn

                def vfn(kblk, vt=vt, vtn=vtn):
                    return vt[:, kblk, :], vtn

                def fin(ob, h=h, sbi=sbi):
                    r = cnt['rot'] % 2
                    cnt['rot'] += 1
                    op('dve', lambda e: e.reciprocal(rr[r][64:128, :], pb[ob][64:128, :]), reads=[pbn[ob]], writes=["rr0"])
                    op('pool', lambda e: e.tensor_copy(r3[r][0:64, :], rr[r][64:128, :]), reads=["rr0"], writes=["r3_0"])
                    po = (h % 2) * 64
                    op('dve', lambda e: e.tensor_tensor(ozT[po:po + 64, h // 2, sbi * 512:(sbi + 1) * 512], pb[ob][0:64, :], r3[r][0:64, :], ALU.mult),
                       reads=[pbn[ob], "r3_0"], writes=[("ozT", h // 2, sbi)])

                groups.append(dict(qtile=qt, qname=qn, sbi=sbi, kfn=kfn, klist=klist, vfn=vfn, K=104, scale=MS,
                                   bias_fn=(lambda isctx: -MS * BIGM), fin=fin))
            attend_stream(groups)
        kb.mark('L%d B-merge' % l)
        zmul_and_merge(C_ZB, I["w_branch_b"][l], C_G + 1024, False)
        tap_big("ozB%d" % l, ozT, 4, ["ozT"])
        tap_big("mergeB%d" % l, mergeT, 8, ["mergeT"])
        if stop == "B":
            return

        kb.mark('L%d C' % l)
        kb.barrier()
        _gdn(kb, I, O, l, dict(arena=arena, pb=pb, pbn=pbn, hT=hT, ozT=ozT, HR=HR, load_w=load_w, lin=lin, ident=ident, ident32=ident32,
                               flags=flags, ones16=ones16, wukv=wukv, run_pairs=run_pairs, xn=xn, xblk=xblk, t1=t1, t2=t2, sig=sig, junk=junk, Win=Win, TAP=TAP, stat=stat, st2=st2, kvn16=kvn16, rr=rr))
        tap_big("ozC%d" % l, ozT, 4, ["ozT"])
        kb.mark('L%d C-merge' % l)
        wo = []

        def _prefetch_wo():
            for ch in range(2):
                wo.append(load_w(I["w_out"][l], 8, [(ch * 512, 512)]))
        zmul_and_merge(C_ZC, I["w_branch_c"][l], C_G + 2048, False, after_loads=_prefetch_wo)
        kb.barrier()
        tap_big("mergeC%d" % l, mergeT, 8, ["mergeT"])
        if stop == "C":
            return

        kb.mark('L%d out' % l)
        MR = [("mergeT", c, s) for c in range(8) for s in range(NSB)]
        if l == 1:
            dma('sp', gbc[:].rearrange("p a b -> p (a b)"), I["final_norm_g"].partition_broadcast(128), writes=["gbc"])
        def gen_out(b):
            xb, xbn = xblk[b % 2], "xblk%d" % (b % 2)
            dma('sp', xb[:], xsrc[b * 128:(b + 1) * 128, :], reads=(["xs"] if l == 1 else []), writes=[xbn])
            yield None
            for ch in range(2):
                bk = 4 + 2 * (b % 2) + ch
                wv, wn = wo[ch]
                mm([(lambda e, c=c, wv=wv: e.matmul(pb[bk][:], mergeT[:, c, b * 128:(b + 1) * 128], wv[:, c, :], start=(c == 0), stop=(c == 7))) for c in range(8)],
                   reads=[wn] + MR, writes=[pbn[bk]])
                yield None
                tt_, ttn_ = ((t1[b % 2], "t1_%d" % (b % 2)) if ch == 0 else (t2[b % 2], "t2_%d" % (b % 2)))
                op('dve', lambda e: e.tensor_tensor(tt_[:], pb[bk][:], gateb[:, ch * 512:(ch + 1) * 512], ALU.mult), reads=[pbn[bk], ("gateb", ch)], writes=[ttn_])
                yield None
                op('pool', lambda e: e.tensor_tensor(xb[:, ch * 512:(ch + 1) * 512], xb[:, ch * 512:(ch + 1) * 512], tt_[:], ALU.add), reads=[ttn_, xbn], writes=[xbn])
                yield None
            if l == 0:
                dma('sp', xs[b * 128:(b + 1) * 128, :], xb[:], reads=[xbn], writes=["xs"])
                yield None
            else:
                op('act', lambda e: e.activation(junk[:], xb[:], AF.Square, accum_out=stat[:, b, 0:1]), reads=[xbn], writes=["junk", ("stat", b)])
                yield None
                op('dve', lambda e: e.tensor_scalar(stat[:, b, 1:2], stat[:, b, 0:1], 1.0 / D, 1e-6, ALU.mult, ALU.add), reads=[("stat", b)], writes=[("stat", b)])
                yield None
                op('act', lambda e: e.activation(stat[:, b, 2:3], stat[:, b, 1:2], AF.Ln), reads=[("stat", b)], writes=[("stat", b)])
                yield None
                op('act', lambda e: e.activation(stat[:, b, 3:4], stat[:, b, 2:3], AF.Exp, scale=-0.5), reads=[("stat", b)], writes=[("stat", b)])
                yield None
                op('dve', lambda e: e.scalar_tensor_tensor(xb[:], xb[:], stat[:, b, 3:4], gbc[:].rearrange("p a b -> p (a b)"), ALU.mult, ALU.mult), reads=[xbn, ("stat", b), "gbc"], writes=[xbn])
                yield None
                dma('sp', O["y"][b * 128:(b + 1) * 128, :], xb[:], reads=[xbn])
                yield None
        run_pairs(gen_out, NB)

def _gdn(kb, I, O, l, env):
    op, dma, mm = kb.op, kb.dma, kb.mmgroup
    arena, pb, pbn, hT, ozT, HR = env["arena"], env["pb"], env["pbn"], env["hT"], env["ozT"], env["HR"]
    load_w, lin, ident, ident32, flags = env["load_w"], env["lin"], env["ident"], env["ident32"], env["flags"]
    xblk, t1, t2, sig, junk, Win, TAP = env["xblk"], env["t1"], env["t2"], env["sig"], env["junk"], env["Win"], env["TAP"]
    st2, kvn16 = env["st2"], env["kvn16"]
    pos = [0]

    def carve(n_units, dt, shape_str=None, **kw):
        a = arena[:, pos[0]:pos[0] + n_units]
        pos[0] += n_units
        if dt == F32:
            a = a.bitcast(F32)
        if shape_str:
            a = a.rearrange(shape_str, **kw)
        return a
    qkvh = carve(3 * T, BF16, "p (c n) -> p c n", c=3)
    oacc = carve(NB * 128, BF16, "p (b d) -> p b d", b=NB)
    ab = carve(2 * NB * 16, F32, "p (b d) -> p b d", b=NB)
    names = ["g", "bt", "nbt", "gam", "ngam", "egam", "begam", "edel", "dec1", "dec2", "gt1", "gt2"]
    stt_ = {n: carve(2 * NB * 8, F32, "p (b d) -> p b d", b=NB) for n in names}
    S = carve(2 * 2 * 128, F32, "p (s d) -> p s d", s=2)
    Sbf = carve(2 * 128, BF16, "p (s d) -> p s d", s=2)
    bt_names = ["Kbg", "Kd0", "Kd1", "Vb", "Qg", "M0", "MTa", "MTb", "Ma", "Mb", "PTb", "Wt", "Wn", "U", "CT", "attnT", "Mc"]
    B = {n: carve(256, BF16, "p (s d) -> p s d", s=2) for n in bt_names}
    X = carve(512, BF16, "p (s h d) -> p s h d", s=2, h=2)
    wk_ = env["wukv"][:].rearrange("p a b -> p (a b)")
    B1 = {}
    for q_, n in enumerate(bt_names):
        if q_ < 6:
            B1[n] = wk_[:, 512 + q_ * 256:512 + (q_ + 1) * 256].rearrange("p (s d) -> p s d", s=2)
        else:
            B1[n] = carve(256, BF16, "p (s d) -> p s d", s=2)
    X1 = wk_[:, 0:512].rearrange("p (s h d) -> p s h d", s=2, h=2)
    cw = carve(2 * 36, F32, "p (c j) -> p c j", c=12)
    nw = carve(2 * 36, F32, "p (c j) -> p c j", c=12)
    pw = carve(2 * 36, F32, "p (c j) -> p c j", c=12)
    gnb = carve(2 * 128, F32)
    dtb = carve(2 * 8, F32)
    negA = carve(2 * 8, F32)
    onorm = carve(128, BF16)
    ident2 = carve(256, BF16, "p (s d) -> p s d", s=2)
    assert pos[0] <= 21952, pos[0]
    tri = t1[0][:].rearrange("p (k n) -> p k n", k=4)
    gmc = t1[1][:].rearrange("p (k s n) -> p k s n", k=2, s=2)
    Grhs = t2[0][:, 0:256].rearrange("p (s n) -> p s n", s=2)
    ones32 = t2[0][:, 256:384]
    Em = t2[1][:].rearrange("p (k s n) -> p k s n", k=2, s=2)
    accs = [xblk[0], xblk[1]]
    SETS = [dict(B=B, X=X, Grhs=Grhs, Em=Em, grn="Grhs0", emn="Em0", sx="", banks=(0, 1, 2, 3)),
            dict(B=B1, X=X1, Grhs=sig[0][:, 0:256].rearrange("p (s n) -> p s n", s=2),
                 Em=sig[1][:].rearrange("p (k s n) -> p k s n", k=2, s=2), grn="sig0", emn="sig1", sx="_1", banks=(4, 5, 6, 7))]

    for k_, nm in enumerate(["triF", "triB", "sel1", "sel2"]):
        dma('sp', tri[:, k_, :], I[nm], writes=["t1_0"])
    for k_, nm in enumerate(["gm1", "gm2"]):
        for s_ in range(2):
            dma('sp', gmc[:, k_, s_, :], I[nm][:, 4 * s_, :], writes=["t1_1"])
    op('dve', lambda e: e.memset(ones32, 1.0), writes=["ones32"])
    op('dve', lambda e: e.memset(B1["Kd0"][:], 0.0), writes=["Kd0_1"])
    op('dve', lambda e: e.memset(B1["Kd1"][:], 0.0), writes=["Kd1_1"])
    op('dve', lambda e: e.memset(B["Kd0"][:], 0.0), writes=["Kd0"])
    for s_ in range(2):
        op('dve', lambda e: e.tensor_copy(ident2[:, s_, :], ident[:]), reads=["ident"], writes=["ident2"])
    op('dve', lambda e: e.memset(B["Kd1"][:], 0.0), writes=["Kd1"])
    for j_ in range(3):
        dma('sp', cw[:, :, j_], I["gdn_conv"][l][j_].rearrange("(c p) -> p c", p=128), writes=["cw"], allow_slow_non_contiguous=True)
    dma('sp', gnb, I["gdn_norm"][l].partition_broadcast(128), writes=["gnb"])
    dma('sp', dtb, I["gdn_dt_bias"][l].partition_broadcast(128), writes=["dtb"])
    dma('sp', negA, I["gdn_a_log"][l].partition_broadcast(128), writes=["negA"])
    op('act', lambda e: e.activation(negA, negA, AF.Exp), reads=["negA"], writes=["negA"])
    op('dve', lambda e: e.tensor_scalar(negA, negA, -1.0, None, ALU.mult), reads=["negA"], writes=["negA"])
    op('dve', lambda e: e.tensor_scalar(nw, cw, flags[:, 2:3], None, ALU.mult), reads=["cw", "flags"], writes=["nw"])
    op('dve', lambda e: e.tensor_tensor(pw, cw, nw, ALU.add), reads=["cw", "nw"], writes=["pw"])

    wab, wabn = load_w(Win, 8, [(C_CA, 16)])
    for b in range(NB):
        mm([(lambda e, c=c: e.matmul(pb[0][:, b * 16:(b + 1) * 16], hT[:, c, b * 128:(b + 1) * 128], wab[:, c, :], start=(c == 0), stop=(c == 7))) for c in range(8)],
           reads=[wabn] + HR, writes=[pbn[0]])
    op('dve', lambda e: e.tensor_copy(ab, pb[0][:, 0:256].rearrange("p (b d) -> p b d", b=NB)), reads=[pbn[0]], writes=["ab"])
    g, bt, nbt, gam, ngam, egam, begam, edel, dec1, dec2, gt1, gt2 = [stt_[n] for n in names]
    bc8 = lambda a: a.unsqueeze(1).to_broadcast([128, NB, 8])
    op('dve', lambda e: e.tensor_tensor(g, ab[:, :, 0:8], bc8(dtb), ALU.add), reads=["ab", "dtb"], writes=["g"])
    op('act', lambda e: e.activation(g, g, AF.Exp), reads=["g"], writes=["g"])
    op('act', lambda e: e.activation(g, g, AF.Ln, bias=1.0), reads=["g"], writes=["g"])
    op('dve', lambda e: e.tensor_tensor(g, g, bc8(negA), ALU.mult), reads=["g", "negA"], writes=["g"])
    op('act', lambda e: e.activation(bt, ab[:, :, 8:16], AF.Sigmoid), reads=["ab"], writes=["bt"])
    op('dve', lambda e: e.tensor_scalar(nbt, bt, -1.0, None, ALU.mult), reads=["bt"], writes=["nbt"])
    for b in range(NB):
        mm([lambda e: e.matmul(pb[1][:, b * 8:b * 8 + 4], tri[:, 0, :], g[:, b, 0:4], start=True, stop=True),
            lambda e: e.matmul(pb[1][:, b * 8 + 4:b * 8 + 8], tri[:, 1, :], g[:, b, 4:8], start=True, stop=True)], reads=["t1_0", "g"], writes=[pbn[1]])
        mm([lambda e: e.matmul(pb[2][:, b * 8:b * 8 + 8], tri[:, 2, :], g[:, b, :], start=True, stop=True)], reads=["t1_0", "g"], writes=[pbn[2]])
        mm([lambda e: e.matmul(pb[3][:, b * 8:b * 8 + 8], tri[:, 3, :], g[:, b, :], start=True, stop=True)], reads=["t1_0", "g"], writes=[pbn[3]])
    v3 = lambda bank: pb[bank][:, 0:128].rearrange("p (b d) -> p b d", b=NB)
    op('dve', lambda e: e.tensor_copy(gam, v3(1)), reads=[pbn[1]], writes=["gam"])
    op('dve', lambda e: e.tensor_copy(gt1, v3(2)), reads=[pbn[2]], writes=["gt1"])
    op('dve', lambda e: e.tensor_copy(gt2, v3(3)), reads=[pbn[3]], writes=["gt2"])
    op('dve', lambda e: e.tensor_scalar(ngam, gam, -1.0, None, ALU.mult), reads=["gam"], writes=["ngam"])
    op('act', lambda e: e.activation(egam, gam, AF.Exp), reads=["gam"], writes=["egam"])
    op('dve', lambda e: e.tensor_tensor(begam, egam, bt, ALU.mult), reads=["egam", "bt"], writes=["begam"])
    op('dve', lambda e: e.tensor_tensor(edel[0:64], gt1[0:64], gam[0:64], ALU.subtract), reads=["gt1", "gam"], writes=["edel"])
    op('dve', lambda e: e.tensor_tensor(edel[64:128], gt2[64:128], gam[64:128], ALU.subtract), reads=["gt2", "gam"], writes=["edel"])
    op('act', lambda e: e.activation(edel, edel, AF.Exp), reads=["edel"], writes=["edel"])
    op('act', lambda e: e.activation(dec1, gt1, AF.Exp), reads=["gt1"], writes=["dec1"])
    op('act', lambda e: e.activation(dec2, gt2, AF.Exp), reads=["gt2"], writes=["dec2"])
    if "gstats%d" % l in TAP:
        for k_, n in enumerate(["g", "bt", "gam", "edel", "dec1", "dec2"]):
            dma('sp', TAP["gstats%d" % l][k_], stt_[n], reads=[n])

    for h in range(4):
        kb.mark('L%d C h%d conv' % (l, h))
        if h == 0:
            nxt_wc = load_w(Win, 8, [(C_CQKV + h * 128, 128), (C_CQKV + 512 + h * 128, 128), (C_CQKV + 1024 + h * 128, 128)])
        wc, wcn = nxt_wc
        for ci in range(3):
            ch = ci * 4 + h
            for sbi in range(NSB):
                lin(sbi, wc, wcn, ci * 128, 128, sbi)
            for sbi in range(NSB):
                acc = accs[sbi // 2][:, (sbi % 2) * 512:(sbi % 2 + 1) * 512]
                an = "xblk%d" % (sbi // 2)
                P_ = pb[sbi]
                rd = [pbn[sbi], "cw", "nw", "pw"]
                op('dve', lambda e: e.tensor_scalar(acc, P_[:], cw[:, ch, 1:2], None, ALU.mult), reads=rd, writes=[an])
                op('dve', lambda e: e.scalar_tensor_tensor(acc[:, 1:512], P_[:, 0:511], cw[:, ch, 0:1], acc[:, 1:512], ALU.mult, ALU.add), reads=rd + [an], writes=[an])
                op('dve', lambda e: e.scalar_tensor_tensor(acc[:, 0:511], P_[:, 1:512], cw[:, ch, 2:3], acc[:, 0:511], ALU.mult, ALU.add), reads=rd + [an], writes=[an])
                op('dve', lambda e: e.scalar_tensor_tensor(acc[:, 256:257], P_[:, 255:256], nw[:, ch, 0:1], acc[:, 256:257], ALU.mult, ALU.add), reads=rd + [an], writes=[an])
                op('dve', lambda e: e.scalar_tensor_tensor(acc[:, 255:256], P_[:, 256:257], nw[:, ch, 2:3], acc[:, 255:256], ALU.mult, ALU.add), reads=rd + [an], writes=[an])
                if sbi > 0:
                    op('dve', lambda e: e.scalar_tensor_tensor(acc[:, 0:1], pb[sbi - 1][:, 511:512], pw[:, ch, 0:1], acc[:, 0:1], ALU.mult, ALU.add),
                       reads=rd + [an, pbn[sbi - 1]], writes=[an])
                if sbi < NSB - 1:
                    op('dve', lambda e: e.scalar_tensor_tensor(acc[:, 511:512], pb[sbi + 1][:, 0:1], pw[:, ch, 2:3], acc[:, 511:512], ALU.mult, ALU.add),
                       reads=rd + [an, pbn[sbi + 1]], writes=[an])
            def gen_l2(sbi, ci=ci):
                acc = accs[sbi // 2][:, (sbi % 2) * 512:(sbi % 2 + 1) * 512]
                an = ("xblk%d" % (sbi // 2), sbi % 2)
                dst = qkvh[:, ci, sbi * 512:(sbi + 1) * 512]
                p_ = sbi % 2
                sqb, sqn = ((sig[0][:].bitcast(BF16)[:, 0:512], "sig0") if p_ == 0 else (env["rr"][0][:].bitcast(BF16)[:, 0:512], "rr0"))
                rvb, rvn = ((sig[1][:], "sig1") if p_ == 0 else (t2[1][:], "Em0"))
                bk_ = 4 + p_
                if ci == 2:
                    op('act', lambda e: e.activation(dst, acc, AF.Silu), reads=[an], writes=[("qkvh", ci, sbi)])
                    yield None
                else:
                    op('act', lambda e: e.activation(acc, acc, AF.Silu), reads=[an], writes=[an])
                    yield None
                    op('pool', lambda e: e.tensor_tensor(sqb, acc, acc, ALU.mult), reads=[an], writes=[sqn])
                    yield None
                    mm([lambda e: e.matmul(pb[bk_][:], env["ones16"][:], sqb, start=True, stop=True)], reads=[sqn, "ones16"], writes=[pbn[bk_]])
                    yield None
                    op('act', lambda e: e.activation(rvb, pb[bk_][:], AF.Ln, bias=1e-6), reads=[pbn[bk_]], writes=[rvn])
                    yield None
                    op('act', lambda e: e.activation(rvb, rvb, AF.Exp, scale=-0.5, bias=(-0.5 * float(np.log(128.0)) if ci == 0 else 0.0)), reads=[rvn], writes=[rvn])
                    yield None
                    op('dve', lambda e: e.tensor_tensor(dst, acc, rvb, ALU.mult), reads=[an, rvn], writes=[("qkvh", ci, sbi)])
                    yield None
            env["run_pairs"](gen_l2, NSB)
        if h == 0 and "qkvh%d" % l in TAP:
            for ci in range(3):
                for hf in range(2):
                    op('dve', lambda e: e.tensor_copy(xblk[0][:], qkvh[:, ci, hf * 1024:(hf + 1) * 1024]), reads=["qkvh"], writes=["xblk0"])
                    dma('sp', TAP["qkvh%d" % l][:, ci, hf * 1024:(hf + 1) * 1024], xblk[0][:], reads=["xblk0"])
        qT_, kT_, vT_ = qkvh[:, 0, :], qkvh[:, 1, :], qkvh[:, 2, :]
        QK = ["qkvh"]
        kb.mark('L%d C h%d scan' % (l, h))
        if h < 3:
            nxt_wc = load_w(Win, 8, [(C_CQKV + (h + 1) * 128, 128), (C_CQKV + 512 + (h + 1) * 128, 128), (C_CQKV + 1024 + (h + 1) * 128, 128)])
        op('pool', lambda e: e.memset(oacc, 0.0), writes=["oacc"])
        def gen_iter(i, SET):
            B, X, Grhs, Em = SET['B'], SET['X'], SET['Grhs'], SET['Em']
            grn, emn, sx = SET['grn'], SET['emn'], SET['sx']
            b0, b1, b2, b3 = SET['banks']
            N = lambda n_: n_ + sx
            slots = [(i, h, 0), (NB - 1 - i, 4 + h, 1)]
            for s_, (blk, dh, d_) in enumerate(slots):
                if i == 0:
                    dma('sp', S[:, s_, :], I["st"][l, dh], writes=[("S", s_)])
                    op('act', lambda e: e.copy(Sbf[:, s_, :], S[:, s_, :]), reads=[("S", s_)], writes=[("Sbf", s_)])
                elif i % 2 == 0:
                    op('dve', lambda e: e.tensor_scalar(S[:, s_, :], S[:, s_, :], flags[:, 0:1], None, ALU.mult), reads=[("S", s_), "flags"], writes=[("S", s_)])
                    op('act', lambda e: e.copy(Sbf[:, s_, :], S[:, s_, :]), reads=[("S", s_)], writes=[("Sbf", s_)])
            yield None
            pv = pb[b0][:].bitcast(BF16)
            fns = []
            for s_, (blk, dh, d_) in enumerate(slots):
                for k_, src in enumerate([kT_, vT_, qT_]):
                    fns.append(lambda e, s_=s_, k_=k_, src=src, blk=blk: e.transpose(pv[:, (k_ * 2 + s_) * 128:(k_ * 2 + s_ + 1) * 128], src[:, blk * 128:(blk + 1) * 128], ident[:]))
            mm(fns, reads=QK + ["ident"], writes=[pbn[b0]])
            yield None
            for s_, (blk, dh, d_) in enumerate(slots):
                for nm, k_, sc in [("Kbg", 0, begam), ("Vb", 1, bt), ("Qg", 2, egam)]:
                    op('act', lambda e: e.activation(B[nm][:, s_, :], pv[:, (k_ * 2 + s_) * 128:(k_ * 2 + s_ + 1) * 128], AF.Copy, scale=sc[:, blk, dh:dh + 1]),
                       reads=[pbn[b0], "begam", "edel", "bt", "egam"], writes=[(N(nm), s_)])
                    yield None
                for hf_ in range(2):
                    R_ = slice(hf_ * 64, (hf_ + 1) * 64)
                    op('act', lambda e: e.activation(B["Kd%d" % hf_][R_, s_, :], pv[R_, s_ * 128:(s_ + 1) * 128], AF.Copy, scale=edel[R_, blk, dh:dh + 1]),
                       reads=[pbn[b0], "edel"], writes=[(N("Kd%d" % hf_), s_)])
                    yield None
            fns = []
            for s_, (blk, dh, d_) in enumerate(slots):
                ks = kT_[:, blk * 128:(blk + 1) * 128]
                qs = qT_[:, blk * 128:(blk + 1) * 128]
                fns.append(lambda e, s_=s_, ks=ks: e.matmul(pb[b1][:, s_ * 128:(s_ + 1) * 128], ks, ks, start=True, stop=True))
                fns.append(lambda e, s_=s_, ks=ks, qs=qs: e.matmul(pb[b1][:, (2 + s_) * 128:(3 + s_) * 128], ks, qs, start=True, stop=True))
            mm(fns, reads=QK, writes=[pbn[b1]])
            yield None
            for s_, (blk, dh, d_) in enumerate(slots):
                op('dve', lambda e: e.tensor_scalar(Grhs[:, s_, :], ident32[:], ngam[:, blk, dh:dh + 1], None, ALU.mult), reads=["ident32", "ngam"], writes=[(grn, s_)])
                yield None
            mm([lambda e: e.matmul(pb[b2][:, 0:256], ones32, Grhs.rearrange("p s n -> p (s n)"), start=True, stop=True)], reads=[grn, "ones32"], writes=[pbn[b2]])
            yield None
            p2 = pb[b2][:, 0:256].rearrange("p (s n) -> p s n", s=2)
            op('dve', lambda e: e.tensor_tensor(Em[:, 0], p2, gmc[:, 0], ALU.add), reads=[pbn[b2], "t1_1"], writes=[(emn, 0)])
            yield None
            op('dve', lambda e: e.scalar_tensor_tensor(Em[:, 1], p2, -1.0, gmc[:, 1], ALU.mult, ALU.add), reads=[pbn[b2], "t1_1"], writes=[(emn, 1)])
            yield None
            for s_, (blk, dh, d_) in enumerate(slots):
                op('act', lambda e: e.activation(Em[:, 0, s_, :], Em[:, 0, s_, :], AF.Exp, bias=gam[:, blk, dh:dh + 1]), reads=[(emn, 0), "gam"], writes=[(emn, 0)])
                yield None
                op('act', lambda e: e.activation(Em[:, 1, s_, :], Em[:, 1, s_, :], AF.Exp, bias=ngam[:, blk, dh:dh + 1]), reads=[(emn, 1), "ngam"], writes=[(emn, 1)])
                yield None
                op('dve', lambda e: e.scalar_tensor_tensor(B["M0"][:, s_, :], pb[b1][:, s_ * 128:(s_ + 1) * 128], nbt[:, blk, dh:dh + 1], Em[:, 0, s_, :], ALU.mult, ALU.mult),
                   reads=[pbn[b1], "nbt", (emn, 0)], writes=[(N("M0"), s_)])
                yield None
            op('dve', lambda e: e.tensor_tensor(B["attnT"][:], pb[b1][:, 256:512].rearrange("p (s n) -> p s n", s=2), Em[:, 1], ALU.mult), reads=[pbn[b1], (emn, 1)], writes=[N("attnT")])
            yield None
            M0 = B["M0"]
            mm([lambda e, s_=s_: e.matmul(pb[b1][:, s_ * 128:(s_ + 1) * 128], M0[:, s_, :], ident[:], start=True, stop=True) for s_ in range(2)], reads=[N("M0"), "ident"], writes=[pbn[b1]])
            yield None
            op('act', lambda e: e.copy(B["MTa"][:], pb[b1][:, 0:256].rearrange("p (s n) -> p s n", s=2)), reads=[pbn[b1]], writes=[N("MTa")])
            yield None
            fns = [lambda e: e.matmul(pb[b3][:, 0:256], ident[:], ident2[:].rearrange("p s d -> p (s d)"), start=True, stop=False)]
            for s_ in range(2):
                fns.append(lambda e, s_=s_: e.matmul(pb[b3][:, s_ * 128:(s_ + 1) * 128], M0[:, s_, :], ident[:], start=False, stop=True))
            mm(fns, reads=[N("M0"), "ident", "ident2"], writes=[pbn[b3]])
            yield None
            Mprev, MTprev, Mn, MTn = "M0", "MTa", "Ma", "MTb"
            pend = None

            def pt_update(mname):
                op('act', lambda e: e.copy(B["PTb"][:], pb[b3][:, 0:256].rearrange("p (s n) -> p s n", s=2)), reads=[pbn[b3]], writes=[N("PTb")])
                mm([lambda e, s_=s_: e.matmul(pb[b3][:, s_ * 128:(s_ + 1) * 128], B[mname][:, s_, :], B["PTb"][:, s_, :], start=False, stop=True) for s_ in range(2)],
                   reads=[N(mname), N("PTb")], writes=[pbn[b3]])
            MN3 = ["Ma", "Mb", "Mc"]
            for k_ in range(1, 6):
                Mn = MN3[k_ % 3]
                mm([lambda e, s_=s_: e.matmul(pb[b2][:, s_ * 128:(s_ + 1) * 128], B[MTprev][:, s_, :], B[Mprev][:, s_, :], start=True, stop=True) for s_ in range(2)],
                   reads=[N(MTprev), N(Mprev)], writes=[pbn[b2]])
                yield None
                if k_ < 5:
                    mm([lambda e, s_=s_: e.matmul(pb[b1][:, s_ * 128:(s_ + 1) * 128], B[Mprev][:, s_, :], B[MTprev][:, s_, :], start=True, stop=True) for s_ in range(2)],
                       reads=[N(MTprev), N(Mprev)], writes=[pbn[b1]])
                    yield None
                if pend is not None:
                    pt_update(pend)
                    yield None
                op('act', lambda e: e.copy(B[Mn][:], pb[b2][:, 0:256].rearrange("p (s n) -> p s n", s=2)), reads=[pbn[b2]], writes=[N(Mn)])
                yield None
                if k_ < 5:
                    op('dve', lambda e: e.tensor_copy(B[MTn][:], pb[b1][:, 0:256].rearrange("p (s n) -> p s n", s=2)), reads=[pbn[b1]], writes=[N(MTn)])
                    yield None
                pend = Mn
                Mprev, MTprev, MTn = Mn, MTn, ("MTa" if MTn == "MTb" else "MTb")
            pt_update(pend)
            yield None
            op('act', lambda e: e.copy(B["PTb"][:], pb[b3][:, 0:256].rearrange("p (s n) -> p s n", s=2)), reads=[pbn[b3]], writes=[N("PTb")])
            yield None
            mm([lambda e, s_=s_: e.matmul(pb[b2][:, s_ * 128:(s_ + 1) * 128], M0[:, s_, :], B["PTb"][:, s_, :], start=True, stop=True) for s_ in range(2)],
               reads=[N("M0"), N("PTb")], writes=[pbn[b2]])
            yield None
            mm([lambda e, s_=s_: e.matmul(pb[b1][:, s_ * 128:(s_ + 1) * 128], B["PTb"][:, s_, :], ident[:], start=True, stop=True) for s_ in range(2)],
               reads=[N("PTb"), "ident"], writes=[pbn[b1]])
            yield None
            op('dve', lambda e: e.scalar_tensor_tensor(Em[:, 0], B["PTb"][:], -1.0, pb[b2][:, 0:256].rearrange("p (s n) -> p s n", s=2), ALU.mult, ALU.add),
               reads=[pbn[b2], N("PTb")], writes=[(emn, 0)])
            yield None
            op('act', lambda e: e.copy(B["Mb"][:], pb[b1][:, 0:256].rearrange("p (s n) -> p s n", s=2)), reads=[pbn[b1]], writes=[N("Mb")])
            yield None
            op('dve', lambda e: e.tensor_tensor(B["Ma"][:], Em[:, 0], ident2[:], ALU.add), reads=[(emn, 0), "ident2"], writes=[N("Ma")])
            yield None
            fns = [lambda e: e.matmul(pb[b3][:, 0:256], ident[:], B["PTb"][:].rearrange("p s d -> p (s d)"), start=True, stop=False)]
            for s_ in range(2):
                fns.append(lambda e, s_=s_: e.matmul(pb[b3][:, s_ * 128:(s_ + 1) * 128], B["Mb"][:, s_, :], B["Ma"][:, s_, :], start=False, stop=True))
            mm(fns, reads=[N("Mb"), N("Ma"), N("PTb"), "ident"], writes=[pbn[b3]])
            yield None
            op('act', lambda e: e.copy(B["Wt"][:], pb[b3][:, 0:256].rearrange("p (s n) -> p s n", s=2)), reads=[pbn[b3]], writes=[N("Wt")])
            yield None
            op('pool', lambda e: e.tensor_copy(B["PTb"][:], B["Wt"][:]), reads=[N("Wt")], writes=[N("PTb")])
            yield None
            AT = B["PTb"]
            fns = []
            for s_ in range(2):
                fns.append(lambda e, s_=s_: e.matmul(pb[b0][:, s_ * 128:(s_ + 1) * 128], AT[:, s_, :], B["Kbg"][:, s_, :], start=True, stop=True))
                fns.append(lambda e, s_=s_: e.matmul(pb[b0][:, (2 + s_) * 128:(3 + s_) * 128], AT[:, s_, :], B["Vb"][:, s_, :], start=True, stop=True))
            mm(fns, reads=[N("PTb"), N("Kbg"), N("Vb")], writes=[pbn[b0]])
            yield None
            p6a = pb[b0][:, 0:256].rearrange("p (s n) -> p s n", s=2)
            p6b = pb[b0][:, 256:512].rearrange("p (s n) -> p s n", s=2)
            op('act', lambda e: e.copy(B["Wt"][:], p6a), reads=[pbn[b0]], writes=[N("Wt")])
            yield None
            op('act', lambda e: e.activation(B["Wn"][:], p6a, AF.Copy, scale=-1.0), reads=[pbn[b0]], writes=[N("Wn")])
            yield None
            op('act', lambda e: e.copy(B["U"][:], p6b), reads=[pbn[b0]], writes=[N("U")])
            yield None
            fns = []
            for s_ in range(2):
                for hf in range(2):
                    fns.append(lambda e, s_=s_, hf=hf: e.matmul(pb[b0][:, (s_ * 2 + hf) * 128:(s_ * 2 + hf + 1) * 128], B["Wt"][:, s_, :], B["Kd%d" % hf][:, s_, :], start=True, stop=True))
            mm(fns, reads=[N("Wt"), N("Kd0"), N("Kd1")], writes=[pbn[b0]])
            yield None
            op('act', lambda e: e.activation(X, pb[b0][:].rearrange("p (s h n) -> p s h n", s=2, h=2), AF.Copy, scale=-1.0), reads=[pbn[b0]], writes=[N("X")])
            yield None
            fns = []
            for s_ in range(2):
                fns.append(lambda e, s_=s_: e.matmul(pb[b1][:, s_ * 128:(s_ + 1) * 128], B["Qg"][:, s_, :], ident[:], start=True, stop=False))
                fns.append(lambda e, s_=s_: e.matmul(pb[b1][:, s_ * 128:(s_ + 1) * 128], B["Wn"][:, s_, :], B["attnT"][:, s_, :], start=False, stop=True))
            mm(fns, reads=[N("Qg"), N("Wn"), N("attnT"), "ident"], writes=[pbn[b1]])
            yield None
            op('act', lambda e: e.copy(B["CT"][:], pb[b1][:, 0:256].rearrange("p (s n) -> p s n", s=2)), reads=[pbn[b1]], writes=[N("CT")])
            yield 'SCAN'
            for step in range(2):
                for s_, (blk, dh, d_) in enumerate(slots):
                    hf = step if d_ == 0 else 1 - step
                    R = slice(hf * 64, (hf + 1) * 64)
                    mm([lambda e: e.matmul(pb[b1][:, s_ * 128:(s_ + 1) * 128], B["attnT"][:, s_, :], B["U"][:, s_, :], start=True, stop=False),
                        lambda e: e.matmul(pb[b1][:, s_ * 128:(s_ + 1) * 128], B["CT"][:, s_, :], Sbf[:, s_, :], start=False, stop=True)],
                       reads=[N("attnT"), N("U"), N("CT"), ("Sbf", s_)], writes=[pbn[b1]])
                    yield None
                    op('dve', lambda e: e.tensor_tensor(oacc[R, blk, :], oacc[R, blk, :], pb[b1][R, s_ * 128:(s_ + 1) * 128], ALU.add), reads=[pbn[b1], ("oacc", blk)], writes=[("oacc", blk)])
                    yield None
                    mm([lambda e: e.matmul(pb[b2][:, s_ * 128:(s_ + 1) * 128], B["Kd%d" % hf][:, s_, :], B["U"][:, s_, :], start=True, stop=False),
                        lambda e: e.matmul(pb[b2][:, s_ * 128:(s_ + 1) * 128], X[:, s_, hf, :], Sbf[:, s_, :], start=False, stop=True)],
                       reads=[N("Kd0"), N("Kd1"), N("U"), N("X"), ("Sbf", s_)], writes=[pbn[b2]])
                    yield None
                    dec = dec1 if hf == 0 else dec2
                    op('dve', lambda e: e.scalar_tensor_tensor(S[:, s_, :], S[:, s_, :], dec[:, blk, dh:dh + 1], pb[b2][:, s_ * 128:(s_ + 1) * 128], ALU.mult, ALU.add),
                       reads=[pbn[b2], ("S", s_), "dec1", "dec2"], writes=[("S", s_)])
                    yield None
                    op('pool', lambda e: e.tensor_copy(Sbf[:, s_, :], S[:, s_, :]), reads=[("S", s_)], writes=[("Sbf", s_)])
                    yield None
            for s_, (blk, dh, d_) in enumerate(slots):
                if (d_ == 0 and blk % 2 == 1) or (d_ == 1 and blk % 2 == 0):
                    dma('sp', O["nst"][l, blk // 2, d_, h], S[:, s_, :], reads=[("S", s_)])
            yield None

        for j in range(NB // 2):
            gA = gen_iter(2 * j, SETS[0])
            gB = gen_iter(2 * j + 1, SETS[1])
            dA = dB = False
            while not (dA and dB):
                if not dA:
                    dA = (next(gA) == 'SCAN')
                if not dB:
                    dB = (next(gB) == 'SCAN')
            for _ in gA:
                pass
            for _ in gB:
                pass

        kb.mark('L%d C h%d post' % (l, h))
        for b in range(NB):
            op('act', lambda e: e.activation(junk[:, 0:128], oacc[:, b, :], AF.Square, accum_out=st2[:, b, 0:1]), reads=[("oacc", b)], writes=["junk", ("st2", b)])
        op('dve', lambda e: e.tensor_scalar(st2[:, :, 1:2], st2[:, :, 0:1], 1.0 / 128, 1e-6, ALU.mult, ALU.add), reads=["st2"], writes=["st2"])
        op('act', lambda e: e.activation(st2[:, :, 2:3], st2[:, :, 1:2], AF.Ln), reads=["st2"], writes=["st2"])
        op('act', lambda e: e.activation(st2[:, :, 3:4], st2[:, :, 2:3], AF.Exp, scale=-0.5), reads=["st2"], writes=["st2"])
        for gi_ in range(2):
            stg, stgn = ((junk, "junk") if gi_ == 0 else (env["xn"][0], "xn0"))
            for j_ in range(8):
                b = gi_ * 8 + j_
                op('dve', lambda e: e.scalar_tensor_tensor(stg[:, j_ * 128:(j_ + 1) * 128], oacc[:, b, :], st2[:, b, 3:4], gnb, ALU.mult, ALU.mult),
                   reads=[("oacc", b), "st2", "gnb"], writes=[(stgn, j_)])
            bk = 4 + gi_
            pview = pb[bk][:].bitcast(BF16)
            mm([(lambda e, j_=j_: e.transpose(pview[:, j_ * 128:(j_ + 1) * 128], stg[:, j_ * 128:(j_ + 1) * 128], ident[:])) for j_ in range(8)],
               reads=[stgn, "ident"], writes=[pbn[bk]])
            op('act', lambda e: e.copy(ozT[:, h, gi_ * 1024:(gi_ + 1) * 1024], pview[:, 0:1024]), reads=[pbn[bk]],
               writes=[("ozT", h, 2 * gi_), ("ozT", h, 2 * gi_ + 1)])


def _rope_tables(sample):
    C = np.ones((128, T), np.float32)
    S = np.zeros((128, T), np.float32)
    if not sample:
        C[96:] = 0
        return C, S
    tok = np.arange(T)
    row = (tok // 64).astype(np.float32)
    col = (tok % 64).astype(np.float32)
    def tab(rot):
        npairs = rot // 4
        inv = (10000.0 ** (-np.arange(npairs, dtype=np.float32) / npairs)).astype(np.float32)
        ang = np.concatenate([row[:, None] * inv, col[:, None] * inv], axis=-1).astype(np.float32)
        c = np.cos(ang).astype(np.float32)
        s = np.sin(ang).astype(np.float32)
        Cd = np.repeat(c, 2, axis=1).T
        Sd = np.repeat(s, 2, axis=1).T
        sign = np.where(np.arange(rot) % 2 == 0, -1.0, 1.0).astype(np.float32)[:, None]
        return Cd, Sd * sign
    Ca, Sa = tab(64)
    Cb, Sb = tab(32)
    C[0:64], S[0:64] = Ca, Sa
    C[64:96], S[64:96] = Cb, Sb
    return C, S


def _mask_a(sample):
    m = np.full((6, 128, 512), NEG, np.float32)
    kj = np.arange(128)[:, None]
    qi = np.arange(128)[None, :]
    for o in range(6):
        for qb in range(4):
            blk = m[o, :, qb * 128:(qb + 1) * 128]
            if sample:
                rel = o - 1 - qb
                if rel == 0:
                    blk[:] = 0
                elif rel == -1:
                    blk[kj >= qi] = 0
                elif rel == 1:
                    blk[kj <= qi] = 0
            else:
                if (o - 1) // 2 == qb // 2 and o >= 1:
                    blk[:] = 0
    return m


def _perm_pairs(n):
    idx = np.arange(n)
    return idx ^ 1


def kernel(**inp):
    f = lambda a: np.ascontiguousarray(np.asarray(a, dtype=np.float32))
    w_in = f(inp["w_in"])
    pcols = np.concatenate([C_AQ + _perm_pairs(512), C_AK + _perm_pairs(128), C_BKPE + _perm_pairs(32)])
    w_inp = np.ascontiguousarray(w_in[:, :, pcols])
    uq = f(inp["mla_w_uq"])
    uqcols = np.arange(768).reshape(8, 96)
    uqcols[:, 64:] = uqcols[:, 64:] ^ 1
    uqp = np.ascontiguousarray(uq[:, :, uqcols.reshape(-1)])
    shared = {k: f(inp[k]) for k in ["norm_g", "w_ada", "b_ada", "attn_sink", "mla_q_norm", "mla_kv_norm", "mla_w_ukv", "gdn_conv",
                                     "gdn_norm", "w_branch_a", "w_branch_b", "w_branch_c", "w_out", "final_norm_g"]}
    shared["w_in"] = w_in
    shared["w_inp"] = w_inp
    shared["mla_w_uq"] = uq
    shared["mla_w_uqp"] = uqp
    shared["gdn_a_log"] = f(inp["gdn_a_log"]).reshape(2, 8)
    shared["gdn_dt_bias"] = f(inp["gdn_dt_bias"]).reshape(2, 8)
    shared["ident"] = np.eye(128, dtype=np.float32)
    a = np.arange(128)
    same = (a[:, None] // 64) == (a[None, :] // 64)
    gm1 = np.full((128, 8, 128), NEG, np.float32)
    gm2 = np.full((128, 8, 128), NEG, np.float32)
    for dh in range(8):
        if dh < 4:
            gm1[:, dh][(a[:, None] > a[None, :]) & same] = 0
            gm2[:, dh][(a[None, :] >= a[:, None]) & same] = 0
        else:
            gm1[:, dh][(a[:, None] < a[None, :]) & same] = 0
            gm2[:, dh][(a[None, :] <= a[:, None]) & same] = 0
    shared["gm1"], shared["gm2"] = gm1, gm2
    shared["triF"] = ((a[:, None] <= a[None, :]) & same).astype(np.float32)
    shared["triB"] = ((a[:, None] >= a[None, :]) & same).astype(np.float32)
    shared["sel1"] = np.repeat((a < 64).astype(np.float32)[:, None], 128, 1)
    shared["sel2"] = np.repeat((a >= 64).astype(np.float32)[:, None], 128, 1)
    xp = f(inp["x_prompt"]); xsm = f(inp["x_sample"])
    in_maps = []
    for c in range(8):
        m = dict(shared)
        sample = c < 4
        if sample:
            m["x"] = xsm[c]
            m["cond"] = f(inp["c"])[c]
            m["ck"] = f(inp["cache_attn_k"])[c].reshape(2, 512, 128)
            m["cv"] = f(inp["cache_attn_v"])[c].reshape(2, 512, 128)
            m["cckv"] = f(inp["cache_mla_ckv"])[c]
            m["ckpe"] = f(inp["cache_mla_kpe"])[c]
            m["st"] = f(inp["state_gdn"])[c].reshape(2, 8, 128, 128)
            qoh = np.zeros((8, T), np.float32); qoh[0] = 1
            koh = np.zeros((8, KT), np.float32); koh[0] = BIGM
            flags = np.zeros((128, 4), np.float32); flags[:, 0] = 1.0
        else:
            k = c - 4
            m["x"] = xp[8 * k:8 * k + 8].reshape(T, D)
            m["cond"] = f(inp["c_ctx"])
            m["ck"] = np.zeros((2, 512, 128), np.float32)
            m["cv"] = np.zeros((2, 512, 128), np.float32)
            m["cckv"] = np.zeros((2, 512, 256), np.float32)
            m["ckpe"] = np.zeros((2, 512, 32), np.float32)
            m["st"] = np.zeros((2, 8, 128, 128), np.float32)
            qoh = np.zeros((8, T), np.float32); koh = np.zeros((8, KT), np.float32)
            for s in range(8):
                qoh[s, s * 256:(s + 1) * 256] = 1
                koh[s, s * 256:(s + 1) * 256] = BIGM
            flags = np.zeros((128, 4), np.float32); flags[:, 1] = NEG; flags[:, 2] = -1.0
        m["qoh"], m["koh"], m["flags"] = qoh, koh, flags
        m["ropeC"], m["ropeS"] = _rope_tables(sample)
        m["maskA"] = _mask_a(sample)
        in_maps.append({n: np.ascontiguousarray(m[n], dtype=np.float32).reshape(s) for n, s in IN_SPECS})
    nc = build()
    res = run_bass_kernel_spmd(nc, in_maps, core_ids=list(range(8)))
    R = res.results
    y_sample = np.stack([R[c]["y"] for c in range(4)], 0)
    y_prompt = np.concatenate([R[c]["y"].reshape(8, 256, D) for c in range(4, 8)], 0)
    def pc(name, tail):
        return np.concatenate([np.moveaxis(R[c][name].reshape(2, 8, 256, *tail), 0, 1) for c in range(4, 8)], 0)
    nk = pc("nk", (2, 64)); nv = pc("nv", (2, 64)); nckv = pc("nckv", (256,)); nkpe = pc("nkpe", (32,))
    nst = np.concatenate([np.moveaxis(R[c]["nst"], 0, 1) for c in range(4, 8)], 0)
    return (y_prompt.astype(np.float32), y_sample.astype(np.float32), nk.astype(np.float32), nv.astype(np.float32),
            nckv.astype(np.float32), nkpe.astype(np.float32), nst.astype(np.float32))
```

```python
import numpy as np
from contextlib import ExitStack
import concourse.bass as bass
import concourse.mybir as mybir
from concourse.bass_utils import run_bass_kernel_spmd

F32 = mybir.dt.float32
BF16 = mybir.dt.bfloat16
AF = mybir.ActivationFunctionType
ALU = mybir.AluOpType

T = 2048
NB = 16
NSB = 4
KT = 2560
NKB = 20
D = 1024
BIGM = 2048.0
NEG = -30000.0
N_DSEM = 40
LIMIT = None
LAST_KB = None
C_AQ, C_AK, C_AV, C_ZA, C_BCQ, C_BCKV, C_BKPE, C_ZB, C_CQKV, C_CA, C_CB, C_ZC, C_G = (
    0, 512, 640, 768, 1280, 1664, 1920, 1952, 2464, 4000, 4008, 4016, 4528)


class KB:
    def __init__(self, nc, es):
        self.nc = nc
        self.E = {'pe': nc.tensor, 'act': nc.scalar, 'dve': nc.vector, 'pool': nc.gpsimd, 'sp': nc.sync}
        self.sem = {e: es.enter_context(nc.semaphore("s_" + e)) for e in self.E}
        self.cnt = {e: 0 for e in self.E}
        self.seen = {e: {} for e in self.E}
        self.dsem = [es.enter_context(nc.semaphore("d%d" % i)) for i in range(N_DSEM)]
        self.dcnt = [0] * N_DSEM
        self.dnext = 0
        self.reg = {}
        self.n_ins = 0
        self.limit = LIMIT
        self.n_calls = 0

    def _wait(self, eng, tok):
        if tok is None:
            return
        key = (tok[0], tok[1])
        if self.seen[eng].get(key, 0) >= tok[2]:
            return
        if eng == 'pe' and tok[0] == 'e' and tok[1] == 'pe':
            return
        if tok[0] == 'e':
            self.E[eng].wait_ge(self.sem[tok[1]], tok[2])
        else:
            self.E[eng].wait_ge(self.dsem[tok[1]], tok[2])
        self.seen[eng][key] = tok[2]

    def _entries(self, r):
        if isinstance(r, tuple):
            name, sub = r[0], (r[1] if len(r) == 2 else r[1:])
        else:
            name, sub = r, None
        d = self.reg.setdefault(name, {})
        if sub is None:
            if None not in d:
                d[None] = [None, []]
            return [d[k] for k in d], d, None
        out = []
        if None in d:
            out.append(d[None])
        if sub not in d:
            d[sub] = [None, []]
        out.append(d[sub])
        return out, d, sub

    @staticmethod
    def _norm(reads, writes):
        r2, w2 = [], []
        for r in reads:
            nm = r[0] if isinstance(r, tuple) else r
            if nm.startswith("pb"):
                w2.append(nm)
            else:
                r2.append(r)
        for w in writes:
            nm = w[0] if isinstance(w, tuple) else w
            w2.append(nm if nm.startswith("pb") else w)
        return r2, w2

    def _deps(self, eng, reads, writes):
        for r in reads:
            for en in self._entries(r)[0]:
                self._wait(eng, en[0])
        for r in writes:
            for en in self._entries(r)[0]:
                self._wait(eng, en[0])
                for t in en[1]:
                    self._wait(eng, t)

    def _record(self, tok, reads, writes):
        for r in reads:
            _, d, sub = self._entries(r)
            lst = d[sub][1]
            if tok[0] == 'e':
                lst[:] = [t for t in lst if not (t[0] == 'e' and t[1] == tok[1])]
            lst.append(tok)
            if len(lst) > 48:
                del lst[0:len(lst) - 48]
        for r in writes:
            _, d, sub = self._entries(r)
            if sub is None:
                for k in list(d.keys()):
                    if k is not None:
                        del d[k]
            d[sub] = [tok, []]

    def op(self, eng, fn, reads=(), writes=()):
        reads, writes = self._norm(reads, writes)
        self.n_calls += 1
        if self.limit is not None and self.n_calls > self.limit:
            return None
        self._deps(eng, reads, writes)
        ins = fn(self.E[eng])
        self.cnt[eng] += 1
        ins.then_inc(self.sem[eng], 1)
        tok = ('e', eng, self.cnt[eng])
        self._record(tok, reads, writes)
        self.n_ins += 1
        return tok

    def mmgroup(self, fns, reads=(), writes=()):
        reads, writes = self._norm(reads, writes)
        self.n_calls += 1
        if self.limit is not None and self.n_calls > self.limit:
            return None
        self._deps('pe', reads, writes)
        ins = None
        for f in fns:
            ins = f(self.E['pe'])
        self.cnt['pe'] += 1
        ins.then_inc(self.sem['pe'], 1)
        tok = ('e', 'pe', self.cnt['pe'])
        self._record(tok, reads, writes)
        self.n_ins += len(fns)
        return tok

    def dma(self, q, out, in_, reads=(), writes=(), **kw):
        reads, writes = self._norm(reads, writes)
        self.n_calls += 1
        if self.limit is not None and self.n_calls > self.limit:
            return None
        self._deps(q, reads, writes)
        s = self.dnext
        self.dnext = (self.dnext + 1) % N_DSEM
        if self.dcnt[s] > 0:
            self._wait(q, ('d', s, 16 * self.dcnt[s]))
        self.dcnt[s] += 1
        self.E[q].dma_start(out=out, in_=in_, **kw).then_inc(self.dsem[s], 16)
        tok = ('d', s, 16 * self.dcnt[s])
        self._record(tok, reads, writes)
        self.n_ins += 1
        return tok

    def mark(self, label):
        self.marks = getattr(self, "marks", [])
        self.marks.append((label, dict(self.cnt)))
        global LAST_KB
        LAST_KB = self

    def barrier(self):
        for e in self.E:
            for e2 in self.E:
                if e2 != e and self.cnt[e2] > 0:
                    self._wait(e, ('e', e2, self.cnt[e2]))
            for sx in range(N_DSEM):
                if self.dcnt[sx] > 0:
                    self._wait(e, ('d', sx, 16 * self.dcnt[sx]))

    def finish(self):
        for e in self.E:
            if self.cnt[e] > 0:
                self._wait('sp', ('e', e, self.cnt[e]))
        for s in range(N_DSEM):
            if self.dcnt[s] > 0:
                self._wait('sp', ('d', s, 16 * self.dcnt[s]))


IN_SPECS = [
    ("x", [T, D]), ("cond", [D]), ("norm_g", [2, D]), ("w_ada", [2, D, 3 * D]), ("b_ada", [2, 3 * D]),
    ("w_in", [2, D, 7600]), ("w_inp", [2, D, 672]), ("attn_sink", [2, 8]), ("mla_q_norm", [2, 384]),
    ("mla_w_uq", [2, 384, 768]), ("mla_w_uqp", [2, 384, 768]), ("mla_kv_norm", [2, 256]),
    ("mla_w_ukv", [2, 256, 1024]), ("gdn_conv", [2, 3, 1536]), ("gdn_a_log", [2, 8]), ("gdn_dt_bias", [2, 8]),
    ("gdn_norm", [2, 128]), ("w_branch_a", [2, 512, D]), ("w_branch_b", [2, 512, D]), ("w_branch_c", [2, 512, D]),
    ("w_out", [2, D, D]), ("final_norm_g", [D]),
    ("ck", [2, 512, 128]), ("cv", [2, 512, 128]), ("cckv", [2, 512, 256]), ("ckpe", [2, 512, 32]),
    ("st", [2, 8, 128, 128]),
    ("ropeC", [128, T]), ("ropeS", [128, T]), ("maskA", [6, 128, 512]), ("qoh", [8, T]), ("koh", [8, KT]),
    ("flags", [128, 4]), ("ident", [128, 128]), ("gm1", [128, 8, 128]), ("gm2", [128, 8, 128]),
    ("triF", [128, 128]), ("triB", [128, 128]), ("sel1", [128, 128]), ("sel2", [128, 128]),
]
OUT_SPECS = [
    ("y", [T, D]), ("nk", [2, T, 128]), ("nv", [2, T, 128]), ("nckv", [2, T, 256]), ("nkpe", [2, T, 32]),
    ("nst", [2, 8, 2, 4, 128, 128]),
]


def build(stop=None, taps=None):
    nc = bass.Bass("TRN2", target_bir_lowering=False)
    I = {n: nc.dram_tensor(n, s, F32, kind="ExternalInput").ap() for n, s in IN_SPECS}
    O = {n: nc.dram_tensor(n, s, F32, kind="ExternalOutput").ap() for n, s in OUT_SPECS}
    xs = nc.dram_tensor("xs", [T, D], F32, kind="Internal").ap()
    TAP = {}
    if taps:
        for n, s in taps.items():
            TAP[n] = nc.dram_tensor("tap_" + n, s, F32, kind="ExternalOutput").ap()
    with ExitStack() as es:
        kb = KB(nc, es)
        SB = lambda name, shape, dt: es.enter_context(nc.sbuf_tensor("sb_" + name, shape, dt))
        PS = lambda name, shape, dt: es.enter_context(nc.psum_tensor("ps_" + name, shape, dt))
        _body(nc, kb, SB, PS, I, O, xs, TAP, stop)
        kb.mark('end')
        kb.finish()
    return nc


def _body(nc, kb, SB, PS, I, O, xs, TAP, stop):
    op, dma, mm = kb.op, kb.dma, kb.mmgroup
    ident = SB("ident", [128, 128], BF16)
    ident32 = SB("ident32", [128, 128], F32)
    ropeC = SB("ropeC", [128, T], BF16)
    ropeS = SB("ropeS", [128, T], BF16)
    flags = SB("flags", [128, 4], F32)
    ones16 = SB("ones16", [128, 128], BF16)
    op('dve', lambda e: e.memset(ones16[:], 1.0), writes=["ones16"])
    dma('pool', ident[:], I["ident"], writes=["ident"])
    dma('sp', ident32[:], I["ident"], writes=["ident32"])
    dma('pool', ropeC[:], I["ropeC"], writes=["ropeC"])
    dma('pool', ropeS[:], I["ropeS"], writes=["ropeS"])
    dma('sp', flags[:], I["flags"], writes=["flags"])

    hT = SB("hT", [128, 8, T], BF16)
    mergeT = SB("mergeT", [128, 8, T], BF16)
    ozT = SB("ozT", [128, 4, T], BF16)
    pb = [PS("pb%d" % i, [128, 512], F32) for i in range(8)]
    pbn = ["pb%d" % i for i in range(8)]

    def hTr(sb):
        return [("hT", sb * 4 + i) for i in range(4)]

    NW = 2
    WCOLS = 512
    wbuf = [SB("wbuf%d" % i, [128, 8 * WCOLS], BF16) for i in range(NW)]
    wstate = {'i': 0}

    def load_w(src2d, kch, cols, q='pool', prows=128):
        i = wstate['i']
        wstate['i'] = (i + 1) % NW
        name = "wbuf%d" % i
        tot = sum(n for _, n in cols)
        assert kch * tot <= 8 * WCOLS, (kch, tot)
        view = wbuf[i][0:prows, 0:kch * tot].rearrange("p (k n) -> p k n", k=kch)
        srcv = src2d.rearrange("(k p) n -> p k n", p=prows)
        o = 0
        for c0, n in cols:
            dma(q, view[:, :, o:o + n], srcv[:, :, c0:c0 + n], writes=[name])
            o += n
        return view, name

    def tap(name, ap_sb, reads):
        if name in TAP:
            dma('sp', TAP[name], ap_sb, reads=reads)

    condsb = SB("condsb", [128, 8], F32)
    scond = SB("scond", [128, 8], BF16)
    modfm = SB("modfm", [128, 24], F32)
    badafm = SB("badafm", [128, 24], F32)
    ngfm = SB("ngfm", [128, 8], F32)
    Afm = SB("Afm", [128, 8], F32)
    gbc = SB("gbc", [128, 8, 128], F32)
    gateb = SB("gateb", [128, D], F32)
    xblk = [SB("xblk%d" % i, [128, D], F32) for i in range(2)]
    xn = [SB("xn%d" % i, [128, D], BF16) for i in range(2)]
    junk = SB("junk", [128, D], BF16)
    stat = SB("stat", [128, NB, 4], F32)
    qTh = [SB("qTh%d" % i, [104, T], BF16) for i in range(1)] * 2
    wukv = SB("wukv", [128, 2, 1024], BF16)
    ARN = 21952
    arena = SB("arena", [128, ARN], BF16)
    maskA = arena[:, 4 * KT:4 * KT + 3072].rearrange("p (o n) -> p o n", o=6)
    o_ = 0
    kTa = arena[0:64, 0:2 * KT].rearrange("p (g n) -> p g n", g=2)
    Va = arena[:, 2 * KT:2 * KT + NKB * 256].rearrange("p (k g d) -> p k g d", k=NKB, g=2)
    kTb1 = arena[0:104, 0:KT]
    kpeT = arena[0:96, KT:2 * KT]
    Vb1 = arena[:, 2 * KT:2 * KT + NKB * 128].rearrange("p (k d) -> p k d", k=NKB)
    o_ = 2 * KT + NKB * 128
    ckvT = arena[:, o_:o_ + 2 * KT].rearrange("p (c n) -> p c n", c=2)
    cqnT = arena[:, o_ + 2 * KT:o_ + 2 * KT + 3 * T].rearrange("p (c n) -> p c n", c=3)
    kTb = [kTb1, kTb1]
    Vb = [Vb1, Vb1]
    pT = [SB("pT%d" % i, [128, 512], BF16) for i in range(3)]
    t1 = [SB("t1_%d" % i, [128, 512], F32) for i in range(2)]
    t2 = [SB("t2_%d" % i, [128, 512], F32) for i in range(2)]
    rr = [SB("rr%d" % i, [128, 512], F32) for i in range(1)] * 2
    r3 = [SB("r3_%d" % i, [64, 512], F32) for i in range(1)] * 2
    kvout = [SB("kvout%d" % i, [128, 288], F32) for i in range(2)]
    kvn16 = [SB("kvn16_%d" % i, [128, 384], BF16) for i in range(2)]
    ctx16 = SB("ctx16", [128, 4, 256], BF16)
    ctxp = SB("ctxp", [128, 4, 96], BF16)
    esink = SB("esink", [128, 8], F32)
    kvng = SB("kvng", [128, 256], F32)
    qng = SB("qng", [128, 384], F32)
    st2 = SB("st2", [128, NB, 4], F32)
    sig = [SB("sig%d" % i, [128, 512], F32) for i in range(2)]
    op('dve', lambda e: e.memset(ctxp[:], 0.0), writes=["ctxp"])
    dma('pool', qTh[0][96:104, :], I["qoh"], writes=["qTh0"])
    cnt = {'rot': 0, 'p': 0, 'o': 0}
    if stop == "c":
        return

    pT.append(SB("pT3", [128, 512], BF16))

    def attend_stream(groups):
        SBK = [2, 3, 6, 7]
        tiles = []
        for gi, g_ in enumerate(groups):
            g_['ob'] = 4 + cnt['o'] % 2
            cnt['o'] += 1
            for idx in range(len(g_['klist'])):
                tiles.append((gi, idx))
        info = {}

        def emit_S(t):
            gi, idx = tiles[t]
            g_ = groups[gi]
            kblk, mi, isctx = g_['klist'][idx]
            sbk = SBK[cnt['p'] % 4]
            pt = cnt['p'] % 4
            cnt['p'] += 1
            kap, kname = g_['kfn'](kblk)
            qtile, sbi, K = g_['qtile'], g_['sbi'], g_['K']
            fns = [lambda e: e.matmul(pb[sbk][:], kap, qtile[0:K, sbi * 512:(sbi + 1) * 512], start=True, stop=(mi is None))]
            rd = [kname, (g_['qname'], sbi)]
            if mi is not None:
                fns.append(lambda e: e.matmul(pb[sbk][:], ident[:], maskA[:, mi, :], start=False, stop=True))
                rd += ["ident", "maskA"]
            mm(fns, reads=rd, writes=[pbn[sbk]])
            b_ = g_['bias_fn'](isctx)
            op('act', lambda e: e.activation(pT[pt][:], pb[sbk][:], AF.Exp, scale=g_['scale'], bias=b_),
               reads=[pbn[sbk], "flags"], writes=["pT%d" % pt])
            info[t] = pt

        LOOK = 3
        nt = len(tiles)
        for t in range(min(LOOK, nt)):
            emit_S(t)
        for t in range(nt):
            if t + LOOK < nt:
                emit_S(t + LOOK)
            gi, idx = tiles[t]
            g_ = groups[gi]
            n = len(g_['klist'])
            kblk = g_['klist'][idx][0]
            pt = info[t]
            ob = g_['ob']
            vap, vname = g_['vfn'](kblk)
            mm([lambda e: e.matmul(pb[ob][:], vap, pT[pt][:], start=(idx == 0), stop=(idx == n - 1))],
               reads=[vname, "pT%d" % pt], writes=[pbn[ob]])
            if idx == n - 1:
                g_['fin'](ob)

    def run_pairs(genf, n):
        for b0_ in range(0, n, 2):
            gs = [genf(b0_), genf(b0_ + 1)]
            alive = [True, True]
            while any(alive):
                for q_ in range(2):
                    if alive[q_]:
                        try:
                            next(gs[q_])
                        except StopIteration:
                            alive[q_] = False

    for l in range(2):
        xsrc = I["x"] if l == 0 else xs
        Win = I["w_in"][l]
        Winp = I["w_inp"][l]
        kb.mark('L%d start' % l)
        dma('sp', condsb[:], I["cond"].rearrange("(c p) -> p c", p=128), writes=["cond"], allow_slow_non_contiguous=True)
        dma('sp', badafm[:], I["b_ada"][l].rearrange("(c p) -> p c", p=128), writes=["bada"], allow_slow_non_contiguous=True)
        dma('sp', ngfm[:], I["norm_g"][l].rearrange("(c p) -> p c", p=128), writes=["ngfm"], allow_slow_non_contiguous=True)
        op('act', lambda e: e.activation(scond[:], condsb[:], AF.Silu), reads=["cond"], writes=["scond"])
        ada_w = []
        for nt in range(6):
            if nt < 5:
                v_ = arena[:, nt * 4096:(nt + 1) * 4096].rearrange("p (k n) -> p k n", k=8)
                dma('pool', v_, I["w_ada"][l].rearrange("(k p) n -> p k n", p=128)[:, :, nt * 512:(nt + 1) * 512], writes=["mw%d" % nt])
                ada_w.append((v_, "mw%d" % nt))
            else:
                ada_w.append(load_w(I["w_ada"][l], 8, [(nt * 512, 512)]))
        for nt in range(6):
            wv, wn = ada_w[nt]
            for jj in range(4):
                j = nt * 4 + jj
                mm([(lambda e, c=c, jj=jj, j=j, wv=wv: e.matmul(pb[0][:, j:j + 1], wv[:, c, jj * 128:(jj + 1) * 128], scond[:, c:c + 1],
                                                               start=(c == 0), stop=(c == 7))) for c in range(8)],
                   reads=[wn, "scond"], writes=[(pbn[0], j)])
        op('dve', lambda e: e.tensor_tensor(modfm[:], pb[0][:, 0:24], badafm[:], ALU.add), reads=[pbn[0], "bada"], writes=["modfm"])
        op('dve', lambda e: e.scalar_tensor_tensor(Afm[:], modfm[:, 8:16], 1.0, ngfm[:], ALU.add, ALU.mult), reads=["modfm", "ngfm"], writes=["Afm"])
        op('dve', lambda e: e.tensor_copy(gbc[:], modfm[:, 16:24].unsqueeze(2).to_broadcast([128, 8, 128])), reads=["modfm"], writes=["gbc"])
        for c in range(8):
            bkc = 1 + c // 4
            mm([lambda e, c=c, bkc=bkc: e.matmul(pb[bkc][:, (c % 4) * 128:(c % 4 + 1) * 128], gbc[:, c, :], ident32[:], start=True, stop=True)],
               reads=["gbc", "ident32"], writes=[(pbn[bkc], c % 4)])
        op('act', lambda e: e.copy(gateb[:, 0:512], pb[1][:]), reads=[pbn[1]], writes=[("gateb", 0)])
        op('act', lambda e: e.copy(gateb[:, 512:1024], pb[2][:]), reads=[pbn[2]], writes=[("gateb", 1)])
        tap("modfm%d" % l, modfm[:], ["modfm"])
        if stop == "p0":
            return

        kb.mark('L%d p1' % l)
        def gen_p1(b):
            xb, xbn = xblk[b % 2], "xblk%d" % (b % 2)
            xnb, xnn = xn[b % 2], "xn%d" % (b % 2)
            dma('sp', xb[:], xsrc[b * 128:(b + 1) * 128, :], reads=(["xs"] if l == 1 else []), writes=[xbn])
            yield None
            op('act', lambda e: e.activation(junk[:], xb[:], AF.Square, accum_out=stat[:, b, 0:1]), reads=[xbn], writes=["junk", ("stat", b)])
            yield None
            op('dve', lambda e: e.tensor_scalar(stat[:, b, 1:2], stat[:, b, 0:1], 1.0 / D, 1e-6, ALU.mult, ALU.add), reads=[("stat", b)], writes=[("stat", b)])
            yield None
            op('act', lambda e: e.activation(stat[:, b, 2:3], stat[:, b, 1:2], AF.Ln), reads=[("stat", b)], writes=[("stat", b)])
            yield None
            op('act', lambda e: e.activation(stat[:, b, 3:4], stat[:, b, 2:3], AF.Exp, scale=-0.5), reads=[("stat", b)], writes=[("stat", b)])
            yield None
            op('dve', lambda e: e.tensor_scalar(xnb[:], xb[:], stat[:, b, 3:4], None, ALU.mult), reads=[xbn, ("stat", b)], writes=[xnn])
            yield None
            for half in range(2):
                bk = 4 + (2 * b + half) % 4
                pview = pb[bk][:].bitcast(BF16)
                mm([(lambda e, c=c, half=half, pview=pview: e.transpose(pview[:, c * 128:(c + 1) * 128], xnb[:, (half * 4 + c) * 128:(half * 4 + c + 1) * 128], ident[:]))
                    for c in range(4)], reads=[xnn, "ident"], writes=[pbn[bk]])
                yield None
                for c in range(4):
                    cc = half * 4 + c
                    if True:
                        op('act', lambda e, c=c, cc=cc, pview=pview: e.activation(hT[:, cc, b * 128:(b + 1) * 128], pview[:, c * 128:(c + 1) * 128], AF.Identity,
                                                                                 scale=Afm[:, cc:cc + 1], bias=modfm[:, cc:cc + 1]),
                           reads=[pbn[bk], "Afm", "modfm"], writes=[("hT", b, cc)])
                        yield None
                    else:
                        op('dve', lambda e, c=c, cc=cc, pview=pview: e.scalar_tensor_tensor(hT[:, cc, b * 128:(b + 1) * 128], pview[:, c * 128:(c + 1) * 128],
                                                                                    Afm[:, cc:cc + 1], modfm[:, cc:cc + 1].to_broadcast([128, 128]), ALU.mult, ALU.add),
                           reads=[pbn[bk], "Afm", "modfm"], writes=[("hT", b, cc)])
                        yield None
        run_pairs(gen_p1, NB)
        op('pool', lambda e: e.memset(junk[0:1, 0:1], 0.0), reads=[], writes=["hT"])
        HR = ["hT"]
        if "hT%d" % l in TAP:
            for c8 in range(8):
                op('dve', lambda e: e.tensor_copy(xblk[0][:].rearrange("p (a b) -> p a b", a=1)[:, 0, :], hT[:, c8, 0:1024]), reads=["hT"], writes=["xblk0"])
                dma('sp', TAP["hT%d" % l][:, c8, 0:1024], xblk[0][:], reads=["xblk0"])
                op('dve', lambda e: e.tensor_copy(xblk[0][:], hT[:, c8, 1024:2048]), reads=["hT"], writes=["xblk0"])
                dma('sp', TAP["hT%d" % l][:, c8, 1024:2048], xblk[0][:], reads=["xblk0"])
        if stop == "p1":
            return

        def lin(bank, wv, wn, col0, M, sbi, kch=8, rhs_fn=None, extra_reads=()):
            if rhs_fn is None:
                rhs_fn = lambda c: hT[:, c, sbi * 512:(sbi + 1) * 512]
            mm([(lambda e, c=c: e.matmul(pb[bank][0:M, :], wv[:, c, col0:col0 + M], rhs_fn(c), start=(c == 0), stop=(c == kch - 1)))
                for c in range(kch)], reads=[wn] + HR + list(extra_reads), writes=[pbn[bank]])

        kb.mark('L%d A-pre' % l)
        dma('sp', esink[:], I["attn_sink"][l].partition_broadcast(128), writes=["esink"])
        op('act', lambda e: e.activation(esink[:], esink[:], AF.Exp), reads=["esink"], writes=["esink"])
        dma('sp', kvng[:], I["mla_kv_norm"][l].partition_broadcast(128), writes=["kvng"])
        kb.barrier()
        op('dve', lambda e: e.memset(Va[:, :, :, 64:128], 1.0), writes=["Va"])
        dma('pool', maskA, I["maskA"].rearrange("o p n -> p o n"), writes=["maskA"])
        wq = arena[:, 13312:13312 + 4096].rearrange("p (k n) -> p k n", k=8)
        wqp = arena[:, 17408:17408 + 4096].rearrange("p (k n) -> p k n", k=8)
        wqn, wqpn = "mwq", "mwqp"
        wkv, wkvn = load_w(Win, 8, [(C_AK, 256)])
        def gen_akv(b):
            bk = 6 + b % 2
            ko = kvout[b % 2]
            kon = "kvout%d" % (b % 2)
            mm([(lambda e, c=c: e.matmul(pb[bk][:, 0:256], hT[:, c, b * 128:(b + 1) * 128], wkv[:, c, 0:256], start=(c == 0), stop=(c == 7))) for c in range(8)],
               reads=[wkvn] + HR, writes=[(pbn[bk], 0)])
            yield None
            op('act', lambda e: e.copy(ko[:, 0:256], pb[bk][:, 0:256]), reads=[(pbn[bk], 0)], writes=[kon])
            yield None
            op('dve', lambda e: e.tensor_copy(Va[:, b, :, 0:64], pb[bk][:, 128:256].rearrange("p (g d) -> p g d", g=2)), reads=[(pbn[bk], 0), kon], writes=[("Va", b)])
            yield None
            dma('sp', O["nk"][l, b * 128:(b + 1) * 128, :], ko[:, 0:128], reads=[kon])
            yield None
            dma('sp', O["nv"][l, b * 128:(b + 1) * 128, :], ko[:, 128:256], reads=[kon])
            yield None
        run_pairs(gen_akv, NB)
        dma('pool', ctx16[:, :, 0:128], I["ck"][l].rearrange("(j p) n -> p j n", p=128), writes=["ctx16"])
        for g in range(2):
            pview = pb[6 + g][:].bitcast(BF16)
            mm([(lambda e, j=j, pview=pview: e.transpose(pview[0:64, j * 128:(j + 1) * 128], ctx16[:, j, g * 64:(g + 1) * 64], ident[:])) for j in range(4)],
               reads=["ctx16", "ident"], writes=[pbn[6 + g]])
            op('dve', lambda e, pview=pview: e.tensor_copy(kTa[:, g, T:KT], pview[0:64, 0:512]), reads=[pbn[6 + g]], writes=[("kTa", g, 4)])
        for g in range(2):
            dma('pool', Va[:, NB:NKB, g, 0:64], I["cv"][l].rearrange("(j p) (g d) -> p j g d", p=128, g=2)[:, :, g, :], writes=[("Va", "ctx", g)])
        wk, wkn = load_w(Win, 8, [(C_AK, 128)])
        wkp, wkpn = load_w(Winp, 8, [(512, 128)])
        dma('pool', wq, Win.rearrange("(k p) n -> p k n", p=128)[:, :, C_AQ:C_AQ + 512], writes=[wqn])
        dma('pool', wqp, Winp.rearrange("(k p) n -> p k n", p=128)[:, :, 0:512], writes=[wqpn])
        for g in range(2):
            def gen_ka(sbi, g=g):
                r = sbi % 2
                ba, bb_ = 2 * r, 2 * r + 1
                lin(ba, wk, wkn, g * 64, 64, sbi)
                yield None
                lin(bb_, wkp, wkpn, g * 64, 64, sbi)
                yield None
                op('dve', lambda e: e.tensor_tensor(t1[r][0:64, :], pb[ba][0:64, :], ropeC[0:64, sbi * 512:(sbi + 1) * 512], ALU.mult), reads=[pbn[ba], "ropeC"], writes=["t1_%d" % r])
                yield None
                op('dve', lambda e: e.tensor_tensor(t2[r][0:64, :], pb[bb_][0:64, :], ropeS[0:64, sbi * 512:(sbi + 1) * 512], ALU.mult), reads=[pbn[bb_], "ropeS"], writes=["t2_%d" % r])
                yield None
                op('pool', lambda e: e.tensor_tensor(kTa[:, g, sbi * 512:(sbi + 1) * 512], t1[r][0:64, :], t2[r][0:64, :], ALU.add), reads=["t1_%d" % r, "t2_%d" % r], writes=[("kTa", g, sbi)])
                yield None
            run_pairs(gen_ka, NSB)
        kb.mark('L%d A-attn' % l)
        for h in range(8):
            g = h // 4
            qt, qn = qTh[h % 2], "qTh0"
            def gen_qa(sbi, h=h, qt=qt, qn=qn):
                r = sbi % 2
                ba, bb_ = 2 * r, 2 * r + 1
                lin(ba, wq, wqn, h * 64, 64, sbi)
                yield None
                lin(bb_, wqp, wqpn, h * 64, 64, sbi)
                yield None
                op('dve', lambda e: e.tensor_tensor(t1[r][0:64, :], pb[ba][0:64, :], ropeC[0:64, sbi * 512:(sbi + 1) * 512], ALU.mult), reads=[pbn[ba], "ropeC"], writes=["t1_%d" % r])
                yield None
                op('dve', lambda e: e.tensor_tensor(t2[r][0:64, :], pb[bb_][0:64, :], ropeS[0:64, sbi * 512:(sbi + 1) * 512], ALU.mult), reads=[pbn[bb_], "ropeS"], writes=["t2_%d" % r])
                yield None
                op('pool', lambda e: e.tensor_tensor(qt[0:64, sbi * 512:(sbi + 1) * 512], t1[r][0:64, :], t2[r][0:64, :], ALU.add), reads=["t1_%d" % r, "t2_%d" % r], writes=[(qn, sbi)])
                yield None
            run_pairs(gen_qa, NSB)
            groups = []
            for sbi in range(NSB):
                klist = []
                for o in range(6):
                    j = 4 * sbi - 1 + o
                    if 0 <= j < NB:
                        klist.append((j, o, False))
                for j in range(NB, NKB):
                    klist.append((j, None, True))

                def kfn(kblk, g=g):
                    return kTa[:, g, kblk * 128:(kblk + 1) * 128], ("kTa", g, kblk // 4)

                def vfn(kblk, g=g):
                    return Va[:, kblk, g, :], (("Va", kblk) if kblk < NB else ("Va", "ctx", g))

                def fin(ob, h=h, sbi=sbi):
                    r = cnt['rot'] % 2
                    cnt['rot'] += 1
                    op('dve', lambda e: e.tensor_scalar(rr[r][64:128, :], pb[ob][64:128, :], esink[64:128, h:h + 1], None, ALU.add), reads=[pbn[ob], "esink"], writes=["rr0"])
                    op('dve', lambda e: e.reciprocal(rr[r][64:128, :], rr[r][64:128, :]), reads=["rr0"], writes=["rr0"])
                    op('pool', lambda e: e.tensor_copy(r3[r][0:64, :], rr[r][64:128, :]), reads=["rr0"], writes=["r3_0"])
                    po = (h % 2) * 64
                    op('dve', lambda e: e.tensor_tensor(ozT[po:po + 64, h // 2, sbi * 512:(sbi + 1) * 512], pb[ob][0:64, :], r3[r][0:64, :], ALU.mult),
                       reads=[pbn[ob], "r3_0"], writes=[("ozT", h // 2, sbi)])

                groups.append(dict(qtile=qt, qname=qn, sbi=sbi, kfn=kfn, klist=klist, vfn=vfn, K=64, scale=0.125,
                                   bias_fn=(lambda isctx: (flags[:, 1:2] if isctx else 0.0)), fin=fin))
            attend_stream(groups)
        if "ozA%d" % l in TAP:
            for c8 in range(4):
                for hf in range(2):
                    op('dve', lambda e: e.tensor_copy(xblk[0][:], ozT[:, c8, hf * 1024:(hf + 1) * 1024]), reads=["ozT"], writes=["xblk0"])
                    dma('sp', TAP["ozA%d" % l][:, c8, hf * 1024:(hf + 1) * 1024], xblk[0][:], reads=["xblk0"])

        def zmul_and_merge(zcol, wbr_src, gcol, first, after_loads=None):
            kb.barrier()

            def load_into(k_, name, src2d, kch, c0, n):
                v = arena[:, k_ * 4096:k_ * 4096 + kch * n].rearrange("p (k n) -> p k n", k=kch)
                dma('pool', v, src2d.rearrange("(k p) n -> p k n", p=128)[:, :, c0:c0 + n], writes=[name])
                return v, name
            wz, wzn = load_into(0, "mw0", Win, 8, zcol, 512)
            wbs, wgs = [], []
            for ch in range(2):
                wbs.append(load_into(1 + 2 * ch, "mw%d" % (1 + 2 * ch), wbr_src, 4, ch * 512, 512))
                wgs.append(load_into(2 + 2 * ch, "mw%d" % (2 + 2 * ch), Win, 8, gcol + ch * 512, 512))
            if after_loads is not None:
                after_loads()
            for c in range(4):
                for sbi in range(NSB):
                    r = cnt['rot'] % 2
                    cnt['rot'] += 1
                    lin(r, wz, wzn, c * 128, 128, sbi)
                    op('act', lambda e: e.activation(t1[r][:], pb[r][:], AF.Silu), reads=[pbn[r]], writes=["t1_%d" % r])
                    op('pool', lambda e: e.tensor_tensor(ozT[:, c, sbi * 512:(sbi + 1) * 512], ozT[:, c, sbi * 512:(sbi + 1) * 512], t1[r][:], ALU.mult),
                       reads=["t1_%d" % r, ("ozT", c, sbi)], writes=[("ozT", c, sbi)])
            for ch in range(2):
                wb, wbn = wbs[ch]
                wg, wgn = wgs[ch]
                for cc in range(4):
                    c = ch * 4 + cc
                    for sbi in range(NSB):
                        r = cnt['rot'] % 2
                        cnt['rot'] += 1
                        lin(r, wg, wgn, cc * 128, 128, sbi)
                        op('act', lambda e: e.activation(sig[r][:], pb[r][:], AF.Sigmoid), reads=[pbn[r]], writes=["sig%d" % r])
                        lin(2 + r, wb, wbn, cc * 128, 128, sbi, kch=4, rhs_fn=lambda k: ozT[:, k, sbi * 512:(sbi + 1) * 512],
                            extra_reads=[("ozT", k, sbi) for k in range(4)])
                        dst = mergeT[:, c, sbi * 512:(sbi + 1) * 512]
                        if first:
                            op('dve', lambda e: e.tensor_tensor(dst, pb[2 + r][:], sig[r][:], ALU.mult), reads=[pbn[2 + r], "sig%d" % r], writes=[("mergeT", c, sbi)])
                        else:
                            op('dve', lambda e: e.tensor_tensor(t2[r][:], pb[2 + r][:], sig[r][:], ALU.mult), reads=[pbn[2 + r], "sig%d" % r], writes=["t2_%d" % r])
                            op('pool', lambda e: e.tensor_tensor(dst, dst, t2[r][:], ALU.add), reads=["t2_%d" % r, ("mergeT", c, sbi)], writes=[("mergeT", c, sbi)])

        kb.mark('L%d A-merge' % l)
        zmul_and_merge(C_ZA, I["w_branch_a"][l], C_G, True)

        def tap_big(nm, src, nch, rd):
            if nm in TAP:
                for c8 in range(nch):
                    for hf in range(2):
                        op('dve', lambda e: e.tensor_copy(xblk[0][:], src[:, c8, hf * 1024:(hf + 1) * 1024]), reads=rd, writes=["xblk0"])
                        dma('sp', TAP[nm][:, c8, hf * 1024:(hf + 1) * 1024], xblk[0][:], reads=["xblk0"])
        tap_big("mergeA%d" % l, mergeT, 8, ["mergeT"])
        if stop == "A":
            return

        kb.mark('L%d B-pre' % l)
        kb.barrier()
        op('dve', lambda e: e.memset(Vb1[:, :, 64:128], 1.0), writes=["Vb0"])
        dma('pool', kTb1[96:104, :], I["koh"], writes=["kTb0"])
        dma('pool', qTh[0][96:104, :], I["qoh"], writes=["qTh0"])
        wkv, wkvn = load_w(Win, 8, [(C_BCKV, 288)])
        def gen_bckv(b):
            bk = 6 + b % 2
            mm([(lambda e, c=c: e.matmul(pb[bk][:, 256:512 + 32 - 512] if False else pb[bk][:, 256:512], hT[:, c, b * 128:(b + 1) * 128], wkv[:, c, 0:256], start=(c == 0), stop=(c == 7))) for c in range(8)],
               reads=[wkvn] + HR, writes=[(pbn[bk], 1)])
            yield None
            bk2 = 0 + b % 2
            mm([(lambda e, c=c: e.matmul(pb[bk2][:, 0:32], hT[:, c, b * 128:(b + 1) * 128], wkv[:, c, 256:288], start=(c == 0), stop=(c == 7))) for c in range(8)],
               reads=[wkvn] + HR, writes=[(pbn[bk2], 0)])
            yield None
            ko2 = kvout[(b + 1) % 2]
            ko2n = "kvout%d" % ((b + 1) % 2)
            op('act', lambda e: e.activation(junk[:, 0:256], pb[bk][:, 256:512], AF.Square, accum_out=st2[:, b, 0:1]), reads=[(pbn[bk], 1)], writes=["junk", ("st2", b)])
            yield None
            op('dve', lambda e: e.tensor_scalar(st2[:, b, 1:2], st2[:, b, 0:1], 1.0 / 256, 1e-6, ALU.mult, ALU.add), reads=[("st2", b)], writes=[("st2", b)])
            yield None
            op('act', lambda e: e.activation(st2[:, b, 2:3], st2[:, b, 1:2], AF.Ln), reads=[("st2", b)], writes=[("st2", b)])
            yield None
            op('act', lambda e: e.activation(st2[:, b, 3:4], st2[:, b, 2:3], AF.Exp, scale=-0.5), reads=[("st2", b)], writes=[("st2", b)])
            yield None
            op('dve', lambda e: e.scalar_tensor_tensor(ko2[:, 0:256], pb[bk][:, 256:512], st2[:, b, 3:4], kvng[:], ALU.mult, ALU.mult),
               reads=[(pbn[bk], 1), ("st2", b), "kvng"], writes=[ko2n])
            yield None
            op('act', lambda e: e.copy(ko2[:, 256:288], pb[bk2][:, 0:32]), reads=[(pbn[bk2], 0)], writes=[ko2n])
            yield None
            dma('sp', O["nckv"][l, b * 128:(b + 1) * 128, :], ko2[:, 0:256], reads=[ko2n])
            yield None
            dma('sp', O["nkpe"][l, b * 128:(b + 1) * 128, :], ko2[:, 256:288], reads=[ko2n])
            yield None
            k16 = kvn16[b % 2]
            k16n = "kvn16_%d" % (b % 2)
            op('dve', lambda e: e.tensor_copy(k16[:, 0:256], ko2[:, 0:256]), reads=[ko2n], writes=[k16n])
            yield None
            bk3 = 2 + b % 2
            pview = pb[bk3][:].bitcast(BF16)
            mm([(lambda e, c=c, pview=pview: e.transpose(pview[:, c * 128:(c + 1) * 128], k16[:, c * 128:(c + 1) * 128], ident[:])) for c in range(2)],
               reads=[k16n, "ident"], writes=[pbn[bk3]])
            yield None
            op('dve', lambda e, pview=pview: e.tensor_copy(ckvT[:, :, b * 128:(b + 1) * 128], pview[:, 0:256].rearrange("p (c n) -> p c n", c=2)),
               reads=[pbn[bk3]], writes=[("ckvT", b)])
            yield None
        run_pairs(gen_bckv, NB)
        ctx16b = ctx16
        dma('pool', ctx16b[:], I["cckv"][l].rearrange("(j p) n -> p j n", p=128), writes=["ctx16"])
        for j in range(4):
            pview = pb[6 + j % 2][:].bitcast(BF16)
            mm([(lambda e, c=c, pview=pview: e.transpose(pview[:, c * 128:(c + 1) * 128], ctx16b[:, j, c * 128:(c + 1) * 128], ident[:])) for c in range(2)],
               reads=["ctx16", "ident"], writes=[pbn[6 + j % 2]])
            op('dve', lambda e, pview=pview: e.tensor_copy(ckvT[:, :, T + j * 128:T + (j + 1) * 128], pview[:, 0:256].rearrange("p (c n) -> p c n", c=2)),
               reads=[pbn[6 + j % 2]], writes=[("ckvT", NB + j)])
        dma('pool', ctxp[:, :, 64:96], I["ckpe"][l].rearrange("(j p) n -> p j n", p=128), writes=["ctxp"])
        pview = pb[6][:].bitcast(BF16)
        mm([(lambda e, j=j, pview=pview: e.transpose(pview[0:96, j * 128:(j + 1) * 128], ctxp[:, j, :], ident[:])) for j in range(4)],
           reads=["ctxp", "ident"], writes=[pbn[6]])
        op('dve', lambda e, pview=pview: e.tensor_copy(kpeT[64:96, T:KT], pview[64:96, 0:512]), reads=[pbn[6]], writes=[("kpeT", 4)])

        dma('sp', qng[:], I["mla_q_norm"][l].partition_broadcast(128), writes=["qng"])
        wcq, wcqn = load_w(Win, 8, [(C_BCQ, 384)])
        def gen_bcq(b):
            bk = 6 + b % 2
            mm([(lambda e, c=c: e.matmul(pb[bk][:, 0:384], hT[:, c, b * 128:(b + 1) * 128], wcq[:, c, 0:384], start=(c == 0), stop=(c == 7))) for c in range(8)],
               reads=[wcqn] + HR, writes=[pbn[bk]])
            yield None
            op('act', lambda e: e.activation(junk[:, 0:384], pb[bk][:, 0:384], AF.Square, accum_out=st2[:, b, 0:1]), reads=[pbn[bk]], writes=["junk", ("st2", b)])
            yield None
            op('dve', lambda e: e.tensor_scalar(st2[:, b, 1:2], st2[:, b, 0:1], 1.0 / 384, 1e-6, ALU.mult, ALU.add), reads=[("st2", b)], writes=[("st2", b)])
            yield None
            op('act', lambda e: e.activation(st2[:, b, 2:3], st2[:, b, 1:2], AF.Ln), reads=[("st2", b)], writes=[("st2", b)])
            yield None
            op('act', lambda e: e.activation(st2[:, b, 3:4], st2[:, b, 2:3], AF.Exp, scale=-0.5), reads=[("st2", b)], writes=[("st2", b)])
            yield None
            k16 = kvn16[b % 2]
            k16n = "kvn16_%d" % (b % 2)
            op('dve', lambda e: e.scalar_tensor_tensor(k16[:, 0:384], pb[bk][:, 0:384], st2[:, b, 3:4], qng[:], ALU.mult, ALU.mult),
               reads=[pbn[bk], ("st2", b), "qng"], writes=[k16n])
            yield None
            bk3 = 2 + b % 2
            pview = pb[bk3][:].bitcast(BF16)
            mm([(lambda e, c=c, pview=pview: e.transpose(pview[:, c * 128:(c + 1) * 128], k16[:, c * 128:(c + 1) * 128], ident[:])) for c in range(3)],
               reads=[k16n, "ident"], writes=[pbn[bk3]])
            yield None
            op('dve', lambda e, pview=pview: e.tensor_copy(cqnT[:, :, b * 128:(b + 1) * 128], pview[:, 0:384].rearrange("p (c n) -> p c n", c=3)),
               reads=[pbn[bk3]], writes=[("cqnT", b)])
            yield None
        run_pairs(gen_bcq, NB)
        wpe, wpen = load_w(Win, 8, [(C_BKPE - 64, 96)])
        wpep, wpepn = load_w(Winp, 8, [(640 - 64, 96)])
        for sbi in range(NSB):
            r = cnt['rot'] % 2
            cnt['rot'] += 1
            lin(0, wpe, wpen, 0, 96, sbi)
            lin(1, wpep, wpepn, 0, 96, sbi)
            op('dve', lambda e: e.tensor_tensor(t1[r][64:96, :], pb[0][64:96, :], ropeC[64:96, sbi * 512:(sbi + 1) * 512], ALU.mult), reads=[pbn[0], "ropeC"], writes=["t1_%d" % r])
            op('dve', lambda e: e.tensor_tensor(t2[r][64:96, :], pb[1][64:96, :], ropeS[64:96, sbi * 512:(sbi + 1) * 512], ALU.mult), reads=[pbn[1], "ropeS"], writes=["t2_%d" % r])
            op('pool', lambda e: e.tensor_tensor(kpeT[64:96, sbi * 512:(sbi + 1) * 512], t1[r][64:96, :], t2[r][64:96, :], ALU.add), reads=["t1_%d" % r, "t2_%d" % r], writes=[("kpeT", sbi)])
        wuq, wuqn = load_w(I["mla_w_uq"][l], 3, [(0, 768)])
        wuqp, wuqpn = load_w(I["mla_w_uqp"][l], 3, [(0, 768)])
        dma('pool', wukv[:], I["mla_w_ukv"][l].rearrange("(k p) n -> p k n", p=128), writes=["wukv"])
        kb.mark('L%d B-attn' % l)
        CQR = [("cqnT", b) for b in range(NB)]
        CKR = [("ckvT", b) for b in range(NKB)]
        for s5 in range(5):
            op('pool', lambda e: e.tensor_copy(kTb1[64:96, s5 * 512:(s5 + 1) * 512], kpeT[64:96, s5 * 512:(s5 + 1) * 512]), reads=[("kpeT", s5)], writes=[("kTb0", s5, 'pe')])
        for h in range(8):
            qt, qn = qTh[h % 2], "qTh0"
            kt, ktn = kTb[0], "kTb0"
            vt, vtn = Vb[0], "Vb0"
            for s5 in range(5):
                bk = 0 + s5 % 2
                mm([(lambda e, c=c: e.matmul(pb[bk][0:64, :], wukv[:, c, h * 128:h * 128 + 64], ckvT[:, c, s5 * 512:(s5 + 1) * 512], start=(c == 0), stop=(c == 1))) for c in range(2)],
                   reads=["wukv"] + CKR, writes=[pbn[bk]])
                op('act', lambda e: e.copy(kt[0:64, s5 * 512:(s5 + 1) * 512], pb[bk][0:64, :]), reads=[pbn[bk]], writes=[(ktn, s5)])
            for gi_, k0 in enumerate(range(0, NKB, 8)):
                nb_ = min(8, NKB - k0)
                bk = 6 + gi_ % 2
                fns = []
                for j_ in range(nb_):
                    kblk = k0 + j_
                    for c in range(2):
                        fns.append(lambda e, c=c, j_=j_, kblk=kblk: e.matmul(pb[bk][:, j_ * 64:(j_ + 1) * 64], ckvT[:, c, kblk * 128:(kblk + 1) * 128],
                                                                            wukv[:, c, h * 128 + 64:h * 128 + 128], start=(c == 0), stop=(c == 1)))
                mm(fns, reads=["wukv"] + CKR, writes=[pbn[bk]])
                op('dve', lambda e: e.tensor_copy(vt[:, k0:k0 + nb_, 0:64], pb[bk][:, 0:nb_ * 64].rearrange("p (j d) -> p j d", j=nb_)),
                   reads=[pbn[bk]], writes=[vtn])
            def gen_qb(sbi, h=h, qt=qt, qn=qn):
                r = sbi % 2
                ba, bb_ = 2 * r, 2 * r + 1
                rf = lambda c: cqnT[:, c, sbi * 512:(sbi + 1) * 512]
                lin(ba, wuq, wuqn, h * 96, 96, sbi, kch=3, rhs_fn=rf, extra_reads=CQR)
                yield None
                lin(bb_, wuqp, wuqpn, h * 96, 96, sbi, kch=3, rhs_fn=rf, extra_reads=CQR)
                yield None
                op('act', lambda e: e.copy(qt[0:64, sbi * 512:(sbi + 1) * 512], pb[ba][0:64, :]), reads=[pbn[ba]], writes=[(qn, sbi)])
                yield None
                op('dve', lambda e: e.tensor_tensor(t1[r][64:96, :], pb[ba][64:96, :], ropeC[64:96, sbi * 512:(sbi + 1) * 512], ALU.mult), reads=[pbn[ba], "ropeC"], writes=["t1_%d" % r])
                yield None
                op('dve', lambda e: e.tensor_tensor(t2[r][64:96, :], pb[bb_][64:96, :], ropeS[64:96, sbi * 512:(sbi + 1) * 512], ALU.mult), reads=[pbn[bb_], "ropeS"], writes=["t2_%d" % r])
                yield None
                op('pool', lambda e: e.tensor_tensor(qt[64:96, sbi * 512:(sbi + 1) * 512], t1[r][64:96, :], t2[r][64:96, :], ALU.add), reads=["t1_%d" % r, "t2_%d" % r], writes=[(qn, sbi, 'pe')])
                yield None
            run_pairs(gen_qb, NSB)
            groups = []
            MS = 96.0 ** -0.5
            for sbi in range(NSB):
                klist = [(j, None, False) for j in range(NKB)]

                def kfn(kblk, kt=kt, ktn=ktn):
                    return kt[0:104, kblk * 128:(kblk + 1) * 128], ktn

                def vfn(kblk, vt=vt, vtn=vtn):
                    return vt[:, kblk, :], vtn

                def fin(ob, h=h, sbi=sbi):
                    r = cnt['rot'] % 2
                    cnt['rot'] += 1
                    op('dve', lambda e: e.reciprocal(rr[r][64:128, :], pb[ob][64:128, :]), reads=[pbn[ob]], writes=["rr0"])
                    op('pool', lambda e: e.tensor_copy(r3[r][0:64, :], rr[r][64:128, :]), reads=["rr0"], writes=["r3_0"])
                    po = (h % 2) * 64
                    op('dve', lambda e: e.tensor_tensor(ozT[po:po + 64, h // 2, sbi * 512:(sbi + 1) * 512], pb[ob][0:64, :], r3[r][0:64, :], ALU.mult),
                       reads=[pbn[ob], "r3_0"], writes=[("ozT", h // 2, sbi)])

                groups.append(dict(qtile=qt, qname=qn, sbi=sbi, kfn=kfn, klist=klist, vfn=vfn, K=104, scale=MS,
                                   bias_fn=(lambda isctx: -MS * BIGM), fin=fin))
            attend_stream(groups)
        kb.mark('L%d B-merge' % l)
        zmul_and_merge(C_ZB, I["w_branch_b"][l], C_G + 1024, False)
        tap_big("ozB%d" % l, ozT, 4, ["ozT"])
        tap_big("mergeB%d" % l, mergeT, 8, ["mergeT"])
        if stop == "B":
            return

        kb.mark('L%d C' % l)
        kb.barrier()
        _gdn(kb, I, O, l, dict(arena=arena, pb=pb, pbn=pbn, hT=hT, ozT=ozT, HR=HR, load_w=load_w, lin=lin, ident=ident, ident32=ident32,
                               flags=flags, ones16=ones16, wukv=wukv, run_pairs=run_pairs, xn=xn, xblk=xblk, t1=t1, t2=t2, sig=sig, junk=junk, Win=Win, TAP=TAP, stat=stat, st2=st2, kvn16=kvn16, rr=rr))
        tap_big("ozC%d" % l, ozT, 4, ["ozT"])
        kb.mark('L%d C-merge' % l)
        wo = []

        def _prefetch_wo():
            for ch in range(2):
                wo.append(load_w(I["w_out"][l], 8, [(ch * 512, 512)]))
        zmul_and_merge(C_ZC, I["w_branch_c"][l], C_G + 2048, False, after_loads=_prefetch_wo)
        tap_big("mergeC%d" % l, mergeT, 8, ["mergeT"])
        if stop == "C":
            return

        kb.mark('L%d out' % l)
        MR = [("mergeT", c, s) for c in range(8) for s in range(NSB)]
        if l == 1:
            dma('sp', gbc[:].rearrange("p a b -> p (a b)"), I["final_norm_g"].partition_broadcast(128), writes=["gbc"])
        def gen_out(b):
            xb, xbn = xblk[b % 2], "xblk%d" % (b % 2)
            dma('sp', xb[:], xsrc[b * 128:(b + 1) * 128, :], reads=(["xs"] if l == 1 else []), writes=[xbn])
            yield None
            for ch in range(2):
                bk = 4 + 2 * (b % 2) + ch
                wv, wn = wo[ch]
                mm([(lambda e, c=c, wv=wv: e.matmul(pb[bk][:], mergeT[:, c, b * 128:(b + 1) * 128], wv[:, c, :], start=(c == 0), stop=(c == 7))) for c in range(8)],
                   reads=[wn] + MR, writes=[pbn[bk]])
                yield None
                tt_, ttn_ = ((t1[b % 2], "t1_%d" % (b % 2)) if ch == 0 else (t2[b % 2], "t2_%d" % (b % 2)))
                op('dve', lambda e: e.tensor_tensor(tt_[:], pb[bk][:], gateb[:, ch * 512:(ch + 1) * 512], ALU.mult), reads=[pbn[bk], ("gateb", ch)], writes=[ttn_])
                yield None
                op('pool', lambda e: e.tensor_tensor(xb[:, ch * 512:(ch + 1) * 512], xb[:, ch * 512:(ch + 1) * 512], tt_[:], ALU.add), reads=[ttn_, xbn], writes=[xbn])
                yield None
            if l == 0:
                dma('sp', xs[b * 128:(b + 1) * 128, :], xb[:], reads=[xbn], writes=["xs"])
                yield None
            else:
                op('act', lambda e: e.activation(junk[:], xb[:], AF.Square, accum_out=stat[:, b, 0:1]), reads=[xbn], writes=["junk", ("stat", b)])
                yield None
                op('dve', lambda e: e.tensor_scalar(stat[:, b, 1:2], stat[:, b, 0:1], 1.0 / D, 1e-6, ALU.mult, ALU.add), reads=[("stat", b)], writes=[("stat", b)])
                yield None
                op('act', lambda e: e.activation(stat[:, b, 2:3], stat[:, b, 1:2], AF.Ln), reads=[("stat", b)], writes=[("stat", b)])
                yield None
                op('act', lambda e: e.activation(stat[:, b, 3:4], stat[:, b, 2:3], AF.Exp, scale=-0.5), reads=[("stat", b)], writes=[("stat", b)])
                yield None
                op('dve', lambda e: e.scalar_tensor_tensor(xb[:], xb[:], stat[:, b, 3:4], gbc[:].rearrange("p a b -> p (a b)"), ALU.mult, ALU.mult), reads=[xbn, ("stat", b), "gbc"], writes=[xbn])
                yield None
                dma('sp', O["y"][b * 128:(b + 1) * 128, :], xb[:], reads=[xbn])
                yield None
        run_pairs(gen_out, NB)

def _gdn(kb, I, O, l, env):
    op, dma, mm = kb.op, kb.dma, kb.mmgroup
    arena, pb, pbn, hT, ozT, HR = env["arena"], env["pb"], env["pbn"], env["hT"], env["ozT"], env["HR"]
    load_w, lin, ident, ident32, flags = env["load_w"], env["lin"], env["ident"], env["ident32"], env["flags"]
    xblk, t1, t2, sig, junk, Win, TAP = env["xblk"], env["t1"], env["t2"], env["sig"], env["junk"], env["Win"], env["TAP"]
    st2, kvn16 = env["st2"], env["kvn16"]
    pos = [0]

    def carve(n_units, dt, shape_str=None, **kw):
        a = arena[:, pos[0]:pos[0] + n_units]
        pos[0] += n_units
        if dt == F32:
            a = a.bitcast(F32)
        if shape_str:
            a = a.rearrange(shape_str, **kw)
        return a
    qkvh = carve(3 * T, BF16, "p (c n) -> p c n", c=3)
    oacc = carve(NB * 128, BF16, "p (b d) -> p b d", b=NB)
    ab = carve(2 * NB * 16, F32, "p (b d) -> p b d", b=NB)
    names = ["g", "bt", "nbt", "gam", "ngam", "egam", "begam", "edel", "dec1", "dec2", "gt1", "gt2"]
    stt_ = {n: carve(2 * NB * 8, F32, "p (b d) -> p b d", b=NB) for n in names}
    S = carve(2 * 2 * 128, F32, "p (s d) -> p s d", s=2)
    Sbf = carve(2 * 128, BF16, "p (s d) -> p s d", s=2)
    bt_names = ["Kbg", "Kd0", "Kd1", "Vb", "Qg", "M0", "MTa", "MTb", "Ma", "Mb", "PTb", "Wt", "Wn", "U", "CT", "attnT", "Mc"]
    B = {n: carve(256, BF16, "p (s d) -> p s d", s=2) for n in bt_names}
    X = carve(512, BF16, "p (s h d) -> p s h d", s=2, h=2)
    wk_ = env["wukv"][:].rearrange("p a b -> p (a b)")
    B1 = {}
    for q_, n in enumerate(bt_names):
        if q_ < 6:
            B1[n] = wk_[:, 512 + q_ * 256:512 + (q_ + 1) * 256].rearrange("p (s d) -> p s d", s=2)
        else:
            B1[n] = carve(256, BF16, "p (s d) -> p s d", s=2)
    X1 = wk_[:, 0:512].rearrange("p (s h d) -> p s h d", s=2, h=2)
    cw = carve(2 * 36, F32, "p (c j) -> p c j", c=12)
    nw = carve(2 * 36, F32, "p (c j) -> p c j", c=12)
    pw = carve(2 * 36, F32, "p (c j) -> p c j", c=12)
    gnb = carve(2 * 128, F32)
    dtb = carve(2 * 8, F32)
    negA = carve(2 * 8, F32)
    onorm = carve(128, BF16)
    ident2 = carve(256, BF16, "p (s d) -> p s d", s=2)
    assert pos[0] <= 21952, pos[0]
    tri = t1[0][:].rearrange("p (k n) -> p k n", k=4)
    gmc = t1[1][:].rearrange("p (k s n) -> p k s n", k=2, s=2)
    Grhs = t2[0][:, 0:256].rearrange("p (s n) -> p s n", s=2)
    ones32 = t2[0][:, 256:384]
    Em = t2[1][:].rearrange("p (k s n) -> p k s n", k=2, s=2)
    accs = [xblk[0], xblk[1]]
    SETS = [dict(B=B, X=X, Grhs=Grhs, Em=Em, grn="Grhs0", emn="Em0", sx="", banks=(0, 1, 2, 3)),
            dict(B=B1, X=X1, Grhs=sig[0][:, 0:256].rearrange("p (s n) -> p s n", s=2),
                 Em=sig[1][:].rearrange("p (k s n) -> p k s n", k=2, s=2), grn="sig0", emn="sig1", sx="_1", banks=(4, 5, 6, 7))]

    for k_, nm in enumerate(["triF", "triB", "sel1", "sel2"]):
        dma('sp', tri[:, k_, :], I[nm], writes=["t1_0"])
    for k_, nm in enumerate(["gm1", "gm2"]):
        for s_ in range(2):
            dma('sp', gmc[:, k_, s_, :], I[nm][:, 4 * s_, :], writes=["t1_1"])
    op('dve', lambda e: e.memset(ones32, 1.0), writes=["ones32"])
    op('dve', lambda e: e.memset(B1["Kd0"][:], 0.0), writes=["Kd0_1"])
    op('dve', lambda e: e.memset(B1["Kd1"][:], 0.0), writes=["Kd1_1"])
    op('dve', lambda e: e.memset(B["Kd0"][:], 0.0), writes=["Kd0"])
    for s_ in range(2):
        op('dve', lambda e: e.tensor_copy(ident2[:, s_, :], ident[:]), reads=["ident"], writes=["ident2"])
    op('dve', lambda e: e.memset(B["Kd1"][:], 0.0), writes=["Kd1"])
    for j_ in range(3):
        dma('sp', cw[:, :, j_], I["gdn_conv"][l][j_].rearrange("(c p) -> p c", p=128), writes=["cw"], allow_slow_non_contiguous=True)
    dma('sp', gnb, I["gdn_norm"][l].partition_broadcast(128), writes=["gnb"])
    dma('sp', dtb, I["gdn_dt_bias"][l].partition_broadcast(128), writes=["dtb"])
    dma('sp', negA, I["gdn_a_log"][l].partition_broadcast(128), writes=["negA"])
    op('act', lambda e: e.activation(negA, negA, AF.Exp), reads=["negA"], writes=["negA"])
    op('dve', lambda e: e.tensor_scalar(negA, negA, -1.0, None, ALU.mult), reads=["negA"], writes=["negA"])
    op('dve', lambda e: e.tensor_scalar(nw, cw, flags[:, 2:3], None, ALU.mult), reads=["cw", "flags"], writes=["nw"])
    op('dve', lambda e: e.tensor_tensor(pw, cw, nw, ALU.add), reads=["cw", "nw"], writes=["pw"])

    wab, wabn = load_w(Win, 8, [(C_CA, 16)])
    for b in range(NB):
        mm([(lambda e, c=c: e.matmul(pb[0][:, b * 16:(b + 1) * 16], hT[:, c, b * 128:(b + 1) * 128], wab[:, c, :], start=(c == 0), stop=(c == 7))) for c in range(8)],
           reads=[wabn] + HR, writes=[pbn[0]])
    op('dve', lambda e: e.tensor_copy(ab, pb[0][:, 0:256].rearrange("p (b d) -> p b d", b=NB)), reads=[pbn[0]], writes=["ab"])
    g, bt, nbt, gam, ngam, egam, begam, edel, dec1, dec2, gt1, gt2 = [stt_[n] for n in names]
    bc8 = lambda a: a.unsqueeze(1).to_broadcast([128, NB, 8])
    op('dve', lambda e: e.tensor_tensor(g, ab[:, :, 0:8], bc8(dtb), ALU.add), reads=["ab", "dtb"], writes=["g"])
    op('act', lambda e: e.activation(g, g, AF.Exp), reads=["g"], writes=["g"])
    op('act', lambda e: e.activation(g, g, AF.Ln, bias=1.0), reads=["g"], writes=["g"])
    op('dve', lambda e: e.tensor_tensor(g, g, bc8(negA), ALU.mult), reads=["g", "negA"], writes=["g"])
    op('act', lambda e: e.activation(bt, ab[:, :, 8:16], AF.Sigmoid), reads=["ab"], writes=["bt"])
    op('dve', lambda e: e.tensor_scalar(nbt, bt, -1.0, None, ALU.mult), reads=["bt"], writes=["nbt"])
    for b in range(NB):
        mm([lambda e: e.matmul(pb[1][:, b * 8:b * 8 + 4], tri[:, 0, :], g[:, b, 0:4], start=True, stop=True),
            lambda e: e.matmul(pb[1][:, b * 8 + 4:b * 8 + 8], tri[:, 1, :], g[:, b, 4:8], start=True, stop=True)], reads=["t1_0", "g"], writes=[pbn[1]])
        mm([lambda e: e.matmul(pb[2][:, b * 8:b * 8 + 8], tri[:, 2, :], g[:, b, :], start=True, stop=True)], reads=["t1_0", "g"], writes=[pbn[2]])
        mm([lambda e: e.matmul(pb[3][:, b * 8:b * 8 + 8], tri[:, 3, :], g[:, b, :], start=True, stop=True)], reads=["t1_0", "g"], writes=[pbn[3]])
    v3 = lambda bank: pb[bank][:, 0:128].rearrange("p (b d) -> p b d", b=NB)
    op('dve', lambda e: e.tensor_copy(gam, v3(1)), reads=[pbn[1]], writes=["gam"])
    op('dve', lambda e: e.tensor_copy(gt1, v3(2)), reads=[pbn[2]], writes=["gt1"])
    op('dve', lambda e: e.tensor_copy(gt2, v3(3)), reads=[pbn[3]], writes=["gt2"])
    op('dve', lambda e: e.tensor_scalar(ngam, gam, -1.0, None, ALU.mult), reads=["gam"], writes=["ngam"])
    op('act', lambda e: e.activation(egam, gam, AF.Exp), reads=["gam"], writes=["egam"])
    op('dve', lambda e: e.tensor_tensor(begam, egam, bt, ALU.mult), reads=["egam", "bt"], writes=["begam"])
    op('dve', lambda e: e.tensor_tensor(edel[0:64], gt1[0:64], gam[0:64], ALU.subtract), reads=["gt1", "gam"], writes=["edel"])
    op('dve', lambda e: e.tensor_tensor(edel[64:128], gt2[64:128], gam[64:128], ALU.subtract), reads=["gt2", "gam"], writes=["edel"])
    op('act', lambda e: e.activation(edel, edel, AF.Exp), reads=["edel"], writes=["edel"])
    op('act', lambda e: e.activation(dec1, gt1, AF.Exp), reads=["gt1"], writes=["dec1"])
    op('act', lambda e: e.activation(dec2, gt2, AF.Exp), reads=["gt2"], writes=["dec2"])
    if "gstats%d" % l in TAP:
        for k_, n in enumerate(["g", "bt", "gam", "edel", "dec1", "dec2"]):
            dma('sp', TAP["gstats%d" % l][k_], stt_[n], reads=[n])

    for h in range(4):
        kb.mark('L%d C h%d conv' % (l, h))
        if h == 0:
            nxt_wc = load_w(Win, 8, [(C_CQKV + h * 128, 128), (C_CQKV + 512 + h * 128, 128), (C_CQKV + 1024 + h * 128, 128)])
        wc, wcn = nxt_wc
        for ci in range(3):
            ch = ci * 4 + h
            for sbi in range(NSB):
                lin(sbi, wc, wcn, ci * 128, 128, sbi)
            for sbi in range(NSB):
                acc = accs[sbi // 2][:, (sbi % 2) * 512:(sbi % 2 + 1) * 512]
                an = "xblk%d" % (sbi // 2)
                P_ = pb[sbi]
                rd = [pbn[sbi], "cw", "nw", "pw"]
                op('dve', lambda e: e.tensor_scalar(acc, P_[:], cw[:, ch, 1:2], None, ALU.mult), reads=rd, writes=[an])
                op('dve', lambda e: e.scalar_tensor_tensor(acc[:, 1:512], P_[:, 0:511], cw[:, ch, 0:1], acc[:, 1:512], ALU.mult, ALU.add), reads=rd + [an], writes=[an])
                op('dve', lambda e: e.scalar_tensor_tensor(acc[:, 0:511], P_[:, 1:512], cw[:, ch, 2:3], acc[:, 0:511], ALU.mult, ALU.add), reads=rd + [an], writes=[an])
                op('dve', lambda e: e.scalar_tensor_tensor(acc[:, 256:257], P_[:, 255:256], nw[:, ch, 0:1], acc[:, 256:257], ALU.mult, ALU.add), reads=rd + [an], writes=[an])
                op('dve', lambda e: e.scalar_tensor_tensor(acc[:, 255:256], P_[:, 256:257], nw[:, ch, 2:3], acc[:, 255:256], ALU.mult, ALU.add), reads=rd + [an], writes=[an])
                if sbi > 0:
                    op('dve', lambda e: e.scalar_tensor_tensor(acc[:, 0:1], pb[sbi - 1][:, 511:512], pw[:, ch, 0:1], acc[:, 0:1], ALU.mult, ALU.add),
                       reads=rd + [an, pbn[sbi - 1]], writes=[an])
                if sbi < NSB - 1:
                    op('dve', lambda e: e.scalar_tensor_tensor(acc[:, 511:512], pb[sbi + 1][:, 0:1], pw[:, ch, 2:3], acc[:, 511:512], ALU.mult, ALU.add),
                       reads=rd + [an, pbn[sbi + 1]], writes=[an])
            def gen_l2(sbi, ci=ci):
                acc = accs[sbi // 2][:, (sbi % 2) * 512:(sbi % 2 + 1) * 512]
                an = ("xblk%d" % (sbi // 2), sbi % 2)
                dst = qkvh[:, ci, sbi * 512:(sbi + 1) * 512]
                p_ = sbi % 2
                sqb, sqn = ((sig[0][:].bitcast(BF16)[:, 0:512], "sig0") if p_ == 0 else (env["rr"][0][:].bitcast(BF16)[:, 0:512], "rr0"))
                rvb, rvn = ((sig[1][:], "sig1") if p_ == 0 else (t2[1][:], "Em0"))
                bk_ = 4 + p_
                if ci == 2:
                    op('act', lambda e: e.activation(dst, acc, AF.Silu), reads=[an], writes=[("qkvh", ci, sbi)])
                    yield None
                else:
                    op('act', lambda e: e.activation(acc, acc, AF.Silu), reads=[an], writes=[an])
                    yield None
                    op('pool', lambda e: e.tensor_tensor(sqb, acc, acc, ALU.mult), reads=[an], writes=[sqn])
                    yield None
                    mm([lambda e: e.matmul(pb[bk_][:], env["ones16"][:], sqb, start=True, stop=True)], reads=[sqn, "ones16"], writes=[pbn[bk_]])
                    yield None
                    op('act', lambda e: e.activation(rvb, pb[bk_][:], AF.Ln, bias=1e-6), reads=[pbn[bk_]], writes=[rvn])
                    yield None
                    op('act', lambda e: e.activation(rvb, rvb, AF.Exp, scale=-0.5, bias=(-0.5 * float(np.log(128.0)) if ci == 0 else 0.0)), reads=[rvn], writes=[rvn])
                    yield None
                    op('dve', lambda e: e.tensor_tensor(dst, acc, rvb, ALU.mult), reads=[an, rvn], writes=[("qkvh", ci, sbi)])
                    yield None
            env["run_pairs"](gen_l2, NSB)
        if h == 0 and "qkvh%d" % l in TAP:
            for ci in range(3):
                for hf in range(2):
                    op('dve', lambda e: e.tensor_copy(xblk[0][:], qkvh[:, ci, hf * 1024:(hf + 1) * 1024]), reads=["qkvh"], writes=["xblk0"])
                    dma('sp', TAP["qkvh%d" % l][:, ci, hf * 1024:(hf + 1) * 1024], xblk[0][:], reads=["xblk0"])
        qT_, kT_, vT_ = qkvh[:, 0, :], qkvh[:, 1, :], qkvh[:, 2, :]
        QK = ["qkvh"]
        kb.mark('L%d C h%d scan' % (l, h))
        if h < 3:
            nxt_wc = load_w(Win, 8, [(C_CQKV + (h + 1) * 128, 128), (C_CQKV + 512 + (h + 1) * 128, 128), (C_CQKV + 1024 + (h + 1) * 128, 128)])
        op('pool', lambda e: e.memset(oacc, 0.0), writes=["oacc"])
        def gen_iter(i, SET):
            B, X, Grhs, Em = SET['B'], SET['X'], SET['Grhs'], SET['Em']
            grn, emn, sx = SET['grn'], SET['emn'], SET['sx']
            b0, b1, b2, b3 = SET['banks']
            N = lambda n_: n_ + sx
            slots = [(i, h, 0), (NB - 1 - i, 4 + h, 1)]
            for s_, (blk, dh, d_) in enumerate(slots):
                if i == 0:
                    dma('sp', S[:, s_, :], I["st"][l, dh], writes=[("S", s_)])
                    op('act', lambda e: e.copy(Sbf[:, s_, :], S[:, s_, :]), reads=[("S", s_)], writes=[("Sbf", s_)])
                elif i % 2 == 0:
                    op('dve', lambda e: e.tensor_scalar(S[:, s_, :], S[:, s_, :], flags[:, 0:1], None, ALU.mult), reads=[("S", s_), "flags"], writes=[("S", s_)])
                    op('act', lambda e: e.copy(Sbf[:, s_, :], S[:, s_, :]), reads=[("S", s_)], writes=[("Sbf", s_)])
            yield None
            pv = pb[b0][:].bitcast(BF16)
            fns = []
            for s_, (blk, dh, d_) in enumerate(slots):
                for k_, src in enumerate([kT_, vT_, qT_]):
                    fns.append(lambda e, s_=s_, k_=k_, src=src, blk=blk: e.transpose(pv[:, (k_ * 2 + s_) * 128:(k_ * 2 + s_ + 1) * 128], src[:, blk * 128:(blk + 1) * 128], ident[:]))
            mm(fns, reads=QK + ["ident"], writes=[pbn[b0]])
            yield None
            for s_, (blk, dh, d_) in enumerate(slots):
                for nm, k_, sc in [("Kbg", 0, begam), ("Vb", 1, bt), ("Qg", 2, egam)]:
                    op('act', lambda e: e.activation(B[nm][:, s_, :], pv[:, (k_ * 2 + s_) * 128:(k_ * 2 + s_ + 1) * 128], AF.Copy, scale=sc[:, blk, dh:dh + 1]),
                       reads=[pbn[b0], "begam", "edel", "bt", "egam"], writes=[(N(nm), s_)])
                    yield None
                for hf_ in range(2):
                    R_ = slice(hf_ * 64, (hf_ + 1) * 64)
                    op('act', lambda e: e.activation(B["Kd%d" % hf_][R_, s_, :], pv[R_, s_ * 128:(s_ + 1) * 128], AF.Copy, scale=edel[R_, blk, dh:dh + 1]),
                       reads=[pbn[b0], "edel"], writes=[(N("Kd%d" % hf_), s_)])
                    yield None
            fns = []
            for s_, (blk, dh, d_) in enumerate(slots):
                ks = kT_[:, blk * 128:(blk + 1) * 128]
                qs = qT_[:, blk * 128:(blk + 1) * 128]
                fns.append(lambda e, s_=s_, ks=ks: e.matmul(pb[b1][:, s_ * 128:(s_ + 1) * 128], ks, ks, start=True, stop=True))
                fns.append(lambda e, s_=s_, ks=ks, qs=qs: e.matmul(pb[b1][:, (2 + s_) * 128:(3 + s_) * 128], ks, qs, start=True, stop=True))
            mm(fns, reads=QK, writes=[pbn[b1]])
            yield None
            for s_, (blk, dh, d_) in enumerate(slots):
                op('dve', lambda e: e.tensor_scalar(Grhs[:, s_, :], ident32[:], ngam[:, blk, dh:dh + 1], None, ALU.mult), reads=["ident32", "ngam"], writes=[(grn, s_)])
                yield None
            mm([lambda e: e.matmul(pb[b2][:, 0:256], ones32, Grhs.rearrange("p s n -> p (s n)"), start=True, stop=True)], reads=[grn, "ones32"], writes=[pbn[b2]])
            yield None
            p2 = pb[b2][:, 0:256].rearrange("p (s n) -> p s n", s=2)
            op('dve', lambda e: e.tensor_tensor(Em[:, 0], p2, gmc[:, 0], ALU.add), reads=[pbn[b2], "t1_1"], writes=[(emn, 0)])
            yield None
            op('dve', lambda e: e.scalar_tensor_tensor(Em[:, 1], p2, -1.0, gmc[:, 1], ALU.mult, ALU.add), reads=[pbn[b2], "t1_1"], writes=[(emn, 1)])
            yield None
            for s_, (blk, dh, d_) in enumerate(slots):
                op('act', lambda e: e.activation(Em[:, 0, s_, :], Em[:, 0, s_, :], AF.Exp, bias=gam[:, blk, dh:dh + 1]), reads=[(emn, 0), "gam"], writes=[(emn, 0)])
                yield None
                op('act', lambda e: e.activation(Em[:, 1, s_, :], Em[:, 1, s_, :], AF.Exp, bias=ngam[:, blk, dh:dh + 1]), reads=[(emn, 1), "ngam"], writes=[(emn, 1)])
                yield None
                op('dve', lambda e: e.scalar_tensor_tensor(B["M0"][:, s_, :], pb[b1][:, s_ * 128:(s_ + 1) * 128], nbt[:, blk, dh:dh + 1], Em[:, 0, s_, :], ALU.mult, ALU.mult),
                   reads=[pbn[b1], "nbt", (emn, 0)], writes=[(N("M0"), s_)])
                yield None
            op('dve', lambda e: e.tensor_tensor(B["attnT"][:], pb[b1][:, 256:512].rearrange("p (s n) -> p s n", s=2), Em[:, 1], ALU.mult), reads=[pbn[b1], (emn, 1)], writes=[N("attnT")])
            yield None
            M0 = B["M0"]
            mm([lambda e, s_=s_: e.matmul(pb[b1][:, s_ * 128:(s_ + 1) * 128], M0[:, s_, :], ident[:], start=True, stop=True) for s_ in range(2)], reads=[N("M0"), "ident"], writes=[pbn[b1]])
            yield None
            op('act', lambda e: e.copy(B["MTa"][:], pb[b1][:, 0:256].rearrange("p (s n) -> p s n", s=2)), reads=[pbn[b1]], writes=[N("MTa")])
            yield None
            fns = [lambda e: e.matmul(pb[b3][:, 0:256], ident[:], ident2[:].rearrange("p s d -> p (s d)"), start=True, stop=False)]
            for s_ in range(2):
                fns.append(lambda e, s_=s_: e.matmul(pb[b3][:, s_ * 128:(s_ + 1) * 128], M0[:, s_, :], ident[:], start=False, stop=True))
            mm(fns, reads=[N("M0"), "ident", "ident2"], writes=[pbn[b3]])
            yield None
            Mprev, MTprev, Mn, MTn = "M0", "MTa", "Ma", "MTb"
            pend = None

            def pt_update(mname):
                op('act', lambda e: e.copy(B["PTb"][:], pb[b3][:, 0:256].rearrange("p (s n) -> p s n", s=2)), reads=[pbn[b3]], writes=[N("PTb")])
                mm([lambda e, s_=s_: e.matmul(pb[b3][:, s_ * 128:(s_ + 1) * 128], B[mname][:, s_, :], B["PTb"][:, s_, :], start=False, stop=True) for s_ in range(2)],
                   reads=[N(mname), N("PTb")], writes=[pbn[b3]])
            MN3 = ["Ma", "Mb", "Mc"]
            for k_ in range(1, 6):
                Mn = MN3[k_ % 3]
                mm([lambda e, s_=s_: e.matmul(pb[b2][:, s_ * 128:(s_ + 1) * 128], B[MTprev][:, s_, :], B[Mprev][:, s_, :], start=True, stop=True) for s_ in range(2)],
                   reads=[N(MTprev), N(Mprev)], writes=[pbn[b2]])
                yield None
                if k_ < 5:
                    mm([lambda e, s_=s_: e.matmul(pb[b1][:, s_ * 128:(s_ + 1) * 128], B[Mprev][:, s_, :], B[MTprev][:, s_, :], start=True, stop=True) for s_ in range(2)],
                       reads=[N(MTprev), N(Mprev)], writes=[pbn[b1]])
                    yield None
                if pend is not None:
                    pt_update(pend)
                    yield None
                op('act', lambda e: e.copy(B[Mn][:], pb[b2][:, 0:256].rearrange("p (s n) -> p s n", s=2)), reads=[pbn[b2]], writes=[N(Mn)])
                yield None
                if k_ < 5:
                    op('dve', lambda e: e.tensor_copy(B[MTn][:], pb[b1][:, 0:256].rearrange("p (s n) -> p s n", s=2)), reads=[pbn[b1]], writes=[N(MTn)])
                    yield None
                pend = Mn
                Mprev, MTprev, MTn = Mn, MTn, ("MTa" if MTn == "MTb" else "MTb")
            pt_update(pend)
            yield None
            op('act', lambda e: e.copy(B["PTb"][:], pb[b3][:, 0:256].rearrange("p (s n) -> p s n", s=2)), reads=[pbn[b3]], writes=[N("PTb")])
            yield None
            mm([lambda e, s_=s_: e.matmul(pb[b2][:, s_ * 128:(s_ + 1) * 128], M0[:, s_, :], B["PTb"][:, s_, :], start=True, stop=True) for s_ in range(2)],
               reads=[N("M0"), N("PTb")], writes=[pbn[b2]])
            yield None
            mm([lambda e, s_=s_: e.matmul(pb[b1][:, s_ * 128:(s_ + 1) * 128], B["PTb"][:, s_, :], ident[:], start=True, stop=True) for s_ in range(2)],
               reads=[N("PTb"), "ident"], writes=[pbn[b1]])
            yield None
            op('dve', lambda e: e.scalar_tensor_tensor(Em[:, 0], B["PTb"][:], -1.0, pb[b2][:, 0:256].rearrange("p (s n) -> p s n", s=2), ALU.mult, ALU.add),
               reads=[pbn[b2], N("PTb")], writes=[(emn, 0)])
            yield None
            op('act', lambda e: e.copy(B["Mb"][:], pb[b1][:, 0:256].rearrange("p (s n) -> p s n", s=2)), reads=[pbn[b1]], writes=[N("Mb")])
            yield None
            op('dve', lambda e: e.tensor_tensor(B["Ma"][:], Em[:, 0], ident2[:], ALU.add), reads=[(emn, 0), "ident2"], writes=[N("Ma")])
            yield None
            fns = [lambda e: e.matmul(pb[b3][:, 0:256], ident[:], B["PTb"][:].rearrange("p s d -> p (s d)"), start=True, stop=False)]
            for s_ in range(2):
                fns.append(lambda e, s_=s_: e.matmul(pb[b3][:, s_ * 128:(s_ + 1) * 128], B["Mb"][:, s_, :], B["Ma"][:, s_, :], start=False, stop=True))
            mm(fns, reads=[N("Mb"), N("Ma"), N("PTb"), "ident"], writes=[pbn[b3]])
            yield None
            op('act', lambda e: e.copy(B["Wt"][:], pb[b3][:, 0:256].rearrange("p (s n) -> p s n", s=2)), reads=[pbn[b3]], writes=[N("Wt")])
            yield None
            op('pool', lambda e: e.tensor_copy(B["PTb"][:], B["Wt"][:]), reads=[N("Wt")], writes=[N("PTb")])
            yield None
            AT = B["PTb"]
            fns = []
            for s_ in range(2):
                fns.append(lambda e, s_=s_: e.matmul(pb[b0][:, s_ * 128:(s_ + 1) * 128], AT[:, s_, :], B["Kbg"][:, s_, :], start=True, stop=True))
                fns.append(lambda e, s_=s_: e.matmul(pb[b0][:, (2 + s_) * 128:(3 + s_) * 128], AT[:, s_, :], B["Vb"][:, s_, :], start=True, stop=True))
            mm(fns, reads=[N("PTb"), N("Kbg"), N("Vb")], writes=[pbn[b0]])
            yield None
            p6a = pb[b0][:, 0:256].rearrange("p (s n) -> p s n", s=2)
            p6b = pb[b0][:, 256:512].rearrange("p (s n) -> p s n", s=2)
            op('act', lambda e: e.copy(B["Wt"][:], p6a), reads=[pbn[b0]], writes=[N("Wt")])
            yield None
            op('act', lambda e: e.activation(B["Wn"][:], p6a, AF.Copy, scale=-1.0), reads=[pbn[b0]], writes=[N("Wn")])
            yield None
            op('act', lambda e: e.copy(B["U"][:], p6b), reads=[pbn[b0]], writes=[N("U")])
            yield None
            fns = []
            for s_ in range(2):
                for hf in range(2):
                    fns.append(lambda e, s_=s_, hf=hf: e.matmul(pb[b0][:, (s_ * 2 + hf) * 128:(s_ * 2 + hf + 1) * 128], B["Wt"][:, s_, :], B["Kd%d" % hf][:, s_, :], start=True, stop=True))
            mm(fns, reads=[N("Wt"), N("Kd0"), N("Kd1")], writes=[pbn[b0]])
            yield None
            op('act', lambda e: e.activation(X, pb[b0][:].rearrange("p (s h n) -> p s h n", s=2, h=2), AF.Copy, scale=-1.0), reads=[pbn[b0]], writes=[N("X")])
            yield None
            fns = []
            for s_ in range(2):
                fns.append(lambda e, s_=s_: e.matmul(pb[b1][:, s_ * 128:(s_ + 1) * 128], B["Qg"][:, s_, :], ident[:], start=True, stop=False))
                fns.append(lambda e, s_=s_: e.matmul(pb[b1][:, s_ * 128:(s_ + 1) * 128], B["Wn"][:, s_, :], B["attnT"][:, s_, :], start=False, stop=True))
            mm(fns, reads=[N("Qg"), N("Wn"), N("attnT"), "ident"], writes=[pbn[b1]])
            yield None
            op('act', lambda e: e.copy(B["CT"][:], pb[b1][:, 0:256].rearrange("p (s n) -> p s n", s=2)), reads=[pbn[b1]], writes=[N("CT")])
            yield 'SCAN'
            for step in range(2):
                for s_, (blk, dh, d_) in enumerate(slots):
                    hf = step if d_ == 0 else 1 - step
                    R = slice(hf * 64, (hf + 1) * 64)
                    mm([lambda e: e.matmul(pb[b1][:, s_ * 128:(s_ + 1) * 128], B["attnT"][:, s_, :], B["U"][:, s_, :], start=True, stop=False),
                        lambda e: e.matmul(pb[b1][:, s_ * 128:(s_ + 1) * 128], B["CT"][:, s_, :], Sbf[:, s_, :], start=False, stop=True)],
                       reads=[N("attnT"), N("U"), N("CT"), ("Sbf", s_)], writes=[pbn[b1]])
                    yield None
                    op('dve', lambda e: e.tensor_tensor(oacc[R, blk, :], oacc[R, blk, :], pb[b1][R, s_ * 128:(s_ + 1) * 128], ALU.add), reads=[pbn[b1], ("oacc", blk)], writes=[("oacc", blk)])
                    yield None
                    mm([lambda e: e.matmul(pb[b2][:, s_ * 128:(s_ + 1) * 128], B["Kd%d" % hf][:, s_, :], B["U"][:, s_, :], start=True, stop=False),
                        lambda e: e.matmul(pb[b2][:, s_ * 128:(s_ + 1) * 128], X[:, s_, hf, :], Sbf[:, s_, :], start=False, stop=True)],
                       reads=[N("Kd0"), N("Kd1"), N("U"), N("X"), ("Sbf", s_)], writes=[pbn[b2]])
                    yield None
                    dec = dec1 if hf == 0 else dec2
                    op('dve', lambda e: e.scalar_tensor_tensor(S[:, s_, :], S[:, s_, :], dec[:, blk, dh:dh + 1], pb[b2][:, s_ * 128:(s_ + 1) * 128], ALU.mult, ALU.add),
                       reads=[pbn[b2], ("S", s_), "dec1", "dec2"], writes=[("S", s_)])
                    yield None
                    op('pool', lambda e: e.tensor_copy(Sbf[:, s_, :], S[:, s_, :]), reads=[("S", s_)], writes=[("Sbf", s_)])
                    yield None
            for s_, (blk, dh, d_) in enumerate(slots):
                if (d_ == 0 and blk % 2 == 1) or (d_ == 1 and blk % 2 == 0):
                    dma('sp', O["nst"][l, blk // 2, d_, h], S[:, s_, :], reads=[("S", s_)])
            yield None

        for j in range(NB // 2):
            gA = gen_iter(2 * j, SETS[0])
            gB = gen_iter(2 * j + 1, SETS[1])
            dA = dB = False
            while not (dA and dB):
                if not dA:
                    dA = (next(gA) == 'SCAN')
                if not dB:
                    dB = (next(gB) == 'SCAN')
            for _ in gA:
                pass
            for _ in gB:
                pass

        kb.mark('L%d C h%d post' % (l, h))
        for b in range(NB):
            op('act', lambda e: e.activation(junk[:, 0:128], oacc[:, b, :], AF.Square, accum_out=st2[:, b, 0:1]), reads=[("oacc", b)], writes=["junk", ("st2", b)])
        op('dve', lambda e: e.tensor_scalar(st2[:, :, 1:2], st2[:, :, 0:1], 1.0 / 128, 1e-6, ALU.mult, ALU.add), reads=["st2"], writes=["st2"])
        op('act', lambda e: e.activation(st2[:, :, 2:3], st2[:, :, 1:2], AF.Ln), reads=["st2"], writes=["st2"])
        op('act', lambda e: e.activation(st2[:, :, 3:4], st2[:, :, 2:3], AF.Exp, scale=-0.5), reads=["st2"], writes=["st2"])
        for gi_ in range(2):
            stg, stgn = ((junk, "junk") if gi_ == 0 else (env["xn"][0], "xn0"))
            for j_ in range(8):
                b = gi_ * 8 + j_
                op('dve', lambda e: e.scalar_tensor_tensor(stg[:, j_ * 128:(j_ + 1) * 128], oacc[:, b, :], st2[:, b, 3:4], gnb, ALU.mult, ALU.mult),
                   reads=[("oacc", b), "st2", "gnb"], writes=[(stgn, j_)])
            bk = 4 + gi_
            pview = pb[bk][:].bitcast(BF16)
            mm([(lambda e, j_=j_: e.transpose(pview[:, j_ * 128:(j_ + 1) * 128], stg[:, j_ * 128:(j_ + 1) * 128], ident[:])) for j_ in range(8)],
               reads=[stgn, "ident"], writes=[pbn[bk]])
            op('act', lambda e: e.copy(ozT[:, h, gi_ * 1024:(gi_ + 1) * 1024], pview[:, 0:1024]), reads=[pbn[bk]],
               writes=[("ozT", h, 2 * gi_), ("ozT", h, 2 * gi_ + 1)])


def _rope_tables(sample):
    C = np.ones((128, T), np.float32)
    S = np.zeros((128, T), np.float32)
    if not sample:
        C[96:] = 0
        return C, S
    tok = np.arange(T)
    row = (tok // 64).astype(np.float32)
    col = (tok % 64).astype(np.float32)
    def tab(rot):
        npairs = rot // 4
        inv = (10000.0 ** (-np.arange(npairs, dtype=np.float32) / npairs)).astype(np.float32)
        ang = np.concatenate([row[:, None] * inv, col[:, None] * inv], axis=-1).astype(np.float32)
        c = np.cos(ang).astype(np.float32)
        s = np.sin(ang).astype(np.float32)
        Cd = np.repeat(c, 2, axis=1).T
        Sd = np.repeat(s, 2, axis=1).T
        sign = np.where(np.arange(rot) % 2 == 0, -1.0, 1.0).astype(np.float32)[:, None]
        return Cd, Sd * sign
    Ca, Sa = tab(64)
    Cb, Sb = tab(32)
    C[0:64], S[0:64] = Ca, Sa
    C[64:96], S[64:96] = Cb, Sb
    return C, S


def _mask_a(sample):
    m = np.full((6, 128, 512), NEG, np.float32)
    kj = np.arange(128)[:, None]
    qi = np.arange(128)[None, :]
    for o in range(6):
        for qb in range(4):
            blk = m[o, :, qb * 128:(qb + 1) * 128]
            if sample:
                rel = o - 1 - qb
                if rel == 0:
                    blk[:] = 0
                elif rel == -1:
                    blk[kj >= qi] = 0
                elif rel == 1:
                    blk[kj <= qi] = 0
            else:
                if (o - 1) // 2 == qb // 2 and o >= 1:
                    blk[:] = 0
    return m


def _perm_pairs(n):
    idx = np.arange(n)
    return idx ^ 1


def kernel(**inp):
    f = lambda a: np.ascontiguousarray(np.asarray(a, dtype=np.float32))
    w_in = f(inp["w_in"])
    pcols = np.concatenate([C_AQ + _perm_pairs(512), C_AK + _perm_pairs(128), C_BKPE + _perm_pairs(32)])
    w_inp = np.ascontiguousarray(w_in[:, :, pcols])
    uq = f(inp["mla_w_uq"])
    uqcols = np.arange(768).reshape(8, 96)
    uqcols[:, 64:] = uqcols[:, 64:] ^ 1
    uqp = np.ascontiguousarray(uq[:, :, uqcols.reshape(-1)])
    shared = {k: f(inp[k]) for k in ["norm_g", "w_ada", "b_ada", "attn_sink", "mla_q_norm", "mla_kv_norm", "mla_w_ukv", "gdn_conv",
                                     "gdn_norm", "w_branch_a", "w_branch_b", "w_branch_c", "w_out", "final_norm_g"]}
    shared["w_in"] = w_in
    shared["w_inp"] = w_inp
    shared["mla_w_uq"] = uq
    shared["mla_w_uqp"] = uqp
    shared["gdn_a_log"] = f(inp["gdn_a_log"]).reshape(2, 8)
    shared["gdn_dt_bias"] = f(inp["gdn_dt_bias"]).reshape(2, 8)
    shared["ident"] = np.eye(128, dtype=np.float32)
    a = np.arange(128)
    same = (a[:, None] // 64) == (a[None, :] // 64)
    gm1 = np.full((128, 8, 128), NEG, np.float32)
    gm2 = np.full((128, 8, 128), NEG, np.float32)
    for dh in range(8):
        if dh < 4:
            gm1[:, dh][(a[:, None] > a[None, :]) & same] = 0
            gm2[:, dh][(a[None, :] >= a[:, None]) & same] = 0
        else:
            gm1[:, dh][(a[:, None] < a[None, :]) & same] = 0
            gm2[:, dh][(a[None, :] <= a[:, None]) & same] = 0
    shared["gm1"], shared["gm2"] = gm1, gm2
    shared["triF"] = ((a[:, None] <= a[None, :]) & same).astype(np.float32)
    shared["triB"] = ((a[:, None] >= a[None, :]) & same).astype(np.float32)
    shared["sel1"] = np.repeat((a < 64).astype(np.float32)[:, None], 128, 1)
    shared["sel2"] = np.repeat((a >= 64).astype(np.float32)[:, None], 128, 1)
    xp = f(inp["x_prompt"]); xsm = f(inp["x_sample"])
    in_maps = []
    for c in range(8):
        m = dict(shared)
        sample = c < 4
        if sample:
            m["x"] = xsm[c]
            m["cond"] = f(inp["c"])[c]
            m["ck"] = f(inp["cache_attn_k"])[c].reshape(2, 512, 128)
            m["cv"] = f(inp["cache_attn_v"])[c].reshape(2, 512, 128)
            m["cckv"] = f(inp["cache_mla_ckv"])[c]
            m["ckpe"] = f(inp["cache_mla_kpe"])[c]
            m["st"] = f(inp["state_gdn"])[c].reshape(2, 8, 128, 128)
            qoh = np.zeros((8, T), np.float32); qoh[0] = 1
            koh = np.zeros((8, KT), np.float32); koh[0] = BIGM
            flags = np.zeros((128, 4), np.float32); flags[:, 0] = 1.0
        else:
            k = c - 4
            m["x"] = xp[8 * k:8 * k + 8].reshape(T, D)
            m["cond"] = f(inp["c_ctx"])
            m["ck"] = np.zeros((2, 512, 128), np.float32)
            m["cv"] = np.zeros((2, 512, 128), np.float32)
            m["cckv"] = np.zeros((2, 512, 256), np.float32)
            m["ckpe"] = np.zeros((2, 512, 32), np.float32)
            m["st"] = np.zeros((2, 8, 128, 128), np.float32)
            qoh = np.zeros((8, T), np.float32); koh = np.zeros((8, KT), np.float32)
            for s in range(8):
                qoh[s, s * 256:(s + 1) * 256] = 1
                koh[s, s * 256:(s + 1) * 256] = BIGM
            flags = np.zeros((128, 4), np.float32); flags[:, 1] = NEG; flags[:, 2] = -1.0
        m["qoh"], m["koh"], m["flags"] = qoh, koh, flags
        m["ropeC"], m["ropeS"] = _rope_tables(sample)
        m["maskA"] = _mask_a(sample)
        in_maps.append({n: np.ascontiguousarray(m[n], dtype=np.float32).reshape(s) for n, s in IN_SPECS})
    nc = build()
    res = run_bass_kernel_spmd(nc, in_maps, core_ids=list(range(8)))
    R = res.results
    y_sample = np.stack([R[c]["y"] for c in range(4)], 0)
    y_prompt = np.concatenate([R[c]["y"].reshape(8, 256, D) for c in range(4, 8)], 0)
    def pc(name, tail):
        return np.concatenate([np.moveaxis(R[c][name].reshape(2, 8, 256, *tail), 0, 1) for c in range(4, 8)], 0)
    nk = pc("nk", (2, 64)); nv = pc("nv", (2, 64)); nckv = pc("nckv", (256,)); nkpe = pc("nkpe", (32,))
    nst = np.concatenate([np.moveaxis(R[c]["nst"], 0, 1) for c in range(4, 8)], 0)
    return (y_prompt.astype(np.float32), y_sample.astype(np.float32), nk.astype(np.float32), nv.astype(np.float32),
            nckv.astype(np.float32), nkpe.astype(np.float32), nst.astype(np.float32))
```

```python
import numpy as np
from contextlib import ExitStack
import concourse.bass as bass
import concourse.mybir as mybir
from concourse.bass_utils import run_bass_kernel_spmd

F32 = mybir.dt.float32
BF16 = mybir.dt.bfloat16
AF = mybir.ActivationFunctionType
ALU = mybir.AluOpType

T = 2048
NB = 16
NSB = 4
KT = 2560
NKB = 20
D = 1024
BIGM = 2048.0
NEG = -30000.0
N_DSEM = 40
LIMIT = None
LAST_KB = None
C_AQ, C_AK, C_AV, C_ZA, C_BCQ, C_BCKV, C_BKPE, C_ZB, C_CQKV, C_CA, C_CB, C_ZC, C_G = (
    0, 512, 640, 768, 1280, 1664, 1920, 1952, 2464, 4000, 4008, 4016, 4528)


class KB:
    def __init__(self, nc, es):
        self.nc = nc
        self.E = {'pe': nc.tensor, 'act': nc.scalar, 'dve': nc.vector, 'pool': nc.gpsimd, 'sp': nc.sync}
        self.sem = {e: es.enter_context(nc.semaphore("s_" + e)) for e in self.E}
        self.cnt = {e: 0 for e in self.E}
        self.seen = {e: {} for e in self.E}
        self.dsem = [es.enter_context(nc.semaphore("d%d" % i)) for i in range(N_DSEM)]
        self.dcnt = [0] * N_DSEM
        self.dnext = 0
        self.reg = {}
        self.n_ins = 0
        self.limit = LIMIT
        self.n_calls = 0

    def _wait(self, eng, tok):
        if tok is None:
            return
        key = (tok[0], tok[1])
        if self.seen[eng].get(key, 0) >= tok[2]:
            return
        if eng == 'pe' and tok[0] == 'e' and tok[1] == 'pe':
            return
        if tok[0] == 'e':
            self.E[eng].wait_ge(self.sem[tok[1]], tok[2])
        else:
            self.E[eng].wait_ge(self.dsem[tok[1]], tok[2])
        self.seen[eng][key] = tok[2]

    def _entries(self, r):
        if isinstance(r, tuple):
            name, sub = r[0], (r[1] if len(r) == 2 else r[1:])
        else:
            name, sub = r, None
        d = self.reg.setdefault(name, {})
        if sub is None:
            if None not in d:
                d[None] = [None, []]
            return [d[k] for k in d], d, None
        out = []
        if None in d:
            out.append(d[None])
        if sub not in d:
            d[sub] = [None, []]
        out.append(d[sub])
        return out, d, sub

    @staticmethod
    def _norm(reads, writes):
        r2, w2 = [], []
        for r in reads:
            nm = r[0] if isinstance(r, tuple) else r
            if nm.startswith("pb"):
                w2.append(nm)
            else:
                r2.append(r)
        for w in writes:
            nm = w[0] if isinstance(w, tuple) else w
            w2.append(nm if nm.startswith("pb") else w)
        return r2, w2

    def _deps(self, eng, reads, writes):
        for r in reads:
            for en in self._entries(r)[0]:
                self._wait(eng, en[0])
        for r in writes:
            for en in self._entries(r)[0]:
                self._wait(eng, en[0])
                for t in en[1]:
                    self._wait(eng, t)

    def _record(self, tok, reads, writes):
        for r in reads:
            _, d, sub = self._entries(r)
            lst = d[sub][1]
            if tok[0] == 'e':
                lst[:] = [t for t in lst if not (t[0] == 'e' and t[1] == tok[1])]
            lst.append(tok)
            if len(lst) > 48:
                del lst[0:len(lst) - 48]
        for r in writes:
            _, d, sub = self._entries(r)
            if sub is None:
                for k in list(d.keys()):
                    if k is not None:
                        del d[k]
            d[sub] = [tok, []]

    def op(self, eng, fn, reads=(), writes=()):
        reads, writes = self._norm(reads, writes)
        self.n_calls += 1
        if self.limit is not None and self.n_calls > self.limit:
            return None
        self._deps(eng, reads, writes)
        ins = fn(self.E[eng])
        self.cnt[eng] += 1
        ins.then_inc(self.sem[eng], 1)
        tok = ('e', eng, self.cnt[eng])
        self._record(tok, reads, writes)
        self.n_ins += 1
        return tok

    def mmgroup(self, fns, reads=(), writes=()):
        reads, writes = self._norm(reads, writes)
        self.n_calls += 1
        if self.limit is not None and self.n_calls > self.limit:
            return None
        self._deps('pe', reads, writes)
        ins = None
        for f in fns:
            ins = f(self.E['pe'])
        self.cnt['pe'] += 1
        ins.then_inc(self.sem['pe'], 1)
        tok = ('e', 'pe', self.cnt['pe'])
        self._record(tok, reads, writes)
        self.n_ins += len(fns)
        return tok

    def dma(self, q, out, in_, reads=(), writes=(), **kw):
        reads, writes = self._norm(reads, writes)
        self.n_calls += 1
        if self.limit is not None and self.n_calls > self.limit:
            return None
        self._deps(q, reads, writes)
        s = self.dnext
        self.dnext = (self.dnext + 1) % N_DSEM
        if self.dcnt[s] > 0:
            self._wait(q, ('d', s, 16 * self.dcnt[s]))
        self.dcnt[s] += 1
        self.E[q].dma_start(out=out, in_=in_, **kw).then_inc(self.dsem[s], 16)
        tok = ('d', s, 16 * self.dcnt[s])
        self._record(tok, reads, writes)
        self.n_ins += 1
        return tok

    def mark(self, label):
        self.marks = getattr(self, "marks", [])
        self.marks.append((label, dict(self.cnt)))
        global LAST_KB
        LAST_KB = self

    def barrier(self):
        for e in self.E:
            for e2 in self.E:
                if e2 != e and self.cnt[e2] > 0:
                    self._wait(e, ('e', e2, self.cnt[e2]))
            for sx in range(N_DSEM):
                if self.dcnt[sx] > 0:
                    self._wait(e, ('d', sx, 16 * self.dcnt[sx]))

    def finish(self):
        for e in self.E:
            if self.cnt[e] > 0:
                self._wait('sp', ('e', e, self.cnt[e]))
        for s in range(N_DSEM):
            if self.dcnt[s] > 0:
                self._wait('sp', ('d', s, 16 * self.dcnt[s]))


IN_SPECS = [
    ("x", [T, D]), ("cond", [D]), ("norm_g", [2, D]), ("w_ada", [2, D, 3 * D]), ("b_ada", [2, 3 * D]),
    ("w_in", [2, D, 7600]), ("w_inp", [2, D, 672]), ("attn_sink", [2, 8]), ("mla_q_norm", [2, 384]),
    ("mla_w_uq", [2, 384, 768]), ("mla_w_uqp", [2, 384, 768]), ("mla_kv_norm", [2, 256]),
    ("mla_w_ukv", [2, 256, 1024]), ("gdn_conv", [2, 3, 1536]), ("gdn_a_log", [2, 8]), ("gdn_dt_bias", [2, 8]),
    ("gdn_norm", [2, 128]), ("w_branch_a", [2, 512, D]), ("w_branch_b", [2, 512, D]), ("w_branch_c", [2, 512, D]),
    ("w_out", [2, D, D]), ("final_norm_g", [D]),
    ("ck", [2, 512, 128]), ("cv", [2, 512, 128]), ("cckv", [2, 512, 256]), ("ckpe", [2, 512, 32]),
    ("st", [2, 8, 128, 128]),
    ("ropeC", [128, T]), ("ropeS", [128, T]), ("maskA", [6, 128, 512]), ("qoh", [8, T]), ("koh", [8, KT]),
    ("flags", [128, 4]), ("ident", [128, 128]), ("gm1", [128, 8, 128]), ("gm2", [128, 8, 128]),
    ("triF", [128, 128]), ("triB", [128, 128]), ("sel1", [128, 128]), ("sel2", [128, 128]),
]
OUT_SPECS = [
    ("y", [T, D]), ("nk", [2, T, 128]), ("nv", [2, T, 128]), ("nckv", [2, T, 256]), ("nkpe", [2, T, 32]),
    ("nst", [2, 8, 2, 4, 128, 128]),
]


def build(stop=None, taps=None):
    nc = bass.Bass("TRN2", target_bir_lowering=False)
    I = {n: nc.dram_tensor(n, s, F32, kind="ExternalInput").ap() for n, s in IN_SPECS}
    O = {n: nc.dram_tensor(n, s, F32, kind="ExternalOutput").ap() for n, s in OUT_SPECS}
    xs = nc.dram_tensor("xs", [T, D], F32, kind="Internal").ap()
    TAP = {}
    if taps:
        for n, s in taps.items():
            TAP[n] = nc.dram_tensor("tap_" + n, s, F32, kind="ExternalOutput").ap()
    with ExitStack() as es:
        kb = KB(nc, es)
        SB = lambda name, shape, dt: es.enter_context(nc.sbuf_tensor("sb_" + name, shape, dt))
        PS = lambda name, shape, dt: es.enter_context(nc.psum_tensor("ps_" + name, shape, dt))
        _body(nc, kb, SB, PS, I, O, xs, TAP, stop)
        kb.mark('end')
        kb.finish()
    return nc


def _body(nc, kb, SB, PS, I, O, xs, TAP, stop):
    op, dma, mm = kb.op, kb.dma, kb.mmgroup
    ident = SB("ident", [128, 128], BF16)
    ident32 = SB("ident32", [128, 128], F32)
    ropeC = SB("ropeC", [128, T], BF16)
    ropeS = SB("ropeS", [128, T], BF16)
    flags = SB("flags", [128, 4], F32)
    ones16 = SB("ones16", [128, 128], BF16)
    op('dve', lambda e: e.memset(ones16[:], 1.0), writes=["ones16"])
    dma('pool', ident[:], I["ident"], writes=["ident"])
    dma('sp', ident32[:], I["ident"], writes=["ident32"])
    dma('pool', ropeC[:], I["ropeC"], writes=["ropeC"])
    dma('pool', ropeS[:], I["ropeS"], writes=["ropeS"])
    dma('sp', flags[:], I["flags"], writes=["flags"])

    hT = SB("hT", [128, 8, T], BF16)
    mergeT = SB("mergeT", [128, 8, T], BF16)
    ozT = SB("ozT", [128, 4, T], BF16)
    pb = [PS("pb%d" % i, [128, 512], F32) for i in range(8)]
    pbn = ["pb%d" % i for i in range(8)]

    def hTr(sb):
        return [("hT", sb * 4 + i) for i in range(4)]

    NW = 2
    WCOLS = 512
    wbuf = [SB("wbuf%d" % i, [128, 8 * WCOLS], BF16) for i in range(NW)]
    wstate = {'i': 0}

    def load_w(src2d, kch, cols, q='pool', prows=128):
        i = wstate['i']
        wstate['i'] = (i + 1) % NW
        name = "wbuf%d" % i
        tot = sum(n for _, n in cols)
        assert kch * tot <= 8 * WCOLS, (kch, tot)
        view = wbuf[i][0:prows, 0:kch * tot].rearrange("p (k n) -> p k n", k=kch)
        srcv = src2d.rearrange("(k p) n -> p k n", p=prows)
        o = 0
        for c0, n in cols:
            dma(q, view[:, :, o:o + n], srcv[:, :, c0:c0 + n], writes=[name])
            o += n
        return view, name

    def tap(name, ap_sb, reads):
        if name in TAP:
            dma('sp', TAP[name], ap_sb, reads=reads)

    condsb = SB("condsb", [128, 8], F32)
    scond = SB("scond", [128, 8], BF16)
    modfm = SB("modfm", [128, 24], F32)
    badafm = SB("badafm", [128, 24], F32)
    ngfm = SB("ngfm", [128, 8], F32)
    Afm = SB("Afm", [128, 8], F32)
    gbc = SB("gbc", [128, 8, 128], F32)
    gateb = SB("gateb", [128, D], F32)
    xblk = [SB("xblk%d" % i, [128, D], F32) for i in range(2)]
    xn = [SB("xn%d" % i, [128, D], BF16) for i in range(2)]
    junk = SB("junk", [128, D], BF16)
    stat = SB("stat", [128, NB, 4], F32)
    qTh = [SB("qTh%d" % i, [104, T], BF16) for i in range(1)] * 2
    wukv = SB("wukv", [128, 2, 1024], BF16)
    ARN = 21952
    arena = SB("arena", [128, ARN], BF16)
    maskA = arena[:, 4 * KT:4 * KT + 3072].rearrange("p (o n) -> p o n", o=6)
    o_ = 0
    kTa = arena[0:64, 0:2 * KT].rearrange("p (g n) -> p g n", g=2)
    Va = arena[:, 2 * KT:2 * KT + NKB * 256].rearrange("p (k g d) -> p k g d", k=NKB, g=2)
    kTb1 = arena[0:104, 0:KT]
    kpeT = arena[0:96, KT:2 * KT]
    Vb1 = arena[:, 2 * KT:2 * KT + NKB * 128].rearrange("p (k d) -> p k d", k=NKB)
    o_ = 2 * KT + NKB * 128
    ckvT = arena[:, o_:o_ + 2 * KT].rearrange("p (c n) -> p c n", c=2)
    cqnT = arena[:, o_ + 2 * KT:o_ + 2 * KT + 3 * T].rearrange("p (c n) -> p c n", c=3)
    kTb = [kTb1, kTb1]
    Vb = [Vb1, Vb1]
    pT = [SB("pT%d" % i, [128, 512], BF16) for i in range(3)]
    t1 = [SB("t1_%d" % i, [128, 512], F32) for i in range(2)]
    t2 = [SB("t2_%d" % i, [128, 512], F32) for i in range(2)]
    rr = [SB("rr%d" % i, [128, 512], F32) for i in range(1)] * 2
    r3 = [SB("r3_%d" % i, [64, 512], F32) for i in range(1)] * 2
    kvout = [SB("kvout%d" % i, [128, 288], F32) for i in range(2)]
    kvn16 = [SB("kvn16_%d" % i, [128, 384], BF16) for i in range(2)]
    ctx16 = SB("ctx16", [128, 4, 256], BF16)
    ctxp = SB("ctxp", [128, 4, 96], BF16)
    esink = SB("esink", [128, 8], F32)
    kvng = SB("kvng", [128, 256], F32)
    qng = SB("qng", [128, 384], F32)
    st2 = SB("st2", [128, NB, 4], F32)
    sig = [SB("sig%d" % i, [128, 512], F32) for i in range(2)]
    op('dve', lambda e: e.memset(ctxp[:], 0.0), writes=["ctxp"])
    dma('pool', qTh[0][96:104, :], I["qoh"], writes=["qTh0"])
    cnt = {'rot': 0, 'p': 0, 'o': 0}
    if stop == "c":
        return

    pT.append(SB("pT3", [128, 512], BF16))

    def attend_stream(groups):
        SBK = [2, 3, 6, 7]
        tiles = []
        for gi, g_ in enumerate(groups):
            g_['ob'] = 4 + cnt['o'] % 2
            cnt['o'] += 1
            for idx in range(len(g_['klist'])):
                tiles.append((gi, idx))
        info = {}

        def emit_S(t):
            gi, idx = tiles[t]
            g_ = groups[gi]
            kblk, mi, isctx = g_['klist'][idx]
            sbk = SBK[cnt['p'] % 4]
            pt = cnt['p'] % 4
            cnt['p'] += 1
            kap, kname = g_['kfn'](kblk)
            qtile, sbi, K = g_['qtile'], g_['sbi'], g_['K']
            fns = [lambda e: e.matmul(pb[sbk][:], kap, qtile[0:K, sbi * 512:(sbi + 1) * 512], start=True, stop=(mi is None))]
            rd = [kname, (g_['qname'], sbi)]
            if mi is not None:
                fns.append(lambda e: e.matmul(pb[sbk][:], ident[:], maskA[:, mi, :], start=False, stop=True))
                rd += ["ident", "maskA"]
            mm(fns, reads=rd, writes=[pbn[sbk]])
            b_ = g_['bias_fn'](isctx)
            op('act', lambda e: e.activation(pT[pt][:], pb[sbk][:], AF.Exp, scale=g_['scale'], bias=b_),
               reads=[pbn[sbk], "flags"], writes=["pT%d" % pt])
            info[t] = pt

        LOOK = 3
        nt = len(tiles)
        for t in range(min(LOOK, nt)):
            emit_S(t)
        for t in range(nt):
            if t + LOOK < nt:
                emit_S(t + LOOK)
            gi, idx = tiles[t]
            g_ = groups[gi]
            n = len(g_['klist'])
            kblk = g_['klist'][idx][0]
            pt = info[t]
            ob = g_['ob']
            vap, vname = g_['vfn'](kblk)
            mm([lambda e: e.matmul(pb[ob][:], vap, pT[pt][:], start=(idx == 0), stop=(idx == n - 1))],
               reads=[vname, "pT%d" % pt], writes=[pbn[ob]])
            if idx == n - 1:
                g_['fin'](ob)

    def run_pairs(genf, n):
        for b0_ in range(0, n, 2):
            gs = [genf(b0_), genf(b0_ + 1)]
            alive = [True, True]
            while any(alive):
                for q_ in range(2):
                    if alive[q_]:
                        try:
                            next(gs[q_])
                        except StopIteration:
                            alive[q_] = False

    for l in range(2):
        xsrc = I["x"] if l == 0 else xs
        Win = I["w_in"][l]
        Winp = I["w_inp"][l]
        kb.mark('L%d start' % l)
        dma('sp', condsb[:], I["cond"].rearrange("(c p) -> p c", p=128), writes=["cond"], allow_slow_non_contiguous=True)
        dma('sp', badafm[:], I["b_ada"][l].rearrange("(c p) -> p c", p=128), writes=["bada"], allow_slow_non_contiguous=True)
        dma('sp', ngfm[:], I["norm_g"][l].rearrange("(c p) -> p c", p=128), writes=["ngfm"], allow_slow_non_contiguous=True)
        op('act', lambda e: e.activation(scond[:], condsb[:], AF.Silu), reads=["cond"], writes=["scond"])
        ada_w = []
        for nt in range(6):
            if nt < 5:
                v_ = arena[:, nt * 4096:(nt + 1) * 4096].rearrange("p (k n) -> p k n", k=8)
                dma('pool', v_, I["w_ada"][l].rearrange("(k p) n -> p k n", p=128)[:, :, nt * 512:(nt + 1) * 512], writes=["mw%d" % nt])
                ada_w.append((v_, "mw%d" % nt))
            else:
                ada_w.append(load_w(I["w_ada"][l], 8, [(nt * 512, 512)]))
        for nt in range(6):
            wv, wn = ada_w[nt]
            for jj in range(4):
                j = nt * 4 + jj
                mm([(lambda e, c=c, jj=jj, j=j, wv=wv: e.matmul(pb[0][:, j:j + 1], wv[:, c, jj * 128:(jj + 1) * 128], scond[:, c:c + 1],
                                                               start=(c == 0), stop=(c == 7))) for c in range(8)],
                   reads=[wn, "scond"], writes=[(pbn[0], j)])
        op('dve', lambda e: e.tensor_tensor(modfm[:], pb[0][:, 0:24], badafm[:], ALU.add), reads=[pbn[0], "bada"], writes=["modfm"])
        op('dve', lambda e: e.scalar_tensor_tensor(Afm[:], modfm[:, 8:16], 1.0, ngfm[:], ALU.add, ALU.mult), reads=["modfm", "ngfm"], writes=["Afm"])
        op('dve', lambda e: e.tensor_copy(gbc[:], modfm[:, 16:24].unsqueeze(2).to_broadcast([128, 8, 128])), reads=["modfm"], writes=["gbc"])
        for c in range(8):
            bkc = 1 + c // 4
            mm([lambda e, c=c, bkc=bkc: e.matmul(pb[bkc][:, (c % 4) * 128:(c % 4 + 1) * 128], gbc[:, c, :], ident32[:], start=True, stop=True)],
               reads=["gbc", "ident32"], writes=[(pbn[bkc], c % 4)])
        op('act', lambda e: e.copy(gateb[:, 0:512], pb[1][:]), reads=[pbn[1]], writes=[("gateb", 0)])
        op('act', lambda e: e.copy(gateb[:, 512:1024], pb[2][:]), reads=[pbn[2]], writes=[("gateb", 1)])
        tap("modfm%d" % l, modfm[:], ["modfm"])
        if stop == "p0":
            return

        kb.mark('L%d p1' % l)
        def gen_p1(b):
            xb, xbn = xblk[b % 2], "xblk%d" % (b % 2)
            xnb, xnn = xn[b % 2], "xn%d" % (b % 2)
            dma('sp', xb[:], xsrc[b * 128:(b + 1) * 128, :], reads=(["xs"] if l == 1 else []), writes=[xbn])
            yield None
            op('act', lambda e: e.activation(junk[:], xb[:], AF.Square, accum_out=stat[:, b, 0:1]), reads=[xbn], writes=["junk", ("stat", b)])
            yield None
            op('dve', lambda e: e.tensor_scalar(stat[:, b, 1:2], stat[:, b, 0:1], 1.0 / D, 1e-6, ALU.mult, ALU.add), reads=[("stat", b)], writes=[("stat", b)])
            yield None
            op('act', lambda e: e.activation(stat[:, b, 2:3], stat[:, b, 1:2], AF.Ln), reads=[("stat", b)], writes=[("stat", b)])
            yield None
            op('act', lambda e: e.activation(stat[:, b, 3:4], stat[:, b, 2:3], AF.Exp, scale=-0.5), reads=[("stat", b)], writes=[("stat", b)])
            yield None
            op('dve', lambda e: e.tensor_scalar(xnb[:], xb[:], stat[:, b, 3:4], None, ALU.mult), reads=[xbn, ("stat", b)], writes=[xnn])
            yield None
            for half in range(2):
                bk = 4 + (2 * b + half) % 4
                pview = pb[bk][:].bitcast(BF16)
                mm([(lambda e, c=c, half=half, pview=pview: e.transpose(pview[:, c * 128:(c + 1) * 128], xnb[:, (half * 4 + c) * 128:(half * 4 + c + 1) * 128], ident[:]))
                    for c in range(4)], reads=[xnn, "ident"], writes=[pbn[bk]])
                yield None
                for c in range(4):
                    cc = half * 4 + c
                    if True:
                        op('act', lambda e, c=c, cc=cc, pview=pview: e.activation(hT[:, cc, b * 128:(b + 1) * 128], pview[:, c * 128:(c + 1) * 128], AF.Identity,
                                                                                 scale=Afm[:, cc:cc + 1], bias=modfm[:, cc:cc + 1]),
                           reads=[pbn[bk], "Afm", "modfm"], writes=[("hT", b, cc)])
                        yield None
                    else:
                        op('dve', lambda e, c=c, cc=cc, pview=pview: e.scalar_tensor_tensor(hT[:, cc, b * 128:(b + 1) * 128], pview[:, c * 128:(c + 1) * 128],
                                                                                    Afm[:, cc:cc + 1], modfm[:, cc:cc + 1].to_broadcast([128, 128]), ALU.mult, ALU.add),
                           reads=[pbn[bk], "Afm", "modfm"], writes=[("hT", b, cc)])
                        yield None
        run_pairs(gen_p1, NB)
        op('pool', lambda e: e.memset(junk[0:1, 0:1], 0.0), reads=[], writes=["hT"])
        HR = ["hT"]
        if "hT%d" % l in TAP:
            for c8 in range(8):
                op('dve', lambda e: e.tensor_copy(xblk[0][:].rearrange("p (a b) -> p a b", a=1)[:, 0, :], hT[:, c8, 0:1024]), reads=["hT"], writes=["xblk0"])
                dma('sp', TAP["hT%d" % l][:, c8, 0:1024], xblk[0][:], reads=["xblk0"])
                op('dve', lambda e: e.tensor_copy(xblk[0][:], hT[:, c8, 1024:2048]), reads=["hT"], writes=["xblk0"])
                dma('sp', TAP["hT%d" % l][:, c8, 1024:2048], xblk[0][:], reads=["xblk0"])
        if stop == "p1":
            return

        def lin(bank, wv, wn, col0, M, sbi, kch=8, rhs_fn=None, extra_reads=()):
            if rhs_fn is None:
                rhs_fn = lambda c: hT[:, c, sbi * 512:(sbi + 1) * 512]
            mm([(lambda e, c=c: e.matmul(pb[bank][0:M, :], wv[:, c, col0:col0 + M], rhs_fn(c), start=(c == 0), stop=(c == kch - 1)))
                for c in range(kch)], reads=[wn] + HR + list(extra_reads), writes=[pbn[bank]])

        kb.mark('L%d A-pre' % l)
        dma('sp', esink[:], I["attn_sink"][l].partition_broadcast(128), writes=["esink"])
        op('act', lambda e: e.activation(esink[:], esink[:], AF.Exp), reads=["esink"], writes=["esink"])
        dma('sp', kvng[:], I["mla_kv_norm"][l].partition_broadcast(128), writes=["kvng"])
        kb.barrier()
        op('dve', lambda e: e.memset(Va[:, :, :, 64:128], 1.0), writes=["Va"])
        dma('pool', maskA, I["maskA"].rearrange("o p n -> p o n"), writes=["maskA"])
        wq = arena[:, 13312:13312 + 4096].rearrange("p (k n) -> p k n", k=8)
        wqp = arena[:, 17408:17408 + 4096].rearrange("p (k n) -> p k n", k=8)
        wqn, wqpn = "mwq", "mwqp"
        wkv, wkvn = load_w(Win, 8, [(C_AK, 256)])
        def gen_akv(b):
            bk = 6 + b % 2
            ko = kvout[b % 2]
            kon = "kvout%d" % (b % 2)
            mm([(lambda e, c=c: e.matmul(pb[bk][:, 0:256], hT[:, c, b * 128:(b + 1) * 128], wkv[:, c, 0:256], start=(c == 0), stop=(c == 7))) for c in range(8)],
               reads=[wkvn] + HR, writes=[(pbn[bk], 0)])
            yield None
            op('act', lambda e: e.copy(ko[:, 0:256], pb[bk][:, 0:256]), reads=[(pbn[bk], 0)], writes=[kon])
            yield None
            op('dve', lambda e: e.tensor_copy(Va[:, b, :, 0:64], pb[bk][:, 128:256].rearrange("p (g d) -> p g d", g=2)), reads=[(pbn[bk], 0), kon], writes=[("Va", b)])
            yield None
            dma('sp', O["nk"][l, b * 128:(b + 1) * 128, :], ko[:, 0:128], reads=[kon])
            yield None
            dma('sp', O["nv"][l, b * 128:(b + 1) * 128, :], ko[:, 128:256], reads=[kon])
            yield None
        run_pairs(gen_akv, NB)
        dma('pool', ctx16[:, :, 0:128], I["ck"][l].rearrange("(j p) n -> p j n", p=128), writes=["ctx16"])
        for g in range(2):
            pview = pb[6 + g][:].bitcast(BF16)
            mm([(lambda e, j=j, pview=pview: e.transpose(pview[0:64, j * 128:(j + 1) * 128], ctx16[:, j, g * 64:(g + 1) * 64], ident[:])) for j in range(4)],
               reads=["ctx16", "ident"], writes=[pbn[6 + g]])
            op('dve', lambda e, pview=pview: e.tensor_copy(kTa[:, g, T:KT], pview[0:64, 0:512]), reads=[pbn[6 + g]], writes=[("kTa", g, 4)])
        for g in range(2):
            dma('pool', Va[:, NB:NKB, g, 0:64], I["cv"][l].rearrange("(j p) (g d) -> p j g d", p=128, g=2)[:, :, g, :], writes=[("Va", "ctx", g)])
        wk, wkn = load_w(Win, 8, [(C_AK, 128)])
        wkp, wkpn = load_w(Winp, 8, [(512, 128)])
        dma('pool', wq, Win.rearrange("(k p) n -> p k n", p=128)[:, :, C_AQ:C_AQ + 512], writes=[wqn])
        dma('pool', wqp, Winp.rearrange("(k p) n -> p k n", p=128)[:, :, 0:512], writes=[wqpn])
        for g in range(2):
            def gen_ka(sbi, g=g):
                r = sbi % 2
                ba, bb_ = 2 * r, 2 * r + 1
                lin(ba, wk, wkn, g * 64, 64, sbi)
                yield None
                lin(bb_, wkp, wkpn, g * 64, 64, sbi)
                yield None
                op('dve', lambda e: e.tensor_tensor(t1[r][0:64, :], pb[ba][0:64, :], ropeC[0:64, sbi * 512:(sbi + 1) * 512], ALU.mult), reads=[pbn[ba], "ropeC"], writes=["t1_%d" % r])
                yield None
                op('dve', lambda e: e.tensor_tensor(t2[r][0:64, :], pb[bb_][0:64, :], ropeS[0:64, sbi * 512:(sbi + 1) * 512], ALU.mult), reads=[pbn[bb_], "ropeS"], writes=["t2_%d" % r])
                yield None
                op('pool', lambda e: e.tensor_tensor(kTa[:, g, sbi * 512:(sbi + 1) * 512], t1[r][0:64, :], t2[r][0:64, :], ALU.add), reads=["t1_%d" % r, "t2_%d" % r], writes=[("kTa", g, sbi)])
                yield None
            run_pairs(gen_ka, NSB)
        kb.mark('L%d A-attn' % l)
        for h in range(8):
            g = h // 4
            qt, qn = qTh[h % 2], "qTh0"
            def gen_qa(sbi, h=h, qt=qt, qn=qn):
                r = sbi % 2
                ba, bb_ = 2 * r, 2 * r + 1
                lin(ba, wq, wqn, h * 64, 64, sbi)
                yield None
                lin(bb_, wqp, wqpn, h * 64, 64, sbi)
                yield None
                op('dve', lambda e: e.tensor_tensor(t1[r][0:64, :], pb[ba][0:64, :], ropeC[0:64, sbi * 512:(sbi + 1) * 512], ALU.mult), reads=[pbn[ba], "ropeC"], writes=["t1_%d" % r])
                yield None
                op('dve', lambda e: e.tensor_tensor(t2[r][0:64, :], pb[bb_][0:64, :], ropeS[0:64, sbi * 512:(sbi + 1) * 512], ALU.mult), reads=[pbn[bb_], "ropeS"], writes=["t2_%d" % r])
                yield None
                op('pool', lambda e: e.tensor_tensor(qt[0:64, sbi * 512:(sbi + 1) * 512], t1[r][0:64, :], t2[r][0:64, :], ALU.add), reads=["t1_%d" % r, "t2_%d" % r], writes=[(qn, sbi)])
                yield None
            run_pairs(gen_qa, NSB)
            groups = []
            for sbi in range(NSB):
                klist = []
                for o in range(6):
                    j = 4 * sbi - 1 + o
                    if 0 <= j < NB:
                        klist.append((j, o, False))
                for j in range(NB, NKB):
                    klist.append((j, None, True))

                def kfn(kblk, g=g):
                    return kTa[:, g, kblk * 128:(kblk + 1) * 128], ("kTa", g, kblk // 4)

                def vfn(kblk, g=g):
                    return Va[:, kblk, g, :], (("Va", kblk) if kblk < NB else ("Va", "ctx", g))

                def fin(ob, h=h, sbi=sbi):
                    r = cnt['rot'] % 2
                    cnt['rot'] += 1
                    op('dve', lambda e: e.tensor_scalar(rr[r][64:128, :], pb[ob][64:128, :], esink[64:128, h:h + 1], None, ALU.add), reads=[pbn[ob], "esink"], writes=["rr0"])
                    op('dve', lambda e: e.reciprocal(rr[r][64:128, :], rr[r][64:128, :]), reads=["rr0"], writes=["rr0"])
                    op('pool', lambda e: e.tensor_copy(r3[r][0:64, :], rr[r][64:128, :]), reads=["rr0"], writes=["r3_0"])
                    po = (h % 2) * 64
                    op('dve', lambda e: e.tensor_tensor(ozT[po:po + 64, h // 2, sbi * 512:(sbi + 1) * 512], pb[ob][0:64, :], r3[r][0:64, :], ALU.mult),
                       reads=[pbn[ob], "r3_0"], writes=[("ozT", h // 2, sbi)])

                groups.append(dict(qtile=qt, qname=qn, sbi=sbi, kfn=kfn, klist=klist, vfn=vfn, K=64, scale=0.125,
                                   bias_fn=(lambda isctx: (flags[:, 1:2] if isctx else 0.0)), fin=fin))
            attend_stream(groups)
        if "ozA%d" % l in TAP:
            for c8 in range(4):
                for hf in range(2):
                    op('dve', lambda e: e.tensor_copy(xblk[0][:], ozT[:, c8, hf * 1024:(hf + 1) * 1024]), reads=["ozT"], writes=["xblk0"])
                    dma('sp', TAP["ozA%d" % l][:, c8, hf * 1024:(hf + 1) * 1024], xblk[0][:], reads=["xblk0"])

        def zmul_and_merge(zcol, wbr_src, gcol, first, after_loads=None):
            kb.barrier()

            def load_into(k_, name, src2d, kch, c0, n):
                v = arena[:, k_ * 4096:k_ * 4096 + kch * n].rearrange("p (k n) -> p k n", k=kch)
                dma('pool', v, src2d.rearrange("(k p) n -> p k n", p=128)[:, :, c0:c0 + n], writes=[name])
                return v, name
            wz, wzn = load_into(0, "mw0", Win, 8, zcol, 512)
            wbs, wgs = [], []
            for ch in range(2):
                wbs.append(load_into(1 + 2 * ch, "mw%d" % (1 + 2 * ch), wbr_src, 4, ch * 512, 512))
                wgs.append(load_into(2 + 2 * ch, "mw%d" % (2 + 2 * ch), Win, 8, gcol + ch * 512, 512))
            if after_loads is not None:
                after_loads()
            for c in range(4):
                for sbi in range(NSB):
                    r = cnt['rot'] % 2
                    cnt['rot'] += 1
                    lin(r, wz, wzn, c * 128, 128, sbi)
                    op('act', lambda e: e.activation(t1[r][:], pb[r][:], AF.Silu), reads=[pbn[r]], writes=["t1_%d" % r])
                    op('pool', lambda e: e.tensor_tensor(ozT[:, c, sbi * 512:(sbi + 1) * 512], ozT[:, c, sbi * 512:(sbi + 1) * 512], t1[r][:], ALU.mult),
                       reads=["t1_%d" % r, ("ozT", c, sbi)], writes=[("ozT", c, sbi)])
            for ch in range(2):
                wb, wbn = wbs[ch]
                wg, wgn = wgs[ch]
                for cc in range(4):
                    c = ch * 4 + cc
                    for sbi in range(NSB):
                        r = cnt['rot'] % 2
                        cnt['rot'] += 1
                        lin(r, wg, wgn, cc * 128, 128, sbi)
                        op('act', lambda e: e.activation(sig[r][:], pb[r][:], AF.Sigmoid), reads=[pbn[r]], writes=["sig%d" % r])
                        lin(2 + r, wb, wbn, cc * 128, 128, sbi, kch=4, rhs_fn=lambda k: ozT[:, k, sbi * 512:(sbi + 1) * 512],
                            extra_reads=[("ozT", k, sbi) for k in range(4)])
                        dst = mergeT[:, c, sbi * 512:(sbi + 1) * 512]
                        if first:
                            op('dve', lambda e: e.tensor_tensor(dst, pb[2 + r][:], sig[r][:], ALU.mult), reads=[pbn[2 + r], "sig%d" % r], writes=[("mergeT", c, sbi)])
                        else:
                            op('dve', lambda e: e.tensor_tensor(t2[r][:], pb[2 + r][:], sig[r][:], ALU.mult), reads=[pbn[2 + r], "sig%d" % r], writes=["t2_%d" % r])
                            op('pool', lambda e: e.tensor_tensor(dst, dst, t2[r][:], ALU.add), reads=["t2_%d" % r, ("mergeT", c, sbi)], writes=[("mergeT", c, sbi)])

        kb.mark('L%d A-merge' % l)
        zmul_and_merge(C_ZA, I["w_branch_a"][l], C_G, True)

        def tap_big(nm, src, nch, rd):
            if nm in TAP:
                for c8 in range(nch):
                    for hf in range(2):
                        op('dve', lambda e: e.tensor_copy(xblk[0][:], src[:, c8, hf * 1024:(hf + 1) * 1024]), reads=rd, writes=["xblk0"])
                        dma('sp', TAP[nm][:, c8, hf * 1024:(hf + 1) * 1024], xblk[0][:], reads=["xblk0"])
        tap_big("mergeA%d" % l, mergeT, 8, ["mergeT"])
        if stop == "A":
            return

        kb.mark('L%d B-pre' % l)
        kb.barrier()
        op('dve', lambda e: e.memset(Vb1[:, :, 64:128], 1.0), writes=["Vb0"])
        dma('pool', kTb1[96:104, :], I["koh"], writes=["kTb0"])
        dma('pool', qTh[0][96:104, :], I["qoh"], writes=["qTh0"])
        wkv, wkvn = load_w(Win, 8, [(C_BCKV, 288)])
        wcq, wcqn = load_w(Win, 8, [(C_BCQ, 384)])
        dma('pool', wukv[:], I["mla_w_ukv"][l].rearrange("(k p) n -> p k n", p=128), writes=["wukv"])
        wuq = arena[:, 18944:18944 + 3 * 768].rearrange("p (k n) -> p k n", k=3)
        wuqn = "mwuq"
        dma('pool', wuq, I["mla_w_uq"][l].rearrange("(k p) n -> p k n", p=128), writes=[wuqn])
        def gen_bckv(b):
            bk = 6 + b % 2
            mm([(lambda e, c=c: e.matmul(pb[bk][:, 256:512 + 32 - 512] if False else pb[bk][:, 256:512], hT[:, c, b * 128:(b + 1) * 128], wkv[:, c, 0:256], start=(c == 0), stop=(c == 7))) for c in range(8)],
               reads=[wkvn] + HR, writes=[(pbn[bk], 1)])
            yield None
            bk2 = 0 + b % 2
            mm([(lambda e, c=c: e.matmul(pb[bk2][:, 0:32], hT[:, c, b * 128:(b + 1) * 128], wkv[:, c, 256:288], start=(c == 0), stop=(c == 7))) for c in range(8)],
               reads=[wkvn] + HR, writes=[(pbn[bk2], 0)])
            yield None
            ko2 = kvout[(b + 1) % 2]
            ko2n = "kvout%d" % ((b + 1) % 2)
            op('act', lambda e: e.activation(junk[:, 0:256], pb[bk][:, 256:512], AF.Square, accum_out=st2[:, b, 0:1]), reads=[(pbn[bk], 1)], writes=["junk", ("st2", b)])
            yield None
            op('dve', lambda e: e.tensor_scalar(st2[:, b, 1:2], st2[:, b, 0:1], 1.0 / 256, 1e-6, ALU.mult, ALU.add), reads=[("st2", b)], writes=[("st2", b)])
            yield None
            op('act', lambda e: e.activation(st2[:, b, 2:3], st2[:, b, 1:2], AF.Ln), reads=[("st2", b)], writes=[("st2", b)])
            yield None
            op('act', lambda e: e.activation(st2[:, b, 3:4], st2[:, b, 2:3], AF.Exp, scale=-0.5), reads=[("st2", b)], writes=[("st2", b)])
            yield None
            op('dve', lambda e: e.scalar_tensor_tensor(ko2[:, 0:256], pb[bk][:, 256:512], st2[:, b, 3:4], kvng[:], ALU.mult, ALU.mult),
               reads=[(pbn[bk], 1), ("st2", b), "kvng"], writes=[ko2n])
            yield None
            op('act', lambda e: e.copy(ko2[:, 256:288], pb[bk2][:, 0:32]), reads=[(pbn[bk2], 0)], writes=[ko2n])
            yield None
            dma('sp', O["nckv"][l, b * 128:(b + 1) * 128, :], ko2[:, 0:256], reads=[ko2n])
            yield None
            dma('sp', O["nkpe"][l, b * 128:(b + 1) * 128, :], ko2[:, 256:288], reads=[ko2n])
            yield None
            k16 = kvn16[b % 2]
            k16n = "kvn16_%d" % (b % 2)
            op('dve', lambda e: e.tensor_copy(k16[:, 0:256], ko2[:, 0:256]), reads=[ko2n], writes=[k16n])
            yield None
            bk3 = 2 + b % 2
            pview = pb[bk3][:].bitcast(BF16)
            mm([(lambda e, c=c, pview=pview: e.transpose(pview[:, c * 128:(c + 1) * 128], k16[:, c * 128:(c + 1) * 128], ident[:])) for c in range(2)],
               reads=[k16n, "ident"], writes=[pbn[bk3]])
            yield None
            op('dve', lambda e, pview=pview: e.tensor_copy(ckvT[:, :, b * 128:(b + 1) * 128], pview[:, 0:256].rearrange("p (c n) -> p c n", c=2)),
               reads=[pbn[bk3]], writes=[("ckvT", b)])
            yield None
        run_pairs(gen_bckv, NB)
        ctx16b = ctx16
        dma('pool', ctx16b[:], I["cckv"][l].rearrange("(j p) n -> p j n", p=128), writes=["ctx16"])
        for j in range(4):
            pview = pb[6 + j % 2][:].bitcast(BF16)
            mm([(lambda e, c=c, pview=pview: e.transpose(pview[:, c * 128:(c + 1) * 128], ctx16b[:, j, c * 128:(c + 1) * 128], ident[:])) for c in range(2)],
               reads=["ctx16", "ident"], writes=[pbn[6 + j % 2]])
            op('dve', lambda e, pview=pview: e.tensor_copy(ckvT[:, :, T + j * 128:T + (j + 1) * 128], pview[:, 0:256].rearrange("p (c n) -> p c n", c=2)),
               reads=[pbn[6 + j % 2]], writes=[("ckvT", NB + j)])
        dma('pool', ctxp[:, :, 64:96], I["ckpe"][l].rearrange("(j p) n -> p j n", p=128), writes=["ctxp"])
        pview = pb[6][:].bitcast(BF16)
        mm([(lambda e, j=j, pview=pview: e.transpose(pview[0:96, j * 128:(j + 1) * 128], ctxp[:, j, :], ident[:])) for j in range(4)],
           reads=["ctxp", "ident"], writes=[pbn[6]])
        op('dve', lambda e, pview=pview: e.tensor_copy(kpeT[64:96, T:KT], pview[64:96, 0:512]), reads=[pbn[6]], writes=[("kpeT", 4)])

        dma('sp', qng[:], I["mla_q_norm"][l].partition_broadcast(128), writes=["qng"])
        def gen_bcq(b):
            bk = 6 + b % 2
            mm([(lambda e, c=c: e.matmul(pb[bk][:, 0:384], hT[:, c, b * 128:(b + 1) * 128], wcq[:, c, 0:384], start=(c == 0), stop=(c == 7))) for c in range(8)],
               reads=[wcqn] + HR, writes=[pbn[bk]])
            yield None
            op('act', lambda e: e.activation(junk[:, 0:384], pb[bk][:, 0:384], AF.Square, accum_out=st2[:, b, 0:1]), reads=[pbn[bk]], writes=["junk", ("st2", b)])
            yield None
            op('dve', lambda e: e.tensor_scalar(st2[:, b, 1:2], st2[:, b, 0:1], 1.0 / 384, 1e-6, ALU.mult, ALU.add), reads=[("st2", b)], writes=[("st2", b)])
            yield None
            op('act', lambda e: e.activation(st2[:, b, 2:3], st2[:, b, 1:2], AF.Ln), reads=[("st2", b)], writes=[("st2", b)])
            yield None
            op('act', lambda e: e.activation(st2[:, b, 3:4], st2[:, b, 2:3], AF.Exp, scale=-0.5), reads=[("st2", b)], writes=[("st2", b)])
            yield None
            k16 = kvn16[b % 2]
            k16n = "kvn16_%d" % (b % 2)
            op('dve', lambda e: e.scalar_tensor_tensor(k16[:, 0:384], pb[bk][:, 0:384], st2[:, b, 3:4], qng[:], ALU.mult, ALU.mult),
               reads=[pbn[bk], ("st2", b), "qng"], writes=[k16n])
            yield None
            bk3 = 2 + b % 2
            pview = pb[bk3][:].bitcast(BF16)
            mm([(lambda e, c=c, pview=pview: e.transpose(pview[:, c * 128:(c + 1) * 128], k16[:, c * 128:(c + 1) * 128], ident[:])) for c in range(3)],
               reads=[k16n, "ident"], writes=[pbn[bk3]])
            yield None
            op('dve', lambda e, pview=pview: e.tensor_copy(cqnT[:, :, b * 128:(b + 1) * 128], pview[:, 0:384].rearrange("p (c n) -> p c n", c=3)),
               reads=[pbn[bk3]], writes=[("cqnT", b)])
            yield None
        run_pairs(gen_bcq, NB)
        wpe, wpen = load_w(Win, 8, [(C_BKPE - 64, 96)])
        wpep, wpepn = load_w(Winp, 8, [(640 - 64, 96)])
        for sbi in range(NSB):
            r = cnt['rot'] % 2
            cnt['rot'] += 1
            lin(0, wpe, wpen, 0, 96, sbi)
            lin(1, wpep, wpepn, 0, 96, sbi)
            op('dve', lambda e: e.tensor_tensor(t1[r][64:96, :], pb[0][64:96, :], ropeC[64:96, sbi * 512:(sbi + 1) * 512], ALU.mult), reads=[pbn[0], "ropeC"], writes=["t1_%d" % r])
            op('dve', lambda e: e.tensor_tensor(t2[r][64:96, :], pb[1][64:96, :], ropeS[64:96, sbi * 512:(sbi + 1) * 512], ALU.mult), reads=[pbn[1], "ropeS"], writes=["t2_%d" % r])
            op('pool', lambda e: e.tensor_tensor(kpeT[64:96, sbi * 512:(sbi + 1) * 512], t1[r][64:96, :], t2[r][64:96, :], ALU.add), reads=["t1_%d" % r, "t2_%d" % r], writes=[("kpeT", sbi)])
        wuqp, wuqpn = load_w(I["mla_w_uqp"][l], 3, [(0, 768)])
        kb.mark('L%d B-attn' % l)
        CQR = [("cqnT", b) for b in range(NB)]
        CKR = [("ckvT", b) for b in range(NKB)]
        for s5 in range(5):
            op('pool', lambda e: e.tensor_copy(kTb1[64:96, s5 * 512:(s5 + 1) * 512], kpeT[64:96, s5 * 512:(s5 + 1) * 512]), reads=[("kpeT", s5)], writes=[("kTb0", s5, 'pe')])
        for h in range(8):
            qt, qn = qTh[h % 2], "qTh0"
            kt, ktn = kTb[0], "kTb0"
            vt, vtn = Vb[0], "Vb0"
            for s5 in range(5):
                bk = 0 + s5 % 2
                mm([(lambda e, c=c: e.matmul(pb[bk][0:64, :], wukv[:, c, h * 128:h * 128 + 64], ckvT[:, c, s5 * 512:(s5 + 1) * 512], start=(c == 0), stop=(c == 1))) for c in range(2)],
                   reads=["wukv"] + CKR, writes=[pbn[bk]])
                op('act', lambda e: e.copy(kt[0:64, s5 * 512:(s5 + 1) * 512], pb[bk][0:64, :]), reads=[pbn[bk]], writes=[(ktn, s5)])
            for gi_, k0 in enumerate(range(0, NKB, 8)):
                nb_ = min(8, NKB - k0)
                bk = 6 + gi_ % 2
                fns = []
                for j_ in range(nb_):
                    kblk = k0 + j_
                    for c in range(2):
                        fns.append(lambda e, c=c, j_=j_, kblk=kblk: e.matmul(pb[bk][:, j_ * 64:(j_ + 1) * 64], ckvT[:, c, kblk * 128:(kblk + 1) * 128],
                                                                            wukv[:, c, h * 128 + 64:h * 128 + 128], start=(c == 0), stop=(c == 1)))
                mm(fns, reads=["wukv"] + CKR, writes=[pbn[bk]])
                op('dve', lambda e: e.tensor_copy(vt[:, k0:k0 + nb_, 0:64], pb[bk][:, 0:nb_ * 64].rearrange("p (j d) -> p j d", j=nb_)),
                   reads=[pbn[bk]], writes=[vtn])
            def gen_qb(sbi, h=h, qt=qt, qn=qn):
                r = sbi % 2
                ba, bb_ = 2 * r, 2 * r + 1
                rf = lambda c: cqnT[:, c, sbi * 512:(sbi + 1) * 512]
                lin(ba, wuq, wuqn, h * 96, 96, sbi, kch=3, rhs_fn=rf, extra_reads=CQR)
                yield None
                lin(bb_, wuqp, wuqpn, h * 96, 96, sbi, kch=3, rhs_fn=rf, extra_reads=CQR)
                yield None
                op('act', lambda e: e.copy(qt[0:64, sbi * 512:(sbi + 1) * 512], pb[ba][0:64, :]), reads=[pbn[ba]], writes=[(qn, sbi)])
                yield None
                op('dve', lambda e: e.tensor_tensor(t1[r][64:96, :], pb[ba][64:96, :], ropeC[64:96, sbi * 512:(sbi + 1) * 512], ALU.mult), reads=[pbn[ba], "ropeC"], writes=["t1_%d" % r])
                yield None
                op('dve', lambda e: e.tensor_tensor(t2[r][64:96, :], pb[bb_][64:96, :], ropeS[64:96, sbi * 512:(sbi + 1) * 512], ALU.mult), reads=[pbn[bb_], "ropeS"], writes=["t2_%d" % r])
                yield None
                op('pool', lambda e: e.tensor_tensor(qt[64:96, sbi * 512:(sbi + 1) * 512], t1[r][64:96, :], t2[r][64:96, :], ALU.add), reads=["t1_%d" % r, "t2_%d" % r], writes=[(qn, sbi, 'pe')])
                yield None
            run_pairs(gen_qb, NSB)
            groups = []
            MS = 96.0 ** -0.5
            for sbi in range(NSB):
                klist = [(j, None, False) for j in range(NKB)]

                def kfn(kblk, kt=kt, ktn=ktn):
                    return kt[0:104, kblk * 128:(kblk + 1) * 128], ktn

                def vfn(kblk, vt=vt, vtn=vtn):
                    return vt[:, kblk, :], vtn

                def fin(ob, h=h, sbi=sbi):
                    r = cnt['rot'] % 2
                    cnt['rot'] += 1
                    op('dve', lambda e: e.reciprocal(rr[r][64:128, :], pb[ob][64:128, :]), reads=[pbn[ob]], writes=["rr0"])
                    op('pool', lambda e: e.tensor_copy(r3[r][0:64, :], rr[r][64:128, :]), reads=["rr0"], writes=["r3_0"])
                    po = (h % 2) * 64
                    op('dve', lambda e: e.tensor_tensor(ozT[po:po + 64, h // 2, sbi * 512:(sbi + 1) * 512], pb[ob][0:64, :], r3[r][0:64, :], ALU.mult),
                       reads=[pbn[ob], "r3_0"], writes=[("ozT", h // 2, sbi)])

                groups.append(dict(qtile=qt, qname=qn, sbi=sbi, kfn=kfn, klist=klist, vfn=vfn, K=104, scale=MS,
                                   bias_fn=(lambda isctx: -MS * BIGM), fin=fin))
            attend_stream(groups)
        kb.mark('L%d B-merge' % l)
        zmul_and_merge(C_ZB, I["w_branch_b"][l], C_G + 1024, False)
        tap_big("ozB%d" % l, ozT, 4, ["ozT"])
        tap_big("mergeB%d" % l, mergeT, 8, ["mergeT"])
        if stop == "B":
            return

        kb.mark('L%d C' % l)
        kb.barrier()
        _gdn(kb, I, O, l, dict(arena=arena, pb=pb, pbn=pbn, hT=hT, ozT=ozT, HR=HR, load_w=load_w, lin=lin, ident=ident, ident32=ident32,
                               flags=flags, ones16=ones16, wukv=wukv, run_pairs=run_pairs, xn=xn, xblk=xblk, t1=t1, t2=t2, sig=sig, junk=junk, Win=Win, TAP=TAP, stat=stat, st2=st2, kvn16=kvn16, rr=rr))
        tap_big("ozC%d" % l, ozT, 4, ["ozT"])
        kb.mark('L%d C-merge' % l)
        wo = []

        def _prefetch_wo():
            for ch in range(2):
                wo.append(load_w(I["w_out"][l], 8, [(ch * 512, 512)]))
        zmul_and_merge(C_ZC, I["w_branch_c"][l], C_G + 2048, False, after_loads=_prefetch_wo)
        tap_big("mergeC%d" % l, mergeT, 8, ["mergeT"])
        if stop == "C":
            return

        kb.mark('L%d out' % l)
        MR = [("mergeT", c, s) for c in range(8) for s in range(NSB)]
        if l == 1:
            dma('sp', gbc[:].rearrange("p a b -> p (a b)"), I["final_norm_g"].partition_broadcast(128), writes=["gbc"])
        def gen_out(b):
            xb, xbn = xblk[b % 2], "xblk%d" % (b % 2)
            dma('sp', xb[:], xsrc[b * 128:(b + 1) * 128, :], reads=(["xs"] if l == 1 else []), writes=[xbn])
            yield None
            for ch in range(2):
                bk = 4 + 2 * (b % 2) + ch
                wv, wn = wo[ch]
                mm([(lambda e, c=c, wv=wv: e.matmul(pb[bk][:], mergeT[:, c, b * 128:(b + 1) * 128], wv[:, c, :], start=(c == 0), stop=(c == 7))) for c in range(8)],
                   reads=[wn] + MR, writes=[pbn[bk]])
                yield None
                tt_, ttn_ = ((t1[b % 2], "t1_%d" % (b % 2)) if ch == 0 else (t2[b % 2], "t2_%d" % (b % 2)))
                op('dve', lambda e: e.tensor_tensor(tt_[:], pb[bk][:], gateb[:, ch * 512:(ch + 1) * 512], ALU.mult), reads=[pbn[bk], ("gateb", ch)], writes=[ttn_])
                yield None
                op('pool', lambda e: e.tensor_tensor(xb[:, ch * 512:(ch + 1) * 512], xb[:, ch * 512:(ch + 1) * 512], tt_[:], ALU.add), reads=[ttn_, xbn], writes=[xbn])
                yield None
            if l == 0:
                dma('sp', xs[b * 128:(b + 1) * 128, :], xb[:], reads=[xbn], writes=["xs"])
                yield None
            else:
                op('act', lambda e: e.activation(junk[:], xb[:], AF.Square, accum_out=stat[:, b, 0:1]), reads=[xbn], writes=["junk", ("stat", b)])
                yield None
                op('dve', lambda e: e.tensor_scalar(stat[:, b, 1:2], stat[:, b, 0:1], 1.0 / D, 1e-6, ALU.mult, ALU.add), reads=[("stat", b)], writes=[("stat", b)])
                yield None
                op('act', lambda e: e.activation(stat[:, b, 2:3], stat[:, b, 1:2], AF.Ln), reads=[("stat", b)], writes=[("stat", b)])
                yield None
                op('act', lambda e: e.activation(stat[:, b, 3:4], stat[:, b, 2:3], AF.Exp, scale=-0.5), reads=[("stat", b)], writes=[("stat", b)])
                yield None
                op('dve', lambda e: e.scalar_tensor_tensor(xb[:], xb[:], stat[:, b, 3:4], gbc[:].rearrange("p a b -> p (a b)"), ALU.mult, ALU.mult), reads=[xbn, ("stat", b), "gbc"], writes=[xbn])
                yield None
                dma('sp', O["y"][b * 128:(b + 1) * 128, :], xb[:], reads=[xbn])
                yield None
        run_pairs(gen_out, NB)

def _gdn(kb, I, O, l, env):
    op, dma, mm = kb.op, kb.dma, kb.mmgroup
    arena, pb, pbn, hT, ozT, HR = env["arena"], env["pb"], env["pbn"], env["hT"], env["ozT"], env["HR"]
    load_w, lin, ident, ident32, flags = env["load_w"], env["lin"], env["ident"], env["ident32"], env["flags"]
    xblk, t1, t2, sig, junk, Win, TAP = env["xblk"], env["t1"], env["t2"], env["sig"], env["junk"], env["Win"], env["TAP"]
    st2, kvn16 = env["st2"], env["kvn16"]
    pos = [0]

    def carve(n_units, dt, shape_str=None, **kw):
        a = arena[:, pos[0]:pos[0] + n_units]
        pos[0] += n_units
        if dt == F32:
            a = a.bitcast(F32)
        if shape_str:
            a = a.rearrange(shape_str, **kw)
        return a
    qkvh = carve(3 * T, BF16, "p (c n) -> p c n", c=3)
    oacc = carve(NB * 128, BF16, "p (b d) -> p b d", b=NB)
    ab = carve(2 * NB * 16, F32, "p (b d) -> p b d", b=NB)
    names = ["g", "bt", "nbt", "gam", "ngam", "egam", "begam", "edel", "dec1", "dec2", "gt1", "gt2"]
    stt_ = {n: carve(2 * NB * 8, F32, "p (b d) -> p b d", b=NB) for n in names}
    S = carve(2 * 2 * 128, F32, "p (s d) -> p s d", s=2)
    Sbf = carve(2 * 128, BF16, "p (s d) -> p s d", s=2)
    bt_names = ["Kbg", "Kd0", "Kd1", "Vb", "Qg", "M0", "MTa", "MTb", "Ma", "Mb", "PTb", "Wt", "Wn", "U", "CT", "attnT", "Mc"]
    B = {n: carve(256, BF16, "p (s d) -> p s d", s=2) for n in bt_names}
    X = carve(512, BF16, "p (s h d) -> p s h d", s=2, h=2)
    wk_ = env["wukv"][:].rearrange("p a b -> p (a b)")
    B1 = {}
    for q_, n in enumerate(bt_names):
        if q_ < 6:
            B1[n] = wk_[:, 512 + q_ * 256:512 + (q_ + 1) * 256].rearrange("p (s d) -> p s d", s=2)
        else:
            B1[n] = carve(256, BF16, "p (s d) -> p s d", s=2)
    X1 = wk_[:, 0:512].rearrange("p (s h d) -> p s h d", s=2, h=2)
    cw = carve(2 * 36, F32, "p (c j) -> p c j", c=12)
    nw = carve(2 * 36, F32, "p (c j) -> p c j", c=12)
    pw = carve(2 * 36, F32, "p (c j) -> p c j", c=12)
    gnb = carve(2 * 128, F32)
    dtb = carve(2 * 8, F32)
    negA = carve(2 * 8, F32)
    onorm = carve(128, BF16)
    ident2 = carve(256, BF16, "p (s d) -> p s d", s=2)
    assert pos[0] <= 21952, pos[0]
    tri = t1[0][:].rearrange("p (k n) -> p k n", k=4)
    gmc = t1[1][:].rearrange("p (k s n) -> p k s n", k=2, s=2)
    Grhs = t2[0][:, 0:256].rearrange("p (s n) -> p s n", s=2)
    ones32 = t2[0][:, 256:384]
    Em = t2[1][:].rearrange("p (k s n) -> p k s n", k=2, s=2)
    accs = [xblk[0], xblk[1]]
    SETS = [dict(B=B, X=X, Grhs=Grhs, Em=Em, grn="Grhs0", emn="Em0", sx="", banks=(0, 1, 2, 3)),
            dict(B=B1, X=X1, Grhs=sig[0][:, 0:256].rearrange("p (s n) -> p s n", s=2),
                 Em=sig[1][:].rearrange("p (k s n) -> p k s n", k=2, s=2), grn="sig0", emn="sig1", sx="_1", banks=(4, 5, 6, 7))]

    for k_, nm in enumerate(["triF", "triB", "sel1", "sel2"]):
        dma('sp', tri[:, k_, :], I[nm], writes=["t1_0"])
    for k_, nm in enumerate(["gm1", "gm2"]):
        for s_ in range(2):
            dma('sp', gmc[:, k_, s_, :], I[nm][:, 4 * s_, :], writes=["t1_1"])
    op('dve', lambda e: e.memset(ones32, 1.0), writes=["ones32"])
    op('dve', lambda e: e.memset(B1["Kd0"][:], 0.0), writes=["Kd0_1"])
    op('dve', lambda e: e.memset(B1["Kd1"][:], 0.0), writes=["Kd1_1"])
    op('dve', lambda e: e.memset(B["Kd0"][:], 0.0), writes=["Kd0"])
    for s_ in range(2):
        op('dve', lambda e: e.tensor_copy(ident2[:, s_, :], ident[:]), reads=["ident"], writes=["ident2"])
    op('dve', lambda e: e.memset(B["Kd1"][:], 0.0), writes=["Kd1"])
    for j_ in range(3):
        dma('sp', cw[:, :, j_], I["gdn_conv"][l][j_].rearrange("(c p) -> p c", p=128), writes=["cw"], allow_slow_non_contiguous=True)
    dma('sp', gnb, I["gdn_norm"][l].partition_broadcast(128), writes=["gnb"])
    dma('sp', dtb, I["gdn_dt_bias"][l].partition_broadcast(128), writes=["dtb"])
    dma('sp', negA, I["gdn_a_log"][l].partition_broadcast(128), writes=["negA"])
    op('act', lambda e: e.activation(negA, negA, AF.Exp), reads=["negA"], writes=["negA"])
    op('dve', lambda e: e.tensor_scalar(negA, negA, -1.0, None, ALU.mult), reads=["negA"], writes=["negA"])
    op('dve', lambda e: e.tensor_scalar(nw, cw, flags[:, 2:3], None, ALU.mult), reads=["cw", "flags"], writes=["nw"])
    op('dve', lambda e: e.tensor_tensor(pw, cw, nw, ALU.add), reads=["cw", "nw"], writes=["pw"])

    wab, wabn = load_w(Win, 8, [(C_CA, 16)])
    for b in range(NB):
        mm([(lambda e, c=c: e.matmul(pb[0][:, b * 16:(b + 1) * 16], hT[:, c, b * 128:(b + 1) * 128], wab[:, c, :], start=(c == 0), stop=(c == 7))) for c in range(8)],
           reads=[wabn] + HR, writes=[pbn[0]])
    op('dve', lambda e: e.tensor_copy(ab, pb[0][:, 0:256].rearrange("p (b d) -> p b d", b=NB)), reads=[pbn[0]], writes=["ab"])
    g, bt, nbt, gam, ngam, egam, begam, edel, dec1, dec2, gt1, gt2 = [stt_[n] for n in names]
    bc8 = lambda a: a.unsqueeze(1).to_broadcast([128, NB, 8])
    op('dve', lambda e: e.tensor_tensor(g, ab[:, :, 0:8], bc8(dtb), ALU.add), reads=["ab", "dtb"], writes=["g"])
    op('act', lambda e: e.activation(g, g, AF.Exp), reads=["g"], writes=["g"])
    op('act', lambda e: e.activation(g, g, AF.Ln, bias=1.0), reads=["g"], writes=["g"])
    op('dve', lambda e: e.tensor_tensor(g, g, bc8(negA), ALU.mult), reads=["g", "negA"], writes=["g"])
    op('act', lambda e: e.activation(bt, ab[:, :, 8:16], AF.Sigmoid), reads=["ab"], writes=["bt"])
    op('dve', lambda e: e.tensor_scalar(nbt, bt, -1.0, None, ALU.mult), reads=["bt"], writes=["nbt"])
    for b in range(NB):
        mm([lambda e: e.matmul(pb[1][:, b * 8:b * 8 + 4], tri[:, 0, :], g[:, b, 0:4], start=True, stop=True),
            lambda e: e.matmul(pb[1][:, b * 8 + 4:b * 8 + 8], tri[:, 1, :], g[:, b, 4:8], start=True, stop=True)], reads=["t1_0", "g"], writes=[pbn[1]])
        mm([lambda e: e.matmul(pb[2][:, b * 8:b * 8 + 8], tri[:, 2, :], g[:, b, :], start=True, stop=True)], reads=["t1_0", "g"], writes=[pbn[2]])
        mm([lambda e: e.matmul(pb[3][:, b * 8:b * 8 + 8], tri[:, 3, :], g[:, b, :], start=True, stop=True)], reads=["t1_0", "g"], writes=[pbn[3]])
    v3 = lambda bank: pb[bank][:, 0:128].rearrange("p (b d) -> p b d", b=NB)
    op('dve', lambda e: e.tensor_copy(gam, v3(1)), reads=[pbn[1]], writes=["gam"])
    op('dve', lambda e: e.tensor_copy(gt1, v3(2)), reads=[pbn[2]], writes=["gt1"])
    op('dve', lambda e: e.tensor_copy(gt2, v3(3)), reads=[pbn[3]], writes=["gt2"])
    op('dve', lambda e: e.tensor_scalar(ngam, gam, -1.0, None, ALU.mult), reads=["gam"], writes=["ngam"])
    op('act', lambda e: e.activation(egam, gam, AF.Exp), reads=["gam"], writes=["egam"])
    op('dve', lambda e: e.tensor_tensor(begam, egam, bt, ALU.mult), reads=["egam", "bt"], writes=["begam"])
    op('dve', lambda e: e.tensor_tensor(edel[0:64], gt1[0:64], gam[0:64], ALU.subtract), reads=["gt1", "gam"], writes=["edel"])
    op('dve', lambda e: e.tensor_tensor(edel[64:128], gt2[64:128], gam[64:128], ALU.subtract), reads=["gt2", "gam"], writes=["edel"])
    op('act', lambda e: e.activation(edel, edel, AF.Exp), reads=["edel"], writes=["edel"])
    op('act', lambda e: e.activation(dec1, gt1, AF.Exp), reads=["gt1"], writes=["dec1"])
    op('act', lambda e: e.activation(dec2, gt2, AF.Exp), reads=["gt2"], writes=["dec2"])
    if "gstats%d" % l in TAP:
        for k_, n in enumerate(["g", "bt", "gam", "edel", "dec1", "dec2"]):
            dma('sp', TAP["gstats%d" % l][k_], stt_[n], reads=[n])

    for h in range(4):
        kb.mark('L%d C h%d conv' % (l, h))
        if h == 0:
            nxt_wc = load_w(Win, 8, [(C_CQKV + h * 128, 128), (C_CQKV + 512 + h * 128, 128), (C_CQKV + 1024 + h * 128, 128)])
        wc, wcn = nxt_wc
        for ci in range(3):
            ch = ci * 4 + h
            for sbi in range(NSB):
                lin(sbi, wc, wcn, ci * 128, 128, sbi)
            for sbi in range(NSB):
                acc = accs[sbi // 2][:, (sbi % 2) * 512:(sbi % 2 + 1) * 512]
                an = "xblk%d" % (sbi // 2)
                P_ = pb[sbi]
                rd = [pbn[sbi], "cw", "nw", "pw"]
                op('dve', lambda e: e.tensor_scalar(acc, P_[:], cw[:, ch, 1:2], None, ALU.mult), reads=rd, writes=[an])
                op('dve', lambda e: e.scalar_tensor_tensor(acc[:, 1:512], P_[:, 0:511], cw[:, ch, 0:1], acc[:, 1:512], ALU.mult, ALU.add), reads=rd + [an], writes=[an])
                op('dve', lambda e: e.scalar_tensor_tensor(acc[:, 0:511], P_[:, 1:512], cw[:, ch, 2:3], acc[:, 0:511], ALU.mult, ALU.add), reads=rd + [an], writes=[an])
                op('dve', lambda e: e.scalar_tensor_tensor(acc[:, 256:257], P_[:, 255:256], nw[:, ch, 0:1], acc[:, 256:257], ALU.mult, ALU.add), reads=rd + [an], writes=[an])
                op('dve', lambda e: e.scalar_tensor_tensor(acc[:, 255:256], P_[:, 256:257], nw[:, ch, 2:3], acc[:, 255:256], ALU.mult, ALU.add), reads=rd + [an], writes=[an])
                if sbi > 0:
                    op('dve', lambda e: e.scalar_tensor_tensor(acc[:, 0:1], pb[sbi - 1][:, 511:512], pw[:, ch, 0:1], acc[:, 0:1], ALU.mult, ALU.add),
                       reads=rd + [an, pbn[sbi - 1]], writes=[an])
                if sbi < NSB - 1:
                    op('dve', lambda e: e.scalar_tensor_tensor(acc[:, 511:512], pb[sbi + 1][:, 0:1], pw[:, ch, 2:3], acc[:, 511:512], ALU.mult, ALU.add),
                       reads=rd + [an, pbn[sbi + 1]], writes=[an])
            def gen_l2(sbi, ci=ci):
                acc = accs[sbi // 2][:, (sbi % 2) * 512:(sbi % 2 + 1) * 512]
                an = ("xblk%d" % (sbi // 2), sbi % 2)
                dst = qkvh[:, ci, sbi * 512:(sbi + 1) * 512]
                p_ = sbi % 2
                sqb, sqn = ((sig[0][:].bitcast(BF16)[:, 0:512], "sig0") if p_ == 0 else (env["rr"][0][:].bitcast(BF16)[:, 0:512], "rr0"))
                rvb, rvn = ((sig[1][:], "sig1") if p_ == 0 else (t2[1][:], "Em0"))
                bk_ = 4 + p_
                if ci == 2:
                    op('act', lambda e: e.activation(dst, acc, AF.Silu), reads=[an], writes=[("qkvh", ci, sbi)])
                    yield None
                else:
                    op('act', lambda e: e.activation(acc, acc, AF.Silu), reads=[an], writes=[an])
                    yield None
                    op('pool', lambda e: e.tensor_tensor(sqb, acc, acc, ALU.mult), reads=[an], writes=[sqn])
                    yield None
                    mm([lambda e: e.matmul(pb[bk_][:], env["ones16"][:], sqb, start=True, stop=True)], reads=[sqn, "ones16"], writes=[pbn[bk_]])
                    yield None
                    op('act', lambda e: e.activation(rvb, pb[bk_][:], AF.Ln, bias=1e-6), reads=[pbn[bk_]], writes=[rvn])
                    yield None
                    op('act', lambda e: e.activation(rvb, rvb, AF.Exp, scale=-0.5, bias=(-0.5 * float(np.log(128.0)) if ci == 0 else 0.0)), reads=[rvn], writes=[rvn])
                    yield None
                    op('dve', lambda e: e.tensor_tensor(dst, acc, rvb, ALU.mult), reads=[an, rvn], writes=[("qkvh", ci, sbi)])
                    yield None
            env["run_pairs"](gen_l2, NSB)
        if h == 0 and "qkvh%d" % l in TAP:
            for ci in range(3):
                for hf in range(2):
                    op('dve', lambda e: e.tensor_copy(xblk[0][:], qkvh[:, ci, hf * 1024:(hf + 1) * 1024]), reads=["qkvh"], writes=["xblk0"])
                    dma('sp', TAP["qkvh%d" % l][:, ci, hf * 1024:(hf + 1) * 1024], xblk[0][:], reads=["xblk0"])
        qT_, kT_, vT_ = qkvh[:, 0, :], qkvh[:, 1, :], qkvh[:, 2, :]
        QK = ["qkvh"]
        kb.mark('L%d C h%d scan' % (l, h))
        if h < 3:
            nxt_wc = load_w(Win, 8, [(C_CQKV + (h + 1) * 128, 128), (C_CQKV + 512 + (h + 1) * 128, 128), (C_CQKV + 1024 + (h + 1) * 128, 128)])
        op('pool', lambda e: e.memset(oacc, 0.0), writes=["oacc"])
        def gen_iter(i, SET):
            B, X, Grhs, Em = SET['B'], SET['X'], SET['Grhs'], SET['Em']
            grn, emn, sx = SET['grn'], SET['emn'], SET['sx']
            b0, b1, b2, b3 = SET['banks']
            N = lambda n_: n_ + sx
            slots = [(i, h, 0), (NB - 1 - i, 4 + h, 1)]
            for s_, (blk, dh, d_) in enumerate(slots):
                if i == 0:
                    dma('sp', S[:, s_, :], I["st"][l, dh], writes=[("S", s_)])
                    op('act', lambda e: e.copy(Sbf[:, s_, :], S[:, s_, :]), reads=[("S", s_)], writes=[("Sbf", s_)])
                elif i % 2 == 0:
                    op('dve', lambda e: e.tensor_scalar(S[:, s_, :], S[:, s_, :], flags[:, 0:1], None, ALU.mult), reads=[("S", s_), "flags"], writes=[("S", s_)])
                    op('act', lambda e: e.copy(Sbf[:, s_, :], S[:, s_, :]), reads=[("S", s_)], writes=[("Sbf", s_)])
            yield None
            pv = pb[b0][:].bitcast(BF16)
            fns = []
            for s_, (blk, dh, d_) in enumerate(slots):
                for k_, src in enumerate([kT_, vT_, qT_]):
                    fns.append(lambda e, s_=s_, k_=k_, src=src, blk=blk: e.transpose(pv[:, (k_ * 2 + s_) * 128:(k_ * 2 + s_ + 1) * 128], src[:, blk * 128:(blk + 1) * 128], ident[:]))
            mm(fns, reads=QK + ["ident"], writes=[pbn[b0]])
            yield None
            for s_, (blk, dh, d_) in enumerate(slots):
                for nm, k_, sc in [("Kbg", 0, begam), ("Vb", 1, bt), ("Qg", 2, egam)]:
                    op('act', lambda e: e.activation(B[nm][:, s_, :], pv[:, (k_ * 2 + s_) * 128:(k_ * 2 + s_ + 1) * 128], AF.Copy, scale=sc[:, blk, dh:dh + 1]),
                       reads=[pbn[b0], "begam", "edel", "bt", "egam"], writes=[(N(nm), s_)])
                    yield None
                for hf_ in range(2):
                    R_ = slice(hf_ * 64, (hf_ + 1) * 64)
                    op('act', lambda e: e.activation(B["Kd%d" % hf_][R_, s_, :], pv[R_, s_ * 128:(s_ + 1) * 128], AF.Copy, scale=edel[R_, blk, dh:dh + 1]),
                       reads=[pbn[b0], "edel"], writes=[(N("Kd%d" % hf_), s_)])
                    yield None
            fns = []
            for s_, (blk, dh, d_) in enumerate(slots):
                ks = kT_[:, blk * 128:(blk + 1) * 128]
                qs = qT_[:, blk * 128:(blk + 1) * 128]
                fns.append(lambda e, s_=s_, ks=ks: e.matmul(pb[b1][:, s_ * 128:(s_ + 1) * 128], ks, ks, start=True, stop=True))
                fns.append(lambda e, s_=s_, ks=ks, qs=qs: e.matmul(pb[b1][:, (2 + s_) * 128:(3 + s_) * 128], ks, qs, start=True, stop=True))
            mm(fns, reads=QK, writes=[pbn[b1]])
            yield None
            for s_, (blk, dh, d_) in enumerate(slots):
                op('dve', lambda e: e.tensor_scalar(Grhs[:, s_, :], ident32[:], ngam[:, blk, dh:dh + 1], None, ALU.mult), reads=["ident32", "ngam"], writes=[(grn, s_)])
                yield None
            mm([lambda e: e.matmul(pb[b2][:, 0:256], ones32, Grhs.rearrange("p s n -> p (s n)"), start=True, stop=True)], reads=[grn, "ones32"], writes=[pbn[b2]])
            yield None
            p2 = pb[b2][:, 0:256].rearrange("p (s n) -> p s n", s=2)
            op('dve', lambda e: e.tensor_tensor(Em[:, 0], p2, gmc[:, 0], ALU.add), reads=[pbn[b2], "t1_1"], writes=[(emn, 0)])
            yield None
            op('dve', lambda e: e.scalar_tensor_tensor(Em[:, 1], p2, -1.0, gmc[:, 1], ALU.mult, ALU.add), reads=[pbn[b2], "t1_1"], writes=[(emn, 1)])
            yield None
            for s_, (blk, dh, d_) in enumerate(slots):
                op('act', lambda e: e.activation(Em[:, 0, s_, :], Em[:, 0, s_, :], AF.Exp, bias=gam[:, blk, dh:dh + 1]), reads=[(emn, 0), "gam"], writes=[(emn, 0)])
                yield None
                op('act', lambda e: e.activation(Em[:, 1, s_, :], Em[:, 1, s_, :], AF.Exp, bias=ngam[:, blk, dh:dh + 1]), reads=[(emn, 1), "ngam"], writes=[(emn, 1)])
                yield None
                op('dve', lambda e: e.scalar_tensor_tensor(B["M0"][:, s_, :], pb[b1][:, s_ * 128:(s_ + 1) * 128], nbt[:, blk, dh:dh + 1], Em[:, 0, s_, :], ALU.mult, ALU.mult),
                   reads=[pbn[b1], "nbt", (emn, 0)], writes=[(N("M0"), s_)])
                yield None
            op('dve', lambda e: e.tensor_tensor(B["attnT"][:], pb[b1][:, 256:512].rearrange("p (s n) -> p s n", s=2), Em[:, 1], ALU.mult), reads=[pbn[b1], (emn, 1)], writes=[N("attnT")])
            yield None
            M0 = B["M0"]
            mm([lambda e, s_=s_: e.matmul(pb[b1][:, s_ * 128:(s_ + 1) * 128], M0[:, s_, :], ident[:], start=True, stop=True) for s_ in range(2)], reads=[N("M0"), "ident"], writes=[pbn[b1]])
            yield None
            op('act', lambda e: e.copy(B["MTa"][:], pb[b1][:, 0:256].rearrange("p (s n) -> p s n", s=2)), reads=[pbn[b1]], writes=[N("MTa")])
            yield None
            fns = [lambda e: e.matmul(pb[b3][:, 0:256], ident[:], ident2[:].rearrange("p s d -> p (s d)"), start=True, stop=False)]
            for s_ in range(2):
                fns.append(lambda e, s_=s_: e.matmul(pb[b3][:, s_ * 128:(s_ + 1) * 128], M0[:, s_, :], ident[:], start=False, stop=True))
            mm(fns, reads=[N("M0"), "ident", "ident2"], writes=[pbn[b3]])
            yield None
            Mprev, MTprev, Mn, MTn = "M0", "MTa", "Ma", "MTb"
            pend = None

            def pt_update(mname):
                op('act', lambda e: e.copy(B["PTb"][:], pb[b3][:, 0:256].rearrange("p (s n) -> p s n", s=2)), reads=[pbn[b3]], writes=[N("PTb")])
                mm([lambda e, s_=s_: e.matmul(pb[b3][:, s_ * 128:(s_ + 1) * 128], B[mname][:, s_, :], B["PTb"][:, s_, :], start=False, stop=True) for s_ in range(2)],
                   reads=[N(mname), N("PTb")], writes=[pbn[b3]])
            MN3 = ["Ma", "Mb", "Mc"]
            for k_ in range(1, 6):
                Mn = MN3[k_ % 3]
                mm([lambda e, s_=s_: e.matmul(pb[b2][:, s_ * 128:(s_ + 1) * 128], B[MTprev][:, s_, :], B[Mprev][:, s_, :], start=True, stop=True) for s_ in range(2)],
                   reads=[N(MTprev), N(Mprev)], writes=[pbn[b2]])
                yield None
                if k_ < 5:
                    mm([lambda e, s_=s_: e.matmul(pb[b1][:, s_ * 128:(s_ + 1) * 128], B[Mprev][:, s_, :], B[MTprev][:, s_, :], start=True, stop=True) for s_ in range(2)],
                       reads=[N(MTprev), N(Mprev)], writes=[pbn[b1]])
                    yield None
                if pend is not None:
                    pt_update(pend)
                    yield None
                op('act', lambda e: e.copy(B[Mn][:], pb[b2][:, 0:256].rearrange("p (s n) -> p s n", s=2)), reads=[pbn[b2]], writes=[N(Mn)])
                yield None
                if k_ < 5:
                    op('dve', lambda e: e.tensor_copy(B[MTn][:], pb[b1][:, 0:256].rearrange("p (s n) -> p s n", s=2)), reads=[pbn[b1]], writes=[N(MTn)])
                    yield None
                pend = Mn
                Mprev, MTprev, MTn = Mn, MTn, ("MTa" if MTn == "MTb" else "MTb")
            pt_update(pend)
            yield None
            op('act', lambda e: e.copy(B["PTb"][:], pb[b3][:, 0:256].rearrange("p (s n) -> p s n", s=2)), reads=[pbn[b3]], writes=[N("PTb")])
            yield None
            mm([lambda e, s_=s_: e.matmul(pb[b2][:, s_ * 128:(s_ + 1) * 128], M0[:, s_, :], B["PTb"][:, s_, :], start=True, stop=True) for s_ in range(2)],
               reads=[N("M0"), N("PTb")], writes=[pbn[b2]])
            yield None
            mm([lambda e, s_=s_: e.matmul(pb[b1][:, s_ * 128:(s_ + 1) * 128], B["PTb"][:, s_, :], ident[:], start=True, stop=True) for s_ in range(2)],
               reads=[N("PTb"), "ident"], writes=[pbn[b1]])
            yield None
            op('dve', lambda e: e.scalar_tensor_tensor(Em[:, 0], B["PTb"][:], -1.0, pb[b2][:, 0:256].rearrange("p (s n) -> p s n", s=2), ALU.mult, ALU.add),
               reads=[pbn[b2], N("PTb")], writes=[(emn, 0)])
            yield None
            op('act', lambda e: e.copy(B["Mb"][:], pb[b1][:, 0:256].rearrange("p (s n) -> p s n", s=2)), reads=[pbn[b1]], writes=[N("Mb")])
            yield None
            op('dve', lambda e: e.tensor_tensor(B["Ma"][:], Em[:, 0], ident2[:], ALU.add), reads=[(emn, 0), "ident2"], writes=[N("Ma")])
            yield None
            fns = [lambda e: e.matmul(pb[b3][:, 0:256], ident[:], B["PTb"][:].rearrange("p s d -> p (s d)"), start=True, stop=False)]
            for s_ in range(2):
                fns.append(lambda e, s_=s_: e.matmul(pb[b3][:, s_ * 128:(s_ + 1) * 128], B["Mb"][:, s_, :], B["Ma"][:, s_, :], start=False, stop=True))
            mm(fns, reads=[N("Mb"), N("Ma"), N("PTb"), "ident"], writes=[pbn[b3]])
            yield None
            op('act', lambda e: e.copy(B["Wt"][:], pb[b3][:, 0:256].rearrange("p (s n) -> p s n", s=2)), reads=[pbn[b3]], writes=[N("Wt")])
            yield None
            op('pool', lambda e: e.tensor_copy(B["PTb"][:], B["Wt"][:]), reads=[N("Wt")], writes=[N("PTb")])
            yield None
            AT = B["PTb"]
            fns = []
            for s_ in range(2):
                fns.append(lambda e, s_=s_: e.matmul(pb[b0][:, s_ * 128:(s_ + 1) * 128], AT[:, s_, :], B["Kbg"][:, s_, :], start=True, stop=True))
                fns.append(lambda e, s_=s_: e.matmul(pb[b0][:, (2 + s_) * 128:(3 + s_) * 128], AT[:, s_, :], B["Vb"][:, s_, :], start=True, stop=True))
            mm(fns, reads=[N("PTb"), N("Kbg"), N("Vb")], writes=[pbn[b0]])
            yield None
            p6a = pb[b0][:, 0:256].rearrange("p (s n) -> p s n", s=2)
            p6b = pb[b0][:, 256:512].rearrange("p (s n) -> p s n", s=2)
            op('act', lambda e: e.copy(B["Wt"][:], p6a), reads=[pbn[b0]], writes=[N("Wt")])
            yield None
            op('act', lambda e: e.activation(B["Wn"][:], p6a, AF.Copy, scale=-1.0), reads=[pbn[b0]], writes=[N("Wn")])
            yield None
            op('act', lambda e: e.copy(B["U"][:], p6b), reads=[pbn[b0]], writes=[N("U")])
            yield None
            fns = []
            for s_ in range(2):
                for hf in range(2):
                    fns.append(lambda e, s_=s_, hf=hf: e.matmul(pb[b0][:, (s_ * 2 + hf) * 128:(s_ * 2 + hf + 1) * 128], B["Wt"][:, s_, :], B["Kd%d" % hf][:, s_, :], start=True, stop=True))
            mm(fns, reads=[N("Wt"), N("Kd0"), N("Kd1")], writes=[pbn[b0]])
            yield None
            op('act', lambda e: e.activation(X, pb[b0][:].rearrange("p (s h n) -> p s h n", s=2, h=2), AF.Copy, scale=-1.0), reads=[pbn[b0]], writes=[N("X")])
            yield None
            fns = []
            for s_ in range(2):
                fns.append(lambda e, s_=s_: e.matmul(pb[b1][:, s_ * 128:(s_ + 1) * 128], B["Qg"][:, s_, :], ident[:], start=True, stop=False))
                fns.append(lambda e, s_=s_: e.matmul(pb[b1][:, s_ * 128:(s_ + 1) * 128], B["Wn"][:, s_, :], B["attnT"][:, s_, :], start=False, stop=True))
            mm(fns, reads=[N("Qg"), N("Wn"), N("attnT"), "ident"], writes=[pbn[b1]])
            yield None
            op('act', lambda e: e.copy(B["CT"][:], pb[b1][:, 0:256].rearrange("p (s n) -> p s n", s=2)), reads=[pbn[b1]], writes=[N("CT")])
            yield 'SCAN'
            for step in range(2):
                for s_, (blk, dh, d_) in enumerate(slots):
                    hf = step if d_ == 0 else 1 - step
                    R = slice(hf * 64, (hf + 1) * 64)
                    mm([lambda e: e.matmul(pb[b1][:, s_ * 128:(s_ + 1) * 128], B["attnT"][:, s_, :], B["U"][:, s_, :], start=True, stop=False),
                        lambda e: e.matmul(pb[b1][:, s_ * 128:(s_ + 1) * 128], B["CT"][:, s_, :], Sbf[:, s_, :], start=False, stop=True)],
                       reads=[N("attnT"), N("U"), N("CT"), ("Sbf", s_)], writes=[pbn[b1]])
                    yield None
                    op('dve', lambda e: e.tensor_tensor(oacc[R, blk, :], oacc[R, blk, :], pb[b1][R, s_ * 128:(s_ + 1) * 128], ALU.add), reads=[pbn[b1], ("oacc", blk)], writes=[("oacc", blk)])
                    yield None
                    mm([lambda e: e.matmul(pb[b2][:, s_ * 128:(s_ + 1) * 128], B["Kd%d" % hf][:, s_, :], B["U"][:, s_, :], start=True, stop=False),
                        lambda e: e.matmul(pb[b2][:, s_ * 128:(s_ + 1) * 128], X[:, s_, hf, :], Sbf[:, s_, :], start=False, stop=True)],
                       reads=[N("Kd0"), N("Kd1"), N("U"), N("X"), ("Sbf", s_)], writes=[pbn[b2]])
                    yield None
                    dec = dec1 if hf == 0 else dec2
                    op('dve', lambda e: e.scalar_tensor_tensor(S[:, s_, :], S[:, s_, :], dec[:, blk, dh:dh + 1], pb[b2][:, s_ * 128:(s_ + 1) * 128], ALU.mult, ALU.add),
                       reads=[pbn[b2], ("S", s_), "dec1", "dec2"], writes=[("S", s_)])
                    yield None
                    op('pool', lambda e: e.tensor_copy(Sbf[:, s_, :], S[:, s_, :]), reads=[("S", s_)], writes=[("Sbf", s_)])
                    yield None
            for s_, (blk, dh, d_) in enumerate(slots):
                if (d_ == 0 and blk % 2 == 1) or (d_ == 1 and blk % 2 == 0):
                    dma('sp', O["nst"][l, blk // 2, d_, h], S[:, s_, :], reads=[("S", s_)])
            yield None

        for j in range(NB // 2):
            gA = gen_iter(2 * j, SETS[0])
            gB = gen_iter(2 * j + 1, SETS[1])
            dA = dB = False
            while not (dA and dB):
                if not dA:
                    dA = (next(gA) == 'SCAN')
                if not dB:
                    dB = (next(gB) == 'SCAN')
            for _ in gA:
                pass
            for _ in gB:
                pass

        kb.mark('L%d C h%d post' % (l, h))
        for b in range(NB):
            op('act', lambda e: e.activation(junk[:, 0:128], oacc[:, b, :], AF.Square, accum_out=st2[:, b, 0:1]), reads=[("oacc", b)], writes=["junk", ("st2", b)])
        op('dve', lambda e: e.tensor_scalar(st2[:, :, 1:2], st2[:, :, 0:1], 1.0 / 128, 1e-6, ALU.mult, ALU.add), reads=["st2"], writes=["st2"])
        op('act', lambda e: e.activation(st2[:, :, 2:3], st2[:, :, 1:2], AF.Ln), reads=["st2"], writes=["st2"])
        op('act', lambda e: e.activation(st2[:, :, 3:4], st2[:, :, 2:3], AF.Exp, scale=-0.5), reads=["st2"], writes=["st2"])
        for gi_ in range(2):
            stg, stgn = ((junk, "junk") if gi_ == 0 else (env["xn"][0], "xn0"))
            for j_ in range(8):
                b = gi_ * 8 + j_
                op('dve', lambda e: e.scalar_tensor_tensor(stg[:, j_ * 128:(j_ + 1) * 128], oacc[:, b, :], st2[:, b, 3:4], gnb, ALU.mult, ALU.mult),
                   reads=[("oacc", b), "st2", "gnb"], writes=[(stgn, j_)])
            bk = 4 + gi_
            pview = pb[bk][:].bitcast(BF16)
            mm([(lambda e, j_=j_: e.transpose(pview[:, j_ * 128:(j_ + 1) * 128], stg[:, j_ * 128:(j_ + 1) * 128], ident[:])) for j_ in range(8)],
               reads=[stgn, "ident"], writes=[pbn[bk]])
            op('act', lambda e: e.copy(ozT[:, h, gi_ * 1024:(gi_ + 1) * 1024], pview[:, 0:1024]), reads=[pbn[bk]],
               writes=[("ozT", h, 2 * gi_), ("ozT", h, 2 * gi_ + 1)])


def _rope_tables(sample):
    C = np.ones((128, T), np.float32)
    S = np.zeros((128, T), np.float32)
    if not sample:
        C[96:] = 0
        return C, S
    tok = np.arange(T)
    row = (tok // 64).astype(np.float32)
    col = (tok % 64).astype(np.float32)
    def tab(rot):
        npairs = rot // 4
        inv = (10000.0 ** (-np.arange(npairs, dtype=np.float32) / npairs)).astype(np.float32)
        ang = np.concatenate([row[:, None] * inv, col[:, None] * inv], axis=-1).astype(np.float32)
        c = np.cos(ang).astype(np.float32)
        s = np.sin(ang).astype(np.float32)
        Cd = np.repeat(c, 2, axis=1).T
        Sd = np.repeat(s, 2, axis=1).T
        sign = np.where(np.arange(rot) % 2 == 0, -1.0, 1.0).astype(np.float32)[:, None]
        return Cd, Sd * sign
    Ca, Sa = tab(64)
    Cb, Sb = tab(32)
    C[0:64], S[0:64] = Ca, Sa
    C[64:96], S[64:96] = Cb, Sb
    return C, S


def _mask_a(sample):
    m = np.full((6, 128, 512), NEG, np.float32)
    kj = np.arange(128)[:, None]
    qi = np.arange(128)[None, :]
    for o in range(6):
        for qb in range(4):
            blk = m[o, :, qb * 128:(qb + 1) * 128]
            if sample:
                rel = o - 1 - qb
                if rel == 0:
                    blk[:] = 0
                elif rel == -1:
                    blk[kj >= qi] = 0
                elif rel == 1:
                    blk[kj <= qi] = 0
            else:
                if (o - 1) // 2 == qb // 2 and o >= 1:
                    blk[:] = 0
    return m


def _perm_pairs(n):
    idx = np.arange(n)
    return idx ^ 1


def kernel(**inp):
    f = lambda a: np.ascontiguousarray(np.asarray(a, dtype=np.float32))
    w_in = f(inp["w_in"])
    pcols = np.concatenate([C_AQ + _perm_pairs(512), C_AK + _perm_pairs(128), C_BKPE + _perm_pairs(32)])
    w_inp = np.ascontiguousarray(w_in[:, :, pcols])
    uq = f(inp["mla_w_uq"])
    uqcols = np.arange(768).reshape(8, 96)
    uqcols[:, 64:] = uqcols[:, 64:] ^ 1
    uqp = np.ascontiguousarray(uq[:, :, uqcols.reshape(-1)])
    shared = {k: f(inp[k]) for k in ["norm_g", "w_ada", "b_ada", "attn_sink", "mla_q_norm", "mla_kv_norm", "mla_w_ukv", "gdn_conv",
                                     "gdn_norm", "w_branch_a", "w_branch_b", "w_branch_c", "w_out", "final_norm_g"]}
    shared["w_in"] = w_in
    shared["w_inp"] = w_inp
    shared["mla_w_uq"] = uq
    shared["mla_w_uqp"] = uqp
    shared["gdn_a_log"] = f(inp["gdn_a_log"]).reshape(2, 8)
    shared["gdn_dt_bias"] = f(inp["gdn_dt_bias"]).reshape(2, 8)
    shared["ident"] = np.eye(128, dtype=np.float32)
    a = np.arange(128)
    same = (a[:, None] // 64) == (a[None, :] // 64)
    gm1 = np.full((128, 8, 128), NEG, np.float32)
    gm2 = np.full((128, 8, 128), NEG, np.float32)
    for dh in range(8):
        if dh < 4:
            gm1[:, dh][(a[:, None] > a[None, :]) & same] = 0
            gm2[:, dh][(a[None, :] >= a[:, None]) & same] = 0
        else:
            gm1[:, dh][(a[:, None] < a[None, :]) & same] = 0
            gm2[:, dh][(a[None, :] <= a[:, None]) & same] = 0
    shared["gm1"], shared["gm2"] = gm1, gm2
    shared["triF"] = ((a[:, None] <= a[None, :]) & same).astype(np.float32)
    shared["triB"] = ((a[:, None] >= a[None, :]) & same).astype(np.float32)
    shared["sel1"] = np.repeat((a < 64).astype(np.float32)[:, None], 128, 1)
    shared["sel2"] = np.repeat((a >= 64).astype(np.float32)[:, None], 128, 1)
    xp = f(inp["x_prompt"]); xsm = f(inp["x_sample"])
    in_maps = []
    for c in range(8):
        m = dict(shared)
        sample = c < 4
        if sample:
            m["x"] = xsm[c]
            m["cond"] = f(inp["c"])[c]
            m["ck"] = f(inp["cache_attn_k"])[c].reshape(2, 512, 128)
            m["cv"] = f(inp["cache_attn_v"])[c].reshape(2, 512, 128)
            m["cckv"] = f(inp["cache_mla_ckv"])[c]
            m["ckpe"] = f(inp["cache_mla_kpe"])[c]
            m["st"] = f(inp["state_gdn"])[c].reshape(2, 8, 128, 128)
            qoh = np.zeros((8, T), np.float32); qoh[0] = 1
            koh = np.zeros((8, KT), np.float32); koh[0] = BIGM
            flags = np.zeros((128, 4), np.float32); flags[:, 0] = 1.0
        else:
            k = c - 4
            m["x"] = xp[8 * k:8 * k + 8].reshape(T, D)
            m["cond"] = f(inp["c_ctx"])
            m["ck"] = np.zeros((2, 512, 128), np.float32)
            m["cv"] = np.zeros((2, 512, 128), np.float32)
            m["cckv"] = np.zeros((2, 512, 256), np.float32)
            m["ckpe"] = np.zeros((2, 512, 32), np.float32)
            m["st"] = np.zeros((2, 8, 128, 128), np.float32)
            qoh = np.zeros((8, T), np.float32); koh = np.zeros((8, KT), np.float32)
            for s in range(8):
                qoh[s, s * 256:(s + 1) * 256] = 1
                koh[s, s * 256:(s + 1) * 256] = BIGM
            flags = np.zeros((128, 4), np.float32); flags[:, 1] = NEG; flags[:, 2] = -1.0
        m["qoh"], m["koh"], m["flags"] = qoh, koh, flags
        m["ropeC"], m["ropeS"] = _rope_tables(sample)
        m["maskA"] = _mask_a(sample)
        in_maps.append({n: np.ascontiguousarray(m[n], dtype=np.float32).reshape(s) for n, s in IN_SPECS})
    nc = build()
    res = run_bass_kernel_spmd(nc, in_maps, core_ids=list(range(8)))
    R = res.results
    y_sample = np.stack([R[c]["y"] for c in range(4)], 0)
    y_prompt = np.concatenate([R[c]["y"].reshape(8, 256, D) for c in range(4, 8)], 0)
    def pc(name, tail):
        return np.concatenate([np.moveaxis(R[c][name].reshape(2, 8, 256, *tail), 0, 1) for c in range(4, 8)], 0)
    nk = pc("nk", (2, 64)); nv = pc("nv", (2, 64)); nckv = pc("nckv", (256,)); nkpe = pc("nkpe", (32,))
    nst = np.concatenate([np.moveaxis(R[c]["nst"], 0, 1) for c in range(4, 8)], 0)
    return (y_prompt.astype(np.float32), y_sample.astype(np.float32), nk.astype(np.float32), nv.astype(np.float32),
            nckv.astype(np.float32), nkpe.astype(np.float32), nst.astype(np.float32))
```

```python
import numpy as np
from contextlib import ExitStack
import concourse.bass as bass
import concourse.mybir as mybir
from concourse.bass_utils import run_bass_kernel_spmd

F32 = mybir.dt.float32
BF16 = mybir.dt.bfloat16
AF = mybir.ActivationFunctionType
ALU = mybir.AluOpType

T = 2048
NB = 16
NSB = 4
KT = 2560
NKB = 20
D = 1024
BIGM = 2048.0
NEG = -30000.0
N_DSEM = 40
LIMIT = None
LAST_KB = None
C_AQ, C_AK, C_AV, C_ZA, C_BCQ, C_BCKV, C_BKPE, C_ZB, C_CQKV, C_CA, C_CB, C_ZC, C_G = (
    0, 512, 640, 768, 1280, 1664, 1920, 1952, 2464, 4000, 4008, 4016, 4528)


class KB:
    def __init__(self, nc, es):
        self.nc = nc
        self.E = {'pe': nc.tensor, 'act': nc.scalar, 'dve': nc.vector, 'pool': nc.gpsimd, 'sp': nc.sync}
        self.sem = {e: es.enter_context(nc.semaphore("s_" + e)) for e in self.E}
        self.cnt = {e: 0 for e in self.E}
        self.seen = {e: {} for e in self.E}
        self.dsem = [es.enter_context(nc.semaphore("d%d" % i)) for i in range(N_DSEM)]
        self.dcnt = [0] * N_DSEM
        self.dnext = 0
        self.reg = {}
        self.n_ins = 0
        self.limit = LIMIT
        self.n_calls = 0

    def _wait(self, eng, tok):
        if tok is None:
            return
        key = (tok[0], tok[1])
        if self.seen[eng].get(key, 0) >= tok[2]:
            return
        if eng == 'pe' and tok[0] == 'e' and tok[1] == 'pe':
            return
        if tok[0] == 'e':
            self.E[eng].wait_ge(self.sem[tok[1]], tok[2])
        else:
            self.E[eng].wait_ge(self.dsem[tok[1]], tok[2])
        self.seen[eng][key] = tok[2]

    def _entries(self, r):
        if isinstance(r, tuple):
            name, sub = r[0], (r[1] if len(r) == 2 else r[1:])
        else:
            name, sub = r, None
        d = self.reg.setdefault(name, {})
        if sub is None:
            if None not in d:
                d[None] = [None, []]
            return [d[k] for k in d], d, None
        out = []
        if None in d:
            out.append(d[None])
        if sub not in d:
            d[sub] = [None, []]
        out.append(d[sub])
        return out, d, sub

    @staticmethod
    def _norm(reads, writes):
        r2, w2 = [], []
        for r in reads:
            nm = r[0] if isinstance(r, tuple) else r
            if nm.startswith("pb"):
                w2.append(nm)
            else:
                r2.append(r)
        for w in writes:
            nm = w[0] if isinstance(w, tuple) else w
            w2.append(nm if nm.startswith("pb") else w)
        return r2, w2

    def _deps(self, eng, reads, writes):
        for r in reads:
            for en in self._entries(r)[0]:
                self._wait(eng, en[0])
        for r in writes:
            for en in self._entries(r)[0]:
                self._wait(eng, en[0])
                for t in en[1]:
                    self._wait(eng, t)

    def _record(self, tok, reads, writes):
        for r in reads:
            _, d, sub = self._entries(r)
            lst = d[sub][1]
            if tok[0] == 'e':
                lst[:] = [t for t in lst if not (t[0] == 'e' and t[1] == tok[1])]
            lst.append(tok)
            if len(lst) > 48:
                del lst[0:len(lst) - 48]
        for r in writes:
            _, d, sub = self._entries(r)
            if sub is None:
                for k in list(d.keys()):
                    if k is not None:
                        del d[k]
            d[sub] = [tok, []]

    def op(self, eng, fn, reads=(), writes=()):
        reads, writes = self._norm(reads, writes)
        self.n_calls += 1
        if self.limit is not None and self.n_calls > self.limit:
            return None
        self._deps(eng, reads, writes)
        ins = fn(self.E[eng])
        self.cnt[eng] += 1
        ins.then_inc(self.sem[eng], 1)
        tok = ('e', eng, self.cnt[eng])
        self._record(tok, reads, writes)
        self.n_ins += 1
        return tok

    def mmgroup(self, fns, reads=(), writes=()):
        reads, writes = self._norm(reads, writes)
        self.n_calls += 1
        if self.limit is not None and self.n_calls > self.limit:
            return None
        self._deps('pe', reads, writes)
        ins = None
        for f in fns:
            ins = f(self.E['pe'])
        self.cnt['pe'] += 1
        ins.then_inc(self.sem['pe'], 1)
        tok = ('e', 'pe', self.cnt['pe'])
        self._record(tok, reads, writes)
        self.n_ins += len(fns)
        return tok

    def dma(self, q, out, in_, reads=(), writes=(), **kw):
        reads, writes = self._norm(reads, writes)
        self.n_calls += 1
        if self.limit is not None and self.n_calls > self.limit:
            return None
        self._deps(q, reads, writes)
        s = self.dnext
        self.dnext = (self.dnext + 1) % N_DSEM
        if self.dcnt[s] > 0:
            self._wait(q, ('d', s, 16 * self.dcnt[s]))
        self.dcnt[s] += 1
        self.E[q].dma_start(out=out, in_=in_, **kw).then_inc(self.dsem[s], 16)
        tok = ('d', s, 16 * self.dcnt[s])
        self._record(tok, reads, writes)
        self.n_ins += 1
        return tok

    def mark(self, label):
        self.marks = getattr(self, "marks", [])
        self.marks.append((label, dict(self.cnt)))
        global LAST_KB
        LAST_KB = self

    def barrier(self):
        for e in self.E:
            for e2 in self.E:
                if e2 != e and self.cnt[e2] > 0:
                    self._wait(e, ('e', e2, self.cnt[e2]))
            for sx in range(N_DSEM):
                if self.dcnt[sx] > 0:
                    self._wait(e, ('d', sx, 16 * self.dcnt[sx]))

    def finish(self):
        for e in self.E:
            if self.cnt[e] > 0:
                self._wait('sp', ('e', e, self.cnt[e]))
        for s in range(N_DSEM):
            if self.dcnt[s] > 0:
                self._wait('sp', ('d', s, 16 * self.dcnt[s]))


IN_SPECS = [
    ("x", [T, D]), ("cond", [D]), ("norm_g", [2, D]), ("w_ada", [2, D, 3 * D]), ("b_ada", [2, 3 * D]),
    ("w_in", [2, D, 7600]), ("w_inp", [2, D, 672]), ("attn_sink", [2, 8]), ("mla_q_norm", [2, 384]),
    ("mla_w_uq", [2, 384, 768]), ("mla_w_uqp", [2, 384, 768]), ("mla_kv_norm", [2, 256]),
    ("mla_w_ukv", [2, 256, 1024]), ("gdn_conv", [2, 3, 1536]), ("gdn_a_log", [2, 8]), ("gdn_dt_bias", [2, 8]),
    ("gdn_norm", [2, 128]), ("w_branch_a", [2, 512, D]), ("w_branch_b", [2, 512, D]), ("w_branch_c", [2, 512, D]),
    ("w_out", [2, D, D]), ("final_norm_g", [D]),
    ("ck", [2, 512, 128]), ("cv", [2, 512, 128]), ("cckv", [2, 512, 256]), ("ckpe", [2, 512, 32]),
    ("st", [2, 8, 128, 128]),
    ("ropeC", [128, T]), ("ropeS", [128, T]), ("maskA", [6, 128, 512]), ("qoh", [8, T]), ("koh", [8, KT]),
    ("flags", [128, 4]), ("ident", [128, 128]), ("gm1", [128, 8, 128]), ("gm2", [128, 8, 128]),
    ("triF", [128, 128]), ("triB", [128, 128]), ("sel1", [128, 128]), ("sel2", [128, 128]),
]
OUT_SPECS = [
    ("y", [T, D]), ("nk", [2, T, 128]), ("nv", [2, T, 128]), ("nckv", [2, T, 256]), ("nkpe", [2, T, 32]),
    ("nst", [2, 8, 2, 4, 128, 128]),
]


def build(stop=None, taps=None):
    nc = bass.Bass("TRN2", target_bir_lowering=False)
    I = {n: nc.dram_tensor(n, s, F32, kind="ExternalInput").ap() for n, s in IN_SPECS}
    O = {n: nc.dram_tensor(n, s, F32, kind="ExternalOutput").ap() for n, s in OUT_SPECS}
    xs = nc.dram_tensor("xs", [T, D], F32, kind="Internal").ap()
    TAP = {}
    if taps:
        for n, s in taps.items():
            TAP[n] = nc.dram_tensor("tap_" + n, s, F32, kind="ExternalOutput").ap()
    with ExitStack() as es:
        kb = KB(nc, es)
        SB = lambda name, shape, dt: es.enter_context(nc.sbuf_tensor("sb_" + name, shape, dt))
        PS = lambda name, shape, dt: es.enter_context(nc.psum_tensor("ps_" + name, shape, dt))
        _body(nc, kb, SB, PS, I, O, xs, TAP, stop)
        kb.mark('end')
        kb.finish()
    return nc


def _body(nc, kb, SB, PS, I, O, xs, TAP, stop):
    op, dma, mm = kb.op, kb.dma, kb.mmgroup
    ident = SB("ident", [128, 128], BF16)
    ident32 = SB("ident32", [128, 128], F32)
    ropeC = SB("ropeC", [128, T], BF16)
    ropeS = SB("ropeS", [128, T], BF16)
    flags = SB("flags", [128, 4], F32)
    ones16 = SB("ones16", [128, 128], BF16)
    op('dve', lambda e: e.memset(ones16[:], 1.0), writes=["ones16"])
    dma('pool', ident[:], I["ident"], writes=["ident"])
    dma('sp', ident32[:], I["ident"], writes=["ident32"])
    dma('pool', ropeC[:], I["ropeC"], writes=["ropeC"])
    dma('pool', ropeS[:], I["ropeS"], writes=["ropeS"])
    dma('sp', flags[:], I["flags"], writes=["flags"])

    hT = SB("hT", [128, 8, T], BF16)
    mergeT = SB("mergeT", [128, 8, T], BF16)
    ozT = SB("ozT", [128, 4, T], BF16)
    pb = [PS("pb%d" % i, [128, 512], F32) for i in range(8)]
    pbn = ["pb%d" % i for i in range(8)]

    def hTr(sb):
        return [("hT", sb * 4 + i) for i in range(4)]

    NW = 2
    WCOLS = 512
    wbuf = [SB("wbuf%d" % i, [128, 8 * WCOLS], BF16) for i in range(NW)]
    wstate = {'i': 0}

    def load_w(src2d, kch, cols, q='pool', prows=128):
        i = wstate['i']
        wstate['i'] = (i + 1) % NW
        name = "wbuf%d" % i
        tot = sum(n for _, n in cols)
        assert kch * tot <= 8 * WCOLS, (kch, tot)
        view = wbuf[i][0:prows, 0:kch * tot].rearrange("p (k n) -> p k n", k=kch)
        srcv = src2d.rearrange("(k p) n -> p k n", p=prows)
        o = 0
        for c0, n in cols:
            dma(q, view[:, :, o:o + n], srcv[:, :, c0:c0 + n], writes=[name])
            o += n
        return view, name

    def tap(name, ap_sb, reads):
        if name in TAP:
            dma('sp', TAP[name], ap_sb, reads=reads)

    condsb = SB("condsb", [128, 8], F32)
    scond = SB("scond", [128, 8], BF16)
    modfm = SB("modfm", [128, 24], F32)
    badafm = SB("badafm", [128, 24], F32)
    ngfm = SB("ngfm", [128, 8], F32)
    Afm = SB("Afm", [128, 8], F32)
    gbc = SB("gbc", [128, 8, 128], F32)
    gateb = SB("gateb", [128, D], F32)
    xblk = [SB("xblk%d" % i, [128, D], F32) for i in range(2)]
    xn = [SB("xn%d" % i, [128, D], BF16) for i in range(2)]
    junk = SB("junk", [128, D], BF16)
    stat = SB("stat", [128, NB, 4], F32)
    qTh = [SB("qTh%d" % i, [104, T], BF16) for i in range(1)] * 2
    wukv = SB("wukv", [128, 2, 1024], BF16)
    ARN = 21952
    arena = SB("arena", [128, ARN], BF16)
    maskA = arena[:, 4 * KT:4 * KT + 3072].rearrange("p (o n) -> p o n", o=6)
    o_ = 0
    kTa = arena[0:64, 0:2 * KT].rearrange("p (g n) -> p g n", g=2)
    Va = arena[:, 2 * KT:2 * KT + NKB * 256].rearrange("p (k g d) -> p k g d", k=NKB, g=2)
    kTb1 = arena[0:104, 0:KT]
    kpeT = arena[0:96, KT:2 * KT]
    Vb1 = arena[:, 2 * KT:2 * KT + NKB * 128].rearrange("p (k d) -> p k d", k=NKB)
    o_ = 2 * KT + NKB * 128
    ckvT = arena[:, o_:o_ + 2 * KT].rearrange("p (c n) -> p c n", c=2)
    cqnT = arena[:, o_ + 2 * KT:o_ + 2 * KT + 3 * T].rearrange("p (c n) -> p c n", c=3)
    kTb = [kTb1, kTb1]
    Vb = [Vb1, Vb1]
    pT = [SB("pT%d" % i, [128, 512], BF16) for i in range(3)]
    t1 = [SB("t1_%d" % i, [128, 512], F32) for i in range(2)]
    t2 = [SB("t2_%d" % i, [128, 512], F32) for i in range(2)]
    rr = [SB("rr%d" % i, [128, 512], F32) for i in range(1)] * 2
    r3 = [SB("r3_%d" % i, [64, 512], F32) for i in range(1)] * 2
    kvout = [SB("kvout%d" % i, [128, 288], F32) for i in range(2)]
    kvn16 = [SB("kvn16_%d" % i, [128, 384], BF16) for i in range(2)]
    ctx16 = SB("ctx16", [128, 4, 256], BF16)
    ctxp = SB("ctxp", [128, 4, 96], BF16)
    esink = SB("esink", [128, 8], F32)
    kvng = SB("kvng", [128, 256], F32)
    qng = SB("qng", [128, 384], F32)
    st2 = SB("st2", [128, NB, 4], F32)
    sig = [SB("sig%d" % i, [128, 512], F32) for i in range(2)]
    op('dve', lambda e: e.memset(ctxp[:], 0.0), writes=["ctxp"])
    dma('pool', qTh[0][96:104, :], I["qoh"], writes=["qTh0"])
    cnt = {'rot': 0, 'p': 0, 'o': 0}
    if stop == "c":
        return

    pT.append(SB("pT3", [128, 512], BF16))

    def attend_stream(groups):
        SBK = [2, 3, 6, 7]
        tiles = []
        for gi, g_ in enumerate(groups):
            g_['ob'] = 4 + cnt['o'] % 2
            cnt['o'] += 1
            for idx in range(len(g_['klist'])):
                tiles.append((gi, idx))
        info = {}

        def emit_S(t):
            gi, idx = tiles[t]
            g_ = groups[gi]
            kblk, mi, isctx = g_['klist'][idx]
            sbk = SBK[cnt['p'] % 4]
            pt = cnt['p'] % 4
            cnt['p'] += 1
            kap, kname = g_['kfn'](kblk)
            qtile, sbi, K = g_['qtile'], g_['sbi'], g_['K']
            fns = [lambda e: e.matmul(pb[sbk][:], kap, qtile[0:K, sbi * 512:(sbi + 1) * 512], start=True, stop=(mi is None))]
            rd = [kname, (g_['qname'], sbi)]
            if mi is not None:
                fns.append(lambda e: e.matmul(pb[sbk][:], ident[:], maskA[:, mi, :], start=False, stop=True))
                rd += ["ident", "maskA"]
            mm(fns, reads=rd, writes=[pbn[sbk]])
            b_ = g_['bias_fn'](isctx)
            op('act', lambda e: e.activation(pT[pt][:], pb[sbk][:], AF.Exp, scale=g_['scale'], bias=b_),
               reads=[pbn[sbk], "flags"], writes=["pT%d" % pt])
            info[t] = pt

        LOOK = 3
        nt = len(tiles)
        for t in range(min(LOOK, nt)):
            emit_S(t)
        for t in range(nt):
            if t + LOOK < nt:
                emit_S(t + LOOK)
            gi, idx = tiles[t]
            g_ = groups[gi]
            n = len(g_['klist'])
            kblk = g_['klist'][idx][0]
            pt = info[t]
            ob = g_['ob']
            vap, vname = g_['vfn'](kblk)
            mm([lambda e: e.matmul(pb[ob][:], vap, pT[pt][:], start=(idx == 0), stop=(idx == n - 1))],
               reads=[vname, "pT%d" % pt], writes=[pbn[ob]])
            if idx == n - 1:
                g_['fin'](ob)

    def run_pairs(genf, n):
        for b0_ in range(0, n, 2):
            gs = [genf(b0_), genf(b0_ + 1)]
            alive = [True, True]
            while any(alive):
                for q_ in range(2):
                    if alive[q_]:
                        try:
                            next(gs[q_])
                        except StopIteration:
                            alive[q_] = False

    for l in range(2):
        xsrc = I["x"] if l == 0 else xs
        Win = I["w_in"][l]
        Winp = I["w_inp"][l]
        kb.mark('L%d start' % l)
        dma('sp', condsb[:], I["cond"].rearrange("(c p) -> p c", p=128), writes=["cond"], allow_slow_non_contiguous=True)
        dma('sp', badafm[:], I["b_ada"][l].rearrange("(c p) -> p c", p=128), writes=["bada"], allow_slow_non_contiguous=True)
        dma('sp', ngfm[:], I["norm_g"][l].rearrange("(c p) -> p c", p=128), writes=["ngfm"], allow_slow_non_contiguous=True)
        op('act', lambda e: e.activation(scond[:], condsb[:], AF.Silu), reads=["cond"], writes=["scond"])
        ada_w = []
        for nt in range(6):
            if nt < 5:
                v_ = arena[:, nt * 4096:(nt + 1) * 4096].rearrange("p (k n) -> p k n", k=8)
                dma('pool', v_, I["w_ada"][l].rearrange("(k p) n -> p k n", p=128)[:, :, nt * 512:(nt + 1) * 512], writes=["mw%d" % nt])
                ada_w.append((v_, "mw%d" % nt))
            else:
                ada_w.append(load_w(I["w_ada"][l], 8, [(nt * 512, 512)]))
        for nt in range(6):
            wv, wn = ada_w[nt]
            for jj in range(4):
                j = nt * 4 + jj
                mm([(lambda e, c=c, jj=jj, j=j, wv=wv: e.matmul(pb[0][:, j:j + 1], wv[:, c, jj * 128:(jj + 1) * 128], scond[:, c:c + 1],
                                                               start=(c == 0), stop=(c == 7))) for c in range(8)],
                   reads=[wn, "scond"], writes=[(pbn[0], j)])
        op('dve', lambda e: e.tensor_tensor(modfm[:], pb[0][:, 0:24], badafm[:], ALU.add), reads=[pbn[0], "bada"], writes=["modfm"])
        op('dve', lambda e: e.scalar_tensor_tensor(Afm[:], modfm[:, 8:16], 1.0, ngfm[:], ALU.add, ALU.mult), reads=["modfm", "ngfm"], writes=["Afm"])
        op('dve', lambda e: e.tensor_copy(gbc[:], modfm[:, 16:24].unsqueeze(2).to_broadcast([128, 8, 128])), reads=["modfm"], writes=["gbc"])
        for c in range(8):
            bkc = 1 + c // 4
            mm([lambda e, c=c, bkc=bkc: e.matmul(pb[bkc][:, (c % 4) * 128:(c % 4 + 1) * 128], gbc[:, c, :], ident32[:], start=True, stop=True)],
               reads=["gbc", "ident32"], writes=[(pbn[bkc], c % 4)])
        op('act', lambda e: e.copy(gateb[:, 0:512], pb[1][:]), reads=[pbn[1]], writes=[("gateb", 0)])
        op('act', lambda e: e.copy(gateb[:, 512:1024], pb[2][:]), reads=[pbn[2]], writes=[("gateb", 1)])
        tap("modfm%d" % l, modfm[:], ["modfm"])
        if stop == "p0":
            return

        kb.mark('L%d p1' % l)
        def gen_p1(b):
            xb, xbn = xblk[b % 2], "xblk%d" % (b % 2)
            xnb, xnn = xn[b % 2], "xn%d" % (b % 2)
            dma('sp', xb[:], xsrc[b * 128:(b + 1) * 128, :], reads=(["xs"] if l == 1 else []), writes=[xbn])
            yield None
            op('act', lambda e: e.activation(junk[:], xb[:], AF.Square, accum_out=stat[:, b, 0:1]), reads=[xbn], writes=["junk", ("stat", b)])
            yield None
            op('dve', lambda e: e.tensor_scalar(stat[:, b, 1:2], stat[:, b, 0:1], 1.0 / D, 1e-6, ALU.mult, ALU.add), reads=[("stat", b)], writes=[("stat", b)])
            yield None
            op('act', lambda e: e.activation(stat[:, b, 2:3], stat[:, b, 1:2], AF.Ln), reads=[("stat", b)], writes=[("stat", b)])
            yield None
            op('act', lambda e: e.activation(stat[:, b, 3:4], stat[:, b, 2:3], AF.Exp, scale=-0.5), reads=[("stat", b)], writes=[("stat", b)])
            yield None
            op('dve', lambda e: e.tensor_scalar(xnb[:], xb[:], stat[:, b, 3:4], None, ALU.mult), reads=[xbn, ("stat", b)], writes=[xnn])
            yield None
            for half in range(2):
                bk = 4 + (2 * b + half) % 4
                pview = pb[bk][:].bitcast(BF16)
                mm([(lambda e, c=c, half=half, pview=pview: e.transpose(pview[:, c * 128:(c + 1) * 128], xnb[:, (half * 4 + c) * 128:(half * 4 + c + 1) * 128], ident[:]))
                    for c in range(4)], reads=[xnn, "ident"], writes=[pbn[bk]])
                yield None
                for c in range(4):
                    cc = half * 4 + c
                    if True:
                        op('act', lambda e, c=c, cc=cc, pview=pview: e.activation(hT[:, cc, b * 128:(b + 1) * 128], pview[:, c * 128:(c + 1) * 128], AF.Identity,
                                                                                 scale=Afm[:, cc:cc + 1], bias=modfm[:, cc:cc + 1]),
                           reads=[pbn[bk], "Afm", "modfm"], writes=[("hT", b, cc)])
                        yield None
                    else:
                        op('dve', lambda e, c=c, cc=cc, pview=pview: e.scalar_tensor_tensor(hT[:, cc, b * 128:(b + 1) * 128], pview[:, c * 128:(c + 1) * 128],
                                                                                    Afm[:, cc:cc + 1], modfm[:, cc:cc + 1].to_broadcast([128, 128]), ALU.mult, ALU.add),
                           reads=[pbn[bk], "Afm", "modfm"], writes=[("hT", b, cc)])
                        yield None
        run_pairs(gen_p1, NB)
        op('pool', lambda e: e.memset(junk[0:1, 0:1], 0.0), reads=[], writes=["hT"])
        HR = ["hT"]
        if "hT%d" % l in TAP:
            for c8 in range(8):
                op('dve', lambda e: e.tensor_copy(xblk[0][:].rearrange("p (a b) -> p a b", a=1)[:, 0, :], hT[:, c8, 0:1024]), reads=["hT"], writes=["xblk0"])
                dma('sp', TAP["hT%d" % l][:, c8, 0:1024], xblk[0][:], reads=["xblk0"])
                op('dve', lambda e: e.tensor_copy(xblk[0][:], hT[:, c8, 1024:2048]), reads=["hT"], writes=["xblk0"])
                dma('sp', TAP["hT%d" % l][:, c8, 1024:2048], xblk[0][:], reads=["xblk0"])
        if stop == "p1":
            return

        def lin(bank, wv, wn, col0, M, sbi, kch=8, rhs_fn=None, extra_reads=()):
            if rhs_fn is None:
                rhs_fn = lambda c: hT[:, c, sbi * 512:(sbi + 1) * 512]
            mm([(lambda e, c=c: e.matmul(pb[bank][0:M, :], wv[:, c, col0:col0 + M], rhs_fn(c), start=(c == 0), stop=(c == kch - 1)))
                for c in range(kch)], reads=[wn] + HR + list(extra_reads), writes=[pbn[bank]])

        kb.mark('L%d A-pre' % l)
        dma('sp', esink[:], I["attn_sink"][l].partition_broadcast(128), writes=["esink"])
        op('act', lambda e: e.activation(esink[:], esink[:], AF.Exp), reads=["esink"], writes=["esink"])
        dma('sp', kvng[:], I["mla_kv_norm"][l].partition_broadcast(128), writes=["kvng"])
        kb.barrier()
        op('dve', lambda e: e.memset(Va[:, :, :, 64:128], 1.0), writes=["Va"])
        dma('pool', maskA, I["maskA"].rearrange("o p n -> p o n"), writes=["maskA"])
        wq = arena[:, 13312:13312 + 4096].rearrange("p (k n) -> p k n", k=8)
        wqp = arena[:, 17408:17408 + 4096].rearrange("p (k n) -> p k n", k=8)
        wqn, wqpn = "mwq", "mwqp"
        wkv, wkvn = load_w(Win, 8, [(C_AK, 256)])
        wk, wkn = load_w(Win, 8, [(C_AK, 128)])
        def gen_akv(b):
            bk = 6 + b % 2
            ko = kvout[b % 2]
            kon = "kvout%d" % (b % 2)
            mm([(lambda e, c=c: e.matmul(pb[bk][:, 0:256], hT[:, c, b * 128:(b + 1) * 128], wkv[:, c, 0:256], start=(c == 0), stop=(c == 7))) for c in range(8)],
               reads=[wkvn] + HR, writes=[(pbn[bk], 0)])
            yield None
            op('act', lambda e: e.copy(ko[:, 0:256], pb[bk][:, 0:256]), reads=[(pbn[bk], 0)], writes=[kon])
            yield None
            op('dve', lambda e: e.tensor_copy(Va[:, b, :, 0:64], pb[bk][:, 128:256].rearrange("p (g d) -> p g d", g=2)), reads=[(pbn[bk], 0), kon], writes=[("Va", b)])
            yield None
            dma('sp', O["nk"][l, b * 128:(b + 1) * 128, :], ko[:, 0:128], reads=[kon])
            yield None
            dma('sp', O["nv"][l, b * 128:(b + 1) * 128, :], ko[:, 128:256], reads=[kon])
            yield None
        run_pairs(gen_akv, NB)
        dma('pool', ctx16[:, :, 0:128], I["ck"][l].rearrange("(j p) n -> p j n", p=128), writes=["ctx16"])
        for g in range(2):
            pview = pb[6 + g][:].bitcast(BF16)
            mm([(lambda e, j=j, pview=pview: e.transpose(pview[0:64, j * 128:(j + 1) * 128], ctx16[:, j, g * 64:(g + 1) * 64], ident[:])) for j in range(4)],
               reads=["ctx16", "ident"], writes=[pbn[6 + g]])
            op('dve', lambda e, pview=pview: e.tensor_copy(kTa[:, g, T:KT], pview[0:64, 0:512]), reads=[pbn[6 + g]], writes=[("kTa", g, 4)])
        for g in range(2):
            dma('pool', Va[:, NB:NKB, g, 0:64], I["cv"][l].rearrange("(j p) (g d) -> p j g d", p=128, g=2)[:, :, g, :], writes=[("Va", "ctx", g)])
        wkp, wkpn = load_w(Winp, 8, [(512, 128)])
        dma('pool', wq, Win.rearrange("(k p) n -> p k n", p=128)[:, :, C_AQ:C_AQ + 512], writes=[wqn])
        dma('pool', wqp, Winp.rearrange("(k p) n -> p k n", p=128)[:, :, 0:512], writes=[wqpn])
        for g in range(2):
            def gen_ka(sbi, g=g):
                r = sbi % 2
                ba, bb_ = 2 * r, 2 * r + 1
                lin(ba, wk, wkn, g * 64, 64, sbi)
                yield None
                lin(bb_, wkp, wkpn, g * 64, 64, sbi)
                yield None
                op('dve', lambda e: e.tensor_tensor(t1[r][0:64, :], pb[ba][0:64, :], ropeC[0:64, sbi * 512:(sbi + 1) * 512], ALU.mult), reads=[pbn[ba], "ropeC"], writes=["t1_%d" % r])
                yield None
                op('dve', lambda e: e.tensor_tensor(t2[r][0:64, :], pb[bb_][0:64, :], ropeS[0:64, sbi * 512:(sbi + 1) * 512], ALU.mult), reads=[pbn[bb_], "ropeS"], writes=["t2_%d" % r])
                yield None
                op('pool', lambda e: e.tensor_tensor(kTa[:, g, sbi * 512:(sbi + 1) * 512], t1[r][0:64, :], t2[r][0:64, :], ALU.add), reads=["t1_%d" % r, "t2_%d" % r], writes=[("kTa", g, sbi)])
                yield None
            run_pairs(gen_ka, NSB)
        kb.mark('L%d A-attn' % l)
        for h in range(8):
            g = h // 4
            qt, qn = qTh[h % 2], "qTh0"
            def gen_qa(sbi, h=h, qt=qt, qn=qn):
                r = sbi % 2
                ba, bb_ = 2 * r, 2 * r + 1
                lin(ba, wq, wqn, h * 64, 64, sbi)
                yield None
                lin(bb_, wqp, wqpn, h * 64, 64, sbi)
                yield None
                op('dve', lambda e: e.tensor_tensor(t1[r][0:64, :], pb[ba][0:64, :], ropeC[0:64, sbi * 512:(sbi + 1) * 512], ALU.mult), reads=[pbn[ba], "ropeC"], writes=["t1_%d" % r])
                yield None
                op('dve', lambda e: e.tensor_tensor(t2[r][0:64, :], pb[bb_][0:64, :], ropeS[0:64, sbi * 512:(sbi + 1) * 512], ALU.mult), reads=[pbn[bb_], "ropeS"], writes=["t2_%d" % r])
                yield None
                op('pool', lambda e: e.tensor_tensor(qt[0:64, sbi * 512:(sbi + 1) * 512], t1[r][0:64, :], t2[r][0:64, :], ALU.add), reads=["t1_%d" % r, "t2_%d" % r], writes=[(qn, sbi)])
                yield None
            run_pairs(gen_qa, NSB)
            groups = []
            for sbi in range(NSB):
                klist = []
                for o in range(6):
                    j = 4 * sbi - 1 + o
                    if 0 <= j < NB:
                        klist.append((j, o, False))
                for j in range(NB, NKB):
                    klist.append((j, None, True))

                def kfn(kblk, g=g):
                    return kTa[:, g, kblk * 128:(kblk + 1) * 128], ("kTa", g, kblk // 4)

                def vfn(kblk, g=g):
                    return Va[:, kblk, g, :], (("Va", kblk) if kblk < NB else ("Va", "ctx", g))

                def fin(ob, h=h, sbi=sbi):
                    r = cnt['rot'] % 2
                    cnt['rot'] += 1
                    op('dve', lambda e: e.tensor_scalar(rr[r][64:128, :], pb[ob][64:128, :], esink[64:128, h:h + 1], None, ALU.add), reads=[pbn[ob], "esink"], writes=["rr0"])
                    op('dve', lambda e: e.reciprocal(rr[r][64:128, :], rr[r][64:128, :]), reads=["rr0"], writes=["rr0"])
                    op('pool', lambda e: e.tensor_copy(r3[r][0:64, :], rr[r][64:128, :]), reads=["rr0"], writes=["r3_0"])
                    po = (h % 2) * 64
                    op('dve', lambda e: e.tensor_tensor(ozT[po:po + 64, h // 2, sbi * 512:(sbi + 1) * 512], pb[ob][0:64, :], r3[r][0:64, :], ALU.mult),
                       reads=[pbn[ob], "r3_0"], writes=[("ozT", h // 2, sbi)])

                groups.append(dict(qtile=qt, qname=qn, sbi=sbi, kfn=kfn, klist=klist, vfn=vfn, K=64, scale=0.125,
                                   bias_fn=(lambda isctx: (flags[:, 1:2] if isctx else 0.0)), fin=fin))
            attend_stream(groups)
        if "ozA%d" % l in TAP:
            for c8 in range(4):
                for hf in range(2):
                    op('dve', lambda e: e.tensor_copy(xblk[0][:], ozT[:, c8, hf * 1024:(hf + 1) * 1024]), reads=["ozT"], writes=["xblk0"])
                    dma('sp', TAP["ozA%d" % l][:, c8, hf * 1024:(hf + 1) * 1024], xblk[0][:], reads=["xblk0"])

        def zmul_and_merge(zcol, wbr_src, gcol, first, after_loads=None):
            kb.barrier()

            def load_into(k_, name, src2d, kch, c0, n):
                v = arena[:, k_ * 4096:k_ * 4096 + kch * n].rearrange("p (k n) -> p k n", k=kch)
                dma('pool', v, src2d.rearrange("(k p) n -> p k n", p=128)[:, :, c0:c0 + n], writes=[name])
                return v, name
            wz, wzn = load_into(0, "mw0", Win, 8, zcol, 512)
            wbs, wgs = [], []
            for ch in range(2):
                wbs.append(load_into(1 + 2 * ch, "mw%d" % (1 + 2 * ch), wbr_src, 4, ch * 512, 512))
                wgs.append(load_into(2 + 2 * ch, "mw%d" % (2 + 2 * ch), Win, 8, gcol + ch * 512, 512))
            if after_loads is not None:
                after_loads()
            for c in range(4):
                for sbi in range(NSB):
                    r = cnt['rot'] % 2
                    cnt['rot'] += 1
                    lin(r, wz, wzn, c * 128, 128, sbi)
                    op('act', lambda e: e.activation(t1[r][:], pb[r][:], AF.Silu), reads=[pbn[r]], writes=["t1_%d" % r])
                    op('pool', lambda e: e.tensor_tensor(ozT[:, c, sbi * 512:(sbi + 1) * 512], ozT[:, c, sbi * 512:(sbi + 1) * 512], t1[r][:], ALU.mult),
                       reads=["t1_%d" % r, ("ozT", c, sbi)], writes=[("ozT", c, sbi)])
            for ch in range(2):
                wb, wbn = wbs[ch]
                wg, wgn = wgs[ch]
                for cc in range(4):
                    c = ch * 4 + cc
                    for sbi in range(NSB):
                        r = cnt['rot'] % 2
                        cnt['rot'] += 1
                        lin(r, wg, wgn, cc * 128, 128, sbi)
                        op('act', lambda e: e.activation(sig[r][:], pb[r][:], AF.Sigmoid), reads=[pbn[r]], writes=["sig%d" % r])
                        lin(2 + r, wb, wbn, cc * 128, 128, sbi, kch=4, rhs_fn=lambda k: ozT[:, k, sbi * 512:(sbi + 1) * 512],
                            extra_reads=[("ozT", k, sbi) for k in range(4)])
                        dst = mergeT[:, c, sbi * 512:(sbi + 1) * 512]
                        if first:
                            op('dve', lambda e: e.tensor_tensor(dst, pb[2 + r][:], sig[r][:], ALU.mult), reads=[pbn[2 + r], "sig%d" % r], writes=[("mergeT", c, sbi)])
                        else:
                            op('dve', lambda e: e.tensor_tensor(t2[r][:], pb[2 + r][:], sig[r][:], ALU.mult), reads=[pbn[2 + r], "sig%d" % r], writes=["t2_%d" % r])
                            op('pool', lambda e: e.tensor_tensor(dst, dst, t2[r][:], ALU.add), reads=["t2_%d" % r, ("mergeT", c, sbi)], writes=[("mergeT", c, sbi)])

        kb.mark('L%d A-merge' % l)
        zmul_and_merge(C_ZA, I["w_branch_a"][l], C_G, True)

        def tap_big(nm, src, nch, rd):
            if nm in TAP:
                for c8 in range(nch):
                    for hf in range(2):
                        op('dve', lambda e: e.tensor_copy(xblk[0][:], src[:, c8, hf * 1024:(hf + 1) * 1024]), reads=rd, writes=["xblk0"])
                        dma('sp', TAP[nm][:, c8, hf * 1024:(hf + 1) * 1024], xblk[0][:], reads=["xblk0"])
        tap_big("mergeA%d" % l, mergeT, 8, ["mergeT"])
        if stop == "A":
            return

        kb.mark('L%d B-pre' % l)
        kb.barrier()
        op('dve', lambda e: e.memset(Vb1[:, :, 64:128], 1.0), writes=["Vb0"])
        dma('pool', kTb1[96:104, :], I["koh"], writes=["kTb0"])
        dma('pool', qTh[0][96:104, :], I["qoh"], writes=["qTh0"])
        wkv, wkvn = load_w(Win, 8, [(C_BCKV, 288)])
        wcq, wcqn = load_w(Win, 8, [(C_BCQ, 384)])
        dma('pool', wukv[:], I["mla_w_ukv"][l].rearrange("(k p) n -> p k n", p=128), writes=["wukv"])
        wuq = arena[:, 18944:18944 + 3 * 768].rearrange("p (k n) -> p k n", k=3)
        wuqn = "mwuq"
        dma('pool', wuq, I["mla_w_uq"][l].rearrange("(k p) n -> p k n", p=128), writes=[wuqn])
        def gen_bckv(b):
            bk = 6 + b % 2
            mm([(lambda e, c=c: e.matmul(pb[bk][:, 256:512 + 32 - 512] if False else pb[bk][:, 256:512], hT[:, c, b * 128:(b + 1) * 128], wkv[:, c, 0:256], start=(c == 0), stop=(c == 7))) for c in range(8)],
               reads=[wkvn] + HR, writes=[(pbn[bk], 1)])
            yield None
            bk2 = 0 + b % 2
            mm([(lambda e, c=c: e.matmul(pb[bk2][:, 0:32], hT[:, c, b * 128:(b + 1) * 128], wkv[:, c, 256:288], start=(c == 0), stop=(c == 7))) for c in range(8)],
               reads=[wkvn] + HR, writes=[(pbn[bk2], 0)])
            yield None
            ko2 = kvout[(b + 1) % 2]
            ko2n = "kvout%d" % ((b + 1) % 2)
            op('act', lambda e: e.activation(junk[:, 0:256], pb[bk][:, 256:512], AF.Square, accum_out=st2[:, b, 0:1]), reads=[(pbn[bk], 1)], writes=["junk", ("st2", b)])
            yield None
            op('dve', lambda e: e.tensor_scalar(st2[:, b, 1:2], st2[:, b, 0:1], 1.0 / 256, 1e-6, ALU.mult, ALU.add), reads=[("st2", b)], writes=[("st2", b)])
            yield None
            op('act', lambda e: e.activation(st2[:, b, 2:3], st2[:, b, 1:2], AF.Ln), reads=[("st2", b)], writes=[("st2", b)])
            yield None
            op('act', lambda e: e.activation(st2[:, b, 3:4], st2[:, b, 2:3], AF.Exp, scale=-0.5), reads=[("st2", b)], writes=[("st2", b)])
            yield None
            op('dve', lambda e: e.scalar_tensor_tensor(ko2[:, 0:256], pb[bk][:, 256:512], st2[:, b, 3:4], kvng[:], ALU.mult, ALU.mult),
               reads=[(pbn[bk], 1), ("st2", b), "kvng"], writes=[ko2n])
            yield None
            op('act', lambda e: e.copy(ko2[:, 256:288], pb[bk2][:, 0:32]), reads=[(pbn[bk2], 0)], writes=[ko2n])
            yield None
            dma('sp', O["nckv"][l, b * 128:(b + 1) * 128, :], ko2[:, 0:256], reads=[ko2n])
            yield None
            dma('sp', O["nkpe"][l, b * 128:(b + 1) * 128, :], ko2[:, 256:288], reads=[ko2n])
            yield None
            k16 = kvn16[b % 2]
            k16n = "kvn16_%d" % (b % 2)
            op('dve', lambda e: e.tensor_copy(k16[:, 0:256], ko2[:, 0:256]), reads=[ko2n], writes=[k16n])
            yield None
            bk3 = 2 + b % 2
            pview = pb[bk3][:].bitcast(BF16)
            mm([(lambda e, c=c, pview=pview: e.transpose(pview[:, c * 128:(c + 1) * 128], k16[:, c * 128:(c + 1) * 128], ident[:])) for c in range(2)],
               reads=[k16n, "ident"], writes=[pbn[bk3]])
            yield None
            op('dve', lambda e, pview=pview: e.tensor_copy(ckvT[:, :, b * 128:(b + 1) * 128], pview[:, 0:256].rearrange("p (c n) -> p c n", c=2)),
               reads=[pbn[bk3]], writes=[("ckvT", b)])
            yield None
        run_pairs(gen_bckv, NB)
        ctx16b = ctx16
        dma('pool', ctx16b[:], I["cckv"][l].rearrange("(j p) n -> p j n", p=128), writes=["ctx16"])
        for j in range(4):
            pview = pb[6 + j % 2][:].bitcast(BF16)
            mm([(lambda e, c=c, pview=pview: e.transpose(pview[:, c * 128:(c + 1) * 128], ctx16b[:, j, c * 128:(c + 1) * 128], ident[:])) for c in range(2)],
               reads=["ctx16", "ident"], writes=[pbn[6 + j % 2]])
            op('dve', lambda e, pview=pview: e.tensor_copy(ckvT[:, :, T + j * 128:T + (j + 1) * 128], pview[:, 0:256].rearrange("p (c n) -> p c n", c=2)),
               reads=[pbn[6 + j % 2]], writes=[("ckvT", NB + j)])
        dma('pool', ctxp[:, :, 64:96], I["ckpe"][l].rearrange("(j p) n -> p j n", p=128), writes=["ctxp"])
        pview = pb[6][:].bitcast(BF16)
        mm([(lambda e, j=j, pview=pview: e.transpose(pview[0:96, j * 128:(j + 1) * 128], ctxp[:, j, :], ident[:])) for j in range(4)],
           reads=["ctxp", "ident"], writes=[pbn[6]])
        op('dve', lambda e, pview=pview: e.tensor_copy(kpeT[64:96, T:KT], pview[64:96, 0:512]), reads=[pbn[6]], writes=[("kpeT", 4)])

        dma('sp', qng[:], I["mla_q_norm"][l].partition_broadcast(128), writes=["qng"])
        def gen_bcq(b):
            bk = 6 + b % 2
            mm([(lambda e, c=c: e.matmul(pb[bk][:, 0:384], hT[:, c, b * 128:(b + 1) * 128], wcq[:, c, 0:384], start=(c == 0), stop=(c == 7))) for c in range(8)],
               reads=[wcqn] + HR, writes=[pbn[bk]])
            yield None
            op('act', lambda e: e.activation(junk[:, 0:384], pb[bk][:, 0:384], AF.Square, accum_out=st2[:, b, 0:1]), reads=[pbn[bk]], writes=["junk", ("st2", b)])
            yield None
            op('dve', lambda e: e.tensor_scalar(st2[:, b, 1:2], st2[:, b, 0:1], 1.0 / 384, 1e-6, ALU.mult, ALU.add), reads=[("st2", b)], writes=[("st2", b)])
            yield None
            op('act', lambda e: e.activation(st2[:, b, 2:3], st2[:, b, 1:2], AF.Ln), reads=[("st2", b)], writes=[("st2", b)])
            yield None
            op('act', lambda e: e.activation(st2[:, b, 3:4], st2[:, b, 2:3], AF.Exp, scale=-0.5), reads=[("st2", b)], writes=[("st2", b)])
            yield None
            k16 = kvn16[b % 2]
            k16n = "kvn16_%d" % (b % 2)
            op('dve', lambda e: e.scalar_tensor_tensor(k16[:, 0:384], pb[bk][:, 0:384], st2[:, b, 3:4], qng[:], ALU.mult, ALU.mult),
               reads=[pbn[bk], ("st2", b), "qng"], writes=[k16n])
            yield None
            bk3 = 2 + b % 2
            pview = pb[bk3][:].bitcast(BF16)
            mm([(lambda e, c=c, pview=pview: e.transpose(pview[:, c * 128:(c + 1) * 128], k16[:, c * 128:(c + 1) * 128], ident[:])) for c in range(3)],
               reads=[k16n, "ident"], writes=[pbn[bk3]])
            yield None
            op('dve', lambda e, pview=pview: e.tensor_copy(cqnT[:, :, b * 128:(b + 1) * 128], pview[:, 0:384].rearrange("p (c n) -> p c n", c=3)),
               reads=[pbn[bk3]], writes=[("cqnT", b)])
            yield None
        run_pairs(gen_bcq, NB)
        wpe, wpen = load_w(Win, 8, [(C_BKPE - 64, 96)])
        wpep, wpepn = load_w(Winp, 8, [(640 - 64, 96)])
        for sbi in range(NSB):
            r = cnt['rot'] % 2
            cnt['rot'] += 1
            lin(0, wpe, wpen, 0, 96, sbi)
            lin(1, wpep, wpepn, 0, 96, sbi)
            op('dve', lambda e: e.tensor_tensor(t1[r][64:96, :], pb[0][64:96, :], ropeC[64:96, sbi * 512:(sbi + 1) * 512], ALU.mult), reads=[pbn[0], "ropeC"], writes=["t1_%d" % r])
            op('dve', lambda e: e.tensor_tensor(t2[r][64:96, :], pb[1][64:96, :], ropeS[64:96, sbi * 512:(sbi + 1) * 512], ALU.mult), reads=[pbn[1], "ropeS"], writes=["t2_%d" % r])
            op('pool', lambda e: e.tensor_tensor(kpeT[64:96, sbi * 512:(sbi + 1) * 512], t1[r][64:96, :], t2[r][64:96, :], ALU.add), reads=["t1_%d" % r, "t2_%d" % r], writes=[("kpeT", sbi)])
        wuqp, wuqpn = load_w(I["mla_w_uqp"][l], 3, [(0, 768)])
        kb.mark('L%d B-attn' % l)
        CQR = [("cqnT", b) for b in range(NB)]
        CKR = [("ckvT", b) for b in range(NKB)]
        for s5 in range(5):
            op('pool', lambda e: e.tensor_copy(kTb1[64:96, s5 * 512:(s5 + 1) * 512], kpeT[64:96, s5 * 512:(s5 + 1) * 512]), reads=[("kpeT", s5)], writes=[("kTb0", s5, 'pe')])
        for h in range(8):
            qt, qn = qTh[h % 2], "qTh0"
            kt, ktn = kTb[0], "kTb0"
            vt, vtn = Vb[0], "Vb0"
            for s5 in range(5):
                bk = 0 + s5 % 2
                mm([(lambda e, c=c: e.matmul(pb[bk][0:64, :], wukv[:, c, h * 128:h * 128 + 64], ckvT[:, c, s5 * 512:(s5 + 1) * 512], start=(c == 0), stop=(c == 1))) for c in range(2)],
                   reads=["wukv"] + CKR, writes=[pbn[bk]])
                op('act', lambda e: e.copy(kt[0:64, s5 * 512:(s5 + 1) * 512], pb[bk][0:64, :]), reads=[pbn[bk]], writes=[(ktn, s5)])
            for gi_, k0 in enumerate(range(0, NKB, 8)):
                nb_ = min(8, NKB - k0)
                bk = 6 + gi_ % 2
                fns = []
                for j_ in range(nb_):
                    kblk = k0 + j_
                    for c in range(2):
                        fns.append(lambda e, c=c, j_=j_, kblk=kblk: e.matmul(pb[bk][:, j_ * 64:(j_ + 1) * 64], ckvT[:, c, kblk * 128:(kblk + 1) * 128],
                                                                            wukv[:, c, h * 128 + 64:h * 128 + 128], start=(c == 0), stop=(c == 1)))
                mm(fns, reads=["wukv"] + CKR, writes=[pbn[bk]])
                op('dve', lambda e: e.tensor_copy(vt[:, k0:k0 + nb_, 0:64], pb[bk][:, 0:nb_ * 64].rearrange("p (j d) -> p j d", j=nb_)),
                   reads=[pbn[bk]], writes=[vtn])
            def gen_qb(sbi, h=h, qt=qt, qn=qn):
                r = sbi % 2
                ba, bb_ = 2 * r, 2 * r + 1
                rf = lambda c: cqnT[:, c, sbi * 512:(sbi + 1) * 512]
                lin(ba, wuq, wuqn, h * 96, 96, sbi, kch=3, rhs_fn=rf, extra_reads=CQR)
                yield None
                lin(bb_, wuqp, wuqpn, h * 96, 96, sbi, kch=3, rhs_fn=rf, extra_reads=CQR)
                yield None
                op('act', lambda e: e.copy(qt[0:64, sbi * 512:(sbi + 1) * 512], pb[ba][0:64, :]), reads=[pbn[ba]], writes=[(qn, sbi)])
                yield None
                op('dve', lambda e: e.tensor_tensor(t1[r][64:96, :], pb[ba][64:96, :], ropeC[64:96, sbi * 512:(sbi + 1) * 512], ALU.mult), reads=[pbn[ba], "ropeC"], writes=["t1_%d" % r])
                yield None
                op('dve', lambda e: e.tensor_tensor(t2[r][64:96, :], pb[bb_][64:96, :], ropeS[64:96, sbi * 512:(sbi + 1) * 512], ALU.mult), reads=[pbn[bb_], "ropeS"], writes=["t2_%d" % r])
                yield None
                op('pool', lambda e: e.tensor_tensor(qt[64:96, sbi * 512:(sbi + 1) * 512], t1[r][64:96, :], t2[r][64:96, :], ALU.add), reads=["t1_%d" % r, "t2_%d" % r], writes=[(qn, sbi, 'pe')])
                yield None
            run_pairs(gen_qb, NSB)
            groups = []
            MS = 96.0 ** -0.5
            for sbi in range(NSB):
                klist = [(j, None, False) for j in range(NKB)]

                def kfn(kblk, kt=kt, ktn=ktn):
                    return kt[0:104, kblk * 128:(kblk + 1) * 128], ktn

                def vfn(kblk, vt=vt, vtn=vtn):
                    return vt[:, kblk, :], vtn

                def fin(ob, h=h, sbi=sbi):
                    r = cnt['rot'] % 2
                    cnt['rot'] += 1
                    op('dve', lambda e: e.reciprocal(rr[r][64:128, :], pb[ob][64:128, :]), reads=[pbn[ob]], writes=["rr0"])
                    op('pool', lambda e: e.tensor_copy(r3[r][0:64, :], rr[r][64:128, :]), reads=["rr0"], writes=["r3_0"])
                    po = (h % 2) * 64
                    op('dve', lambda e: e.tensor_tensor(ozT[po:po + 64, h // 2, sbi * 512:(sbi + 1) * 512], pb[ob][0:64, :], r3[r][0:64, :], ALU.mult),
                       reads=[pbn[ob], "r3_0"], writes=[("ozT", h // 2, sbi)])

                groups.append(dict(qtile=qt, qname=qn, sbi=sbi, kfn=kfn, klist=klist, vfn=vfn, K=104, scale=MS,
                                   bias_fn=(lambda isctx: -MS * BIGM), fin=fin))
            attend_stream(groups)
        kb.mark('L%d B-merge' % l)
        zmul_and_merge(C_ZB, I["w_branch_b"][l], C_G + 1024, False)
        tap_big("ozB%d" % l, ozT, 4, ["ozT"])
        tap_big("mergeB%d" % l, mergeT, 8, ["mergeT"])
        if stop == "B":
            return

        kb.mark('L%d C' % l)
        kb.barrier()
        _gdn(kb, I, O, l, dict(arena=arena, pb=pb, pbn=pbn, hT=hT, ozT=ozT, HR=HR, load_w=load_w, lin=lin, ident=ident, ident32=ident32,
                               flags=flags, ones16=ones16, wukv=wukv, run_pairs=run_pairs, xn=xn, xblk=xblk, t1=t1, t2=t2, sig=sig, junk=junk, Win=Win, TAP=TAP, stat=stat, st2=st2, kvn16=kvn16, rr=rr))
        tap_big("ozC%d" % l, ozT, 4, ["ozT"])
        kb.mark('L%d C-merge' % l)
        wo = []

        def _prefetch_wo():
            for ch in range(2):
                wo.append(load_w(I["w_out"][l], 8, [(ch * 512, 512)]))
        zmul_and_merge(C_ZC, I["w_branch_c"][l], C_G + 2048, False, after_loads=_prefetch_wo)
        tap_big("mergeC%d" % l, mergeT, 8, ["mergeT"])
        if stop == "C":
            return

        kb.mark('L%d out' % l)
        MR = [("mergeT", c, s) for c in range(8) for s in range(NSB)]
        if l == 1:
            dma('sp', gbc[:].rearrange("p a b -> p (a b)"), I["final_norm_g"].partition_broadcast(128), writes=["gbc"])
        def gen_out(b):
            xb, xbn = xblk[b % 2], "xblk%d" % (b % 2)
            dma('sp', xb[:], xsrc[b * 128:(b + 1) * 128, :], reads=(["xs"] if l == 1 else []), writes=[xbn])
            yield None
            for ch in range(2):
                bk = 4 + 2 * (b % 2) + ch
                wv, wn = wo[ch]
                mm([(lambda e, c=c, wv=wv: e.matmul(pb[bk][:], mergeT[:, c, b * 128:(b + 1) * 128], wv[:, c, :], start=(c == 0), stop=(c == 7))) for c in range(8)],
                   reads=[wn] + MR, writes=[pbn[bk]])
                yield None
                tt_, ttn_ = ((t1[b % 2], "t1_%d" % (b % 2)) if ch == 0 else (t2[b % 2], "t2_%d" % (b % 2)))
                op('dve', lambda e: e.tensor_tensor(tt_[:], pb[bk][:], gateb[:, ch * 512:(ch + 1) * 512], ALU.mult), reads=[pbn[bk], ("gateb", ch)], writes=[ttn_])
                yield None
                op('pool', lambda e: e.tensor_tensor(xb[:, ch * 512:(ch + 1) * 512], xb[:, ch * 512:(ch + 1) * 512], tt_[:], ALU.add), reads=[ttn_, xbn], writes=[xbn])
                yield None
            if l == 0:
                dma('sp', xs[b * 128:(b + 1) * 128, :], xb[:], reads=[xbn], writes=["xs"])
                yield None
            else:
                op('act', lambda e: e.activation(junk[:], xb[:], AF.Square, accum_out=stat[:, b, 0:1]), reads=[xbn], writes=["junk", ("stat", b)])
                yield None
                op('dve', lambda e: e.tensor_scalar(stat[:, b, 1:2], stat[:, b, 0:1], 1.0 / D, 1e-6, ALU.mult, ALU.add), reads=[("stat", b)], writes=[("stat", b)])
                yield None
                op('act', lambda e: e.activation(stat[:, b, 2:3], stat[:, b, 1:2], AF.Ln), reads=[("stat", b)], writes=[("stat", b)])
                yield None
                op('act', lambda e: e.activation(stat[:, b, 3:4], stat[:, b, 2:3], AF.Exp, scale=-0.5), reads=[("stat", b)], writes=[("stat", b)])
                yield None
                op('dve', lambda e: e.scalar_tensor_tensor(xb[:], xb[:], stat[:, b, 3:4], gbc[:].rearrange("p a b -> p (a b)"), ALU.mult, ALU.mult), reads=[xbn, ("stat", b), "gbc"], writes=[xbn])
                yield None
                dma('sp', O["y"][b * 128:(b + 1) * 128, :], xb[:], reads=[xbn])
                yield None
        run_pairs(gen_out, NB)

def _gdn(kb, I, O, l, env):
    op, dma, mm = kb.op, kb.dma, kb.mmgroup
    arena, pb, pbn, hT, ozT, HR = env["arena"], env["pb"], env["pbn"], env["hT"], env["ozT"], env["HR"]
    load_w, lin, ident, ident32, flags = env["load_w"], env["lin"], env["ident"], env["ident32"], env["flags"]
    xblk, t1, t2, sig, junk, Win, TAP = env["xblk"], env["t1"], env["t2"], env["sig"], env["junk"], env["Win"], env["TAP"]
    st2, kvn16 = env["st2"], env["kvn16"]
    pos = [0]

    def carve(n_units, dt, shape_str=None, **kw):
        a = arena[:, pos[0]:pos[0] + n_units]
        pos[0] += n_units
        if dt == F32:
            a = a.bitcast(F32)
        if shape_str:
            a = a.rearrange(shape_str, **kw)
        return a
    qkvh = carve(3 * T, BF16, "p (c n) -> p c n", c=3)
    oacc = carve(NB * 128, BF16, "p (b d) -> p b d", b=NB)
    ab = carve(2 * NB * 16, F32, "p (b d) -> p b d", b=NB)
    names = ["g", "bt", "nbt", "gam", "ngam", "egam", "begam", "edel", "dec1", "dec2", "gt1", "gt2"]
    stt_ = {n: carve(2 * NB * 8, F32, "p (b d) -> p b d", b=NB) for n in names}
    S = carve(2 * 2 * 128, F32, "p (s d) -> p s d", s=2)
    Sbf = carve(2 * 128, BF16, "p (s d) -> p s d", s=2)
    bt_names = ["Kbg", "Kd0", "Kd1", "Vb", "Qg", "M0", "MTa", "MTb", "Ma", "Mb", "PTb", "Wt", "Wn", "U", "CT", "attnT", "Mc"]
    B = {n: carve(256, BF16, "p (s d) -> p s d", s=2) for n in bt_names}
    X = carve(512, BF16, "p (s h d) -> p s h d", s=2, h=2)
    wk_ = env["wukv"][:].rearrange("p a b -> p (a b)")
    B1 = {}
    for q_, n in enumerate(bt_names):
        if q_ < 6:
            B1[n] = wk_[:, 512 + q_ * 256:512 + (q_ + 1) * 256].rearrange("p (s d) -> p s d", s=2)
        else:
            B1[n] = carve(256, BF16, "p (s d) -> p s d", s=2)
    X1 = wk_[:, 0:512].rearrange("p (s h d) -> p s h d", s=2, h=2)
    cw = carve(2 * 36, F32, "p (c j) -> p c j", c=12)
    nw = carve(2 * 36, F32, "p (c j) -> p c j", c=12)
    pw = carve(2 * 36, F32, "p (c j) -> p c j", c=12)
    gnb = carve(2 * 128, F32)
    dtb = carve(2 * 8, F32)
    negA = carve(2 * 8, F32)
    onorm = carve(128, BF16)
    ident2 = carve(256, BF16, "p (s d) -> p s d", s=2)
    assert pos[0] <= 21952, pos[0]
    tri = t1[0][:].rearrange("p (k n) -> p k n", k=4)
    gmc = t1[1][:].rearrange("p (k s n) -> p k s n", k=2, s=2)
    Grhs = t2[0][:, 0:256].rearrange("p (s n) -> p s n", s=2)
    ones32 = t2[0][:, 256:384]
    Em = t2[1][:].rearrange("p (k s n) -> p k s n", k=2, s=2)
    accs = [xblk[0], xblk[1]]
    SETS = [dict(B=B, X=X, Grhs=Grhs, Em=Em, grn="Grhs0", emn="Em0", sx="", banks=(0, 1, 2, 3)),
            dict(B=B1, X=X1, Grhs=sig[0][:, 0:256].rearrange("p (s n) -> p s n", s=2),
                 Em=sig[1][:].rearrange("p (k s n) -> p k s n", k=2, s=2), grn="sig0", emn="sig1", sx="_1", banks=(4, 5, 6, 7))]

    for k_, nm in enumerate(["triF", "triB", "sel1", "sel2"]):
        dma('sp', tri[:, k_, :], I[nm], writes=["t1_0"])
    for k_, nm in enumerate(["gm1", "gm2"]):
        for s_ in range(2):
            dma('sp', gmc[:, k_, s_, :], I[nm][:, 4 * s_, :], writes=["t1_1"])
    op('dve', lambda e: e.memset(ones32, 1.0), writes=["ones32"])
    op('dve', lambda e: e.memset(B1["Kd0"][:], 0.0), writes=["Kd0_1"])
    op('dve', lambda e: e.memset(B1["Kd1"][:], 0.0), writes=["Kd1_1"])
    op('dve', lambda e: e.memset(B["Kd0"][:], 0.0), writes=["Kd0"])
    for s_ in range(2):
        op('dve', lambda e: e.tensor_copy(ident2[:, s_, :], ident[:]), reads=["ident"], writes=["ident2"])
    op('dve', lambda e: e.memset(B["Kd1"][:], 0.0), writes=["Kd1"])
    for j_ in range(3):
        dma('sp', cw[:, :, j_], I["gdn_conv"][l][j_].rearrange("(c p) -> p c", p=128), writes=["cw"], allow_slow_non_contiguous=True)
    dma('sp', gnb, I["gdn_norm"][l].partition_broadcast(128), writes=["gnb"])
    dma('sp', dtb, I["gdn_dt_bias"][l].partition_broadcast(128), writes=["dtb"])
    dma('sp', negA, I["gdn_a_log"][l].partition_broadcast(128), writes=["negA"])
    op('act', lambda e: e.activation(negA, negA, AF.Exp), reads=["negA"], writes=["negA"])
    op('dve', lambda e: e.tensor_scalar(negA, negA, -1.0, None, ALU.mult), reads=["negA"], writes=["negA"])
    op('dve', lambda e: e.tensor_scalar(nw, cw, flags[:, 2:3], None, ALU.mult), reads=["cw", "flags"], writes=["nw"])
    op('dve', lambda e: e.tensor_tensor(pw, cw, nw, ALU.add), reads=["cw", "nw"], writes=["pw"])

    wab, wabn = load_w(Win, 8, [(C_CA, 16)])
    for b in range(NB):
        mm([(lambda e, c=c: e.matmul(pb[0][:, b * 16:(b + 1) * 16], hT[:, c, b * 128:(b + 1) * 128], wab[:, c, :], start=(c == 0), stop=(c == 7))) for c in range(8)],
           reads=[wabn] + HR, writes=[pbn[0]])
    op('dve', lambda e: e.tensor_copy(ab, pb[0][:, 0:256].rearrange("p (b d) -> p b d", b=NB)), reads=[pbn[0]], writes=["ab"])
    g, bt, nbt, gam, ngam, egam, begam, edel, dec1, dec2, gt1, gt2 = [stt_[n] for n in names]
    bc8 = lambda a: a.unsqueeze(1).to_broadcast([128, NB, 8])
    op('dve', lambda e: e.tensor_tensor(g, ab[:, :, 0:8], bc8(dtb), ALU.add), reads=["ab", "dtb"], writes=["g"])
    op('act', lambda e: e.activation(g, g, AF.Exp), reads=["g"], writes=["g"])
    op('act', lambda e: e.activation(g, g, AF.Ln, bias=1.0), reads=["g"], writes=["g"])
    op('dve', lambda e: e.tensor_tensor(g, g, bc8(negA), ALU.mult), reads=["g", "negA"], writes=["g"])
    op('act', lambda e: e.activation(bt, ab[:, :, 8:16], AF.Sigmoid), reads=["ab"], writes=["bt"])
    op('dve', lambda e: e.tensor_scalar(nbt, bt, -1.0, None, ALU.mult), reads=["bt"], writes=["nbt"])
    for b in range(NB):
        mm([lambda e: e.matmul(pb[1][:, b * 8:b * 8 + 4], tri[:, 0, :], g[:, b, 0:4], start=True, stop=True),
            lambda e: e.matmul(pb[1][:, b * 8 + 4:b * 8 + 8], tri[:, 1, :], g[:, b, 4:8], start=True, stop=True)], reads=["t1_0", "g"], writes=[pbn[1]])
        mm([lambda e: e.matmul(pb[2][:, b * 8:b * 8 + 8], tri[:, 2, :], g[:, b, :], start=True, stop=True)], reads=["t1_0", "g"], writes=[pbn[2]])
        mm([lambda e: e.matmul(pb[3][:, b * 8:b * 8 + 8], tri[:, 3, :], g[:, b, :], start=True, stop=True)], reads=["t1_0", "g"], writes=[pbn[3]])
    v3 = lambda bank: pb[bank][:, 0:128].rearrange("p (b d) -> p b d", b=NB)
    op('dve', lambda e: e.tensor_copy(gam, v3(1)), reads=[pbn[1]], writes=["gam"])
    op('dve', lambda e: e.tensor_copy(gt1, v3(2)), reads=[pbn[2]], writes=["gt1"])
    op('dve', lambda e: e.tensor_copy(gt2, v3(3)), reads=[pbn[3]], writes=["gt2"])
    op('dve', lambda e: e.tensor_scalar(ngam, gam, -1.0, None, ALU.mult), reads=["gam"], writes=["ngam"])
    op('act', lambda e: e.activation(egam, gam, AF.Exp), reads=["gam"], writes=["egam"])
    op('dve', lambda e: e.tensor_tensor(begam, egam, bt, ALU.mult), reads=["egam", "bt"], writes=["begam"])
    op('dve', lambda e: e.tensor_tensor(edel[0:64], gt1[0:64], gam[0:64], ALU.subtract), reads=["gt1", "gam"], writes=["edel"])
    op('dve', lambda e: e.tensor_tensor(edel[64:128], gt2[64:128], gam[64:128], ALU.subtract), reads=["gt2", "gam"], writes=["edel"])
    op('act', lambda e: e.activation(edel, edel, AF.Exp), reads=["edel"], writes=["edel"])
    op('act', lambda e: e.activation(dec1, gt1, AF.Exp), reads=["gt1"], writes=["dec1"])
    op('act', lambda e: e.activation(dec2, gt2, AF.Exp), reads=["gt2"], writes=["dec2"])
    if "gstats%d" % l in TAP:
        for k_, n in enumerate(["g", "bt", "gam", "edel", "dec1", "dec2"]):
            dma('sp', TAP["gstats%d" % l][k_], stt_[n], reads=[n])

    for h in range(4):
        kb.mark('L%d C h%d conv' % (l, h))
        if h == 0:
            nxt_wc = load_w(Win, 8, [(C_CQKV + h * 128, 128), (C_CQKV + 512 + h * 128, 128), (C_CQKV + 1024 + h * 128, 128)])
        wc, wcn = nxt_wc
        for ci in range(3):
            ch = ci * 4 + h
            for sbi in range(NSB):
                lin(sbi, wc, wcn, ci * 128, 128, sbi)
            for sbi in range(NSB):
                acc = accs[sbi // 2][:, (sbi % 2) * 512:(sbi % 2 + 1) * 512]
                an = "xblk%d" % (sbi // 2)
                P_ = pb[sbi]
                rd = [pbn[sbi], "cw", "nw", "pw"]
                op('dve', lambda e: e.tensor_scalar(acc, P_[:], cw[:, ch, 1:2], None, ALU.mult), reads=rd, writes=[an])
                op('dve', lambda e: e.scalar_tensor_tensor(acc[:, 1:512], P_[:, 0:511], cw[:, ch, 0:1], acc[:, 1:512], ALU.mult, ALU.add), reads=rd + [an], writes=[an])
                op('dve', lambda e: e.scalar_tensor_tensor(acc[:, 0:511], P_[:, 1:512], cw[:, ch, 2:3], acc[:, 0:511], ALU.mult, ALU.add), reads=rd + [an], writes=[an])
                op('dve', lambda e: e.scalar_tensor_tensor(acc[:, 256:257], P_[:, 255:256], nw[:, ch, 0:1], acc[:, 256:257], ALU.mult, ALU.add), reads=rd + [an], writes=[an])
                op('dve', lambda e: e.scalar_tensor_tensor(acc[:, 255:256], P_[:, 256:257], nw[:, ch, 2:3], acc[:, 255:256], ALU.mult, ALU.add), reads=rd + [an], writes=[an])
                if sbi > 0:
                    op('dve', lambda e: e.scalar_tensor_tensor(acc[:, 0:1], pb[sbi - 1][:, 511:512], pw[:, ch, 0:1], acc[:, 0:1], ALU.mult, ALU.add),
                       reads=rd + [an, pbn[sbi - 1]], writes=[an])
                if sbi < NSB - 1:
                    op('dve', lambda e: e.scalar_tensor_tensor(acc[:, 511:512], pb[sbi + 1][:, 0:1], pw[:, ch, 2:3], acc[:, 511:512], ALU.mult, ALU.add),
                       reads=rd + [an, pbn[sbi + 1]], writes=[an])
            def gen_l2(sbi, ci=ci):
                acc = accs[sbi // 2][:, (sbi % 2) * 512:(sbi % 2 + 1) * 512]
                an = ("xblk%d" % (sbi // 2), sbi % 2)
                dst = qkvh[:, ci, sbi * 512:(sbi + 1) * 512]
                p_ = sbi % 2
                sqb, sqn = ((sig[0][:].bitcast(BF16)[:, 0:512], "sig0") if p_ == 0 else (env["rr"][0][:].bitcast(BF16)[:, 0:512], "rr0"))
                rvb, rvn = ((sig[1][:], "sig1") if p_ == 0 else (t2[1][:], "Em0"))
                bk_ = 4 + p_
                if ci == 2:
                    op('act', lambda e: e.activation(dst, acc, AF.Silu), reads=[an], writes=[("qkvh", ci, sbi)])
                    yield None
                else:
                    op('act', lambda e: e.activation(acc, acc, AF.Silu), reads=[an], writes=[an])
                    yield None
                    op('pool', lambda e: e.tensor_tensor(sqb, acc, acc, ALU.mult), reads=[an], writes=[sqn])
                    yield None
                    mm([lambda e: e.matmul(pb[bk_][:], env["ones16"][:], sqb, start=True, stop=True)], reads=[sqn, "ones16"], writes=[pbn[bk_]])
                    yield None
                    op('act', lambda e: e.activation(rvb, pb[bk_][:], AF.Ln, bias=1e-6), reads=[pbn[bk_]], writes=[rvn])
                    yield None
                    op('act', lambda e: e.activation(rvb, rvb, AF.Exp, scale=-0.5, bias=(-0.5 * float(np.log(128.0)) if ci == 0 else 0.0)), reads=[rvn], writes=[rvn])
                    yield None
                    op('dve', lambda e: e.tensor_tensor(dst, acc, rvb, ALU.mult), reads=[an, rvn], writes=[("qkvh", ci, sbi)])
                    yield None
            env["run_pairs"](gen_l2, NSB)
        if h == 0 and "qkvh%d" % l in TAP:
            for ci in range(3):
                for hf in range(2):
                    op('dve', lambda e: e.tensor_copy(xblk[0][:], qkvh[:, ci, hf * 1024:(hf + 1) * 1024]), reads=["qkvh"], writes=["xblk0"])
                    dma('sp', TAP["qkvh%d" % l][:, ci, hf * 1024:(hf + 1) * 1024], xblk[0][:], reads=["xblk0"])
        qT_, kT_, vT_ = qkvh[:, 0, :], qkvh[:, 1, :], qkvh[:, 2, :]
        QK = ["qkvh"]
        kb.mark('L%d C h%d scan' % (l, h))
        if h < 3:
            nxt_wc = load_w(Win, 8, [(C_CQKV + (h + 1) * 128, 128), (C_CQKV + 512 + (h + 1) * 128, 128), (C_CQKV + 1024 + (h + 1) * 128, 128)])
        op('pool', lambda e: e.memset(oacc, 0.0), writes=["oacc"])
        def gen_iter(i, SET):
            B, X, Grhs, Em = SET['B'], SET['X'], SET['Grhs'], SET['Em']
            grn, emn, sx = SET['grn'], SET['emn'], SET['sx']
            b0, b1, b2, b3 = SET['banks']
            N = lambda n_: n_ + sx
            slots = [(i, h, 0), (NB - 1 - i, 4 + h, 1)]
            for s_, (blk, dh, d_) in enumerate(slots):
                if i == 0:
                    dma('sp', S[:, s_, :], I["st"][l, dh], writes=[("S", s_)])
                    op('act', lambda e: e.copy(Sbf[:, s_, :], S[:, s_, :]), reads=[("S", s_)], writes=[("Sbf", s_)])
                elif i % 2 == 0:
                    op('dve', lambda e: e.tensor_scalar(S[:, s_, :], S[:, s_, :], flags[:, 0:1], None, ALU.mult), reads=[("S", s_), "flags"], writes=[("S", s_)])
                    op('act', lambda e: e.copy(Sbf[:, s_, :], S[:, s_, :]), reads=[("S", s_)], writes=[("Sbf", s_)])
            yield None
            pv = pb[b0][:].bitcast(BF16)
            fns = []
            for s_, (blk, dh, d_) in enumerate(slots):
                for k_, src in enumerate([kT_, vT_, qT_]):
                    fns.append(lambda e, s_=s_, k_=k_, src=src, blk=blk: e.transpose(pv[:, (k_ * 2 + s_) * 128:(k_ * 2 + s_ + 1) * 128], src[:, blk * 128:(blk + 1) * 128], ident[:]))
            mm(fns, reads=QK + ["ident"], writes=[pbn[b0]])
            yield None
            for s_, (blk, dh, d_) in enumerate(slots):
                for nm, k_, sc in [("Kbg", 0, begam), ("Vb", 1, bt), ("Qg", 2, egam)]:
                    op('act', lambda e: e.activation(B[nm][:, s_, :], pv[:, (k_ * 2 + s_) * 128:(k_ * 2 + s_ + 1) * 128], AF.Copy, scale=sc[:, blk, dh:dh + 1]),
                       reads=[pbn[b0], "begam", "edel", "bt", "egam"], writes=[(N(nm), s_)])
                    yield None
                for hf_ in range(2):
                    R_ = slice(hf_ * 64, (hf_ + 1) * 64)
                    op('act', lambda e: e.activation(B["Kd%d" % hf_][R_, s_, :], pv[R_, s_ * 128:(s_ + 1) * 128], AF.Copy, scale=edel[R_, blk, dh:dh + 1]),
                       reads=[pbn[b0], "edel"], writes=[(N("Kd%d" % hf_), s_)])
                    yield None
            fns = []
            for s_, (blk, dh, d_) in enumerate(slots):
                ks = kT_[:, blk * 128:(blk + 1) * 128]
                qs = qT_[:, blk * 128:(blk + 1) * 128]
                fns.append(lambda e, s_=s_, ks=ks: e.matmul(pb[b1][:, s_ * 128:(s_ + 1) * 128], ks, ks, start=True, stop=True))
                fns.append(lambda e, s_=s_, ks=ks, qs=qs: e.matmul(pb[b1][:, (2 + s_) * 128:(3 + s_) * 128], ks, qs, start=True, stop=True))
            mm(fns, reads=QK, writes=[pbn[b1]])
            yield None
            for s_, (blk, dh, d_) in enumerate(slots):
                op('dve', lambda e: e.tensor_scalar(Grhs[:, s_, :], ident32[:], ngam[:, blk, dh:dh + 1], None, ALU.mult), reads=["ident32", "ngam"], writes=[(grn, s_)])
                yield None
            mm([lambda e: e.matmul(pb[b2][:, 0:256], ones32, Grhs.rearrange("p s n -> p (s n)"), start=True, stop=True)], reads=[grn, "ones32"], writes=[pbn[b2]])
            yield None
            p2 = pb[b2][:, 0:256].rearrange("p (s n) -> p s n", s=2)
            op('dve', lambda e: e.tensor_tensor(Em[:, 0], p2, gmc[:, 0], ALU.add), reads=[pbn[b2], "t1_1"], writes=[(emn, 0)])
            yield None
            op('dve', lambda e: e.scalar_tensor_tensor(Em[:, 1], p2, -1.0, gmc[:, 1], ALU.mult, ALU.add), reads=[pbn[b2], "t1_1"], writes=[(emn, 1)])
            yield None
            for s_, (blk, dh, d_) in enumerate(slots):
                op('act', lambda e: e.activation(Em[:, 0, s_, :], Em[:, 0, s_, :], AF.Exp, bias=gam[:, blk, dh:dh + 1]), reads=[(emn, 0), "gam"], writes=[(emn, 0)])
                yield None
                op('act', lambda e: e.activation(Em[:, 1, s_, :], Em[:, 1, s_, :], AF.Exp, bias=ngam[:, blk, dh:dh + 1]), reads=[(emn, 1), "ngam"], writes=[(emn, 1)])
                yield None
                op('dve', lambda e: e.scalar_tensor_tensor(B["M0"][:, s_, :], pb[b1][:, s_ * 128:(s_ + 1) * 128], nbt[:, blk, dh:dh + 1], Em[:, 0, s_, :], ALU.mult, ALU.mult),
                   reads=[pbn[b1], "nbt", (emn, 0)], writes=[(N("M0"), s_)])
                yield None
            op('dve', lambda e: e.tensor_tensor(B["attnT"][:], pb[b1][:, 256:512].rearrange("p (s n) -> p s n", s=2), Em[:, 1], ALU.mult), reads=[pbn[b1], (emn, 1)], writes=[N("attnT")])
            yield None
            M0 = B["M0"]
            mm([lambda e, s_=s_: e.matmul(pb[b1][:, s_ * 128:(s_ + 1) * 128], M0[:, s_, :], ident[:], start=True, stop=True) for s_ in range(2)], reads=[N("M0"), "ident"], writes=[pbn[b1]])
            yield None
            op('act', lambda e: e.copy(B["MTa"][:], pb[b1][:, 0:256].rearrange("p (s n) -> p s n", s=2)), reads=[pbn[b1]], writes=[N("MTa")])
            yield None
            fns = [lambda e: e.matmul(pb[b3][:, 0:256], ident[:], ident2[:].rearrange("p s d -> p (s d)"), start=True, stop=False)]
            for s_ in range(2):
                fns.append(lambda e, s_=s_: e.matmul(pb[b3][:, s_ * 128:(s_ + 1) * 128], M0[:, s_, :], ident[:], start=False, stop=True))
            mm(fns, reads=[N("M0"), "ident", "ident2"], writes=[pbn[b3]])
            yield None
            Mprev, MTprev, Mn, MTn = "M0", "MTa", "Ma", "MTb"
            pend = None

            def pt_update(mname):
                op('act', lambda e: e.copy(B["PTb"][:], pb[b3][:, 0:256].rearrange("p (s n) -> p s n", s=2)), reads=[pbn[b3]], writes=[N("PTb")])
                mm([lambda e, s_=s_: e.matmul(pb[b3][:, s_ * 128:(s_ + 1) * 128], B[mname][:, s_, :], B["PTb"][:, s_, :], start=False, stop=True) for s_ in range(2)],
                   reads=[N(mname), N("PTb")], writes=[pbn[b3]])
            MN3 = ["Ma", "Mb", "Mc"]
            for k_ in range(1, 6):
                Mn = MN3[k_ % 3]
                mm([lambda e, s_=s_: e.matmul(pb[b2][:, s_ * 128:(s_ + 1) * 128], B[MTprev][:, s_, :], B[Mprev][:, s_, :], start=True, stop=True) for s_ in range(2)],
                   reads=[N(MTprev), N(Mprev)], writes=[pbn[b2]])
                yield None
                if k_ < 5:
                    mm([lambda e, s_=s_: e.matmul(pb[b1][:, s_ * 128:(s_ + 1) * 128], B[Mprev][:, s_, :], B[MTprev][:, s_, :], start=True, stop=True) for s_ in range(2)],
                       reads=[N(MTprev), N(Mprev)], writes=[pbn[b1]])
                    yield None
                if pend is not None:
                    pt_update(pend)
                    yield None
                op('act', lambda e: e.copy(B[Mn][:], pb[b2][:, 0:256].rearrange("p (s n) -> p s n", s=2)), reads=[pbn[b2]], writes=[N(Mn)])
                yield None
                if k_ < 5:
                    op('dve', lambda e: e.tensor_copy(B[MTn][:], pb[b1][:, 0:256].rearrange("p (s n) -> p s n", s=2)), reads=[pbn[b1]], writes=[N(MTn)])
                    yield None
                pend = Mn
                Mprev, MTprev, MTn = Mn, MTn, ("MTa" if MTn == "MTb" else "MTb")
            pt_update(pend)
            yield None
            op('act', lambda e: e.copy(B["PTb"][:], pb[b3][:, 0:256].rearrange("p (s n) -> p s n", s=2)), reads=[pbn[b3]], writes=[N("PTb")])
            yield None
            mm([lambda e, s_=s_: e.matmul(pb[b2][:, s_ * 128:(s_ + 1) * 128], M0[:, s_, :], B["PTb"][:, s_, :], start=True, stop=True) for s_ in range(2)],
               reads=[N("M0"), N("PTb")], writes=[pbn[b2]])
            yield None
            mm([lambda e, s_=s_: e.matmul(pb[b1][:, s_ * 128:(s_ + 1) * 128], B["PTb"][:, s_, :], ident[:], start=True, stop=True) for s_ in range(2)],
               reads=[N("PTb"), "ident"], writes=[pbn[b1]])
            yield None
            op('dve', lambda e: e.scalar_tensor_tensor(Em[:, 0], B["PTb"][:], -1.0, pb[b2][:, 0:256].rearrange("p (s n) -> p s n", s=2), ALU.mult, ALU.add),
               reads=[pbn[b2], N("PTb")], writes=[(emn, 0)])
            yield None
            op('act', lambda e: e.copy(B["Mb"][:], pb[b1][:, 0:256].rearrange("p (s n) -> p s n", s=2)), reads=[pbn[b1]], writes=[N("Mb")])
            yield None
            op('dve', lambda e: e.tensor_tensor(B["Ma"][:], Em[:, 0], ident2[:], ALU.add), reads=[(emn, 0), "ident2"], writes=[N("Ma")])
            yield None
            fns = [lambda e: e.matmul(pb[b3][:, 0:256], ident[:], B["PTb"][:].rearrange("p s d -> p (s d)"), start=True, stop=False)]
            for s_ in range(2):
                fns.append(lambda e, s_=s_: e.matmul(pb[b3][:, s_ * 128:(s_ + 1) * 128], B["Mb"][:, s_, :], B["Ma"][:, s_, :], start=False, stop=True))
            mm(fns, reads=[N("Mb"), N("Ma"), N("PTb"), "ident"], writes=[pbn[b3]])
            yield None
            op('act', lambda e: e.copy(B["Wt"][:], pb[b3][:, 0:256].rearrange("p (s n) -> p s n", s=2)), reads=[pbn[b3]], writes=[N("Wt")])
            yield None
            op('pool', lambda e: e.tensor_copy(B["PTb"][:], B["Wt"][:]), reads=[N("Wt")], writes=[N("PTb")])
            yield None
            AT = B["PTb"]
            fns = []
            for s_ in range(2):
                fns.append(lambda e, s_=s_: e.matmul(pb[b0][:, s_ * 128:(s_ + 1) * 128], AT[:, s_, :], B["Kbg"][:, s_, :], start=True, stop=True))
                fns.append(lambda e, s_=s_: e.matmul(pb[b0][:, (2 + s_) * 128:(3 + s_) * 128], AT[:, s_, :], B["Vb"][:, s_, :], start=True, stop=True))
            mm(fns, reads=[N("PTb"), N("Kbg"), N("Vb")], writes=[pbn[b0]])
            yield None
            p6a = pb[b0][:, 0:256].rearrange("p (s n) -> p s n", s=2)
            p6b = pb[b0][:, 256:512].rearrange("p (s n) -> p s n", s=2)
            op('act', lambda e: e.copy(B["Wt"][:], p6a), reads=[pbn[b0]], writes=[N("Wt")])
            yield None
            op('act', lambda e: e.activation(B["Wn"][:], p6a, AF.Copy, scale=-1.0), reads=[pbn[b0]], writes=[N("Wn")])
            yield None
            op('act', lambda e: e.copy(B["U"][:], p6b), reads=[pbn[b0]], writes=[N("U")])
            yield None
            fns = []
            for s_ in range(2):
                for hf in range(2):
                    fns.append(lambda e, s_=s_, hf=hf: e.matmul(pb[b0][:, (s_ * 2 + hf) * 128:(s_ * 2 + hf + 1) * 128], B["Wt"][:, s_, :], B["Kd%d" % hf][:, s_, :], start=True, stop=True))
            mm(fns, reads=[N("Wt"), N("Kd0"), N("Kd1")], writes=[pbn[b0]])
            yield None
            op('act', lambda e: e.activation(X, pb[b0][:].rearrange("p (s h n) -> p s h n", s=2, h=2), AF.Copy, scale=-1.0), reads=[pbn[b0]], writes=[N("X")])
            yield None
            fns = []
            for s_ in range(2):
                fns.append(lambda e, s_=s_: e.matmul(pb[b1][:, s_ * 128:(s_ + 1) * 128], B["Qg"][:, s_, :], ident[:], start=True, stop=False))
                fns.append(lambda e, s_=s_: e.matmul(pb[b1][:, s_ * 128:(s_ + 1) * 128], B["Wn"][:, s_, :], B["attnT"][:, s_, :], start=False, stop=True))
            mm(fns, reads=[N("Qg"), N("Wn"), N("attnT"), "ident"], writes=[pbn[b1]])
            yield None
            op('act', lambda e: e.copy(B["CT"][:], pb[b1][:, 0:256].rearrange("p (s n) -> p s n", s=2)), reads=[pbn[b1]], writes=[N("CT")])
            yield 'SCAN'
            for step in range(2):
                for s_, (blk, dh, d_) in enumerate(slots):
                    hf = step if d_ == 0 else 1 - step
                    R = slice(hf * 64, (hf + 1) * 64)
                    mm([lambda e: e.matmul(pb[b1][:, s_ * 128:(s_ + 1) * 128], B["attnT"][:, s_, :], B["U"][:, s_, :], start=True, stop=False),
                        lambda e: e.matmul(pb[b1][:, s_ * 128:(s_ + 1) * 128], B["CT"][:, s_, :], Sbf[:, s_, :], start=False, stop=True)],
                       reads=[N("attnT"), N("U"), N("CT"), ("Sbf", s_)], writes=[pbn[b1]])
                    yield None
                    op('dve', lambda e: e.tensor_tensor(oacc[R, blk, :], oacc[R, blk, :], pb[b1][R, s_ * 128:(s_ + 1) * 128], ALU.add), reads=[pbn[b1], ("oacc", blk)], writes=[("oacc", blk)])
                    yield None
                    mm([lambda e: e.matmul(pb[b2][:, s_ * 128:(s_ + 1) * 128], B["Kd%d" % hf][:, s_, :], B["U"][:, s_, :], start=True, stop=False),
                        lambda e: e.matmul(pb[b2][:, s_ * 128:(s_ + 1) * 128], X[:, s_, hf, :], Sbf[:, s_, :], start=False, stop=True)],
                       reads=[N("Kd0"), N("Kd1"), N("U"), N("X"), ("Sbf", s_)], writes=[pbn[b2]])
                    yield None
                    dec = dec1 if hf == 0 else dec2
                    op('dve', lambda e: e.scalar_tensor_tensor(S[:, s_, :], S[:, s_, :], dec[:, blk, dh:dh + 1], pb[b2][:, s_ * 128:(s_ + 1) * 128], ALU.mult, ALU.add),
                       reads=[pbn[b2], ("S", s_), "dec1", "dec2"], writes=[("S", s_)])
                    yield None
                    op('pool', lambda e: e.tensor_copy(Sbf[:, s_, :], S[:, s_, :]), reads=[("S", s_)], writes=[("Sbf", s_)])
                    yield None
            for s_, (blk, dh, d_) in enumerate(slots):
                if (d_ == 0 and blk % 2 == 1) or (d_ == 1 and blk % 2 == 0):
                    dma('sp', O["nst"][l, blk // 2, d_, h], S[:, s_, :], reads=[("S", s_)])
            yield None

        for j in range(NB // 2):
            gA = gen_iter(2 * j, SETS[0])
            gB = gen_iter(2 * j + 1, SETS[1])
            dA = dB = False
            while not (dA and dB):
                if not dA:
                    dA = (next(gA) == 'SCAN')
                if not dB:
                    dB = (next(gB) == 'SCAN')
            for _ in gA:
                pass
            for _ in gB:
                pass

        kb.mark('L%d C h%d post' % (l, h))
        for b in range(NB):
            op('act', lambda e: e.activation(junk[:, 0:128], oacc[:, b, :], AF.Square, accum_out=st2[:, b, 0:1]), reads=[("oacc", b)], writes=["junk", ("st2", b)])
        op('dve', lambda e: e.tensor_scalar(st2[:, :, 1:2], st2[:, :, 0:1], 1.0 / 128, 1e-6, ALU.mult, ALU.add), reads=["st2"], writes=["st2"])
        op('act', lambda e: e.activation(st2[:, :, 2:3], st2[:, :, 1:2], AF.Ln), reads=["st2"], writes=["st2"])
        op('act', lambda e: e.activation(st2[:, :, 3:4], st2[:, :, 2:3], AF.Exp, scale=-0.5), reads=["st2"], writes=["st2"])
        for gi_ in range(2):
            stg, stgn = ((junk, "junk") if gi_ == 0 else (env["xn"][0], "xn0"))
            for j_ in range(8):
                b = gi_ * 8 + j_
                op('dve', lambda e: e.scalar_tensor_tensor(stg[:, j_ * 128:(j_ + 1) * 128], oacc[:, b, :], st2[:, b, 3:4], gnb, ALU.mult, ALU.mult),
                   reads=[("oacc", b), "st2", "gnb"], writes=[(stgn, j_)])
            bk = 4 + gi_
            pview = pb[bk][:].bitcast(BF16)
            mm([(lambda e, j_=j_: e.transpose(pview[:, j_ * 128:(j_ + 1) * 128], stg[:, j_ * 128:(j_ + 1) * 128], ident[:])) for j_ in range(8)],
               reads=[stgn, "ident"], writes=[pbn[bk]])
            op('act', lambda e: e.copy(ozT[:, h, gi_ * 1024:(gi_ + 1) * 1024], pview[:, 0:1024]), reads=[pbn[bk]],
               writes=[("ozT", h, 2 * gi_), ("ozT", h, 2 * gi_ + 1)])


def _rope_tables(sample):
    C = np.ones((128, T), np.float32)
    S = np.zeros((128, T), np.float32)
    if not sample:
        C[96:] = 0
        return C, S
    tok = np.arange(T)
    row = (tok // 64).astype(np.float32)
    col = (tok % 64).astype(np.float32)
    def tab(rot):
        npairs = rot // 4
        inv = (10000.0 ** (-np.arange(npairs, dtype=np.float32) / npairs)).astype(np.float32)
        ang = np.concatenate([row[:, None] * inv, col[:, None] * inv], axis=-1).astype(np.float32)
        c = np.cos(ang).astype(np.float32)
        s = np.sin(ang).astype(np.float32)
        Cd = np.repeat(c, 2, axis=1).T
        Sd = np.repeat(s, 2, axis=1).T
        sign = np.where(np.arange(rot) % 2 == 0, -1.0, 1.0).astype(np.float32)[:, None]
        return Cd, Sd * sign
    Ca, Sa = tab(64)
    Cb, Sb = tab(32)
    C[0:64], S[0:64] = Ca, Sa
    C[64:96], S[64:96] = Cb, Sb
    return C, S


def _mask_a(sample):
    m = np.full((6, 128, 512), NEG, np.float32)
    kj = np.arange(128)[:, None]
    qi = np.arange(128)[None, :]
    for o in range(6):
        for qb in range(4):
            blk = m[o, :, qb * 128:(qb + 1) * 128]
            if sample:
                rel = o - 1 - qb
                if rel == 0:
                    blk[:] = 0
                elif rel == -1:
                    blk[kj >= qi] = 0
                elif rel == 1:
                    blk[kj <= qi] = 0
            else:
                if (o - 1) // 2 == qb // 2 and o >= 1:
                    blk[:] = 0
    return m


def _perm_pairs(n):
    idx = np.arange(n)
    return idx ^ 1


def kernel(**inp):
    f = lambda a: np.ascontiguousarray(np.asarray(a, dtype=np.float32))
    w_in = f(inp["w_in"])
    pcols = np.concatenate([C_AQ + _perm_pairs(512), C_AK + _perm_pairs(128), C_BKPE + _perm_pairs(32)])
    w_inp = np.ascontiguousarray(w_in[:, :, pcols])
    uq = f(inp["mla_w_uq"])
    uqcols = np.arange(768).reshape(8, 96)
    uqcols[:, 64:] = uqcols[:, 64:] ^ 1
    uqp = np.ascontiguousarray(uq[:, :, uqcols.reshape(-1)])
    shared = {k: f(inp[k]) for k in ["norm_g", "w_ada", "b_ada", "attn_sink", "mla_q_norm", "mla_kv_norm", "mla_w_ukv", "gdn_conv",
                                     "gdn_norm", "w_branch_a", "w_branch_b", "w_branch_c", "w_out", "final_norm_g"]}
    shared["w_in"] = w_in
    shared["w_inp"] = w_inp
    shared["mla_w_uq"] = uq
    shared["mla_w_uqp"] = uqp
    shared["gdn_a_log"] = f(inp["gdn_a_log"]).reshape(2, 8)
    shared["gdn_dt_bias"] = f(inp["gdn_dt_bias"]).reshape(2, 8)
    shared["ident"] = np.eye(128, dtype=np.float32)
    a = np.arange(128)
    same = (a[:, None] // 64) == (a[None, :] // 64)
    gm1 = np.full((128, 8, 128), NEG, np.float32)
    gm2 = np.full((128, 8, 128), NEG, np.float32)
    for dh in range(8):
        if dh < 4:
            gm1[:, dh][(a[:, None] > a[None, :]) & same] = 0
            gm2[:, dh][(a[None, :] >= a[:, None]) & same] = 0
        else:
            gm1[:, dh][(a[:, None] < a[None, :]) & same] = 0
            gm2[:, dh][(a[None, :] <= a[:, None]) & same] = 0
    shared["gm1"], shared["gm2"] = gm1, gm2
    shared["triF"] = ((a[:, None] <= a[None, :]) & same).astype(np.float32)
    shared["triB"] = ((a[:, None] >= a[None, :]) & same).astype(np.float32)
    shared["sel1"] = np.repeat((a < 64).astype(np.float32)[:, None], 128, 1)
    shared["sel2"] = np.repeat((a >= 64).astype(np.float32)[:, None], 128, 1)
    xp = f(inp["x_prompt"]); xsm = f(inp["x_sample"])
    in_maps = []
    for c in range(8):
        m = dict(shared)
        sample = c < 4
        if sample:
            m["x"] = xsm[c]
            m["cond"] = f(inp["c"])[c]
            m["ck"] = f(inp["cache_attn_k"])[c].reshape(2, 512, 128)
            m["cv"] = f(inp["cache_attn_v"])[c].reshape(2, 512, 128)
            m["cckv"] = f(inp["cache_mla_ckv"])[c]
            m["ckpe"] = f(inp["cache_mla_kpe"])[c]
            m["st"] = f(inp["state_gdn"])[c].reshape(2, 8, 128, 128)
            qoh = np.zeros((8, T), np.float32); qoh[0] = 1
            koh = np.zeros((8, KT), np.float32); koh[0] = BIGM
            flags = np.zeros((128, 4), np.float32); flags[:, 0] = 1.0
        else:
            k = c - 4
            m["x"] = xp[8 * k:8 * k + 8].reshape(T, D)
            m["cond"] = f(inp["c_ctx"])
            m["ck"] = np.zeros((2, 512, 128), np.float32)
            m["cv"] = np.zeros((2, 512, 128), np.float32)
            m["cckv"] = np.zeros((2, 512, 256), np.float32)
            m["ckpe"] = np.zeros((2, 512, 32), np.float32)
            m["st"] = np.zeros((2, 8, 128, 128), np.float32)
            qoh = np.zeros((8, T), np.float32); koh = np.zeros((8, KT), np.float32)
            for s in range(8):
                qoh[s, s * 256:(s + 1) * 256] = 1
                koh[s, s * 256:(s + 1) * 256] = BIGM
            flags = np.zeros((128, 4), np.float32); flags[:, 1] = NEG; flags[:, 2] = -1.0
        m["qoh"], m["koh"], m["flags"] = qoh, koh, flags
        m["ropeC"], m["ropeS"] = _rope_tables(sample)
        m["maskA"] = _mask_a(sample)
        in_maps.append({n: np.ascontiguousarray(m[n], dtype=np.float32).reshape(s) for n, s in IN_SPECS})
    nc = build()
    res = run_bass_kernel_spmd(nc, in_maps, core_ids=list(range(8)))
    R = res.results
    y_sample = np.stack([R[c]["y"] for c in range(4)], 0)
    y_prompt = np.concatenate([R[c]["y"].reshape(8, 256, D) for c in range(4, 8)], 0)
    def pc(name, tail):
        return np.concatenate([np.moveaxis(R[c][name].reshape(2, 8, 256, *tail), 0, 1) for c in range(4, 8)], 0)
    nk = pc("nk", (2, 64)); nv = pc("nv", (2, 64)); nckv = pc("nckv", (256,)); nkpe = pc("nkpe", (32,))
    nst = np.concatenate([np.moveaxis(R[c]["nst"], 0, 1) for c in range(4, 8)], 0)
    return (y_prompt.astype(np.float32), y_sample.astype(np.float32), nk.astype(np.float32), nv.astype(np.float32),
            nckv.astype(np.float32), nkpe.astype(np.float32), nst.astype(np.float32))
```
